# Optimizing a Trainium2 kernel written in Bass

```python
import jax, jax.numpy as jnp
from jax import lax
import numpy as np

D_MODEL = 1024
BATCH = 8
SEQ = 2048
DEPTH = 4
DEC_BATCH = 128
DEC_SEQ = 8
PAST_LEN = 16384
PAGE_SIZE = 128

PLE_DIM = 256
N_MIXERS = 3
EPS = 1e-6
D_FF = ((8 * D_MODEL + 3 * 256 - 1) // (3 * 256)) * 256
RW_HEAD = 64
RW_HEADS = D_MODEL // RW_HEAD
RW_LORA_W = 64
RW_LORA_A = 64
RW_LORA_V = 32
RW_LORA_G = 160
RW_GN_EPS = 64e-5
S5_GROUP = 16
S5_GROUPS = D_MODEL // S5_GROUP
S5_STATE = 64
M_INNER = 2 * D_MODEL
M_HEADDIM = 64
M_HEADS = M_INNER // M_HEADDIM
M_STATE = 128
M_GROUPS = 4
M_CONV = 4
M_CHUNK = 64
M_CONV_DIM = M_INNER + 2 * M_GROUPS * M_STATE

kernel_name = 'hybrid_rwkv7_s5_mamba2_step'


def rmsnorm(x, g):
    xf = x.astype(jnp.float32)
    y = xf * lax.rsqrt(jnp.mean(xf * xf, axis=-1, keepdims=True) + EPS)
    return (y * g.astype(jnp.float32)).astype(x.dtype)


def wkv7_scan(r, w, k, v, a, b, s0):
    def step(s, inp):
        r_t, w_t, k_t, v_t, a_t, b_t = inp
        sa = jnp.einsum('bhij,bhj->bhi', s, a_t)
        s = (s * w_t[:, :, None, :] + sa[..., None] * b_t[:, :, None, :]
             + v_t[..., None] * k_t[:, :, None, :])
        return s, jnp.einsum('bhij,bhj->bhi', s, r_t)
    seq = tuple(jnp.moveaxis(t.astype(jnp.float32), 1, 0) for t in (r, w, k, v, a, b))
    s_fin, out = lax.scan(step, s0.astype(jnp.float32), seq)
    return jnp.moveaxis(out, 0, 1), s_fin


def rwkv7_mix(x, shift_prev, wkv_prev, v_first, mu, w0, w1, w2, a0, a1, a2, g1, g2,
              k_k, k_a, r_k, w_rkv, w_o, lnx_w, lnx_b, v_lora=None):
    bsz, L, D = x.shape
    H, N = RW_HEADS, RW_HEAD
    x_prev = jnp.concatenate([shift_prev[:, None].astype(x.dtype), x[:, :-1]], axis=1)
    xx = x_prev - x
    xr, xw, xk, xv, xa, xg = (x + xx * mu[j] for j in range(6))
    r = xr @ w_rkv[0]
    k = xk @ w_rkv[1]
    v = xv @ w_rkv[2]
    w_log = -jax.nn.softplus(-(w0 + jnp.tanh(xw @ w1) @ w2)) - 0.5
    decay = jnp.exp(-jnp.exp(w_log.astype(jnp.float32)))
    if v_first is None:
        v_first = v
    else:
        v0, v1, v2 = v_lora
        v = v + (v_first - v) * jax.nn.sigmoid(v0 + (xv @ v1) @ v2)
    a = jax.nn.sigmoid(a0 + (xa @ a1) @ a2)
    g = jax.nn.sigmoid(xg @ g1) @ g2
    heads = lambda t: t.reshape(bsz, L, H, N)
    kk = heads(k * k_k).astype(jnp.float32)
    kk = kk * lax.rsqrt(jnp.maximum(jnp.sum(kk * kk, -1, keepdims=True), 1e-24))
    k = k * (1 + (a - 1) * k_a)
    r_h, k_h, v_h, a_h = heads(r), heads(k), heads(v), heads(a)
    out, wkv_new = wkv7_scan(r_h, heads(decay), k_h, v_h, -kk, kk * a_h, wkv_prev)
    mean = jnp.mean(out, -1, keepdims=True)
    var = jnp.mean(jnp.square(out - mean), -1, keepdims=True)
    o = ((out - mean) * lax.rsqrt(var + RW_GN_EPS)).reshape(bsz, L, D) * lnx_w + lnx_b
    bonus = (jnp.sum(r_h * k_h * r_k, -1, keepdims=True) * v_h).reshape(bsz, L, D)
    y = ((o + bonus) * g) @ w_o
    return y.astype(x.dtype), x[:, -1], wkv_new, v_first


def _complex_affine_combine(e1, e2):
    a1r, a1i, b1r, b1i = e1
    a2r, a2i, b2r, b2i = e2
    return (a2r * a1r - a2i * a1i, a2r * a1i + a2i * a1r,
            a2r * b1r - a2i * b1i + b2r, a2r * b1i + a2i * b1r + b2i)


def s5_mix(x, h_re0, h_im0, a_re, a_im, log_dt, b_re, b_im, c_re, c_im, d_skip, glu_v, glu_g):
    bsz, L, D = x.shape
    f32 = jnp.float32
    dt = jnp.exp(log_dt.astype(f32))[:, None]
    lr, li = a_re.astype(f32), a_im.astype(f32)
    mag = jnp.exp(lr * dt)
    ab_re, ab_im = mag * jnp.cos(li * dt), mag * jnp.sin(li * dt)
    den = lr * lr + li * li
    q_re = ((ab_re - 1.0) * lr + ab_im * li) / den
    q_im = (ab_im * lr - (ab_re - 1.0) * li) / den
    br, bi = b_re.astype(f32), b_im.astype(f32)
    bb_re = q_re[..., None] * br - q_im[..., None] * bi
    bb_im = q_re[..., None] * bi + q_im[..., None] * br
    u = x.astype(f32).reshape(bsz, L, S5_GROUPS, S5_GROUP)
    bu_re = jnp.einsum('blgh,gph->blgp', u, bb_re)
    bu_im = jnp.einsum('blgh,gph->blgp', u, bb_im)
    h_re0 = h_re0.astype(f32)
    h_im0 = h_im0.astype(f32)
    bu_re = bu_re.at[:, 0].add(ab_re * h_re0 - ab_im * h_im0)
    bu_im = bu_im.at[:, 0].add(ab_re * h_im0 + ab_im * h_re0)
    a_seq_re = jnp.broadcast_to(ab_re, (1, L) + ab_re.shape)
    a_seq_im = jnp.broadcast_to(ab_im, (1, L) + ab_im.shape)
    _, _, h_re, h_im = lax.associative_scan(
        _complex_affine_combine, (a_seq_re, a_seq_im, bu_re, bu_im), axis=1)
    y = (jnp.einsum('blgp,ghp->blgh', h_re, c_re.astype(f32))
         - jnp.einsum('blgp,ghp->blgh', h_im, c_im.astype(f32)))
    y = y.reshape(bsz, L, D) + d_skip * x.astype(f32)
    g = jax.nn.gelu(y)
    out = (g @ glu_v) * jax.nn.sigmoid(g @ glu_g)
    return out.astype(x.dtype), h_re[:, -1], h_im[:, -1]


def segsum(x):
    T = x.shape[-1]
    xr = jnp.broadcast_to(x[..., :, None], x.shape + (T,))
    strict = jnp.tril(jnp.ones((T, T), bool), -1)
    cs = jnp.cumsum(jnp.where(strict, xr, 0.0), axis=-2)
    return jnp.where(jnp.tril(jnp.ones((T, T), bool)), cs, -jnp.inf)


def ssd_chunked(X, A, Bm, Cm, s0):
    b, L, H, P = X.shape
    G, N = Bm.shape[2], Bm.shape[3]
    R = H // G
    T = min(M_CHUNK, L)
    Lp = -(-L // T) * T
    pad = Lp - L
    if pad:
        padw = lambda t: jnp.pad(t, [(0, 0), (0, pad)] + [(0, 0)] * (t.ndim - 2))
        X, A, Bm, Cm = padw(X), padw(A), padw(Bm), padw(Cm)
    c = Lp // T
    Xc = X.reshape(b, c, T, G, R, P)
    Ac = A.reshape(b, c, T, G, R).transpose(0, 1, 3, 4, 2)
    Bc = Bm.reshape(b, c, T, G, N)
    Cc = Cm.reshape(b, c, T, G, N)
    A_cum = jnp.cumsum(Ac, axis=-1)
    Lm = jnp.exp(segsum(Ac))
    CB = jnp.einsum('bclgn,bcsgn->bcgls', Cc, Bc)
    y_diag = jnp.einsum('bcgls,bcgrls,bcsgrp->bclgrp', CB, Lm, Xc)
    decay_states = jnp.exp(A_cum[..., -1:] - A_cum)
    states = jnp.einsum('bclgn,bcgrl,bclgrp->bcgrpn', Bc, decay_states, Xc)
    states = jnp.concatenate([s0.astype(states.dtype).reshape(b, 1, G, R, P, N), states], axis=1)
    chunk_a = jnp.pad(A_cum[..., -1], ((0, 0), (1, 0), (0, 0), (0, 0)))
    decay_chunk = jnp.exp(segsum(jnp.moveaxis(chunk_a, 1, -1)))
    new_states = jnp.einsum('bgrzc,bcgrpn->bzgrpn', decay_chunk, states)
    prev_states, final = new_states[:, :-1], new_states[:, -1]
    y_off = jnp.einsum('bclgn,bcgrpn,bcgrl->bclgrp', Cc, prev_states, jnp.exp(A_cum))
    y = (y_diag + y_off).reshape(b, Lp, H, P)[:, :L]
    return y, final.reshape(b, H, P, N)


def mamba2_mix(x, conv_prev, ssm_prev, in_proj, conv_w, conv_b, dt_bias, a_log, d_skip, norm_w, out_proj):
    bsz, L, _ = x.shape
    f32 = jnp.float32
    zxbcdt = x @ in_proj
    z = zxbcdt[..., :M_INNER]
    xbc = zxbcdt[..., M_INNER:M_INNER + M_CONV_DIM]
    dt_raw = zxbcdt[..., M_INNER + M_CONV_DIM:]
    xp = jnp.concatenate([conv_prev.astype(xbc.dtype), xbc], axis=1)
    conv = conv_b + sum(xp[:, j:j + L] * conv_w[j] for j in range(M_CONV))
    new_conv = xp[:, -(M_CONV - 1):]
    xbc = jax.nn.silu(conv)
    xs = xbc[..., :M_INNER].reshape(bsz, L, M_HEADS, M_HEADDIM)
    Bm = xbc[..., M_INNER:M_INNER + M_GROUPS * M_STATE].reshape(bsz, L, M_GROUPS, M_STATE)
    Cm = xbc[..., M_INNER + M_GROUPS * M_STATE:].reshape(bsz, L, M_GROUPS, M_STATE)
    dt = jax.nn.softplus(dt_raw.astype(f32) + dt_bias)
    A = -jnp.exp(a_log.astype(f32))
    y, ssm_new = ssd_chunked(xs * dt[..., None], dt * A, Bm, Cm, ssm_prev)
    y = y + d_skip[:, None] * xs
    y = y.reshape(bsz, L, M_INNER) * jax.nn.silu(z)
    yg = y.astype(f32).reshape(bsz, L, M_GROUPS, M_INNER // M_GROUPS)
    yg = yg * lax.rsqrt(jnp.mean(yg * yg, -1, keepdims=True) + EPS)
    y = yg.reshape(bsz, L, M_INNER) * norm_w
    return (y @ out_proj).astype(x.dtype), new_conv, ssm_new


def trunk(x, p, states, layer_params, norm_mix, norm_ffn, norm_ple, ffn_w1, ffn_w3, ffn_w2,
          ple_proj, ple_gate, final_norm):
    h = x
    v_first = None
    new_states = []
    for i in range(DEPTH):
        st_a, st_b = states[2 * i], states[2 * i + 1]
        prm = layer_params[i]
        xn = rmsnorm(h, norm_mix[i])
        kind = i % N_MIXERS
        if kind == 0:
            y, n_a, n_b, v_first = rwkv7_mix(xn, st_a, st_b, v_first, *prm)
        elif kind == 1:
            y, n_a, n_b = s5_mix(xn, st_a, st_b, *prm)
        else:
            y, n_a, n_b = mamba2_mix(xn, st_a, st_b, *prm)
        new_states += [n_a, n_b]
        h = h + y
        hn = rmsnorm(h, norm_ffn[i])
        h = h + (jax.nn.silu(hn @ ffn_w1[i]) * (hn @ ffn_w3[i])) @ ffn_w2[i]
        gate = jax.nn.sigmoid(rmsnorm(h, norm_ple[i]) @ ple_gate[i])
        h = h + gate * (p[i] @ ple_proj[i])
    return rmsnorm(h, final_norm), new_states


def setup_inputs(seed: int = 0) -> dict:
    key = jax.random.key(seed)
    keys = iter(jax.random.split(key, 256))
    f32 = jnp.float32
    D = D_MODEL

    def nrm(shape, scale):
        return scale * jax.random.normal(next(keys), shape, f32)

    def uni(shape, lo, hi):
        return jax.random.uniform(next(keys), shape, f32, lo, hi)

    d = {}
    d['x_prompt'] = nrm((BATCH, SEQ, D), 1.0)
    d['x_sample'] = nrm((DEC_BATCH, DEC_SEQ, D), 1.0)
    d['p_prompt'] = nrm((DEPTH, BATCH, SEQ, PLE_DIM), 1.0)
    d['p_sample'] = nrm((DEPTH, DEC_BATCH, DEC_SEQ, PLE_DIM), 1.0)
    d['state_l0_shift'] = nrm((DEC_BATCH, D), 1.0)
    d['state_l0_wkv'] = nrm((DEC_BATCH, RW_HEADS, RW_HEAD, RW_HEAD), 0.3)
    d['state_l1_s5_re'] = nrm((DEC_BATCH, S5_GROUPS, S5_STATE), 0.3)
    d['state_l1_s5_im'] = nrm((DEC_BATCH, S5_GROUPS, S5_STATE), 0.3)
    d['state_l2_conv'] = nrm((DEC_BATCH, M_CONV - 1, M_CONV_DIM), 1.0)
    d['state_l2_ssm'] = nrm((DEC_BATCH, M_HEADS, M_HEADDIM, M_STATE), 0.1)
    d['state_l3_shift'] = nrm((DEC_BATCH, D), 1.0)
    d['state_l3_wkv'] = nrm((DEC_BATCH, RW_HEADS, RW_HEAD, RW_HEAD), 0.3)

    def add_rwkv(pre, with_v):
        d[pre + 'mu'] = uni((6, D), 0.0, 1.0)
        d[pre + 'w0'] = uni((D,), -6.0, 1.0)
        d[pre + 'w1'] = nrm((D, RW_LORA_W), D ** -0.5)
        d[pre + 'w2'] = nrm((RW_LORA_W, D), 0.1 * RW_LORA_W ** -0.5)
        d[pre + 'a0'] = nrm((D,), 0.1)
        d[pre + 'a1'] = nrm((D, RW_LORA_A), D ** -0.5)
        d[pre + 'a2'] = nrm((RW_LORA_A, D), 0.1 * RW_LORA_A ** -0.5)
        d[pre + 'g1'] = nrm((D, RW_LORA_G), D ** -0.5)
        d[pre + 'g2'] = nrm((RW_LORA_G, D), RW_LORA_G ** -0.5)
        d[pre + 'k_k'] = 0.85 + nrm((D,), 0.05)
        d[pre + 'k_a'] = 1.0 + nrm((D,), 0.05)
        d[pre + 'r_k'] = nrm((RW_HEADS, RW_HEAD), 0.1)
        d[pre + 'w_rkv'] = nrm((3, D, D), D ** -0.5)
        d[pre + 'w_o'] = nrm((D, D), D ** -0.5)
        d[pre + 'lnx_w'] = 1.0 + nrm((D,), 0.05)
        d[pre + 'lnx_b'] = nrm((D,), 0.01)
        if with_v:
            d[pre + 'v0'] = nrm((D,), 0.1)
            d[pre + 'v1'] = nrm((D, RW_LORA_V), D ** -0.5)
            d[pre + 'v2'] = nrm((RW_LORA_V, D), 0.1 * RW_LORA_V ** -0.5)

    add_rwkv('l0_', False)
    n = jnp.arange(S5_STATE, dtype=f32)
    d['l1_a_re'] = -0.5 + nrm((S5_GROUPS, S5_STATE), 0.01)
    d['l1_a_im'] = np.pi * n[None, :] + nrm((S5_GROUPS, S5_STATE), 0.01)
    d['l1_log_dt'] = uni((S5_GROUPS,), float(np.log(1e-3)), float(np.log(1e-1)))
    d['l1_b_re'] = nrm((S5_GROUPS, S5_STATE, S5_GROUP), (2 * S5_GROUP) ** -0.5)
    d['l1_b_im'] = nrm((S5_GROUPS, S5_STATE, S5_GROUP), (2 * S5_GROUP) ** -0.5)
    d['l1_c_re'] = nrm((S5_GROUPS, S5_GROUP, S5_STATE), S5_STATE ** -0.5)
    d['l1_c_im'] = nrm((S5_GROUPS, S5_GROUP, S5_STATE), S5_STATE ** -0.5)
    d['l1_d'] = nrm((D,), 1.0)
    d['l1_glu_v'] = nrm((D, D), D ** -0.5)
    d['l1_glu_g'] = nrm((D, D), D ** -0.5)

    d['l2_in_proj'] = nrm((D, M_INNER + M_CONV_DIM + M_HEADS), D ** -0.5)
    d['l2_conv_w'] = nrm((M_CONV, M_CONV_DIM), 0.5)
    d['l2_conv_b'] = nrm((M_CONV_DIM,), 0.1)
    dt0 = jnp.exp(uni((M_HEADS,), float(np.log(1e-3)), float(np.log(1e-1))))
    d['l2_dt_bias'] = dt0 + jnp.log(-jnp.expm1(-dt0))
    d['l2_a_log'] = jnp.log(uni((M_HEADS,), 1.0, 16.0))
    d['l2_d'] = 1.0 + nrm((M_HEADS,), 0.1)
    d['l2_norm_w'] = 1.0 + nrm((M_INNER,), 0.05)
    d['l2_out_proj'] = nrm((M_INNER, D), M_INNER ** -0.5)

    add_rwkv('l3_', True)

    d['norm_mix'] = 1.0 + nrm((DEPTH, D), 0.05)
    d['norm_ffn'] = 1.0 + nrm((DEPTH, D), 0.05)
    d['norm_ple'] = 1.0 + nrm((DEPTH, D), 0.05)
    d['ffn_w1'] = nrm((DEPTH, D, D_FF), D ** -0.5)
    d['ffn_w3'] = nrm((DEPTH, D, D_FF), D ** -0.5)
    d['ffn_w2'] = nrm((DEPTH, D_FF, D), D_FF ** -0.5)
    d['ple_proj'] = nrm((DEPTH, PLE_DIM, D), PLE_DIM ** -0.5)
    d['ple_gate'] = nrm((DEPTH, D, D), D ** -0.5)
    d['final_norm'] = 1.0 + nrm((D,), 0.05)
    return d


def reference(x_prompt, x_sample, p_prompt, p_sample,
              state_l0_shift, state_l0_wkv, state_l1_s5_re, state_l1_s5_im,
              state_l2_conv, state_l2_ssm, state_l3_shift, state_l3_wkv,
              l0_mu, l0_w0, l0_w1, l0_w2, l0_a0, l0_a1, l0_a2, l0_g1, l0_g2,
              l0_k_k, l0_k_a, l0_r_k, l0_w_rkv, l0_w_o, l0_lnx_w, l0_lnx_b,
              l1_a_re, l1_a_im, l1_log_dt, l1_b_re, l1_b_im, l1_c_re, l1_c_im,
              l1_d, l1_glu_v, l1_glu_g,
              l2_in_proj, l2_conv_w, l2_conv_b, l2_dt_bias, l2_a_log, l2_d,
              l2_norm_w, l2_out_proj,
              l3_mu, l3_w0, l3_w1, l3_w2, l3_a0, l3_a1, l3_a2, l3_g1, l3_g2,
              l3_k_k, l3_k_a, l3_r_k, l3_w_rkv, l3_w_o, l3_lnx_w, l3_lnx_b,
              l3_v0, l3_v1, l3_v2,
              norm_mix, norm_ffn, norm_ple, ffn_w1, ffn_w3, ffn_w2,
              ple_proj, ple_gate, final_norm):
    rw0 = (l0_mu, l0_w0, l0_w1, l0_w2, l0_a0, l0_a1, l0_a2, l0_g1, l0_g2,
           l0_k_k, l0_k_a, l0_r_k, l0_w_rkv, l0_w_o, l0_lnx_w, l0_lnx_b)
    s5p = (l1_a_re, l1_a_im, l1_log_dt, l1_b_re, l1_b_im, l1_c_re, l1_c_im,
           l1_d, l1_glu_v, l1_glu_g)
    mbp = (l2_in_proj, l2_conv_w, l2_conv_b, l2_dt_bias, l2_a_log, l2_d, l2_norm_w, l2_out_proj)
    rw3 = (l3_mu, l3_w0, l3_w1, l3_w2, l3_a0, l3_a1, l3_a2, l3_g1, l3_g2,
           l3_k_k, l3_k_a, l3_r_k, l3_w_rkv, l3_w_o, l3_lnx_w, l3_lnx_b,
           (l3_v0, l3_v1, l3_v2))
    layer_params = (rw0, s5p, mbp, rw3)
    sample_states = [state_l0_shift, state_l0_wkv, state_l1_s5_re, state_l1_s5_im,
                     state_l2_conv, state_l2_ssm, state_l3_shift, state_l3_wkv]
    bp = x_prompt.shape[0]
    prompt_states = [jnp.zeros((bp,) + s.shape[1:], jnp.float32) for s in sample_states]

    y_prompt, new_p = trunk(x_prompt, p_prompt, prompt_states, layer_params,
                            norm_mix, norm_ffn, norm_ple, ffn_w1, ffn_w3, ffn_w2,
                            ple_proj, ple_gate, final_norm)
    y_sample, new_s = trunk(x_sample, p_sample, sample_states, layer_params,
                            norm_mix, norm_ffn, norm_ple, ffn_w1, ffn_w3, ffn_w2,
                            ple_proj, ple_gate, final_norm)
    (p_shift0, p_wkv0, p_s5re1, p_s5im1, p_conv2, p_ssm2, p_shift3, p_wkv3) = new_p
    (s_shift0, s_wkv0, s_s5re1, s_s5im1, s_conv2, s_ssm2, s_shift3, s_wkv3) = new_s
    return (y_prompt, y_sample,
            p_shift0, p_wkv0, p_s5re1, p_s5im1, p_conv2, p_ssm2, p_shift3, p_wkv3,
            s_shift0, s_wkv0, s_s5re1, s_s5im1, s_conv2, s_ssm2, s_shift3, s_wkv3)
```

```python
import numpy as np
import concourse.bass as bass
import concourse.mybir as mybir
from concourse.ap import AP
from concourse.bass_utils import run_bass_kernel_spmd

F32 = mybir.dt.float32
BF16 = mybir.dt.bfloat16
I32 = mybir.dt.int32
ALU = mybir.AluOpType
AF = mybir.ActivationFunctionType
AX = mybir.AxisListType

SEM_ROT = 30000


class Eng:
    def __init__(self, kb, name, eng):
        self.kb = kb
        self.name = name
        self.eng = eng
        self.sem = kb.nc.alloc_semaphore("es_%s_0" % name)
        self.nsem = 1
        self.cnt = 0
        self.seen = {}

    def rotate(self):
        if self.cnt >= SEM_ROT:
            self.sem = self.kb.nc.alloc_semaphore("es_%s_%d" % (self.name, self.nsem))
            self.nsem += 1
            self.cnt = 0


class DSem:
    def __init__(self, kb, name):
        self.kb = kb
        self.name = name
        self.sem = kb.nc.alloc_semaphore("ds_" + name)
        self.n = 0
        self.gen = 0


class Tn:
    def __init__(self, kb, h, name, space):
        self.kb = kb
        self.h = h
        self.name = name
        self.space = space
        self.lw = None
        self.rd = []
        self.ds = None
        self.shape = list(h.shape)

    def __getitem__(self, idx):
        return V(self, self.h[idx])

    def v(self, offset, ap):
        return V(self, AP(self.h, offset, ap))

    @property
    def a(self):
        return V(self, self.h[:])


class SubTn(Tn):
    def __init__(self, parent, col0, ncols, name):
        self.kb = parent.kb
        self.h = parent.h
        self.name = name
        self.space = parent.space
        self.lw = None
        self.rd = []
        self.ds = None
        self.col0 = col0
        self.ncols = ncols
        self.shape = [parent.shape[0], ncols]

    def __getitem__(self, idx):
        r, c = idx
        a = 0 if c.start is None else c.start
        b = self.ncols if c.stop is None else c.stop
        return V(self, self.h[r, self.col0 + a:self.col0 + b])

    @property
    def a(self):
        return self[:, 0:self.ncols]


class V:
    def __init__(self, t, ap):
        self.t = t
        self.ap = ap

    def __getitem__(self, idx):
        return V(self.t, self.ap[idx])

    def re(self, s, **kw):
        return V(self.t, self.ap.rearrange(s, **kw))

    def bc(self, shape):
        return V(self.t, self.ap.broadcast_to(shape))

    def bitcast(self, dt):
        return V(self.t, self.ap.bitcast(dt))

    @property
    def shape(self):
        return self.ap.shape


def _ap(x):
    return x.ap if isinstance(x, V) else x


class KB:
    def __init__(self):
        self.nc = bass.Bass("TRN2", target_bir_lowering=False)
        nc = self.nc
        self.E = {
            'pe': Eng(self, 'pe', nc.tensor),
            'dve': Eng(self, 'dve', nc.vector),
            'act': Eng(self, 'act', nc.scalar),
            'pool': Eng(self, 'pool', nc.gpsimd),
            'sp': Eng(self, 'sp', nc.sync),
        }
        self.dsems = []
        self._frozen = {}
        self._phase = []
        self._dspool = []
        self._dspool_sw = []
        self.ntens = 0
        self.ninst = 0

    def sb(self, name, shape, dtype=F32):
        h = self.nc.alloc_sbuf_tensor(name, list(shape), dtype)
        return Tn(self, h, name, 'sb')

    def sbp(self, name, shape, dtype=F32):
        self._sbn = getattr(self, "_sbn", 0) + 1
        cm = self.nc.sbuf_tensor("%s_t%d" % (name, self._sbn), list(shape), dtype)
        h = cm.__enter__()
        t = Tn(self, h, name, 'sb')
        self.min_free = min(getattr(self, "min_free", 1 << 30), self.nc.sbuf_bytes_remaining)
        self._phase.append((cm, t))
        return t

    def barrier(self):
        ents = []
        for en, EE in self.E.items():
            if EE.cnt > 0:
                ents.append(('e', EE.sem, EE.cnt, None))
        for ds in self.dsems:
            if ds.n > 0:
                ents.append(('d', ds, ds.gen))
        for en, EE in self.E.items():
            self._wait(EE, ents)

    def mark(self):
        return len(self._phase)

    def free_to(self, mark):
        self.barrier()
        while len(self._phase) > mark:
            cm, t = self._phase.pop()
            if t.ds is not None:
                self._dspool.append(t.ds)
                t.ds = None
            if getattr(t, "ds_sw", None) is not None:
                self._dspool_sw.append(t.ds_sw)
                t.ds_sw = None
            cm.__exit__(None, None, None)

    def end_phase(self):
        self.barrier()
        while self._phase:
            cm, t = self._phase.pop()
            if t.ds is not None:
                self._dspool.append(t.ds)
                t.ds = None
            if getattr(t, "ds_sw", None) is not None:
                self._dspool_sw.append(t.ds_sw)
                t.ds_sw = None
            cm.__exit__(None, None, None)

    def ps(self, name, shape, dtype=F32):
        h = self.nc.alloc_psum_tensor(name, list(shape), dtype)
        return Tn(self, h, name, 'ps')

    def dram_in(self, name, shape, dtype=F32):
        h = self.nc.dram_tensor(name, list(shape), dtype, kind="ExternalInput")
        return Tn(self, h, name, 'din')

    def dram_out(self, name, shape, dtype=F32):
        h = self.nc.dram_tensor(name, list(shape), dtype, kind="ExternalOutput")
        return Tn(self, h, name, 'dout')

    def dram_tmp(self, name, shape, dtype=F32):
        h = self.nc.dram_tensor(name, list(shape), dtype, kind="Internal")
        return Tn(self, h, name, 'dtmp')

    def _entry_val(self, ent):
        kind = ent[0]
        if kind == 'e':
            return ent[1], ent[2]
        ds = ent[1]
        if ent[2] == ds.gen:
            return ds.sem, ds.n * 16
        return self._frozen[(id(ds), ent[2])]

    def _wait(self, E, ents):
        need = {}
        for ent in ents:
            if ent is None:
                continue
            if ent[0] == 'e' and ent[3] is E and E.name == 'pe':
                continue
            sem, val = self._entry_val(ent)
            key = id(sem)
            if E.seen.get(key, 0) >= val:
                continue
            if key not in need or need[key][1] < val:
                need[key] = (sem, val)
        for key, (sem, val) in need.items():
            E.eng.wait_ge(sem, val)
            E.seen[key] = val

    def _collect(self, outs, ins):
        ents = []
        for v in ins:
            if isinstance(v, V) and v.t is not None and v.t.space != 'din':
                ents.append(v.t.lw)
        for v in outs:
            if isinstance(v, V) and v.t is not None:
                ents.append(v.t.lw)
                ents.extend(v.t.rd)
        return ents

    def _record(self, ent, outs, ins):
        for v in ins:
            if isinstance(v, V) and v.t is not None and v.t.space != 'din':
                t = v.t
                t.rd = [r for r in t.rd if not (r[0] == ent[0] and r[1] is ent[1])]
                t.rd.append(ent)
        for v in outs:
            if isinstance(v, V) and v.t is not None:
                v.t.lw = ent
                v.t.rd = []

    def begin_defer(self):
        self._defer = []
        self._dtrk = {}

    def _drecord(self, kind, en, payload, outs, ins, est):
        idx = len(self._defer)
        deps = set()
        for v in ins:
            if isinstance(v, V) and v.t is not None and v.t.space != 'din':
                tr = self._dtrk.setdefault(id(v.t), [None, []])
                if tr[0] is not None:
                    deps.add(tr[0])
        for v in outs:
            if isinstance(v, V) and v.t is not None:
                tr = self._dtrk.setdefault(id(v.t), [None, []])
                if tr[0] is not None:
                    deps.add(tr[0])
                deps.update(tr[1])
        for v in ins:
            if isinstance(v, V) and v.t is not None and v.t.space != 'din':
                self._dtrk[id(v.t)][1].append(idx)
        for v in outs:
            if isinstance(v, V) and v.t is not None:
                self._dtrk[id(v.t)] = [idx, []]
        deps.discard(idx)
        self._defer.append((kind, en, payload, outs, ins, est, deps))

    def end_defer(self, sync_lat=0.25):
        import heapq
        ops = self._defer
        self._defer = None
        n = len(ops)
        succ = [[] for _ in range(n)]
        ndep = [0] * n
        for i, o in enumerate(ops):
            ndep[i] = len(o[6])
            for d in o[6]:
                succ[d].append(i)
        ready_t = [0.0] * n
        fin = [0.0] * n
        start = [0.0] * n
        efree = {}
        heap = [(0.0, i) for i in range(n) if ndep[i] == 0]
        heapq.heapify(heap)
        while heap:
            rt, i = heapq.heappop(heap)
            kind, en, payload, outs, ins, est, deps = ops[i]
            st = max(rt, efree.get(en, 0.0))
            start[i] = st
            if kind == 'dma':
                efree[en] = st + 0.05
                fin[i] = st + est
            else:
                efree[en] = st + est
                fin[i] = st + est
            for j in succ[i]:
                ready_t[j] = max(ready_t[j], fin[i] + sync_lat)
                ndep[j] -= 1
                if ndep[j] == 0:
                    heapq.heappush(heap, (ready_t[j], j))
        order = sorted(range(n), key=lambda i: (start[i], i))
        for i in order:
            kind, en, payload, outs, ins, est, deps = ops[i]
            if kind == 'op':
                fn, inc = payload
                self.op(en, fn, outs, ins, inc=True)
            else:
                out, in_, q, owner, kw = payload
                self.dma(out, in_, q=q, owner=owner, **kw)

    def _est(self, en, outs, ins):
        try:
            sh = outs[0].ap.shape
            free = 1
            for d in sh[1:]:
                free *= d
        except Exception:
            free = 128
        if en == 'pe':
            f32 = False
            try:
                f32 = any(isinstance(v, V) and v.t.space == 'sb' and str(v.ap.dtype).endswith('float32') for v in ins[:1])
            except Exception:
                pass
            return (0.07 + free / 2400.0) * (4.0 if f32 else 1.0)
        if en == 'dve':
            return 0.07 + free / 960.0
        if en == 'act':
            return 0.2 + free / 1200.0
        if en == 'pool':
            return 0.5 + free / 400.0
        return 0.1

    def op(self, en, fn, outs, ins, inc=True):
        if getattr(self, "_defer", None) is not None:
            self._drecord('op', en, (fn, inc), outs, ins, self._est(en, outs, ins))
            return None
        E = self.E[en]
        self._wait(E, self._collect(outs, ins))
        inst = fn(E.eng)
        self.ninst += 1
        if inc:
            E.cnt += 1
            inst.then_inc(E.sem, 1)
            ent = ('e', E.sem, E.cnt, E)
            self._record(ent, outs, ins)
            E.rotate()
        else:
            ent = ('e', E.sem, E.cnt + 1, E)
            self._record(ent, outs, ins)
        return inst

    def dma(self, out, in_, q='sp', owner=None, **kw):
        if getattr(self, "_defer", None) is not None:
            self._drecord('dma', q, (out, in_, q, owner, kw), [out], [in_], 2.2)
            return None
        E = self.E[q]
        self._wait(E, self._collect([out], [in_]))
        if owner is None:
            cands = [v.t for v in (out, in_) if v.t.space in ('sb',)]
            if not cands:
                cands = [v.t for v in (out, in_) if v.t.space in ('dtmp',)]
            if not cands:
                cands = [out.t]
            owner = cands[0]
        sw = (q == 'pool')
        attr = 'ds_sw' if sw else 'ds'
        pool = self._dspool_sw if sw else self._dspool
        if getattr(owner, attr, None) is None:
            if pool:
                setattr(owner, attr, pool.pop())
            else:
                nd = DSem(self, "%s%s_%d" % ("sw_" if sw else "", owner.name, len(self.dsems)))
                setattr(owner, attr, nd)
                self.dsems.append(nd)
        ds = getattr(owner, attr)
        inst = E.eng.dma_start(out=out.ap, in_=in_.ap, **kw)
        ds.n += 1
        inst.then_inc(ds.sem, 16)
        self.ninst += 1
        ent = ('d', ds, ds.gen)
        self._record(ent, [out], [in_])
        if ds.n >= 1800:
            old = (ds.sem, ds.n * 16)
            ds.gen += 1
            self._frozen[(id(ds), ds.gen - 1)] = old
            ds.sem = self.nc.alloc_semaphore("ds_%s_%d" % (owner.name, ds.gen))
            ds.n = 0
        return inst

    def finish(self):
        E = self.E['sp']
        for ds in self.dsems:
            if ds.n > 0:
                E.eng.wait_ge(ds.sem, ds.n * 16)
        for (k, g), (sem, val) in self._frozen.items():
            E.eng.wait_ge(sem, val)
        for en, EE in self.E.items():
            if en != 'sp' and EE.cnt > 0:
                E.eng.wait_ge(EE.sem, EE.cnt)

    def tt(self, out, a, b, op, en='dve'):
        return self.op(en, lambda e: e.tensor_tensor(out=out.ap, in0=a.ap, in1=b.ap, op=op), [out], [a, b])

    def ts(self, out, a, s1, op0, s2=None, op1=None, en='dve'):
        def f(e):
            if op1 is None:
                return e.tensor_scalar(out=out.ap, in0=a.ap, scalar1=_ap(s1), scalar2=None, op0=op0)
            return e.tensor_scalar(out=out.ap, in0=a.ap, scalar1=_ap(s1), scalar2=_ap(s2), op0=op0, op1=op1)
        return self.op(en, f, [out], [a, s1, s2])

    def stt(self, out, a, s, b, op0, op1, en='dve'):
        return self.op(en, lambda e: e.scalar_tensor_tensor(out=out.ap, in0=a.ap, scalar=_ap(s), in1=b.ap, op0=op0, op1=op1), [out], [a, s, b])

    def copy(self, out, a, en='dve'):
        if en == 'act':
            return self.op(en, lambda e: e.copy(out=out.ap, in_=a.ap), [out], [a])
        return self.op(en, lambda e: e.tensor_copy(out=out.ap, in_=a.ap), [out], [a])

    def memset(self, out, val, en='dve'):
        return self.op(en, lambda e: e.memset(out.ap, val), [out], [])

    def act(self, out, a, func, bias=None, scale=None, accum=None):
        def f(e):
            kw = {}
            if bias is not None:
                kw['bias'] = _ap(bias)
            if scale is not None:
                kw['scale'] = _ap(scale)
            if accum is not None:
                kw['accum_out'] = accum.ap
            return e.activation(out=out.ap, in_=a.ap, func=func, **kw)
        outs = [out] + ([accum] if accum is not None else [])
        return self.op('act', f, outs, [a, bias, scale])

    def mm(self, out, lhsT, rhs, start=True, stop=True, inc=None):
        if inc is None:
            inc = stop
        return self.op('pe', lambda e: e.matmul(out.ap, lhsT.ap, rhs.ap, start=start, stop=stop), [out], [lhsT, rhs], inc=inc)

    def tr(self, out, a, ident, inc=True):
        return self.op('pe', lambda e: e.transpose(out.ap, a.ap, ident.ap), [out], [a, ident], inc=inc)

    def scan(self, out, d0, d1, init, op0=ALU.mult, op1=ALU.add):
        return self.op('dve', lambda e: e.tensor_tensor_scan(out=out.ap, data0=d0.ap, data1=d1.ap, initial=_ap(init), op0=op0, op1=op1), [out], [d0, d1, init])

    def reduce(self, out, a, op=ALU.add, axis=AX.X):
        return self.op('dve', lambda e: e.tensor_reduce(out=out.ap, in_=a.ap, axis=axis, op=op), [out], [a])

    def recip(self, out, a):
        return self.op('dve', lambda e: e.reciprocal(out=out.ap, in_=a.ap), [out], [a])
D = 1024
KC = 8
TP = 2048
TS = 128
TT = TP + TS
DFF = 2816
NFC = DFF // 128
BLK = 256
TBS = [(i * BLK, BLK) for i in range(TP // BLK)] + [(TP, TS)]
EPS = 1e-6


class MKBase:
    def __init__(self, dbg=None):
        self.k = KB()
        self.dbg = dbg or {}
        self.outs = {}
        self.ins = {}
        k = self.k
        self.ps_banks = [k.ps("psb%d" % i, [128, 512]) for i in range(8)]
        self.ps_i = 0
        self._uid = 0

    def inp(self, name, shape, dtype=F32):
        t = self.k.dram_in(name, shape, dtype)
        self.ins[name] = t
        return t

    def out(self, name, shape):
        t = self.k.dram_out(name, shape)
        self.outs[name] = t
        return t

    def psum(self, lo=0, hi=8):
        n = hi - lo
        if not hasattr(self, "_psc"):
            self._psc = {}
        c = self._psc.get((lo, hi), 0)
        self._psc[(lo, hi)] = c + 1
        return self.ps_banks[lo + (c % n)]

    def uid(self, s):
        self._uid += 1
        return "%s_%d" % (s, self._uid)

    def dump(self, name, view, shape):
        o = self.out("dbg_" + name, shape)
        self.k.dma(o.a, view)

    def setup_consts(self):
        k = self.k
        ident_d = self.inp("c_ident", [128, 128])
        self.ident = k.sb("ident", [128, 128])
        k.dma(self.ident.a, ident_d.a)
        self.ones = k.sb("ones", [128, 128])
        k.memset(self.ones.a, 1.0)
        gd = self.inp("c_gains", [128, 13 * 8])
        self.gains = k.sb("gains", [128, 13 * 8])
        k.dma(self.gains.a, gd.a)

    def gain(self, typ, l):
        idx = (typ * 4 + l) if typ < 3 else 12
        return self.gains[:, idx * 8:(idx + 1) * 8]

    def rmsnorm(self, src, dst, gain, n, tmp_sq, tmp_r):
        k = self.k
        pss = self.psum(4, 8)
        for kk in range(KC):
            k.act(tmp_sq[:, kk, 0:n], src(kk), AF.Square)
        for kk in range(KC):
            k.mm(pss[:, 0:n], self.ones.a, tmp_sq[:, kk, 0:n], start=(kk == 0), stop=(kk == KC - 1))
        k.act(tmp_r[:, 0:n], pss[:, 0:n], AF.Sqrt, bias=self.eps_col[:, 0:1], scale=1.0 / D)
        k.recip(tmp_r[:, 0:n], tmp_r[:, 0:n])
        for kk in range(KC):
            k.stt(dst(kk), src(kk), gain[:, kk:kk + 1], tmp_r[:, 0:n], ALU.mult, ALU.mult)

    def setup_state(self):
        k = self.k
        self.h = [k.sb("h%d" % i, [128, KC, n]) for i, (s, n) in enumerate(TBS)]
        self.eps_col = k.sb("eps_col", [128, 1])
        k.memset(self.eps_col.a, EPS)
        self.tmp_sq = k.sb("tmp_sq", [128, KC, BLK])
        self.tmp_r = k.sb("tmp_r", [128, BLK])

    def load_x(self, xT):
        k = self.k
        for i, (s, n) in enumerate(TBS):
            k.dma(self.h[i].a, xT.v(s, [[TT, 128], [128 * TT, KC], [1, n]]))

    def ffn(self, l, w1, w3, w2, xn_all, wbuf):
        k = self.k
        def norm_blk(i):
            s, n = TBS[i]
            self.rmsnorm(lambda kk, i=i: self.h[i][:, kk, :], lambda kk, i=i, n=n: xn_all[i][:, kk, 0:n],
                         self.gain(1, l), n, self.tmp_sq, self.tmp_r)
        norm_blk(0)
        G = 4
        groups = [(c, min(G, NFC - c)) for c in range(0, NFC, G)]
        a_t = [[k.sbp(self.uid("ffn_a"), [128, BLK], BF16) for _ in range(G)] for _ in range(2)]
        s_t = [k.sbp(self.uid("ffn_s"), [128, BLK]) for _ in range(2)]

        stg = getattr(self, "_ffn_stage", None)

        def load(gi):
            c0, g = groups[gi]
            w1g, w3g, w2g = wbuf[gi % 2]
            base = l * D * DFF
            if stg is None:
                k.dma(w1g[:, :, 0:g * 128], w1.v(base + c0 * 128, [[DFF, 128], [128 * DFF, KC], [1, g * 128]]), q='pool')
                k.dma(w3g[:, :, 0:g * 128], w3.v(base + c0 * 128, [[DFF, 128], [128 * DFF, KC], [1, g * 128]]), q='pool')
                k.dma(w2g[:, 0:g, :], w2.v(l * DFF * D + c0 * 128 * D, [[D, 128], [128 * D, g], [1, D]]), q='pool')
            else:
                s1, s3, s2 = stg
                k.dma(s1[:, :, 0:g * 128], w1.v(base + c0 * 128, [[DFF, 128], [128 * DFF, KC], [1, g * 128]]))
                k.dma(s3[:, :, 0:g * 128], w3.v(base + c0 * 128, [[DFF, 128], [128 * DFF, KC], [1, g * 128]]))
                k.dma(s2[:, 0:g, :], w2.v(l * DFF * D + c0 * 128 * D, [[D, 128], [128 * D, g], [1, D]]))
                k.copy(w1g[:, :, 0:g * 128], s1[:, :, 0:g * 128], en='act')
                k.copy(w3g[:, :, 0:g * 128], s3[:, :, 0:g * 128], en='pool')
                k.copy(w2g[:, 0:g, :], s2[:, 0:g, :], en='act')

        load(0)
        cnt = 0
        bcnt = 0
        for gi, (c0, g) in enumerate(groups):
            if gi + 1 < len(groups):
                load(gi + 1)
            w1g, w3g, w2g = wbuf[gi % 2]
            for i, (s, n) in enumerate(TBS):
                if gi == 0 and i + 1 < len(TBS):
                    norm_blk(i + 1)
                py = self.ps_banks[0:4]
                ats = a_t[bcnt % 2]
                bcnt += 1
                for c in range(g):
                    ph1 = self.psum(4, 8)
                    for kk in range(KC):
                        k.mm(ph1[:, 0:n], w1g[:, kk, c * 128:(c + 1) * 128], xn_all[i][:, kk, 0:n], start=(kk == 0), stop=(kk == KC - 1))
                    ph3 = self.psum(4, 8)
                    for kk in range(KC):
                        k.mm(ph3[:, 0:n], w3g[:, kk, c * 128:(c + 1) * 128], xn_all[i][:, kk, 0:n], start=(kk == 0), stop=(kk == KC - 1))
                    st = s_t[cnt % 2]
                    cnt += 1
                    k.act(st[:, 0:n], ph1[:, 0:n], AF.Silu)
                    k.tt(ats[c][:, 0:n], st[:, 0:n], ph3[:, 0:n], ALU.mult)
                for j in range(KC):
                    for c in range(g):
                        k.mm(py[j // 2][:, (j % 2) * BLK:(j % 2) * BLK + n], w2g[:, c, j * 128:(j + 1) * 128], ats[c][:, 0:n],
                             start=(c == 0), stop=(c == g - 1))
                for jj in range(4):
                    hv = self.h[i][:, 2 * jj:2 * jj + 2, :]
                    pv = py[jj].a.re("p (a b) -> p a b", a=2)[:, :, 0:n]
                    k.tt(hv, hv, pv, ALU.add)

    def ple(self, l, pT, ple_proj, ple_gate, wg, wp, xn_blk, p_blk):
        k = self.k
        k.dma(wg.a, ple_gate.v(l * D * D, [[D, 128], [128 * D, KC], [1, D]]), q='pool')
        k.dma(wp.a, ple_proj.v(l * 256 * D, [[D, 128], [128 * D, 2], [1, D]]), q='pool')
        sg = [k.sbp(self.uid("ple_sg"), [128, BLK]) for _ in range(2)]
        def prep(i):
            s, n = TBS[i]
            xb = xn_blk[i % 2]
            pb = p_blk[i % 2]
            k.dma(pb[:, :, 0:n], pT.v(l * 256 * TT + s, [[TT, 128], [128 * TT, 2], [1, n]]), q='pool')
            self.rmsnorm(lambda kk, i=i: self.h[i][:, kk, :], lambda kk, xb=xb, n=n: xb[:, kk, 0:n],
                         self.gain(2, l), n, self.tmp_sq, self.tmp_r)
        prep(0)
        for i, (s, n) in enumerate(TBS):
            xb = xn_blk[i % 2]
            pb = p_blk[i % 2]
            for j in range(KC):
                pg = self.psum()
                for kk in range(KC):
                    k.mm(pg[:, 0:n], wg[:, kk, j * 128:(j + 1) * 128], xb[:, kk, 0:n], start=(kk == 0), stop=(kk == KC - 1))
                pp = self.psum()
                for kk in range(2):
                    k.mm(pp[:, 0:n], wp[:, kk, j * 128:(j + 1) * 128], pb[:, kk, 0:n], start=(kk == 0), stop=(kk == 1))
                sgt = sg[j % 2]
                k.act(sgt[:, 0:n], pg[:, 0:n], AF.Sigmoid)
                k.tt(sgt[:, 0:n], sgt[:, 0:n], pp[:, 0:n], ALU.mult)
                k.tt(self.h[i][:, j, :], self.h[i][:, j, :], sgt[:, 0:n], ALU.add)
                if j == 0 and i + 1 < len(TBS):
                    prep(i + 1)

    def final(self, yT, ybuf):
        k = self.k
        k.begin_defer()
        for i, (s, n) in enumerate(TBS):
            yb = ybuf[i % 2]
            self.rmsnorm(lambda kk, i=i: self.h[i][:, kk, :], lambda kk, yb=yb, n=n: yb[:, kk, 0:n],
                         self.gain(3, 0), n, self.tmp_sq, self.tmp_r)
            k.dma(yT.v(s, [[TT, 128], [128 * TT, KC], [1, n]]), yb[:, :, 0:n])
        k.end_defer()


PI = 3.14159265358979
PIC = 3.141592


class S5Mixin:
    def trig(self, ang, sin_out, cos_out, tf, ti, tr):
        k = self.k
        for out, shift in ((sin_out, 0.0), (cos_out, PI / 2)):
            if shift != 0.0:
                k.ts(tr, ang, shift, ALU.add)
                src = tr
            else:
                src = ang
            k.ts(ti, src, 1.0 / (2 * PI), ALU.mult)
            k.copy(tf, ti)
            k.stt(tr, tf, -2 * PI, src, ALU.mult, ALU.add)
            k.ts(tr, tr, PIC, ALU.min, -PIC, ALU.max)
            k.act(out, tr, AF.Sin)

    def s5_params(self, ach, afe, out_q=False):
        raise NotImplementedError

    def s5_layer(self, l, I, O):
        k = self.k
        sb = k.sbp
        ch = sb("s5_ch", [128, 3, 32]); k.dma(ch.a, I["s5_chan"].a)
        dcol = sb("s5_d", [128, 8]); k.dma(dcol.a, I["s5_d"].a)
        maskB = sb("s5_maskB", [128, 8]); k.dma(maskB.a, I["c_maskB"].a)
        hst = sb("s5_hst", [128, 2, 32, 16]); k.dma(hst.a, I["s5_state"].a)
        hpr = sb("s5_hpr", [128, 2, 32]); k.memset(hpr.a, 0.0)
        hso = sb("s5_hso", [128, 2, 32, 16])
        rho = sb("s5_rho", [128, 32]); th = sb("s5_th", [128, 32]); dtc = sb("s5_dtc", [128, 32])
        LC = 128
        cosT = sb("s5_cosT", [128, 32, LC]); sinT = sb("s5_sinT", [128, 32, LC])
        LB = [sb("s5_LBr", [128, 32, 128], BF16), sb("s5_LBi", [128, 32, 128], BF16)]
        LCm = [sb("s5_LCr", [128, 32, 128], BF16), sb("s5_LCi", [128, 32, 128], BF16)]
        mk_ = k.mark()
        fe = sb("s5_fe", [128, 3, 512]); k.dma(fe.a, I["s5_feat"].a)
        bT = sb("s5_bT", [128, 2, 512]); k.dma(bT.a, I["s5_bT"].a)
        cF = sb("s5_cF", [128, 2, 512]); k.dma(cF.a, I["s5_cF"].a)
        iota = sb("s5_iota", [128, 128]); k.dma(iota.a, I["c_iota"].a)
        k.act(dtc.a, ch[:, 2, :], AF.Exp)
        k.tt(th.a, ch[:, 1, :], dtc.a, ALU.mult)
        k.tt(rho.a, ch[:, 0, :], dtc.a, ALU.mult)
        k.act(rho.a, rho.a, AF.Exp)
        TW = 8 * LC
        ang = sb("s5_ang", [128, TW]); tf = sb("s5_tf", [128, TW]); ti = sb("s5_ti", [128, TW], I32)
        tr = sb("s5_tr", [128, TW])
        for q in range(4):
            for c8 in range(8):
                ct = 8 * q + c8
                k.ts(ang[:, c8 * LC:(c8 + 1) * LC], iota.a, th[:, ct:ct + 1], ALU.mult)
            self.trig(ang.a, sinT[:, 8 * q:8 * q + 8, :].re("p a b -> p (a b)"), cosT[:, 8 * q:8 * q + 8, :].re("p a b -> p (a b)"), tf.a, ti.a, tr.a)
        F = 512
        dtf = sb("s5_dtf", [128, F]); thf = sb("s5_thf", [128, F]); magf = sb("s5_magf", [128, F])
        k.act(dtf.a, fe[:, 2, :], AF.Exp)
        k.tt(thf.a, fe[:, 1, :], dtf.a, ALU.mult)
        k.tt(magf.a, fe[:, 0, :], dtf.a, ALU.mult)
        k.act(magf.a, magf.a, AF.Exp)
        sf = sb("s5_sf", [128, F]); cf = sb("s5_cf", [128, F])
        self.trig(thf.a, sf.a, cf.a, tf[:, 0:F], ti[:, 0:F], tr[:, 0:F])
        abr = sb("s5_abr", [128, F]); abi = sb("s5_abi", [128, F])
        k.tt(abr.a, magf.a, cf.a, ALU.mult)
        k.ts(abr.a, abr.a, -1.0, ALU.add)
        k.tt(abi.a, magf.a, sf.a, ALU.mult)
        lr = fe[:, 0, :]; li = fe[:, 1, :]
        den = dtf; t1 = thf; t2 = magf
        k.tt(den.a, lr, lr, ALU.mult); k.tt(t1.a, li, li, ALU.mult); k.tt(den.a, den.a, t1.a, ALU.add)
        k.recip(den.a, den.a)
        qr = sf; qi = cf
        k.tt(t1.a, abr.a, lr, ALU.mult); k.tt(t2.a, abi.a, li, ALU.mult); k.tt(qr.a, t1.a, t2.a, ALU.add); k.tt(qr.a, qr.a, den.a, ALU.mult)
        k.tt(t1.a, abi.a, lr, ALU.mult); k.tt(t2.a, abr.a, li, ALU.mult); k.tt(qi.a, t1.a, t2.a, ALU.subtract); k.tt(qi.a, qi.a, den.a, ALU.mult)
        bbr = abr; bbi = abi
        k.tt(t1.a, qr.a, bT[:, 0, :], ALU.mult); k.tt(t2.a, qi.a, bT[:, 1, :], ALU.mult); k.tt(bbr.a, t1.a, t2.a, ALU.subtract)
        k.tt(t1.a, qr.a, bT[:, 1, :], ALU.mult); k.tt(t2.a, qi.a, bT[:, 0, :], ALU.mult); k.tt(bbi.a, t1.a, t2.a, ALU.add)
        for ri, bb in enumerate((bbr, bbi)):
            for ft in range(8):
                o = LB[ri][:, 4 * ft:4 * ft + 4, :].re("p a (g q) -> p (a g) q", g=2)
                i0 = V(bb, AP(bb.h, ft * 64, [[F, 128], [0, 8], [1, 64]]))
                i1 = V(maskB, AP(maskB.h, 0, [[8, 128], [1, 8], [0, 64]]))
                k.tt(o, i0, i1, ALU.mult)
        xt = [sb("s5_xt%d" % i, [128, 128]) for i in range(2)]
        n_ = 0
        for ri in range(2):
            for ft in range(8):
                for cl in range(4):
                    x = xt[n_ % 2]; n_ += 1
                    i0 = V(cF, AP(cF.h, ri * 512 + ft * 64, [[1024, 128], [0, 2], [1, 64]]))
                    i1 = V(maskB, AP(maskB.h, 2 * cl, [[8, 128], [1, 2], [0, 64]]))
                    k.tt(x.a.re("p (g q) -> p g q", g=2), i0, i1, ALU.mult)
                    pt = self.psum()
                    k.tr(pt[:, 0:128], x.a, self.ident.a)
                    k.act(LCm[ri][:, 4 * ft + cl, :], pt[:, 0:128], AF.Copy, scale=(1.0 if ri == 0 else -1.0))
        k.free_to(mk_)
        x_ = sb("s5_xn", [128, 8, BLK]); xb_ = sb("s5_xnb", [128, 8, BLK], BF16)
        gbf = sb("s5_gbf", [128, 8, BLK], BF16)
        Hre4 = sb("s5_Hre4", [128, 4, BLK], BF16); Him4 = sb("s5_Him4", [128, 4, BLK], BF16)
        W4 = 4 * 128
        ur = sb("s5_ur", [128, W4]); ui = sb("s5_ui", [128, W4]); ta = sb("s5_ta", [128, W4]); tb = sb("s5_tb", [128, W4])
        tc = sb("s5_tc", [128, W4]); td = sb("s5_td", [128, W4])
        hr = sb("s5_hr", [128, W4]); hi = sb("s5_hi", [128, W4]); h2r = sb("s5_h2r", [128, W4]); h2i = sb("s5_h2i", [128, W4])
        sgt = [sb("s5_sg%d" % i, [128, BLK]) for i in range(2)]
        yt = sb("s5_yt", [128, BLK]); gt = sb("s5_gt", [128, BLK])
        wvj = [sb("s5_wv%d" % i, [128, 8, 128], BF16) for i in range(3)]
        wgj = [sb("s5_wg%d" % i, [128, 8, 128], BF16) for i in range(3)]
        wn = 0
        k.begin_defer()
        for i, (s0, n) in enumerate(TBS):
            sample = (s0 >= TP)
            self.rmsnorm(lambda kk, i=i: self.h[i][:, kk, :], lambda kk, n=n: x_[:, kk, 0:n], self.gain(0, l), n, self.tmp_sq, self.tmp_r)
            for kk in range(KC):
                k.copy(xb_[:, kk, 0:n], x_[:, kk, 0:n], en='act')
            for ft in range(8):
                nch = 1 if sample else n // LC
                for c in range(nch):
                    sl = slice(c * LC, (c + 1) * LC)
                    pbr = self.psum(); pbi = self.psum()
                    for cl in range(4):
                        k.mm(pbr[:, cl * 128:(cl + 1) * 128], LB[0][:, 4 * ft + cl, :], xb_[:, ft, sl])
                    for cl in range(4):
                        k.mm(pbi[:, cl * 128:(cl + 1) * 128], LB[1][:, 4 * ft + cl, :], xb_[:, ft, sl])
                    if sample:
                        cs = V(cosT, AP(cosT.h, 4 * ft * LC, [[32 * LC, 128], [LC, 4], [0, 16], [1, 8]]))
                        sn = V(sinT, AP(sinT.h, 4 * ft * LC, [[32 * LC, 128], [LC, 4], [0, 16], [1, 8]]))
                        w3 = lambda v: v.re("p (a b t) -> p a b t", a=4, t=8)
                    else:
                        cs = cosT[:, 4 * ft:4 * ft + 4, :].re("p a b -> p (a b)"); sn = sinT[:, 4 * ft:4 * ft + 4, :].re("p a b -> p (a b)")
                        w3 = lambda v: v
                    br = w3(pbr[:, 0:W4]); bi = w3(pbi[:, 0:W4])
                    k.tt(w3(ta.a), cs, br, ALU.mult); k.tt(w3(tb.a), sn, bi, ALU.mult); k.tt(ur.a, ta.a, tb.a, ALU.add)
                    k.tt(w3(ta.a), cs, bi, ALU.mult); k.tt(w3(tb.a), sn, br, ALU.mult); k.tt(ui.a, ta.a, tb.a, ALU.subtract)
                    for cl in range(4):
                        ct = 4 * ft + cl
                        o_ = cl * 128
                        rb = V(rho, AP(rho.h, ct, [[32, 128], [0, 8 if sample else LC]]))
                        if sample:
                            for b in range(16):
                                k.scan(hr[:, o_ + b * 8:o_ + (b + 1) * 8], rb, ur[:, o_ + b * 8:o_ + (b + 1) * 8], hst[:, 0, ct, b:b + 1])
                                k.scan(hi[:, o_ + b * 8:o_ + (b + 1) * 8], rb, ui[:, o_ + b * 8:o_ + (b + 1) * 8], hst[:, 1, ct, b:b + 1])
                        else:
                            k.scan(hr[:, o_:o_ + 128], rb, ur[:, o_:o_ + 128], hpr[:, 0, ct:ct + 1])
                            k.scan(hi[:, o_:o_ + 128], rb, ui[:, o_:o_ + 128], hpr[:, 1, ct:ct + 1])
                    k.tt(w3(ta.a), cs, w3(hr.a), ALU.mult); k.tt(w3(tb.a), sn, w3(hi.a), ALU.mult); k.tt(h2r.a, ta.a, tb.a, ALU.subtract)
                    k.tt(w3(ta.a), cs, w3(hi.a), ALU.mult); k.tt(w3(tb.a), sn, w3(hr.a), ALU.mult); k.tt(h2i.a, ta.a, tb.a, ALU.add)
                    k.copy(Hre4[:, :, sl], h2r.a.re("p (a b) -> p a b", a=4), en='act')
                    k.copy(Him4[:, :, sl], h2i.a.re("p (a b) -> p a b", a=4), en='act')
                    if sample:
                        k.copy(hso[:, 0, 4 * ft:4 * ft + 4, :], h2r.a.re("p (a b t) -> p a b t", a=4, t=8)[:, :, :, 7], en='act')
                        k.copy(hso[:, 1, 4 * ft:4 * ft + 4, :], h2i.a.re("p (a b t) -> p a b t", a=4, t=8)[:, :, :, 7], en='act')
                    else:
                        k.copy(hpr[:, 0, 4 * ft:4 * ft + 4], h2r.a.re("p (a b) -> p a b", a=4)[:, :, LC - 1], en='act')
                        k.copy(hpr[:, 1, 4 * ft:4 * ft + 4], h2i.a.re("p (a b) -> p a b", a=4)[:, :, LC - 1], en='act')
                py = self.psum()
                for cl in range(4):
                    k.mm(py[:, 0:n], LCm[0][:, 4 * ft + cl, :], Hre4[:, cl, 0:n], start=(cl == 0), stop=False, inc=False)
                    k.mm(py[:, 0:n], LCm[1][:, 4 * ft + cl, :], Him4[:, cl, 0:n], start=False, stop=(cl == 3), inc=True)
                k.stt(yt[:, 0:n], x_[:, ft, 0:n], dcol[:, ft:ft + 1], py[:, 0:n], ALU.mult, ALU.add)
                k.tt(gt[:, 0:n], yt[:, 0:n], yt[:, 0:n], ALU.mult)
                k.ts(gt[:, 0:n], gt[:, 0:n], 0.044715, ALU.mult, 1.0, ALU.add)
                k.tt(gt[:, 0:n], gt[:, 0:n], yt[:, 0:n], ALU.mult)
                k.act(gt[:, 0:n], gt[:, 0:n], AF.Sigmoid, scale=2.0 * 0.7978845608028654)
                k.tt(gbf[:, ft, 0:n], gt[:, 0:n], yt[:, 0:n], ALU.mult)
            for j in range(KC):
                wv_ = wvj[wn % 3]; wg_ = wgj[wn % 3]; wn += 1
                k.dma(wv_.a, I["l1_glu_v"].v(j * 128, [[D, 128], [128 * D, KC], [1, 128]]), q='pool')
                k.dma(wg_.a, I["l1_glu_g"].v(j * 128, [[D, 128], [128 * D, KC], [1, 128]]), q='pool')
                pv = self.psum(); pg = self.psum()
                for kk in range(KC):
                    k.mm(pv[:, 0:n], wv_[:, kk, :], gbf[:, kk, 0:n], start=(kk == 0), stop=(kk == KC - 1))
                for kk in range(KC):
                    k.mm(pg[:, 0:n], wg_[:, kk, :], gbf[:, kk, 0:n], start=(kk == 0), stop=(kk == KC - 1))
                sg_ = sgt[j % 2]
                k.act(sg_[:, 0:n], pg[:, 0:n], AF.Sigmoid)
                k.tt(sg_[:, 0:n], sg_[:, 0:n], pv[:, 0:n], ALU.mult)
                k.tt(self.h[i][:, j, :], self.h[i][:, j, :], sg_[:, 0:n], ALU.add)
        k.dma(O["s5_pstate"].a, hpr.a)
        k.dma(O["s5_sstate"].a, hso.a)
        k.end_defer()
        k.end_phase()


M_IN = 2048
M_CD = 3072
NEG = -1.0e5


class MambaMixin:
    def precast(self, name, src, row_len, col0, ncols, nk, tile_cols=128):
        k = self.k
        nt = (ncols + tile_cols - 1) // tile_cols
        dst = k.dram_tmp(name, [nt, 128, nk, tile_cols], BF16)
        for m in range(nt):
            w = min(tile_cols, ncols - m * tile_cols)
            k.dma(dst.v(m * 128 * nk * tile_cols, [[nk * tile_cols, 128], [tile_cols, nk], [1, w]]),
                  src.v(col0 + m * tile_cols, [[row_len, 128], [128 * row_len, nk], [1, w]]), q='pool')
        return dst

    def load_tile(self, dst_sb, scratch, m, nk, tile_cols=128, w=None, q='sp'):
        w = w or tile_cols
        self.k.dma(dst_sb[:, :, 0:w], scratch.v(m * 128 * nk * tile_cols, [[nk * tile_cols, 128], [tile_cols, nk], [1, w]]), q=q)

    def mamba_layer(self, l, I, O, win_bf, wout_bf):
        k = self.k
        sb = k.sbp
        cw = sb("m_cw", [128, 24, 4]); k.dma(cw.a, I["m_convw"].a)
        cb = sb("m_cb", [128, 24]); k.dma(cb.a, I["m_convb"].a)
        dtb = sb("m_dtb", [32, 1]); k.dma(dtb.a, I["m_dtb"].a)
        aneg = sb("m_aneg", [128, 32]); k.dma(aneg.a, I["m_alog"].a)
        k.act(aneg.a, aneg.a, AF.Exp); k.ts(aneg.a, aneg.a, -1.0, ALU.mult)
        dcol = sb("m_dcol", [128, 16]); k.dma(dcol.a, I["m_dcol"].a)
        nw = sb("m_nw", [128, 16]); k.dma(nw.a, I["m_normw"].a)
        U = sb("m_U", [128, 128]); k.dma(U.a, I["c_U"].a)
        negm = sb("m_negm", [128, 128]); k.dma(negm.a, I["c_negP"].a)
        carry = sb("m_carry", [128, 24, 3]); k.memset(carry.a, 0.0)
        eps512 = self.eps_col
        wt = [sb("m_wt%d" % i, [128, 8, 128], BF16) for i in range(4)]
        wo = [sb("m_wo%d" % i, [128, 16, 128], BF16) for i in range(2)]
        wcnt = [0, 0]

        def front(i, n, xn, zs, xc, dtf, sample, convst=None, sconv=None):
            self.rmsnorm(lambda kk, i=i: self.h[i][:, kk, :], lambda kk: xn[:, kk, 0:n], self.gain(0, l), n, self.tmp_sq, self.tmp_r)
            for m in range(41):
                w_ = wt[wcnt[0] % 4]; wcnt[0] += 1
                wd = 128 if m < 40 else 32
                self.load_tile(w_, win_bf, m, 8, w=wd)
                pp = self.psum()
                for kk in range(KC):
                    k.mm(pp[0:wd, 0:n], w_[:, kk, 0:wd], xn[:, kk, 0:n], start=(kk == 0), stop=(kk == KC - 1))
                if m < 16:
                    k.act(zs[:, m, 0:n], pp[:, 0:n], AF.Silu)
                elif m < 40:
                    mc = m - 16
                    if sample:
                        xr3 = xr_s.a
                        k.copy(xr3[:, :, 0:3], convst[:, mc, :, :], en='act')
                        k.copy(xr3[:, :, 3:11], pp[:, 0:n].re("p (b t) -> p b t", t=8), en='act')
                        acc = xc[:, mc, 0:n].re("p (b t) -> p b t", t=8)
                        k.ts(acc, xr3[:, :, 3:11], cw[:, mc, 3:4], ALU.mult, cb[:, mc:mc + 1], ALU.add)
                        for j in range(3):
                            k.stt(acc, xr3[:, :, j:j + 8], cw[:, mc, j:j + 1], acc, ALU.mult, ALU.add)
                        k.copy(sconv[:, mc, :, :], xr3[:, :, 8:11], en='act')
                    else:
                        xr_p = xr_pl[mc % 2]
                        k.copy(xr_p[:, 0:3], carry[:, mc, :], en='act')
                        k.copy(xr_p[:, 3:3 + n], pp[:, 0:n], en='act')
                        acc = xc[:, mc, 0:n]
                        k.ts(acc, xr_p[:, 3:3 + n], cw[:, mc, 3:4], ALU.mult, cb[:, mc:mc + 1], ALU.add)
                        for j in range(3):
                            k.stt(acc, xr_p[:, j:j + n], cw[:, mc, j:j + 1], acc, ALU.mult, ALU.add)
                        k.copy(carry[:, mc, :], xr_p[:, n:n + 3], en='act')
                    k.act(xc[:, mc, 0:n], xc[:, mc, 0:n], AF.Silu)
                else:
                    k.act(dtf[0:32, 0:n], pp[0:32, 0:n], AF.Exp, bias=dtb[:, 0:1])
                    k.ts(dtf[0:32, 0:n], dtf[0:32, 0:n], 1.0, ALU.add)
                    k.act(dtf[0:32, 0:n], dtf[0:32, 0:n], AF.Ln)

        def to_tm(xc, dtf, c0, Xdt, dt_tm, A_tm):
            pt = self.psum()
            k.tr(pt[:, 0:32], dtf[0:32, c0:c0 + 128], self.ident[0:32, 0:32])
            k.copy(dt_tm.a, pt[:, 0:32], en='act')
            k.tt(A_tm.a, dt_tm.a, aneg.a, ALU.mult)
            for kt in range(16):
                pt = self.psum()
                k.tr(pt[:, 0:128], xc[:, kt, c0:c0 + 128], self.ident.a)
                k.tt(Xdt[:, kt * 128:(kt + 1) * 128].re("p (a b) -> p a b", a=2), pt[:, 0:128].re("p (a b) -> p a b", a=2),
                     V(dt_tm, AP(dt_tm.h, 2 * kt, [[32, 128], [1, 2], [0, 64]])), ALU.mult)

        def back_fm(y_tm, xc, y_fm, c0):
            for kt in range(16):
                pt = self.psum()
                k.tr(pt[:, 0:128], y_tm[:, kt * 128:(kt + 1) * 128], self.ident.a)
                k.stt(y_fm[:, kt, c0:c0 + 128], xc[:, kt, c0:c0 + 128], dcol[:, kt:kt + 1], pt[:, 0:128], ALU.mult, ALU.add)

        def post(i, n, y_fm, zs, yn):
            for kt in range(16):
                k.tt(y_fm[:, kt, 0:n], y_fm[:, kt, 0:n], zs[:, kt, 0:n], ALU.mult)
            for gq in range(4):
                pss = self.psum()
                for a in range(4):
                    k.act(self.tmp_sq[:, a, 0:n], y_fm[:, 4 * gq + a, 0:n], AF.Square)
                for a in range(4):
                    k.mm(pss[:, 0:n], self.ones.a, self.tmp_sq[:, a, 0:n], start=(a == 0), stop=(a == 3))
                k.act(self.tmp_r[:, 0:n], pss[:, 0:n], AF.Sqrt, bias=self.eps_col[:, 0:1], scale=1.0 / 512)
                k.recip(self.tmp_r[:, 0:n], self.tmp_r[:, 0:n])
                for a in range(4):
                    kt = 4 * gq + a
                    k.stt(yn[:, kt, 0:n], y_fm[:, kt, 0:n], nw[:, kt:kt + 1], self.tmp_r[:, 0:n], ALU.mult, ALU.mult)
            for j in range(KC):
                w_ = wo[wcnt[1] % 2]; wcnt[1] += 1
                self.load_tile(w_, wout_bf, j, 16)
                pj = self.psum()
                for kt in range(16):
                    k.mm(pj[:, 0:n], w_[:, kt, :], yn[:, kt, 0:n], start=(kt == 0), stop=(kt == 15))
                k.tt(self.h[i][:, j, :], self.h[i][:, j, :], pj[:, 0:n], ALU.add)

        mk_ = k.mark()
        xn = sb("m_xn", [128, 8, BLK], BF16); zs = sb("m_zs", [128, 16, BLK], BF16)
        xc = sb("m_xc", [128, 24, BLK]); dtf = sb("m_dtf", [32, BLK]); xr_pl = [sb("m_xrp%d" % i, [128, 3 + BLK]) for i in range(2)]
        y_fm = sb("m_yfm", [128, 16, BLK]); yn = sb("m_yn", [128, 16, BLK], BF16)
        Xdt = sb("m_Xdt", [128, 2048]); y_tm = sb("m_ytm", [128, 2048])
        dt_tm = sb("m_dttm", [128, 32]); A_tm = sb("m_Atm", [128, 32]); decT = sb("m_decT", [128, 32]); E_tm = sb("m_Etm", [128, 32])
        B_tm = sb("m_Btm", [128, 4, 128]); CBt = sb("m_CBt", [128, 4, 128])
        ST = sb("m_ST", [128, 32, 64]); k.memset(ST.a, 0.0)
        ND = 4
        rhs4 = [sb("m_r4%d" % i, [128, 512]) for i in range(3)]; tD = [sb("m_tD%d" % i, [128, 128]) for i in range(ND)]
        negAc = sb("m_negAc", [128, 32]); m01 = sb("m_m01", [128, 128]); k.dma(m01.a, I["c_m01"].a)
        Lt = [sb("m_Lt%d" % i, [128, 128]) for i in range(ND)]; Gt = [sb("m_Gt%d" % i, [128, 128]) for i in range(ND)]
        yo = [sb("m_yo%d" % i, [128, 64]) for i in range(ND)]; Xde = [sb("m_Xde%d" % i, [128, 64]) for i in range(ND)]
        k.begin_defer()
        for i, (s0, n) in enumerate(TBS):
            if s0 >= TP:
                continue
            front(i, n, xn, zs, xc, dtf, False)
            for c in range(n // 128):
                c0 = c * 128
                first = (s0 + c0 == 0)
                to_tm(xc, dtf, c0, Xdt, dt_tm, A_tm)
                for g in range(4):
                    pt = self.psum()
                    k.tr(pt[:, 0:128], xc[:, 16 + g, c0:c0 + 128], self.ident.a)
                    k.copy(B_tm[:, g, :], pt[:, 0:128], en='act')
                    pc = self.psum()
                    k.mm(pc[:, 0:128], xc[:, 16 + g, c0:c0 + 128], xc[:, 20 + g, c0:c0 + 128])
                    k.tt(CBt[:, g, :], pc[:, 0:128], m01.a, ALU.mult)
                pa = self.psum()
                k.mm(pa[:, 0:32], self.ones.a, A_tm.a)
                k.act(decT.a, pa[:, 0:32], AF.Exp)
                pa = self.psum()
                k.mm(pa[:, 0:32], U.a, A_tm.a)
                k.act(E_tm.a, pa[:, 0:32], AF.Exp)
                k.act(negAc.a, pa[:, 0:32], AF.Copy, scale=-1.0)
                pD4 = {}

                def stageA1(hq):
                    r4 = rhs4[hq % 3]
                    k.tt(r4.a.re("p (a b) -> p a b", a=4), V(U, AP(U.h, 0, [[128, 128], [0, 4], [1, 128]])),
                         V(A_tm, AP(A_tm.h, 4 * hq, [[32, 128], [1, 4], [0, 128]])), ALU.mult)
                    pD4[hq] = self.psum(0, 3)
                    k.mm(pD4[hq][:, 0:512], self.ones.a, r4.a)

                def stageA2a(h):
                    td = tD[h % ND]; lt = Lt[h % ND]
                    hh = h % 4
                    k.ts(td.a, pD4[h // 4][:, hh * 128:(hh + 1) * 128], negAc[:, h:h + 1], ALU.add, 0.0, ALU.min)
                    k.act(lt.a, td.a, AF.Exp)

                def stageA2b(h):
                    g = h // 8
                    lt = Lt[h % ND]; gt = Gt[h % ND]; xde = Xde[h % ND]
                    k.tt(gt.a, lt.a, CBt[:, g, :], ALU.mult)
                    k.ts(xde.a, Xdt[:, h * 64:(h + 1) * 64], lt[:, 127:128], ALU.mult)

                pBd = {}

                def stageBpe(h):
                    g = h // 8
                    gt = Gt[h % ND]; xde = Xde[h % ND]
                    pB = self.psum(3, 8)
                    pBd[h] = pB
                    k.mm(pB[:, 0:64], gt.a, Xdt[:, h * 64:(h + 1) * 64])
                    if not first:
                        k.mm(pB[:, 64:128], xc[:, 20 + g, c0:c0 + 128], ST[:, h, :])
                    k.mm(pB[:, 128:192], B_tm[:, g, :], xde.a)

                def stageBev(h):
                    yo_ = yo[h % ND]
                    pB = pBd.pop(h)
                    if not first:
                        k.ts(yo_.a, pB[:, 64:128], E_tm[:, h:h + 1], ALU.mult)
                        k.tt(y_tm[:, h * 64:(h + 1) * 64], yo_.a, pB[:, 0:64], ALU.add)
                    else:
                        k.copy(y_tm[:, h * 64:(h + 1) * 64], pB[:, 0:64], en='dve')
                    k.stt(ST[:, h, :], ST[:, h, :], decT[:, h:h + 1], pB[:, 128:192], ALU.mult, ALU.add)

                stageA1(0)
                stageA1(1)
                stageA2a(0)
                for h in range(32):
                    if h % 4 == 0 and h // 4 + 2 < 8:
                        stageA1(h // 4 + 2)
                    if h + 1 < 32:
                        stageA2a(h + 1)
                    if h >= 1:
                        stageBpe(h - 1)
                    stageA2b(h)
                    if h >= 1:
                        stageBev(h - 1)
                stageBpe(31)
                stageBev(31)
                back_fm(y_tm, xc, y_fm, c0)
            post(i, n, y_fm, zs, yn)
        k.dma(O["m_pconv"].a, carry.a)
        k.dma(O["m_pssmT"].a, ST.a)
        k.end_defer()
        k.free_to(mk_)
        i = len(TBS) - 1
        n = TS
        xn = sb("ms_xn", [128, 8, TS], BF16); zs = sb("ms_zs", [128, 16, TS], BF16)
        xc = sb("ms_xc", [128, 24, TS]); dtf = sb("ms_dtf", [32, TS]); xr_s = sb("ms_xrs", [128, 16, 11])
        Xdt = sb("ms_Xdt", [128, 2048]); y_tm = sb("ms_ytm", [128, 2048])
        dt_tm = sb("ms_dttm", [128, 32]); A_tm = sb("ms_Atm", [128, 32])
        BC_tm = sb("ms_BCtm", [128, 8, 128])
        convst = sb("ms_convst", [128, 24, 16, 3]); k.dma(convst.a, I["m_convst"].a)
        sconv = sb("ms_sconv", [128, 24, 16, 3])
        alq = sb("ms_alq", [128, 1]); k.dma(alq.a, I["m_alogq"].a)
        k.act(alq.a, alq.a, AF.Exp); k.ts(alq.a, alq.a, -1.0, ALU.mult)
        k.begin_defer()
        front(i, n, xn, zs, xc, dtf, True, convst, sconv)
        k.dma(O["m_sconv"].a, sconv.a)
        to_tm(xc, dtf, 0, Xdt, dt_tm, A_tm)
        for g8 in range(8):
            pt = self.psum()
            k.tr(pt[:, 0:128], xc[:, 16 + g8, 0:128], self.ident.a)
            k.copy(BC_tm[:, g8, :], pt[:, 0:128], en='act')
        sx = k.dram_tmp("ms_sx", [128, 2048]); sbc = k.dram_tmp("ms_sbc", [128, 1024]); sdt = k.dram_tmp("ms_sdt", [128, 32])
        sy = k.dram_tmp("ms_sy", [128, 2048])
        k.dma(sx.a, Xdt.a); k.dma(sbc.a, BC_tm.a.re("p a b -> p (a b)")); k.dma(sdt.a, dt_tm.a)
        mk_r = k.mark()
        Xq = sb("ms_Xq", [128, 8, 64]); Bq = sb("ms_Bq", [128, 8, 128]); Cq = sb("ms_Cq", [128, 8, 128])
        dtq = sb("ms_dtq", [128, 8, 1]); dAq = sb("ms_dAq", [128, 8]); yq = sb("ms_yq", [128, 8, 64])
        S = sb("ms_S", [128, 32, 128]); tmp = sb("ms_tmp", [128, 32, 128])
        ssm_in = I["m_ssm"]; ssm_out = O["m_sssm"]
        for r in range(4):
            for b4 in range(4):
                tok0 = (4 * r + b4) * 8
                k.dma(Xq[b4 * 32:(b4 + 1) * 32, :, :], sx.v(tok0 * 2048, [[64, 32], [2048, 8], [1, 64]]))
                k.dma(dtq[b4 * 32:(b4 + 1) * 32, :, :], sdt.v(tok0 * 32, [[1, 32], [32, 8], [1, 1]]), allow_slow_non_contiguous=True)
                for g in range(4):
                    p0 = b4 * 32 + g * 8
                    k.dma(Bq[p0:p0 + 8, :, :], sbc.v(tok0 * 1024 + g * 128, [[0, 8], [1024, 8], [1, 128]]))
                    k.dma(Cq[p0:p0 + 8, :, :], sbc.v(tok0 * 1024 + 512 + g * 128, [[0, 8], [1024, 8], [1, 128]]))
            k.act(dAq.a, dtq.a.re("p a b -> p (a b)"), AF.Exp, scale=alq[:, 0:1])
            for ph in range(2):
                k.dma(S.a.re("p a b -> p (a b)"), ssm_in.v(4 * r * 32 * 8192 + ph * 4096, [[8192, 128], [1, 4096]]))
                for t in range(8):
                    xin = V(Xq, AP(Xq.h, t * 64 + ph * 32, [[512, 128], [1, 32], [0, 128]]))
                    bin_ = V(Bq, AP(Bq.h, t * 128, [[1024, 128], [0, 32], [1, 128]]))
                    cin = V(Cq, AP(Cq.h, t * 128, [[1024, 128], [0, 32], [1, 128]]))
                    k.tt(tmp.a, xin, bin_, ALU.mult)
                    k.stt(S.a, S.a, dAq[:, t:t + 1], tmp.a, ALU.mult, ALU.add)
                    k.tt(tmp.a, S.a, cin, ALU.mult)
                    k.reduce(yq[:, t, ph * 32:(ph + 1) * 32], tmp.a)
                k.dma(ssm_out.v(4 * r * 32 * 8192 + ph * 4096, [[8192, 128], [1, 4096]]), S.a.re("p a b -> p (a b)"))
            for b4 in range(4):
                tok0 = (4 * r + b4) * 8
                k.dma(sy.v(tok0 * 2048, [[64, 32], [2048, 8], [1, 64]]), yq[b4 * 32:(b4 + 1) * 32, :, :])
        k.end_defer()
        k.free_to(mk_r)
        y_fm = sb("ms_yfm", [128, 16, TS]); yn = sb("ms_yn", [128, 16, TS], BF16)
        k.begin_defer()
        k.dma(y_tm.a, sy.a)
        back_fm(y_tm, xc, y_fm, 0)
        post(i, n, y_fm, zs, yn)
        k.end_defer()
        k.end_phase()


SB = 128
CH = 64
NP = 4


class RwkvMixin:
    def rwkv_layer(self, l, I, O, W, pre, vfirst):
        k = self.k
        sb = k.sbp
        has_v = (l == 3)
        n = SB
        vec = sb("rw_vec", [128, 8, 8]); k.dma(vec.a, I[pre + "vec"].a)
        mu = sb("rw_mu", [128, 6, 8]); k.dma(mu.a, I[pre + "mu"].a)
        w1 = sb("rw_w1", [128, 8, 64], BF16); k.dma(w1.a, I[pre + "w1"].v(0, [[64, 128], [128 * 64, 8], [1, 64]]), q='pool')
        a1 = sb("rw_a1", [128, 8, 64], BF16); k.dma(a1.a, I[pre + "a1"].v(0, [[64, 128], [128 * 64, 8], [1, 64]]), q='pool')
        g1 = sb("rw_g1", [128, 8, 160], BF16); k.dma(g1.a, I[pre + "g1"].v(0, [[160, 128], [128 * 160, 8], [1, 160]]), q='pool')
        w2 = sb("rw_w2", [64, 1024], BF16); k.dma(w2.a, I[pre + "w2"].a, q='pool')
        a2 = sb("rw_a2", [64, 1024], BF16); k.dma(a2.a, I[pre + "a2"].a, q='pool')
        g2a = sb("rw_g2a", [128, 1024], BF16); k.dma(g2a.a, I[pre + "g2"].v(0, [[1024, 128], [1, 1024]]), q='pool')
        g2b = sb("rw_g2b", [32, 1024], BF16); k.dma(g2b.a, I[pre + "g2"].v(128 * 1024, [[1024, 32], [1, 1024]]), q='pool')
        if has_v:
            v1 = sb("rw_v1", [128, 8, 32], BF16); k.dma(v1.a, I[pre + "v1"].v(0, [[32, 128], [128 * 32, 8], [1, 32]]), q='pool')
            v2 = sb("rw_v2", [32, 1024], BF16); k.dma(v2.a, I[pre + "v2"].a, q='pool')
        bones = sb("rw_bones", [128, 128]); k.dma(bones.a, I["c_bones"].a)
        gneps = sb("rw_gneps", [128, 1]); k.memset(gneps.a, 64e-5)
        VW0, VA0, VKK, VKA, VRK, VLW, VLB, VV0 = range(8)
        vech = sb("rw_vech", [128, 8, 8]); k.ts(vech.a, vec.a, 0.5, ALU.mult)

        def colh(t, j):
            return vech[:, t, j:j + 1]

        def col(t, j):
            return vec[:, t, j:j + 1]

        xx = sb("rw_xx", [128, 8, SB])
        xm = [sb("rw_xm%d" % c, [128, 8, SB], BF16) for c in range(6)]
        tw = sb("rw_tw", [64, SB], BF16); ta = sb("rw_ta", [64, SB], BF16)
        tgf = sb("rw_tgf", [128, SB]); tga = sb("rw_tga", [128, SB], BF16); tgb = sb("rw_tgb", [32, SB], BF16)
        tv = sb("rw_tv", [32, SB], BF16)
        att = [sb("rw_att%d" % j, [128, SB], BF16) for j in range(8)]
        wtl = [[sb("rw_wt%d_%d" % (i, c), [128, 8, 128], BF16) for c in range(3)] for i in range(2)]

        wc = [0, 0]
        names = ["r", "lw", "k", "v", "an", "bn", "g", "bonus", "out", "t0", "t1", "t2", "cum", "P", "Pi", "Pe"]
        PT = [{nm: sb("rw_%s_%d" % (nm, q), [128, SB]) for nm in names} for q in range(2)]
        mk_p = k.mark()
        PT += [{nm: sb("rw_%s_%d" % (nm, q), [128, SB]) for nm in names} for q in range(2, NP)]
        MSU = sb("rw_MSU", [128, 128]); k.dma(MSU.a, I["c_MSU"].a)
        MIU = sb("rw_MIU", [128, 128]); k.dma(MIU.a, I["c_MIU"].a)
        MSL = sb("rw_MSL", [128, 128]); k.dma(MSL.a, I["c_MSL"].a)
        XNp = sb("rw_XNp", [128, 8, SB + 1]); k.memset(XNp.a, 0.0)
        ssp = sb("rw_ssp", [128, 8])
        woR = sb("rw_woR", [128, 8, 8, 128], BF16)
        for jo in range(8):
            self.load_tile(woR[:, jo, :, :], W["o"], jo, 8)
        bdn = ["b_bd", "k_bd", "v_bd", "Mrb", "Mak", "Mrk", "A0", "At0", "A1", "At1", "VT", "X", "bT", "kT", "S0T", "ts"]
        BD = [{nm: sb("rw_%s_%d" % (nm, q), [128, 128], (F32 if nm == "ts" else BF16)) for nm in bdn} for q in range(NP)]
        AR = [sb("rw_AR_%d" % q, [128, 256], BF16) for q in range(NP)]
        identb = sb("rw_identb", [128, 128], BF16); k.copy(identb.a, self.ident.a)
        for q in range(NP):
            for nm in ["b_bd", "k_bd", "v_bd", "S0T"]:
                k.memset(BD[q][nm].a, 0.0)
            k.memset(AR[q].a, 0.0)
        SALL = [sb("rw_SALL%d" % j, [128, 128]) for j in range(8)]
        for j in range(8):
            k.memset(SALL[j].a, 0.0)

        def lo(q):
            return slice(0, 64) if q == 0 else slice(64, 128)

        def front(sbi):
            sample = (sbi == 16)
            if not sample:
                blk, half = sbi // 2, sbi % 2
                hv = lambda kk: self.h[blk][:, kk, half * SB:(half + 1) * SB]
                self.rmsnorm(hv, lambda kk: XNp[:, kk, 1:SB + 1], self.gain(0, l), n, self.tmp_sq, self.tmp_r)
                X = lambda kk: XNp[:, kk, 1:SB + 1]
                for kk in range(8):
                    k.tt(xx[:, kk, :], XNp[:, kk, 0:SB], XNp[:, kk, 1:SB + 1], ALU.subtract)
            else:
                hv = lambda kk: self.h[8][:, kk, :]
                self.rmsnorm(hv, lambda kk: xnc[:, kk, :], self.gain(0, l), n, self.tmp_sq, self.tmp_r)
                k.dma(sst.a, I[pre + "shift"].a)
                for kk in range(8):
                    k.copy(XNs[:, kk, :, 0], sst[:, kk, :], en='act')
                    k.copy(XNs[:, kk, :, 1:9], xnc[:, kk, :].re("p (b t) -> p b t", t=8), en='act')
                    k.tt(xx[:, kk, :].re("p (b t) -> p b t", t=8), XNs[:, kk, :, 0:8], XNs[:, kk, :, 1:9], ALU.subtract)
                X = lambda kk: xnc[:, kk, :]
            for c in range(6):
                for kk in range(8):
                    k.stt(xm[c][:, kk, :], xx[:, kk, :], mu[:, c, kk:kk + 1], X(kk), ALU.mult, ALU.add)
            p = self.psum()
            for kk in range(8):
                k.mm(p[0:64, 0:n], w1[:, kk, :], xm[1][:, kk, :], start=(kk == 0), stop=(kk == 7))
            k.act(tw.a, p[0:64, 0:n], AF.Tanh)
            p = self.psum()
            for kk in range(8):
                k.mm(p[0:64, 0:n], a1[:, kk, :], xm[4][:, kk, :], start=(kk == 0), stop=(kk == 7))
            k.copy(ta.a, p[0:64, 0:n], en='act')
            p = self.psum()
            for kk in range(8):
                k.mm(p[:, 0:n], g1[:, kk, 0:128], xm[5][:, kk, :], start=(kk == 0), stop=(kk == 7))
            k.act(tgf.a, p[:, 0:n], AF.Tanh, scale=0.5)
            k.ts(tga.a, tgf.a, 0.5, ALU.mult, 0.5, ALU.add)
            p = self.psum()
            for kk in range(8):
                k.mm(p[0:32, 0:n], g1[:, kk, 128:160], xm[5][:, kk, :], start=(kk == 0), stop=(kk == 7))
            k.act(tgf[0:32, :], p[0:32, 0:n], AF.Tanh, scale=0.5)
            k.ts(tgb.a, tgf[0:32, :], 0.5, ALU.mult, 0.5, ALU.add)
            if has_v:
                p = self.psum()
                for kk in range(8):
                    k.mm(p[0:32, 0:n], v1[:, kk, :], xm[3][:, kk, :], start=(kk == 0), stop=(kk == 7))
                k.copy(tv.a, p[0:32, 0:n], en='act')
            if not sample:
                if sbi == 15:
                    k.copy(ssp.a, XNp[:, :, SB], en='act')
                    k.dma(O[pre + "shift_p"].a, ssp.a)
                else:
                    for kk in range(8):
                        k.copy(XNp[:, kk, 0:1], XNp[:, kk, SB:SB + 1], en='act')
            else:
                for kk in range(8):
                    k.copy(sso[:, kk, :], xnc[:, kk, :].re("p (b t) -> p b t", t=8)[:, :, 7], en='act')
                k.dma(O[pre + "shift_s"].a, sso.a)

        def partA(sbi, j, q, pr=(0, 8)):
            T = PT[q]
            tok0 = sbi * SB
            jc = slice(j * 128, (j + 1) * 128)
            ws = wtl[(q // 2) % len(wtl)]
            for c, nm in enumerate(["r", "k", "v"]):
                self.load_tile(ws[c], W[nm], j, 8)
            for c, (nm, mi) in enumerate([("r", 0), ("t0", 2), ("v", 3)]):
                p = self.psum(*pr)
                for kk in range(8):
                    k.mm(p[:, 0:n], ws[c][:, kk, :], xm[mi][:, kk, :], start=(kk == 0), stop=(kk == 7))
                k.copy(T[nm].a, p[:, 0:n], en='act')
            k0 = T["t0"]
            p = self.psum(*pr)
            k.mm(p[:, 0:n], w2[0:64, jc], tw.a)
            k.act(T["lw"].a, p[:, 0:n], AF.Tanh, bias=colh(VW0, j), scale=0.5)
            k.ts(T["lw"].a, T["lw"].a, -0.3032653298563167, ALU.mult, -0.3032653298563167, ALU.add)
            p = self.psum(*pr)
            k.mm(p[:, 0:n], a2[0:64, jc], ta.a)
            a_ = T["t1"]
            k.act(a_.a, p[:, 0:n], AF.Tanh, bias=colh(VA0, j), scale=0.5)
            k.ts(a_.a, a_.a, 0.5, ALU.mult, 0.5, ALU.add)
            p = self.psum(*pr)
            k.mm(p[:, 0:n], g2a[:, jc], tga.a, start=True, stop=False, inc=False)
            k.mm(p[:, 0:n], g2b[0:32, jc], tgb.a, start=False, stop=True)
            k.copy(T["g"].a, p[:, 0:n], en='act')
            if has_v:
                p = self.psum(*pr)
                k.mm(p[:, 0:n], v2[0:32, jc], tv.a)
                sv = T["t2"]
                k.act(sv.a, p[:, 0:n], AF.Tanh, bias=colh(VV0, j), scale=0.5)
                k.ts(sv.a, sv.a, 0.5, ALU.mult, 0.5, ALU.add)
                vf = T["cum"]
                k.dma(vf.a, vfirst.v(j * 128 * TT + tok0, [[TT, 128], [1, n]]))
                k.tt(vf.a, vf.a, T["v"].a, ALU.subtract)
                k.tt(vf.a, vf.a, sv.a, ALU.mult)
                k.tt(T["v"].a, T["v"].a, vf.a, ALU.add)
            else:
                k.dma(vfirst.v(j * 128 * TT + tok0, [[TT, 128], [1, n]]), T["v"].a)
            kkn = T["t2"]
            k.ts(kkn.a, k0.a, col(VKK, j), ALU.mult)
            sq = T["cum"]
            k.tt(sq.a, kkn.a, kkn.a, ALU.mult)
            p = self.psum(*pr)
            k.mm(p[:, 0:n], bones.a, sq.a)
            k.ts(sq.a, p[:, 0:n], 1e-24, ALU.max)
            k.act(sq.a, sq.a, AF.Sqrt)
            k.recip(sq.a, sq.a)
            k.tt(kkn.a, kkn.a, sq.a, ALU.mult)
            tt_ = T["P"]
            k.ts(tt_.a, a_.a, -1.0, ALU.add, col(VKA, j), ALU.mult)
            k.ts(tt_.a, tt_.a, 1.0, ALU.add)
            k.tt(T["k"].a, k0.a, tt_.a, ALU.mult)
            k.ts(T["an"].a, kkn.a, -1.0, ALU.mult)
            k.tt(T["bn"].a, kkn.a, a_.a, ALU.mult)
            k.stt(tt_.a, T["r"].a, col(VRK, j), T["k"].a, ALU.mult, ALU.mult)
            p = self.psum(*pr)
            k.mm(p[:, 0:n], bones.a, tt_.a)
            k.tt(T["bonus"].a, T["v"].a, p[:, 0:n], ALU.mult)

        def chunk(js, c, slots=None, pr=(0, 8)):
            slots = list(range(len(js))) if slots is None else slots
            cs = slice(c * CH, (c + 1) * CH)
            onesb = V(self.ones, AP(self.ones.h, 0, [[128, 128], [0, CH]]))
            for u, q in enumerate(slots):
                T = PT[q]
                k.scan(T["cum"][:, cs], onesb, T["lw"][:, cs], 0.0)
                k.act(T["P"][:, cs], T["cum"][:, cs], AF.Exp)
                k.act(T["Pi"][:, cs], T["cum"][:, cs], AF.Exp, scale=-1.0)
                k.tt(T["Pe"][:, cs], T["cum"][:, cs], T["lw"][:, cs], ALU.subtract)
                k.act(T["Pe"][:, cs], T["Pe"][:, cs], AF.Exp)
            yield
            for u, q in enumerate(slots):
                T = PT[q]; B = BD[q]
                for hh in range(2):
                    ps_ = lo(hh); fs = slice(hh * 64, hh * 64 + 64)
                    k.tt(AR[q][ps_, fs], T["an"][ps_, cs], T["Pe"][ps_, cs], ALU.mult)
                    k.tt(AR[q][ps_, 128 + hh * 64:128 + hh * 64 + 64], T["r"][ps_, cs], T["P"][ps_, cs], ALU.mult)
                    k.tt(B["b_bd"][ps_, fs], T["bn"][ps_, cs], T["Pi"][ps_, cs], ALU.mult)
                    k.tt(B["k_bd"][ps_, fs], T["k"][ps_, cs], T["Pi"][ps_, cs], ALU.mult)
                    k.copy(B["v_bd"][ps_, fs], T["v"][ps_, cs], en='act')
                k.copy(B["S0T"].a, SALL[js[u]].a, en='act')
                yield
            b1 = {}; b2 = {}
            for q in slots:
                B = BD[q]
                b1[q] = self.psum(*pr)
                k.mm(b1[q][:, 0:256], B["b_bd"].a, AR[q].a)
                k.mm(b1[q][:, 256:512], B["k_bd"].a, AR[q].a)
                b2[q] = self.psum(*pr)
                k.mm(b2[q][:, 128:256], B["v_bd"].a, identb.a)
                k.mm(b2[q][:, 256:384], B["b_bd"].a, identb.a)
                k.mm(b2[q][:, 384:512], B["k_bd"].a, identb.a)
            yield
            for q in slots:
                B = BD[q]
                k.tt(B["A0"].a, b1[q][:, 0:128], MSU.a, ALU.mult)
                k.tt(B["Mrb"].a, b1[q][:, 128:256], MIU.a, ALU.mult)
                k.tt(B["Mak"].a, b1[q][:, 256:384], MSU.a, ALU.mult)
                k.tt(B["Mrk"].a, b1[q][:, 384:512], MIU.a, ALU.mult)
                k.copy(B["VT"].a, b2[q][:, 128:256], en='act')
                k.copy(B["bT"].a, b2[q][:, 256:384], en='act')
                k.copy(B["kT"].a, b2[q][:, 384:512], en='act')
                yield
            pW = {}
            for q in slots:
                B = BD[q]
                pW[q] = self.psum(*pr)
                k.mm(pW[q][:, 0:128], AR[q][:, 0:128], B["S0T"].a, start=True, stop=False, inc=False)
                k.mm(pW[q][:, 0:128], B["Mak"].a, B["VT"].a, start=False, stop=True)
                k.mm(pW[q][:, 128:256], B["A0"].a, identb.a)
            for q in slots:
                k.copy(BD[q]["X"].a, pW[q][:, 0:128], en='act')
                k.copy(BD[q]["At0"].a, pW[q][:, 128:256], en='act')
            yield
            for lev in range(6):
                A = "A%d" % (lev % 2); At = "At%d" % (lev % 2); An = "A%d" % ((lev + 1) % 2); Atn = "At%d" % ((lev + 1) % 2)
                pN = {}
                for q in slots:
                    B = BD[q]
                    pN[q] = self.psum(*pr)
                    k.mm(pN[q][:, 0:128], B[A].a, B["X"].a, start=True, stop=False, inc=False)
                    k.mm(pN[q][:, 0:128], identb.a, B["X"].a, start=False, stop=True)
                    if lev < 5:
                        k.mm(pN[q][:, 128:256], B[At].a, B[A].a)
                        k.mm(pN[q][:, 256:384], B[A].a, B[At].a)
                for u, q in enumerate(slots):
                    B = BD[q]
                    en_ = 'act'
                    k.copy(B["X"].a, pN[q][:, 0:128], en=en_)
                    if lev < 5:
                        k.copy(B[An].a, pN[q][:, 128:256], en=en_)
                        k.copy(B[Atn].a, pN[q][:, 256:384], en=en_)
                yield
            pO = {}
            for q in slots:
                B = BD[q]
                pO[q] = self.psum(*pr)
                k.mm(pO[q][:, 0:128], B["S0T"].a, AR[q][:, 128:256], start=True, stop=False, inc=False)
                k.mm(pO[q][:, 0:128], B["X"].a, B["Mrb"].a, start=False, stop=False, inc=False)
                k.mm(pO[q][:, 0:128], B["VT"].a, B["Mrk"].a, start=False, stop=True)
                k.mm(pO[q][:, 128:256], B["bT"].a, B["X"].a, start=True, stop=False, inc=False)
                k.mm(pO[q][:, 128:256], B["kT"].a, B["VT"].a, start=False, stop=True)
            yield
            for u, q in enumerate(slots):
                T = PT[q]; B = BD[q]
                for hh in range(2):
                    ps_ = lo(hh)
                    k.copy(T["out"][ps_, cs], pO[q][ps_, hh * 64:hh * 64 + 64], en='act')
                ptot = T["P"][:, c * CH + CH - 1:c * CH + CH]
                k.act(B["ts"].a, pO[q][:, 128:256], AF.Copy, scale=ptot)
                k.stt(SALL[js[u]].a, SALL[js[u]].a, ptot, B["ts"].a, ALU.mult, ALU.add)
            yield

        def partB(j, outv, bonv, gv, q=0, pr=(0, 8)):
            T = PT[q]
            p = self.psum(*pr)
            k.mm(p[:, 0:n], bones.a, outv)
            cen = T["t0"]
            k.stt(cen.a, p[:, 0:n], -1.0 / 64, outv, ALU.mult, ALU.add)
            sq = T["t1"]
            k.tt(sq.a, cen.a, cen.a, ALU.mult)
            p = self.psum(*pr)
            k.mm(p[:, 0:n], bones.a, sq.a)
            k.act(sq.a, p[:, 0:n], AF.Sqrt, bias=gneps[:, 0:1], scale=1.0 / 64)
            k.recip(sq.a, sq.a)
            k.tt(cen.a, cen.a, sq.a, ALU.mult)
            k.ts(cen.a, cen.a, col(VLW, j), ALU.mult, col(VLB, j), ALU.add)
            k.tt(cen.a, cen.a, bonv, ALU.add)
            k.tt(att[j].a, cen.a, gv, ALU.mult)

        def wo_apply(sbi):
            for jo in range(8):
                p = self.psum()
                if sbi < 16:
                    wv_ = lambda kk: woR[:, jo, kk, :]
                else:
                    w_ = wol[jo % 2]
                    self.load_tile(w_, W["o"], jo, 8)
                    wv_ = lambda kk: w_[:, kk, :]
                for kk in range(8):
                    k.mm(p[:, 0:n], wv_(kk), att[kk].a, start=(kk == 0), stop=(kk == 7))
                if sbi < 16:
                    blk, half = sbi // 2, sbi % 2
                    hv = self.h[blk][:, jo, half * SB:(half + 1) * SB]
                else:
                    hv = self.h[8][:, jo, :]
                k.tt(hv, hv, p[:, 0:n], ALU.add)

        def group_gen(sbi, js, slots, pr):
            for q_, j in zip(slots, js):
                partA(sbi, j, q_, pr)
                yield
            for c in range(SB // CH):
                yield from chunk(js, c, slots, pr)
            for q_, j in zip(slots, js):
                partB(j, PT[q_]["out"].a, PT[q_]["bonus"].a, PT[q_]["g"].a, q_, pr)
                yield

        LAG = 6
        k.begin_defer()
        for sbi in range(16):
            front(sbi)
            for g0 in (0, 4):
                gA = group_gen(sbi, [g0, g0 + 1], [0, 1], (0, 4))
                gB = group_gen(sbi, [g0 + 2, g0 + 3], [2, 3], (4, 8))
                aliveA = aliveB = True
                steps = 0
                while aliveA or aliveB:
                    if aliveA:
                        try:
                            next(gA)
                        except StopIteration:
                            aliveA = False
                    steps += 1
                    if aliveB and (steps > LAG or not aliveA):
                        try:
                            next(gB)
                        except StopIteration:
                            aliveB = False
            wo_apply(sbi)
            if sbi % 4 == 3:
                k.end_defer()
                if sbi < 15:
                    k.begin_defer()
        for j in range(8):
            p = self.psum()
            k.tr(p[:, 0:128], SALL[j].a, self.ident.a)
            sot = PT[(j // 2) % NP]["t0"]
            for hh in range(2):
                k.copy(sot[lo(hh), (j % 2) * 64:(j % 2) * 64 + 64], p[lo(hh), hh * 64:hh * 64 + 64], en='act')
            if j % 2 == 1:
                k.dma(O[pre + "wkv_p"][:, j - 1:j + 1, :], sot.a.re("p (a b) -> p a b", a=2))
        k.free_to(mk_p)
        gS = sb("rw_gS", [128, 8, SB]); bonS = sb("rw_bonS", [128, 8, SB])
        XNs = sb("rw_XNs", [128, 8, 16, 9]); xnc = sb("rw_xnc", [128, 8, SB])
        sst = sb("rw_sst", [128, 8, 16]); sso = sb("rw_sso", [128, 8, 16])
        wol = [sb("rw_wo%d" % i, [128, 8, 128], BF16) for i in range(2)]
        sbi = 16
        k.begin_defer()
        front(sbi)
        scr = {nm: k.dram_tmp("rw%d_s_%s" % (l, nm), [8, 128, 128]) for nm in ["an", "bn", "w", "k", "v", "r", "o"]}
        tmt = [sb("rw_tmt%d" % i, [128, 128]) for i in range(2)]
        tc_ = 0
        for j in range(8):
            partA(sbi, j, 0)
            T = PT[0]
            k.copy(gS[:, j, :], T["g"].a, en='act')
            k.copy(bonS[:, j, :], T["bonus"].a, en='act')
            k.act(T["P"].a, T["lw"].a, AF.Exp)
            for nm, src in [("an", "an"), ("bn", "bn"), ("w", "P"), ("k", "k"), ("v", "v"), ("r", "r")]:
                p = self.psum()
                k.tr(p[:, 0:128], T[src].a, self.ident.a)
                t_ = tmt[tc_ % 2]; tc_ += 1
                k.copy(t_.a, p[:, 0:128], en='act')
                k.dma(scr[nm].v(j * 128 * 128, [[128, 128], [1, 128]]), t_.a)
        Tq = {nm: sb("rw_q_%s" % nm, [128, 8, 128]) for nm in ["an", "bn", "w", "k", "v", "r", "o"]}
        for nm in ["an", "bn", "w", "k", "v", "r"]:
            for b in range(16):
                k.dma(Tq[nm][b * 8:(b + 1) * 8, :, :], scr[nm].v(b * 8 * 128, [[128 * 128, 8], [128, 8], [1, 128]]))
        NI = 16
        S = sb("rw_S", [128, NI, 64]); tmp = sb("rw_tmpS", [128, NI, 64]); sa = sb("rw_sa", [128, NI])
        wkv_in = I[pre + "wkv"]; wkv_out = O[pre + "wkv_s"]
        for h2 in range(2):
          for ih in range(64 // NI):
            soff = h2 * 4096 + ih * NI * 64
            k.dma(S.a.re("p a b -> p (a b)"), wkv_in.v(soff, [[8192, 128], [1, NI * 64]]))
            for t in range(8):
                def bi(nm):
                    return V(Tq[nm], AP(Tq[nm].h, t * 128 + h2 * 64, [[1024, 128], [0, NI], [1, 64]]))
                def bd_(nm):
                    return V(Tq[nm], AP(Tq[nm].h, t * 128 + h2 * 64 + ih * NI, [[1024, 128], [1, NI], [0, 64]]))
                k.tt(tmp.a, S.a, bi("an"), ALU.mult)
                k.reduce(sa.a, tmp.a)
                k.tt(S.a, S.a, bi("w"), ALU.mult)
                k.tt(tmp.a, V(sa, AP(sa.h, 0, [[NI, 128], [1, NI], [0, 64]])), bi("bn"), ALU.mult)
                k.tt(S.a, S.a, tmp.a, ALU.add)
                k.tt(tmp.a, bd_("v"), bi("k"), ALU.mult)
                k.tt(S.a, S.a, tmp.a, ALU.add)
                k.tt(tmp.a, S.a, bi("r"), ALU.mult)
                k.reduce(Tq["o"][:, t, h2 * 64 + ih * NI:h2 * 64 + ih * NI + NI], tmp.a)
            k.dma(wkv_out.v(soff, [[8192, 128], [1, NI * 64]]), S.a.re("p a b -> p (a b)"))
        for b in range(16):
            k.dma(scr["o"].v(b * 8 * 128, [[128 * 128, 8], [128, 8], [1, 128]]), Tq["o"][b * 8:(b + 1) * 8, :, :])
        for j in range(8):
            t_ = tmt[j % 2]
            k.dma(t_.a, scr["o"].v(j * 128 * 128, [[128, 128], [1, 128]]))
            p = self.psum()
            k.tr(p[:, 0:128], t_.a, self.ident.a)
            k.copy(PT[1]["out"].a, p[:, 0:128], en='act')
            partB(j, PT[1]["out"].a, bonS[:, j, :], gS[:, j, :])
        wo_apply(sbi)
        k.end_defer()
        k.end_phase()


class MK(MKBase, S5Mixin, MambaMixin, RwkvMixin):
    pass


def _fm(a):
    return np.ascontiguousarray(a.T.reshape(a.shape[1] // 128, 128, a.shape[0]))


def host_consts():
    c = {}
    c["c_ident"] = np.eye(128, dtype=np.float32)
    c["c_iota"] = np.ascontiguousarray(np.tile(np.arange(1, 129, dtype=np.float32), (128, 1)))
    f = np.arange(128)
    c["c_maskB"] = (f[:, None] // 16 == np.arange(8)[None, :]).astype(np.float32)
    return c


def host_shared(inp):
    d = {}
    gains = np.zeros((128, 13 * 8), np.float32)
    for typ, nm in enumerate(["norm_mix", "norm_ffn", "norm_ple"]):
        for l in range(4):
            gains[:, (typ * 4 + l) * 8:(typ * 4 + l + 1) * 8] = inp[nm][l].reshape(8, 128).T
    gains[:, 96:104] = inp["final_norm"].reshape(8, 128).T
    d["c_gains"] = gains
    for nm in ["ffn_w1", "ffn_w3", "ffn_w2", "ple_proj", "ple_gate", "l1_glu_v", "l1_glu_g"]:
        d[nm] = inp[nm]
    are, aim, ld = inp["l1_a_re"], inp["l1_a_im"], inp["l1_log_dt"]
    ldb = np.broadcast_to(ld[:, None], (64, 64))
    chan = np.stack([x.reshape(32, 2, 64).transpose(1, 2, 0).reshape(128, 32) for x in (are, aim, ldb)], 1)
    d["s5_chan"] = np.ascontiguousarray(chan)
    feat = np.stack([np.broadcast_to(x.reshape(8, 8, 1, 64), (8, 8, 16, 64)).transpose(1, 2, 0, 3).reshape(128, 512) for x in (are, aim, ldb)], 1)
    d["s5_feat"] = np.ascontiguousarray(feat)
    d["s5_bT"] = np.ascontiguousarray(np.stack([x.reshape(8, 8, 64, 16).transpose(1, 3, 0, 2).reshape(128, 512) for x in (inp["l1_b_re"], inp["l1_b_im"])], 1))
    d["s5_cF"] = np.ascontiguousarray(np.stack([x.reshape(8, 8, 16, 64).transpose(1, 2, 0, 3).reshape(128, 512) for x in (inp["l1_c_re"], inp["l1_c_im"])], 1))
    d["s5_d"] = np.ascontiguousarray(inp["l1_d"].reshape(8, 128).T)
    return d


def host_core(inp, c):
    d = {}
    xs = inp["x_sample"][16 * c:16 * c + 16].reshape(128, 1024)
    d["xT"] = _fm(np.concatenate([inp["x_prompt"][c], xs], 0))
    d["pT"] = np.stack([_fm(np.concatenate([inp["p_prompt"][l, c], inp["p_sample"][l, 16 * c:16 * c + 16].reshape(128, 256)], 0)) for l in range(4)])
    st = [inp["state_l1_s5_re"][16 * c:16 * c + 16], inp["state_l1_s5_im"][16 * c:16 * c + 16]]
    d["s5_state"] = np.ascontiguousarray(np.stack([x.reshape(16, 32, 2, 64).transpose(2, 3, 1, 0).reshape(128, 32, 16) for x in st], 1))
    return d


def unpack_s5(pst, sst):
    p = [pst[:, ri, :].reshape(2, 64, 32).transpose(2, 0, 1).reshape(1, 64, 64) for ri in range(2)]
    s = [sst[:, ri].reshape(2, 64, 32, 16).transpose(3, 2, 0, 1).reshape(16, 64, 64) for ri in range(2)]
    return p, s


def host_consts2(c):
    kk = np.arange(128)
    c["c_U"] = (kk[:, None] <= kk[None, :]).astype(np.float32)
    c["c_negP"] = np.where(kk[None, :] >= kk[:, None], 0.0, -1.0e5).astype(np.float32)
    c["c_m01"] = (kk[None, :] >= kk[:, None]).astype(np.float32)
    return c


def host_shared_mamba(inp, d):
    d["l2_in_proj"] = inp["l2_in_proj"]; d["l2_out_proj"] = inp["l2_out_proj"]
    d["m_convw"] = np.ascontiguousarray(inp["l2_conv_w"].T.reshape(24, 128, 4).transpose(1, 0, 2))
    d["m_convb"] = np.ascontiguousarray(inp["l2_conv_b"].reshape(24, 128).T)
    d["m_dtb"] = np.ascontiguousarray(inp["l2_dt_bias"].reshape(32, 1))
    d["m_alog"] = np.ascontiguousarray(np.broadcast_to(inp["l2_a_log"][None, :], (128, 32)))
    d["m_alogq"] = np.ascontiguousarray(np.tile(inp["l2_a_log"], 4).reshape(128, 1))
    d["m_dcol"] = np.ascontiguousarray(np.repeat(inp["l2_d"], 64).reshape(16, 128).T)
    d["m_normw"] = np.ascontiguousarray(inp["l2_norm_w"].reshape(16, 128).T)
    return d


def host_core_mamba(inp, c, d):
    st = inp["state_l2_conv"][16 * c:16 * c + 16]
    d["m_convst"] = np.ascontiguousarray(st.reshape(16, 3, 24, 128).transpose(3, 2, 0, 1))
    d["m_ssm"] = np.ascontiguousarray(inp["state_l2_ssm"][16 * c:16 * c + 16])
    return d


def unpack_mamba(r):
    pconv = np.asarray(r["m_pconv"]).transpose(2, 1, 0).reshape(1, 3, 3072)
    sconv = np.asarray(r["m_sconv"]).transpose(2, 3, 1, 0).reshape(16, 3, 3072)
    pssm = np.asarray(r["m_pssmT"]).transpose(1, 2, 0).reshape(1, 32, 64, 128)
    sssm = np.asarray(r["m_sssm"]).reshape(16, 32, 64, 128)
    return pconv, sconv, pssm, sssm


def host_consts3(c):
    kk = np.arange(128)
    c["c_bones"] = (kk[:, None] // 64 == kk[None, :] // 64).astype(np.float32)
    r = kk % 64
    c["c_MSU"] = (r[:, None] < r[None, :]).astype(np.float32)
    c["c_MIU"] = (r[:, None] <= r[None, :]).astype(np.float32)
    c["c_MSL"] = (r[:, None] > r[None, :]).astype(np.float32)
    return c


def host_shared_rwkv(inp, d, pre):
    names = ["w0", "a0", "k_k", "k_a", "r_k", "lnx_w", "lnx_b"]
    vs = [inp[pre + nm].reshape(1024) for nm in names]
    vs.append(inp[pre + "v0"].reshape(1024) if (pre + "v0") in inp else inp[pre + "w0"].reshape(1024))
    d[pre + "vec"] = np.ascontiguousarray(np.stack([v.reshape(8, 128).T for v in vs], 1))
    d[pre + "mu"] = np.ascontiguousarray(inp[pre + "mu"].reshape(6, 8, 128).transpose(2, 0, 1))
    for nm in ["w1", "a1", "g1", "w2", "a2", "g2", "w_rkv", "w_o"] + (["v1", "v2"] if (pre + "v1") in inp else []):
        d[pre + nm] = inp[pre + nm]
    return d


def host_core_rwkv(inp, c, d, l):
    pre = "l%d_" % l
    if l == 0:
        sh_all, wkv_all = inp["state_l0_shift"], inp["state_l0_wkv"]
    else:
        sh_all, wkv_all = inp["state_l3_shift"], inp["state_l3_wkv"]
    sh = sh_all[16 * c:16 * c + 16]
    d[pre + "shift"] = np.ascontiguousarray(sh.reshape(16, 8, 128).transpose(2, 1, 0))
    d[pre + "wkv"] = np.ascontiguousarray(wkv_all[16 * c:16 * c + 16])
    return d


def unpack_rwkv(r, pre):
    shp = np.asarray(r[pre + "shift_p"]).T.reshape(1, 1024)
    shs = np.asarray(r[pre + "shift_s"]).transpose(2, 1, 0).reshape(16, 1024)
    wp = np.asarray(r[pre + "wkv_p"]).reshape(2, 64, 8, 64).transpose(2, 0, 1, 3).reshape(1, 16, 64, 64)
    ws = np.asarray(r[pre + "wkv_s"]).reshape(16, 16, 64, 64)
    return shp, shs, wp, ws


def build_program():
    m = MK()
    k = m.k
    m.setup_consts()
    m.setup_state()
    I = {}

    def din(nm, shape):
        I[nm] = m.inp(nm, list(shape))

    for nm, shp in INPUT_SHAPES.items():
        if nm not in ("c_ident", "c_gains"):
            din(nm, shp)
    O = {nm: m.out(nm, list(shp)) for nm, shp in OUTPUT_SHAPES.items()}
    m.load_x(I["xT"])
    W0 = {nm: m.precast("w0%s_bf" % nm, I["l0_w_rkv"], 1024, c * 1024 * 1024, 1024, 8) for c, nm in enumerate(["r", "k", "v"])}
    W0["o"] = m.precast("w0o_bf", I["l0_w_o"], 1024, 0, 1024, 8)
    win_bf = m.precast("win_bf", I["l2_in_proj"], 5152, 0, 5152, 8)
    wout_bf = m.precast("wout_bf", I["l2_out_proj"], 1024, 0, 1024, 16)
    W3 = {nm: m.precast("w3%s_bf" % nm, I["l3_w_rkv"], 1024, c * 1024 * 1024, 1024, 8) for c, nm in enumerate(["r", "k", "v"])}
    W3["o"] = m.precast("w3o_bf", I["l3_w_o"], 1024, 0, 1024, 8)
    vfirst = k.dram_tmp("vfirst", [8, 128, TT])

    def ffn_ple(l):
        xn_all = [k.sbp("xn%d" % i, [128, 8, n], BF16) for i, (s, n) in enumerate(TBS)]
        wbuf = [(k.sbp("w1g%d" % i, [128, 8, 512], BF16), k.sbp("w3g%d" % i, [128, 8, 512], BF16), k.sbp("w2g%d" % i, [128, 4, 1024], BF16)) for i in range(2)]
        m.ffn(l, I["ffn_w1"], I["ffn_w3"], I["ffn_w2"], xn_all, wbuf)
        k.end_phase()
        wg = k.sbp("wg", [128, 8, 1024], BF16); wp = k.sbp("wp", [128, 2, 1024], BF16)
        xn_blk = [k.sbp("xnb%d" % i, [128, 8, BLK], BF16) for i in range(2)]
        p_blk = [k.sbp("pb%d" % i, [128, 2, BLK], BF16) for i in range(2)]
        m.ple(l, I["pT"], I["ple_proj"], I["ple_gate"], wg, wp, xn_blk, p_blk)
        k.end_phase()

    m.rwkv_layer(0, I, O, W0, "l0_", vfirst)
    ffn_ple(0)
    m.s5_layer(1, I, O)
    ffn_ple(1)
    m.mamba_layer(2, I, O, win_bf, wout_bf)
    ffn_ple(2)
    m.rwkv_layer(3, I, O, W3, "l3_", vfirst)
    ffn_ple(3)
    ybuf = [k.sbp("ybuf%d" % i, [128, 8, BLK]) for i in range(2)]
    m.final(O["yT"], ybuf)
    k.end_phase()
    k.finish()
    return m


OUTPUT_SHAPES = {
    "yT": (8, 128, TT),
    "l0_shift_p": (128, 8), "l0_shift_s": (128, 8, 16), "l0_wkv_p": (128, 8, 64), "l0_wkv_s": (16, 16, 64, 64),
    "s5_pstate": (128, 2, 32), "s5_sstate": (128, 2, 32, 16),
    "m_pconv": (128, 24, 3), "m_sconv": (128, 24, 16, 3), "m_pssmT": (128, 32, 64), "m_sssm": (16, 32, 64, 128),
    "l3_shift_p": (128, 8), "l3_shift_s": (128, 8, 16), "l3_wkv_p": (128, 8, 64), "l3_wkv_s": (16, 16, 64, 64),
}
INPUT_SHAPES = {}


def kernel(**inputs):
    inp = {k_: np.asarray(v) for k_, v in inputs.items()}
    consts = host_consts3(host_consts2(host_consts()))
    shared = host_shared(inp)
    host_shared_mamba(inp, shared)
    host_shared_rwkv(inp, shared, "l0_")
    host_shared_rwkv(inp, shared, "l3_")
    in_maps = []
    for c in range(8):
        d = dict(consts)
        d.update(shared)
        pc = host_core(inp, c)
        host_core_mamba(inp, c, pc)
        host_core_rwkv(inp, c, pc, 0)
        host_core_rwkv(inp, c, pc, 3)
        d.update(pc)
        in_maps.append({k_: np.ascontiguousarray(v, dtype=np.float32) for k_, v in d.items()})
    INPUT_SHAPES.clear()
    for k_, v in in_maps[0].items():
        INPUT_SHAPES[k_] = v.shape
    m = build_program()
    res = run_bass_kernel_spmd(m.k.nc, in_maps, core_ids=list(range(8)))
    R = res.results
    yp = np.zeros((8, 2048, 1024), np.float32); ys = np.zeros((128, 8, 1024), np.float32)
    outs = {nm: [] for nm in ["shp0", "shs0", "wp0", "ws0", "s5p_re", "s5p_im", "s5s_re", "s5s_im", "pconv", "sconv", "pssm", "sssm", "shp3", "shs3", "wp3", "ws3"]}
    for c in range(8):
        r = R[c]
        y = np.asarray(r["yT"]).reshape(1024, TT).T
        yp[c] = y[:2048]
        ys[16 * c:16 * c + 16] = y[2048:].reshape(16, 8, 1024)
        a, b, cc, dd = unpack_rwkv(r, "l0_")
        outs["shp0"].append(a); outs["shs0"].append(b); outs["wp0"].append(cc); outs["ws0"].append(dd)
        p, s = unpack_s5(np.asarray(r["s5_pstate"]), np.asarray(r["s5_sstate"]))
        outs["s5p_re"].append(p[0]); outs["s5p_im"].append(p[1]); outs["s5s_re"].append(s[0]); outs["s5s_im"].append(s[1])
        a, b, cc, dd = unpack_mamba(r)
        outs["pconv"].append(a); outs["sconv"].append(b); outs["pssm"].append(cc); outs["sssm"].append(dd)
        a, b, cc, dd = unpack_rwkv(r, "l3_")
        outs["shp3"].append(a); outs["shs3"].append(b); outs["wp3"].append(cc); outs["ws3"].append(dd)
    cat = lambda nm: np.ascontiguousarray(np.concatenate(outs[nm], 0), dtype=np.float32)
    return (yp, ys,
            cat("shp0"), cat("wp0"), cat("s5p_re"), cat("s5p_im"), cat("pconv"), cat("pssm"), cat("shp3"), cat("wp3"),
            cat("shs0"), cat("ws0"), cat("s5s_re"), cat("s5s_im"), cat("sconv"), cat("sssm"), cat("shs3"), cat("ws3"))
```

```python
import numpy as np
import concourse.bass as bass
import concourse.mybir as mybir
from concourse.ap import AP
from concourse.bass_utils import run_bass_kernel_spmd

F32 = mybir.dt.float32
BF16 = mybir.dt.bfloat16
I32 = mybir.dt.int32
ALU = mybir.AluOpType
AF = mybir.ActivationFunctionType
AX = mybir.AxisListType

SEM_ROT = 30000


class Eng:
    def __init__(self, kb, name, eng):
        self.kb = kb
        self.name = name
        self.eng = eng
        self.sem = kb.nc.alloc_semaphore("es_%s_0" % name)
        self.nsem = 1
        self.cnt = 0
        self.seen = {}

    def rotate(self):
        if self.cnt >= SEM_ROT:
            self.sem = self.kb.nc.alloc_semaphore("es_%s_%d" % (self.name, self.nsem))
            self.nsem += 1
            self.cnt = 0


class DSem:
    def __init__(self, kb, name):
        self.kb = kb
        self.name = name
        self.sem = kb.nc.alloc_semaphore("ds_" + name)
        self.n = 0
        self.gen = 0


class Tn:
    def __init__(self, kb, h, name, space):
        self.kb = kb
        self.h = h
        self.name = name
        self.space = space
        self.lw = None
        self.rd = []
        self.ds = None
        self.shape = list(h.shape)

    def __getitem__(self, idx):
        return V(self, self.h[idx])

    def v(self, offset, ap):
        return V(self, AP(self.h, offset, ap))

    @property
    def a(self):
        return V(self, self.h[:])


class SubTn(Tn):
    def __init__(self, parent, col0, ncols, name):
        self.kb = parent.kb
        self.h = parent.h
        self.name = name
        self.space = parent.space
        self.lw = None
        self.rd = []
        self.ds = None
        self.col0 = col0
        self.ncols = ncols
        self.shape = [parent.shape[0], ncols]

    def __getitem__(self, idx):
        r, c = idx
        a = 0 if c.start is None else c.start
        b = self.ncols if c.stop is None else c.stop
        return V(self, self.h[r, self.col0 + a:self.col0 + b])

    @property
    def a(self):
        return self[:, 0:self.ncols]


class V:
    def __init__(self, t, ap):
        self.t = t
        self.ap = ap

    def __getitem__(self, idx):
        return V(self.t, self.ap[idx])

    def re(self, s, **kw):
        return V(self.t, self.ap.rearrange(s, **kw))

    def bc(self, shape):
        return V(self.t, self.ap.broadcast_to(shape))

    def bitcast(self, dt):
        return V(self.t, self.ap.bitcast(dt))

    @property
    def shape(self):
        return self.ap.shape


def _ap(x):
    return x.ap if isinstance(x, V) else x


class KB:
    def __init__(self):
        self.nc = bass.Bass("TRN2", target_bir_lowering=False)
        nc = self.nc
        self.E = {
            'pe': Eng(self, 'pe', nc.tensor),
            'dve': Eng(self, 'dve', nc.vector),
            'act': Eng(self, 'act', nc.scalar),
            'pool': Eng(self, 'pool', nc.gpsimd),
            'sp': Eng(self, 'sp', nc.sync),
        }
        self.dsems = []
        self._frozen = {}
        self._phase = []
        self._dspool = []
        self._dspool_sw = []
        self.ntens = 0
        self.ninst = 0

    def sb(self, name, shape, dtype=F32):
        h = self.nc.alloc_sbuf_tensor(name, list(shape), dtype)
        return Tn(self, h, name, 'sb')

    def sbp(self, name, shape, dtype=F32):
        self._sbn = getattr(self, "_sbn", 0) + 1
        cm = self.nc.sbuf_tensor("%s_t%d" % (name, self._sbn), list(shape), dtype)
        h = cm.__enter__()
        t = Tn(self, h, name, 'sb')
        self.min_free = min(getattr(self, "min_free", 1 << 30), self.nc.sbuf_bytes_remaining)
        self._phase.append((cm, t))
        return t

    def barrier(self):
        ents = []
        for en, EE in self.E.items():
            if EE.cnt > 0:
                ents.append(('e', EE.sem, EE.cnt, None))
        for ds in self.dsems:
            if ds.n > 0:
                ents.append(('d', ds, ds.gen))
        for en, EE in self.E.items():
            self._wait(EE, ents)

    def mark(self):
        return len(self._phase)

    def free_to(self, mark):
        self.barrier()
        while len(self._phase) > mark:
            cm, t = self._phase.pop()
            if t.ds is not None:
                self._dspool.append(t.ds)
                t.ds = None
            if getattr(t, "ds_sw", None) is not None:
                self._dspool_sw.append(t.ds_sw)
                t.ds_sw = None
            cm.__exit__(None, None, None)

    def end_phase(self):
        self.barrier()
        while self._phase:
            cm, t = self._phase.pop()
            if t.ds is not None:
                self._dspool.append(t.ds)
                t.ds = None
            if getattr(t, "ds_sw", None) is not None:
                self._dspool_sw.append(t.ds_sw)
                t.ds_sw = None
            cm.__exit__(None, None, None)

    def ps(self, name, shape, dtype=F32):
        h = self.nc.alloc_psum_tensor(name, list(shape), dtype)
        return Tn(self, h, name, 'ps')

    def dram_in(self, name, shape, dtype=F32):
        h = self.nc.dram_tensor(name, list(shape), dtype, kind="ExternalInput")
        return Tn(self, h, name, 'din')

    def dram_out(self, name, shape, dtype=F32):
        h = self.nc.dram_tensor(name, list(shape), dtype, kind="ExternalOutput")
        return Tn(self, h, name, 'dout')

    def dram_tmp(self, name, shape, dtype=F32):
        h = self.nc.dram_tensor(name, list(shape), dtype, kind="Internal")
        return Tn(self, h, name, 'dtmp')

    def _entry_val(self, ent):
        kind = ent[0]
        if kind == 'e':
            return ent[1], ent[2]
        ds = ent[1]
        if ent[2] == ds.gen:
            return ds.sem, ds.n * 16
        return self._frozen[(id(ds), ent[2])]

    def _wait(self, E, ents):
        need = {}
        for ent in ents:
            if ent is None:
                continue
            if ent[0] == 'e' and ent[3] is E and E.name == 'pe':
                continue
            sem, val = self._entry_val(ent)
            key = id(sem)
            if E.seen.get(key, 0) >= val:
                continue
            if key not in need or need[key][1] < val:
                need[key] = (sem, val)
        for key, (sem, val) in need.items():
            E.eng.wait_ge(sem, val)
            E.seen[key] = val

    def _collect(self, outs, ins):
        ents = []
        for v in ins:
            if isinstance(v, V) and v.t is not None and v.t.space != 'din':
                ents.append(v.t.lw)
        for v in outs:
            if isinstance(v, V) and v.t is not None:
                ents.append(v.t.lw)
                ents.extend(v.t.rd)
        return ents

    def _record(self, ent, outs, ins):
        for v in ins:
            if isinstance(v, V) and v.t is not None and v.t.space != 'din':
                t = v.t
                t.rd = [r for r in t.rd if not (r[0] == ent[0] and r[1] is ent[1])]
                t.rd.append(ent)
        for v in outs:
            if isinstance(v, V) and v.t is not None:
                v.t.lw = ent
                v.t.rd = []

    def begin_defer(self):
        self._defer = []
        self._dtrk = {}

    def _drecord(self, kind, en, payload, outs, ins, est):
        idx = len(self._defer)
        deps = set()
        for v in ins:
            if isinstance(v, V) and v.t is not None and v.t.space != 'din':
                tr = self._dtrk.setdefault(id(v.t), [None, []])
                if tr[0] is not None:
                    deps.add(tr[0])
        for v in outs:
            if isinstance(v, V) and v.t is not None:
                tr = self._dtrk.setdefault(id(v.t), [None, []])
                if tr[0] is not None:
                    deps.add(tr[0])
                deps.update(tr[1])
        for v in ins:
            if isinstance(v, V) and v.t is not None and v.t.space != 'din':
                self._dtrk[id(v.t)][1].append(idx)
        for v in outs:
            if isinstance(v, V) and v.t is not None:
                self._dtrk[id(v.t)] = [idx, []]
        deps.discard(idx)
        self._defer.append((kind, en, payload, outs, ins, est, deps))

    def end_defer(self, sync_lat=0.25):
        import heapq
        ops = self._defer
        self._defer = None
        n = len(ops)
        succ = [[] for _ in range(n)]
        ndep = [0] * n
        for i, o in enumerate(ops):
            ndep[i] = len(o[6])
            for d in o[6]:
                succ[d].append(i)
        ready_t = [0.0] * n
        fin = [0.0] * n
        start = [0.0] * n
        efree = {}
        heap = [(0.0, i) for i in range(n) if ndep[i] == 0]
        heapq.heapify(heap)
        while heap:
            rt, i = heapq.heappop(heap)
            kind, en, payload, outs, ins, est, deps = ops[i]
            st = max(rt, efree.get(en, 0.0))
            start[i] = st
            if kind == 'dma':
                efree[en] = st + 0.05
                fin[i] = st + est
            else:
                efree[en] = st + est
                fin[i] = st + est
            for j in succ[i]:
                ready_t[j] = max(ready_t[j], fin[i] + sync_lat)
                ndep[j] -= 1
                if ndep[j] == 0:
                    heapq.heappush(heap, (ready_t[j], j))
        order = sorted(range(n), key=lambda i: (start[i], i))
        for i in order:
            kind, en, payload, outs, ins, est, deps = ops[i]
            if kind == 'op':
                fn, inc = payload
                self.op(en, fn, outs, ins, inc=True)
            else:
                out, in_, q, owner, kw = payload
                self.dma(out, in_, q=q, owner=owner, **kw)

    def _est(self, en, outs, ins):
        try:
            sh = outs[0].ap.shape
            free = 1
            for d in sh[1:]:
                free *= d
        except Exception:
            free = 128
        if en == 'pe':
            f32 = False
            try:
                f32 = any(isinstance(v, V) and v.t.space == 'sb' and str(v.ap.dtype).endswith('float32') for v in ins[:1])
            except Exception:
                pass
            return (0.07 + free / 2400.0) * (4.0 if f32 else 1.0)
        if en == 'dve':
            return 0.07 + free / 960.0
        if en == 'act':
            return 0.2 + free / 1200.0
        if en == 'pool':
            return 0.5 + free / 400.0
        return 0.1

    def op(self, en, fn, outs, ins, inc=True):
        if getattr(self, "_defer", None) is not None:
            self._drecord('op', en, (fn, inc), outs, ins, self._est(en, outs, ins))
            return None
        E = self.E[en]
        self._wait(E, self._collect(outs, ins))
        inst = fn(E.eng)
        self.ninst += 1
        if inc:
            E.cnt += 1
            inst.then_inc(E.sem, 1)
            ent = ('e', E.sem, E.cnt, E)
            self._record(ent, outs, ins)
            E.rotate()
        else:
            ent = ('e', E.sem, E.cnt + 1, E)
            self._record(ent, outs, ins)
        return inst

    def dma(self, out, in_, q='sp', owner=None, **kw):
        if getattr(self, "_defer", None) is not None:
            self._drecord('dma', q, (out, in_, q, owner, kw), [out], [in_], 2.2)
            return None
        E = self.E[q]
        self._wait(E, self._collect([out], [in_]))
        if owner is None:
            cands = [v.t for v in (out, in_) if v.t.space in ('sb',)]
            if not cands:
                cands = [v.t for v in (out, in_) if v.t.space in ('dtmp',)]
            if not cands:
                cands = [out.t]
            owner = cands[0]
        sw = (q == 'pool')
        attr = 'ds_sw' if sw else 'ds'
        pool = self._dspool_sw if sw else self._dspool
        if getattr(owner, attr, None) is None:
            if pool:
                setattr(owner, attr, pool.pop())
            else:
                nd = DSem(self, "%s%s_%d" % ("sw_" if sw else "", owner.name, len(self.dsems)))
                setattr(owner, attr, nd)
                self.dsems.append(nd)
        ds = getattr(owner, attr)
        inst = E.eng.dma_start(out=out.ap, in_=in_.ap, **kw)
        ds.n += 1
        inst.then_inc(ds.sem, 16)
        self.ninst += 1
        ent = ('d', ds, ds.gen)
        self._record(ent, [out], [in_])
        if ds.n >= 1800:
            old = (ds.sem, ds.n * 16)
            ds.gen += 1
            self._frozen[(id(ds), ds.gen - 1)] = old
            ds.sem = self.nc.alloc_semaphore("ds_%s_%d" % (owner.name, ds.gen))
            ds.n = 0
        return inst

    def finish(self):
        E = self.E['sp']
        for ds in self.dsems:
            if ds.n > 0:
                E.eng.wait_ge(ds.sem, ds.n * 16)
        for (k, g), (sem, val) in self._frozen.items():
            E.eng.wait_ge(sem, val)
        for en, EE in self.E.items():
            if en != 'sp' and EE.cnt > 0:
                E.eng.wait_ge(EE.sem, EE.cnt)

    def tt(self, out, a, b, op, en='dve'):
        return self.op(en, lambda e: e.tensor_tensor(out=out.ap, in0=a.ap, in1=b.ap, op=op), [out], [a, b])

    def ts(self, out, a, s1, op0, s2=None, op1=None, en='dve'):
        def f(e):
            if op1 is None:
                return e.tensor_scalar(out=out.ap, in0=a.ap, scalar1=_ap(s1), scalar2=None, op0=op0)
            return e.tensor_scalar(out=out.ap, in0=a.ap, scalar1=_ap(s1), scalar2=_ap(s2), op0=op0, op1=op1)
        return self.op(en, f, [out], [a, s1, s2])

    def stt(self, out, a, s, b, op0, op1, en='dve'):
        return self.op(en, lambda e: e.scalar_tensor_tensor(out=out.ap, in0=a.ap, scalar=_ap(s), in1=b.ap, op0=op0, op1=op1), [out], [a, s, b])

    def copy(self, out, a, en='dve'):
        if en == 'act':
            return self.op(en, lambda e: e.copy(out=out.ap, in_=a.ap), [out], [a])
        return self.op(en, lambda e: e.tensor_copy(out=out.ap, in_=a.ap), [out], [a])

    def memset(self, out, val, en='dve'):
        return self.op(en, lambda e: e.memset(out.ap, val), [out], [])

    def act(self, out, a, func, bias=None, scale=None, accum=None):
        def f(e):
            kw = {}
            if bias is not None:
                kw['bias'] = _ap(bias)
            if scale is not None:
                kw['scale'] = _ap(scale)
            if accum is not None:
                kw['accum_out'] = accum.ap
            return e.activation(out=out.ap, in_=a.ap, func=func, **kw)
        outs = [out] + ([accum] if accum is not None else [])
        return self.op('act', f, outs, [a, bias, scale])

    def mm(self, out, lhsT, rhs, start=True, stop=True, inc=None):
        if inc is None:
            inc = stop
        return self.op('pe', lambda e: e.matmul(out.ap, lhsT.ap, rhs.ap, start=start, stop=stop), [out], [lhsT, rhs], inc=inc)

    def tr(self, out, a, ident, inc=True):
        return self.op('pe', lambda e: e.transpose(out.ap, a.ap, ident.ap), [out], [a, ident], inc=inc)

    def scan(self, out, d0, d1, init, op0=ALU.mult, op1=ALU.add):
        return self.op('dve', lambda e: e.tensor_tensor_scan(out=out.ap, data0=d0.ap, data1=d1.ap, initial=_ap(init), op0=op0, op1=op1), [out], [d0, d1, init])

    def reduce(self, out, a, op=ALU.add, axis=AX.X):
        return self.op('dve', lambda e: e.tensor_reduce(out=out.ap, in_=a.ap, axis=axis, op=op), [out], [a])

    def recip(self, out, a):
        return self.op('dve', lambda e: e.reciprocal(out=out.ap, in_=a.ap), [out], [a])
D = 1024
KC = 8
TP = 2048
TS = 128
TT = TP + TS
DFF = 2816
NFC = DFF // 128
BLK = 256
TBS = [(i * BLK, BLK) for i in range(TP // BLK)] + [(TP, TS)]
EPS = 1e-6


class MKBase:
    def __init__(self, dbg=None):
        self.k = KB()
        self.dbg = dbg or {}
        self.outs = {}
        self.ins = {}
        k = self.k
        self.ps_banks = [k.ps("psb%d" % i, [128, 512]) for i in range(8)]
        self.ps_i = 0
        self._uid = 0

    def inp(self, name, shape, dtype=F32):
        t = self.k.dram_in(name, shape, dtype)
        self.ins[name] = t
        return t

    def out(self, name, shape):
        t = self.k.dram_out(name, shape)
        self.outs[name] = t
        return t

    def psum(self, lo=0, hi=8):
        n = hi - lo
        if not hasattr(self, "_psc"):
            self._psc = {}
        c = self._psc.get((lo, hi), 0)
        self._psc[(lo, hi)] = c + 1
        return self.ps_banks[lo + (c % n)]

    def uid(self, s):
        self._uid += 1
        return "%s_%d" % (s, self._uid)

    def dump(self, name, view, shape):
        o = self.out("dbg_" + name, shape)
        self.k.dma(o.a, view)

    def setup_consts(self):
        k = self.k
        ident_d = self.inp("c_ident", [128, 128])
        self.ident = k.sb("ident", [128, 128])
        k.dma(self.ident.a, ident_d.a)
        self.ones = k.sb("ones", [128, 128])
        k.memset(self.ones.a, 1.0)
        gd = self.inp("c_gains", [128, 13 * 8])
        self.gains = k.sb("gains", [128, 13 * 8])
        k.dma(self.gains.a, gd.a)

    def gain(self, typ, l):
        idx = (typ * 4 + l) if typ < 3 else 12
        return self.gains[:, idx * 8:(idx + 1) * 8]

    def rmsnorm(self, src, dst, gain, n, tmp_sq, tmp_r):
        k = self.k
        pss = self.psum(4, 8)
        for kk in range(KC):
            k.act(tmp_sq[:, kk, 0:n], src(kk), AF.Square)
        for kk in range(KC):
            k.mm(pss[:, 0:n], self.ones.a, tmp_sq[:, kk, 0:n], start=(kk == 0), stop=(kk == KC - 1))
        k.act(tmp_r[:, 0:n], pss[:, 0:n], AF.Sqrt, bias=self.eps_col[:, 0:1], scale=1.0 / D)
        k.recip(tmp_r[:, 0:n], tmp_r[:, 0:n])
        for kk in range(KC):
            k.stt(dst(kk), src(kk), gain[:, kk:kk + 1], tmp_r[:, 0:n], ALU.mult, ALU.mult)

    def setup_state(self):
        k = self.k
        self.h = [k.sb("h%d" % i, [128, KC, n]) for i, (s, n) in enumerate(TBS)]
        self.eps_col = k.sb("eps_col", [128, 1])
        k.memset(self.eps_col.a, EPS)
        self.tmp_sq = k.sb("tmp_sq", [128, KC, BLK])
        self.tmp_r = k.sb("tmp_r", [128, BLK])

    def load_x(self, xT):
        k = self.k
        for i, (s, n) in enumerate(TBS):
            k.dma(self.h[i].a, xT.v(s, [[TT, 128], [128 * TT, KC], [1, n]]))

    def ffn(self, l, w1, w3, w2, xn_all, wbuf):
        k = self.k
        def norm_blk(i):
            s, n = TBS[i]
            self.rmsnorm(lambda kk, i=i: self.h[i][:, kk, :], lambda kk, i=i, n=n: xn_all[i][:, kk, 0:n],
                         self.gain(1, l), n, self.tmp_sq, self.tmp_r)
        norm_blk(0)
        G = 4
        groups = [(c, min(G, NFC - c)) for c in range(0, NFC, G)]
        a_t = [[k.sbp(self.uid("ffn_a"), [128, BLK], BF16) for _ in range(G)] for _ in range(2)]
        s_t = [k.sbp(self.uid("ffn_s"), [128, BLK]) for _ in range(2)]

        stg = getattr(self, "_ffn_stage", None)

        def load(gi):
            c0, g = groups[gi]
            w1g, w3g, w2g = wbuf[gi % 2]
            base = l * D * DFF
            if stg is None:
                k.dma(w1g[:, :, 0:g * 128], w1.v(base + c0 * 128, [[DFF, 128], [128 * DFF, KC], [1, g * 128]]), q='pool')
                k.dma(w3g[:, :, 0:g * 128], w3.v(base + c0 * 128, [[DFF, 128], [128 * DFF, KC], [1, g * 128]]), q='pool')
                k.dma(w2g[:, 0:g, :], w2.v(l * DFF * D + c0 * 128 * D, [[D, 128], [128 * D, g], [1, D]]), q='pool')
            else:
                s1, s3, s2 = stg
                k.dma(s1[:, :, 0:g * 128], w1.v(base + c0 * 128, [[DFF, 128], [128 * DFF, KC], [1, g * 128]]))
                k.dma(s3[:, :, 0:g * 128], w3.v(base + c0 * 128, [[DFF, 128], [128 * DFF, KC], [1, g * 128]]))
                k.dma(s2[:, 0:g, :], w2.v(l * DFF * D + c0 * 128 * D, [[D, 128], [128 * D, g], [1, D]]))
                k.copy(w1g[:, :, 0:g * 128], s1[:, :, 0:g * 128], en='act')
                k.copy(w3g[:, :, 0:g * 128], s3[:, :, 0:g * 128], en='pool')
                k.copy(w2g[:, 0:g, :], s2[:, 0:g, :], en='act')

        load(0)
        cnt = 0
        bcnt = 0
        for gi, (c0, g) in enumerate(groups):
            if gi + 1 < len(groups):
                load(gi + 1)
            w1g, w3g, w2g = wbuf[gi % 2]
            for i, (s, n) in enumerate(TBS):
                if gi == 0 and i + 1 < len(TBS):
                    norm_blk(i + 1)
                py = self.ps_banks[0:4]
                ats = a_t[bcnt % 2]
                bcnt += 1
                for c in range(g):
                    ph1 = self.psum(4, 8)
                    for kk in range(KC):
                        k.mm(ph1[:, 0:n], w1g[:, kk, c * 128:(c + 1) * 128], xn_all[i][:, kk, 0:n], start=(kk == 0), stop=(kk == KC - 1))
                    ph3 = self.psum(4, 8)
                    for kk in range(KC):
                        k.mm(ph3[:, 0:n], w3g[:, kk, c * 128:(c + 1) * 128], xn_all[i][:, kk, 0:n], start=(kk == 0), stop=(kk == KC - 1))
                    st = s_t[cnt % 2]
                    cnt += 1
                    k.act(st[:, 0:n], ph1[:, 0:n], AF.Silu)
                    k.tt(ats[c][:, 0:n], st[:, 0:n], ph3[:, 0:n], ALU.mult)
                for j in range(KC):
                    for c in range(g):
                        k.mm(py[j // 2][:, (j % 2) * BLK:(j % 2) * BLK + n], w2g[:, c, j * 128:(j + 1) * 128], ats[c][:, 0:n],
                             start=(c == 0), stop=(c == g - 1))
                for jj in range(4):
                    hv = self.h[i][:, 2 * jj:2 * jj + 2, :]
                    pv = py[jj].a.re("p (a b) -> p a b", a=2)[:, :, 0:n]
                    k.tt(hv, hv, pv, ALU.add)

    def ple(self, l, pT, ple_proj, ple_gate, wg, wp, xn_blk, p_blk):
        k = self.k
        k.dma(wg.a, ple_gate.v(l * D * D, [[D, 128], [128 * D, KC], [1, D]]), q='pool')
        k.dma(wp.a, ple_proj.v(l * 256 * D, [[D, 128], [128 * D, 2], [1, D]]), q='pool')
        sg = [k.sbp(self.uid("ple_sg"), [128, BLK]) for _ in range(2)]
        def prep(i):
            s, n = TBS[i]
            xb = xn_blk[i % 2]
            pb = p_blk[i % 2]
            k.dma(pb[:, :, 0:n], pT.v(l * 256 * TT + s, [[TT, 128], [128 * TT, 2], [1, n]]), q='pool')
            self.rmsnorm(lambda kk, i=i: self.h[i][:, kk, :], lambda kk, xb=xb, n=n: xb[:, kk, 0:n],
                         self.gain(2, l), n, self.tmp_sq, self.tmp_r)
        prep(0)
        for i, (s, n) in enumerate(TBS):
            xb = xn_blk[i % 2]
            pb = p_blk[i % 2]
            for j in range(KC):
                pg = self.psum()
                for kk in range(KC):
                    k.mm(pg[:, 0:n], wg[:, kk, j * 128:(j + 1) * 128], xb[:, kk, 0:n], start=(kk == 0), stop=(kk == KC - 1))
                pp = self.psum()
                for kk in range(2):
                    k.mm(pp[:, 0:n], wp[:, kk, j * 128:(j + 1) * 128], pb[:, kk, 0:n], start=(kk == 0), stop=(kk == 1))
                sgt = sg[j % 2]
                k.act(sgt[:, 0:n], pg[:, 0:n], AF.Sigmoid)
                k.tt(sgt[:, 0:n], sgt[:, 0:n], pp[:, 0:n], ALU.mult)
                k.tt(self.h[i][:, j, :], self.h[i][:, j, :], sgt[:, 0:n], ALU.add)
                if j == 0 and i + 1 < len(TBS):
                    prep(i + 1)

    def final(self, yT, ybuf):
        k = self.k
        k.begin_defer()
        for i, (s, n) in enumerate(TBS):
            yb = ybuf[i % 2]
            self.rmsnorm(lambda kk, i=i: self.h[i][:, kk, :], lambda kk, yb=yb, n=n: yb[:, kk, 0:n],
                         self.gain(3, 0), n, self.tmp_sq, self.tmp_r)
            k.dma(yT.v(s, [[TT, 128], [128 * TT, KC], [1, n]]), yb[:, :, 0:n])
        k.end_defer()


PI = 3.14159265358979
PIC = 3.141592


class S5Mixin:
    def trig(self, ang, sin_out, cos_out, tf, ti, tr):
        k = self.k
        for out, shift in ((sin_out, 0.0), (cos_out, PI / 2)):
            if shift != 0.0:
                k.ts(tr, ang, shift, ALU.add)
                src = tr
            else:
                src = ang
            k.ts(ti, src, 1.0 / (2 * PI), ALU.mult)
            k.copy(tf, ti)
            k.stt(tr, tf, -2 * PI, src, ALU.mult, ALU.add)
            k.ts(tr, tr, PIC, ALU.min, -PIC, ALU.max)
            k.act(out, tr, AF.Sin)

    def s5_params(self, ach, afe, out_q=False):
        raise NotImplementedError

    def s5_layer(self, l, I, O):
        k = self.k
        sb = k.sbp
        ch = sb("s5_ch", [128, 3, 32]); k.dma(ch.a, I["s5_chan"].a)
        dcol = sb("s5_d", [128, 8]); k.dma(dcol.a, I["s5_d"].a)
        maskB = sb("s5_maskB", [128, 8]); k.dma(maskB.a, I["c_maskB"].a)
        hst = sb("s5_hst", [128, 2, 32, 16]); k.dma(hst.a, I["s5_state"].a)
        hpr = sb("s5_hpr", [128, 2, 32]); k.memset(hpr.a, 0.0)
        hso = sb("s5_hso", [128, 2, 32, 16])
        rho = sb("s5_rho", [128, 32]); th = sb("s5_th", [128, 32]); dtc = sb("s5_dtc", [128, 32])
        LC = 128
        cosT = sb("s5_cosT", [128, 32, LC]); sinT = sb("s5_sinT", [128, 32, LC])
        LB = [sb("s5_LBr", [128, 32, 128], BF16), sb("s5_LBi", [128, 32, 128], BF16)]
        LCm = [sb("s5_LCr", [128, 32, 128], BF16), sb("s5_LCi", [128, 32, 128], BF16)]
        mk_ = k.mark()
        fe = sb("s5_fe", [128, 3, 512]); k.dma(fe.a, I["s5_feat"].a)
        bT = sb("s5_bT", [128, 2, 512]); k.dma(bT.a, I["s5_bT"].a)
        cF = sb("s5_cF", [128, 2, 512]); k.dma(cF.a, I["s5_cF"].a)
        iota = sb("s5_iota", [128, 128]); k.dma(iota.a, I["c_iota"].a)
        k.act(dtc.a, ch[:, 2, :], AF.Exp)
        k.tt(th.a, ch[:, 1, :], dtc.a, ALU.mult)
        k.tt(rho.a, ch[:, 0, :], dtc.a, ALU.mult)
        k.act(rho.a, rho.a, AF.Exp)
        TW = 8 * LC
        ang = sb("s5_ang", [128, TW]); tf = sb("s5_tf", [128, TW]); ti = sb("s5_ti", [128, TW], I32)
        tr = sb("s5_tr", [128, TW])
        for q in range(4):
            for c8 in range(8):
                ct = 8 * q + c8
                k.ts(ang[:, c8 * LC:(c8 + 1) * LC], iota.a, th[:, ct:ct + 1], ALU.mult)
            self.trig(ang.a, sinT[:, 8 * q:8 * q + 8, :].re("p a b -> p (a b)"), cosT[:, 8 * q:8 * q + 8, :].re("p a b -> p (a b)"), tf.a, ti.a, tr.a)
        F = 512
        dtf = sb("s5_dtf", [128, F]); thf = sb("s5_thf", [128, F]); magf = sb("s5_magf", [128, F])
        k.act(dtf.a, fe[:, 2, :], AF.Exp)
        k.tt(thf.a, fe[:, 1, :], dtf.a, ALU.mult)
        k.tt(magf.a, fe[:, 0, :], dtf.a, ALU.mult)
        k.act(magf.a, magf.a, AF.Exp)
        sf = sb("s5_sf", [128, F]); cf = sb("s5_cf", [128, F])
        self.trig(thf.a, sf.a, cf.a, tf[:, 0:F], ti[:, 0:F], tr[:, 0:F])
        abr = sb("s5_abr", [128, F]); abi = sb("s5_abi", [128, F])
        k.tt(abr.a, magf.a, cf.a, ALU.mult)
        k.ts(abr.a, abr.a, -1.0, ALU.add)
        k.tt(abi.a, magf.a, sf.a, ALU.mult)
        lr = fe[:, 0, :]; li = fe[:, 1, :]
        den = dtf; t1 = thf; t2 = magf
        k.tt(den.a, lr, lr, ALU.mult); k.tt(t1.a, li, li, ALU.mult); k.tt(den.a, den.a, t1.a, ALU.add)
        k.recip(den.a, den.a)
        qr = sf; qi = cf
        k.tt(t1.a, abr.a, lr, ALU.mult); k.tt(t2.a, abi.a, li, ALU.mult); k.tt(qr.a, t1.a, t2.a, ALU.add); k.tt(qr.a, qr.a, den.a, ALU.mult)
        k.tt(t1.a, abi.a, lr, ALU.mult); k.tt(t2.a, abr.a, li, ALU.mult); k.tt(qi.a, t1.a, t2.a, ALU.subtract); k.tt(qi.a, qi.a, den.a, ALU.mult)
        bbr = abr; bbi = abi
        k.tt(t1.a, qr.a, bT[:, 0, :], ALU.mult); k.tt(t2.a, qi.a, bT[:, 1, :], ALU.mult); k.tt(bbr.a, t1.a, t2.a, ALU.subtract)
        k.tt(t1.a, qr.a, bT[:, 1, :], ALU.mult); k.tt(t2.a, qi.a, bT[:, 0, :], ALU.mult); k.tt(bbi.a, t1.a, t2.a, ALU.add)
        for ri, bb in enumerate((bbr, bbi)):
            for ft in range(8):
                o = LB[ri][:, 4 * ft:4 * ft + 4, :].re("p a (g q) -> p (a g) q", g=2)
                i0 = V(bb, AP(bb.h, ft * 64, [[F, 128], [0, 8], [1, 64]]))
                i1 = V(maskB, AP(maskB.h, 0, [[8, 128], [1, 8], [0, 64]]))
                k.tt(o, i0, i1, ALU.mult)
        xt = [sb("s5_xt%d" % i, [128, 128]) for i in range(2)]
        n_ = 0
        for ri in range(2):
            for ft in range(8):
                for cl in range(4):
                    x = xt[n_ % 2]; n_ += 1
                    i0 = V(cF, AP(cF.h, ri * 512 + ft * 64, [[1024, 128], [0, 2], [1, 64]]))
                    i1 = V(maskB, AP(maskB.h, 2 * cl, [[8, 128], [1, 2], [0, 64]]))
                    k.tt(x.a.re("p (g q) -> p g q", g=2), i0, i1, ALU.mult)
                    pt = self.psum()
                    k.tr(pt[:, 0:128], x.a, self.ident.a)
                    k.act(LCm[ri][:, 4 * ft + cl, :], pt[:, 0:128], AF.Copy, scale=(1.0 if ri == 0 else -1.0))
        k.free_to(mk_)
        x_ = sb("s5_xn", [128, 8, BLK]); xb_ = sb("s5_xnb", [128, 8, BLK], BF16)
        gbf = sb("s5_gbf", [128, 8, BLK], BF16)
        Hre4 = sb("s5_Hre4", [128, 4, BLK], BF16); Him4 = sb("s5_Him4", [128, 4, BLK], BF16)
        W4 = 4 * 128
        ur = sb("s5_ur", [128, W4]); ui = sb("s5_ui", [128, W4]); ta = sb("s5_ta", [128, W4]); tb = sb("s5_tb", [128, W4])
        tc = sb("s5_tc", [128, W4]); td = sb("s5_td", [128, W4])
        hr = sb("s5_hr", [128, W4]); hi = sb("s5_hi", [128, W4]); h2r = sb("s5_h2r", [128, W4]); h2i = sb("s5_h2i", [128, W4])
        sgt = [sb("s5_sg%d" % i, [128, BLK]) for i in range(2)]
        yt = sb("s5_yt", [128, BLK]); gt = sb("s5_gt", [128, BLK])
        wvj = [sb("s5_wv%d" % i, [128, 8, 128], BF16) for i in range(3)]
        wgj = [sb("s5_wg%d" % i, [128, 8, 128], BF16) for i in range(3)]
        wn = 0
        k.begin_defer()
        for i, (s0, n) in enumerate(TBS):
            sample = (s0 >= TP)
            self.rmsnorm(lambda kk, i=i: self.h[i][:, kk, :], lambda kk, n=n: x_[:, kk, 0:n], self.gain(0, l), n, self.tmp_sq, self.tmp_r)
            for kk in range(KC):
                k.copy(xb_[:, kk, 0:n], x_[:, kk, 0:n], en='act')
            for ft in range(8):
                nch = 1 if sample else n // LC
                for c in range(nch):
                    sl = slice(c * LC, (c + 1) * LC)
                    pbr = self.psum(); pbi = self.psum()
                    for cl in range(4):
                        k.mm(pbr[:, cl * 128:(cl + 1) * 128], LB[0][:, 4 * ft + cl, :], xb_[:, ft, sl])
                    for cl in range(4):
                        k.mm(pbi[:, cl * 128:(cl + 1) * 128], LB[1][:, 4 * ft + cl, :], xb_[:, ft, sl])
                    if sample:
                        cs = V(cosT, AP(cosT.h, 4 * ft * LC, [[32 * LC, 128], [LC, 4], [0, 16], [1, 8]]))
                        sn = V(sinT, AP(sinT.h, 4 * ft * LC, [[32 * LC, 128], [LC, 4], [0, 16], [1, 8]]))
                        w3 = lambda v: v.re("p (a b t) -> p a b t", a=4, t=8)
                    else:
                        cs = cosT[:, 4 * ft:4 * ft + 4, :].re("p a b -> p (a b)"); sn = sinT[:, 4 * ft:4 * ft + 4, :].re("p a b -> p (a b)")
                        w3 = lambda v: v
                    br = w3(pbr[:, 0:W4]); bi = w3(pbi[:, 0:W4])
                    k.tt(w3(ta.a), cs, br, ALU.mult); k.tt(w3(tb.a), sn, bi, ALU.mult); k.tt(ur.a, ta.a, tb.a, ALU.add)
                    k.tt(w3(ta.a), cs, bi, ALU.mult); k.tt(w3(tb.a), sn, br, ALU.mult); k.tt(ui.a, ta.a, tb.a, ALU.subtract)
                    for cl in range(4):
                        ct = 4 * ft + cl
                        o_ = cl * 128
                        rb = V(rho, AP(rho.h, ct, [[32, 128], [0, 8 if sample else LC]]))
                        if sample:
                            for b in range(16):
                                k.scan(hr[:, o_ + b * 8:o_ + (b + 1) * 8], rb, ur[:, o_ + b * 8:o_ + (b + 1) * 8], hst[:, 0, ct, b:b + 1])
                                k.scan(hi[:, o_ + b * 8:o_ + (b + 1) * 8], rb, ui[:, o_ + b * 8:o_ + (b + 1) * 8], hst[:, 1, ct, b:b + 1])
                        else:
                            k.scan(hr[:, o_:o_ + 128], rb, ur[:, o_:o_ + 128], hpr[:, 0, ct:ct + 1])
                            k.scan(hi[:, o_:o_ + 128], rb, ui[:, o_:o_ + 128], hpr[:, 1, ct:ct + 1])
                    k.tt(w3(ta.a), cs, w3(hr.a), ALU.mult); k.tt(w3(tb.a), sn, w3(hi.a), ALU.mult); k.tt(h2r.a, ta.a, tb.a, ALU.subtract)
                    k.tt(w3(ta.a), cs, w3(hi.a), ALU.mult); k.tt(w3(tb.a), sn, w3(hr.a), ALU.mult); k.tt(h2i.a, ta.a, tb.a, ALU.add)
                    k.copy(Hre4[:, :, sl], h2r.a.re("p (a b) -> p a b", a=4), en='act')
                    k.copy(Him4[:, :, sl], h2i.a.re("p (a b) -> p a b", a=4), en='act')
                    if sample:
                        k.copy(hso[:, 0, 4 * ft:4 * ft + 4, :], h2r.a.re("p (a b t) -> p a b t", a=4, t=8)[:, :, :, 7], en='act')
                        k.copy(hso[:, 1, 4 * ft:4 * ft + 4, :], h2i.a.re("p (a b t) -> p a b t", a=4, t=8)[:, :, :, 7], en='act')
                    else:
                        k.copy(hpr[:, 0, 4 * ft:4 * ft + 4], h2r.a.re("p (a b) -> p a b", a=4)[:, :, LC - 1], en='act')
                        k.copy(hpr[:, 1, 4 * ft:4 * ft + 4], h2i.a.re("p (a b) -> p a b", a=4)[:, :, LC - 1], en='act')
                py = self.psum()
                for cl in range(4):
                    k.mm(py[:, 0:n], LCm[0][:, 4 * ft + cl, :], Hre4[:, cl, 0:n], start=(cl == 0), stop=False, inc=False)
                    k.mm(py[:, 0:n], LCm[1][:, 4 * ft + cl, :], Him4[:, cl, 0:n], start=False, stop=(cl == 3), inc=True)
                k.stt(yt[:, 0:n], x_[:, ft, 0:n], dcol[:, ft:ft + 1], py[:, 0:n], ALU.mult, ALU.add)
                k.tt(gt[:, 0:n], yt[:, 0:n], yt[:, 0:n], ALU.mult)
                k.ts(gt[:, 0:n], gt[:, 0:n], 0.044715, ALU.mult, 1.0, ALU.add)
                k.tt(gt[:, 0:n], gt[:, 0:n], yt[:, 0:n], ALU.mult)
                k.act(gt[:, 0:n], gt[:, 0:n], AF.Sigmoid, scale=2.0 * 0.7978845608028654)
                k.tt(gbf[:, ft, 0:n], gt[:, 0:n], yt[:, 0:n], ALU.mult)
            for j in range(KC):
                wv_ = wvj[wn % 3]; wg_ = wgj[wn % 3]; wn += 1
                k.dma(wv_.a, I["l1_glu_v"].v(j * 128, [[D, 128], [128 * D, KC], [1, 128]]), q='pool')
                k.dma(wg_.a, I["l1_glu_g"].v(j * 128, [[D, 128], [128 * D, KC], [1, 128]]), q='pool')
                pv = self.psum(); pg = self.psum()
                for kk in range(KC):
                    k.mm(pv[:, 0:n], wv_[:, kk, :], gbf[:, kk, 0:n], start=(kk == 0), stop=(kk == KC - 1))
                for kk in range(KC):
                    k.mm(pg[:, 0:n], wg_[:, kk, :], gbf[:, kk, 0:n], start=(kk == 0), stop=(kk == KC - 1))
                sg_ = sgt[j % 2]
                k.act(sg_[:, 0:n], pg[:, 0:n], AF.Sigmoid)
                k.tt(sg_[:, 0:n], sg_[:, 0:n], pv[:, 0:n], ALU.mult)
                k.tt(self.h[i][:, j, :], self.h[i][:, j, :], sg_[:, 0:n], ALU.add)
        k.dma(O["s5_pstate"].a, hpr.a)
        k.dma(O["s5_sstate"].a, hso.a)
        k.end_defer()
        k.end_phase()


M_IN = 2048
M_CD = 3072
NEG = -1.0e5


class MambaMixin:
    def precast(self, name, src, row_len, col0, ncols, nk, tile_cols=128):
        k = self.k
        nt = (ncols + tile_cols - 1) // tile_cols
        dst = k.dram_tmp(name, [nt, 128, nk, tile_cols], BF16)
        for m in range(nt):
            w = min(tile_cols, ncols - m * tile_cols)
            k.dma(dst.v(m * 128 * nk * tile_cols, [[nk * tile_cols, 128], [tile_cols, nk], [1, w]]),
                  src.v(col0 + m * tile_cols, [[row_len, 128], [128 * row_len, nk], [1, w]]), q='pool')
        return dst

    def load_tile(self, dst_sb, scratch, m, nk, tile_cols=128, w=None, q='sp'):
        w = w or tile_cols
        self.k.dma(dst_sb[:, :, 0:w], scratch.v(m * 128 * nk * tile_cols, [[nk * tile_cols, 128], [tile_cols, nk], [1, w]]), q=q)

    def mamba_layer(self, l, I, O, win_bf, wout_bf):
        k = self.k
        sb = k.sbp
        cw = sb("m_cw", [128, 24, 4]); k.dma(cw.a, I["m_convw"].a)
        cb = sb("m_cb", [128, 24]); k.dma(cb.a, I["m_convb"].a)
        dtb = sb("m_dtb", [32, 1]); k.dma(dtb.a, I["m_dtb"].a)
        aneg = sb("m_aneg", [128, 32]); k.dma(aneg.a, I["m_alog"].a)
        k.act(aneg.a, aneg.a, AF.Exp); k.ts(aneg.a, aneg.a, -1.0, ALU.mult)
        dcol = sb("m_dcol", [128, 16]); k.dma(dcol.a, I["m_dcol"].a)
        nw = sb("m_nw", [128, 16]); k.dma(nw.a, I["m_normw"].a)
        U = sb("m_U", [128, 128]); k.dma(U.a, I["c_U"].a)
        negm = sb("m_negm", [128, 128]); k.dma(negm.a, I["c_negP"].a)
        carry = sb("m_carry", [128, 24, 3]); k.memset(carry.a, 0.0)
        eps512 = self.eps_col
        wt = [sb("m_wt%d" % i, [128, 8, 128], BF16) for i in range(4)]
        wo = [sb("m_wo%d" % i, [128, 16, 128], BF16) for i in range(2)]
        wcnt = [0, 0]

        def front(i, n, xn, zs, xc, dtf, sample, convst=None, sconv=None):
            self.rmsnorm(lambda kk, i=i: self.h[i][:, kk, :], lambda kk: xn[:, kk, 0:n], self.gain(0, l), n, self.tmp_sq, self.tmp_r)
            for m in range(41):
                w_ = wt[wcnt[0] % 4]; wcnt[0] += 1
                wd = 128 if m < 40 else 32
                self.load_tile(w_, win_bf, m, 8, w=wd)
                pp = self.psum()
                for kk in range(KC):
                    k.mm(pp[0:wd, 0:n], w_[:, kk, 0:wd], xn[:, kk, 0:n], start=(kk == 0), stop=(kk == KC - 1))
                if m < 16:
                    k.act(zs[:, m, 0:n], pp[:, 0:n], AF.Silu)
                elif m < 40:
                    mc = m - 16
                    if sample:
                        xr3 = xr_s.a
                        k.copy(xr3[:, :, 0:3], convst[:, mc, :, :], en='act')
                        k.copy(xr3[:, :, 3:11], pp[:, 0:n].re("p (b t) -> p b t", t=8), en='act')
                        acc = xc[:, mc, 0:n].re("p (b t) -> p b t", t=8)
                        k.ts(acc, xr3[:, :, 3:11], cw[:, mc, 3:4], ALU.mult, cb[:, mc:mc + 1], ALU.add)
                        for j in range(3):
                            k.stt(acc, xr3[:, :, j:j + 8], cw[:, mc, j:j + 1], acc, ALU.mult, ALU.add)
                        k.copy(sconv[:, mc, :, :], xr3[:, :, 8:11], en='act')
                    else:
                        xr_p = xr_pl[mc % 2]
                        k.copy(xr_p[:, 0:3], carry[:, mc, :], en='act')
                        k.copy(xr_p[:, 3:3 + n], pp[:, 0:n], en='act')
                        acc = xc[:, mc, 0:n]
                        k.ts(acc, xr_p[:, 3:3 + n], cw[:, mc, 3:4], ALU.mult, cb[:, mc:mc + 1], ALU.add)
                        for j in range(3):
                            k.stt(acc, xr_p[:, j:j + n], cw[:, mc, j:j + 1], acc, ALU.mult, ALU.add)
                        k.copy(carry[:, mc, :], xr_p[:, n:n + 3], en='act')
                    k.act(xc[:, mc, 0:n], xc[:, mc, 0:n], AF.Silu)
                else:
                    k.act(dtf[0:32, 0:n], pp[0:32, 0:n], AF.Exp, bias=dtb[:, 0:1])
                    k.ts(dtf[0:32, 0:n], dtf[0:32, 0:n], 1.0, ALU.add)
                    k.act(dtf[0:32, 0:n], dtf[0:32, 0:n], AF.Ln)

        def to_tm(xc, dtf, c0, Xdt, dt_tm, A_tm):
            pt = self.psum()
            k.tr(pt[:, 0:32], dtf[0:32, c0:c0 + 128], self.ident[0:32, 0:32])
            k.copy(dt_tm.a, pt[:, 0:32], en='act')
            k.tt(A_tm.a, dt_tm.a, aneg.a, ALU.mult)
            for kt in range(16):
                pt = self.psum()
                k.tr(pt[:, 0:128], xc[:, kt, c0:c0 + 128], self.ident.a)
                k.tt(Xdt[:, kt * 128:(kt + 1) * 128].re("p (a b) -> p a b", a=2), pt[:, 0:128].re("p (a b) -> p a b", a=2),
                     V(dt_tm, AP(dt_tm.h, 2 * kt, [[32, 128], [1, 2], [0, 64]])), ALU.mult)

        def back_fm(y_tm, xc, y_fm, c0):
            for kt in range(16):
                pt = self.psum()
                k.tr(pt[:, 0:128], y_tm[:, kt * 128:(kt + 1) * 128], self.ident.a)
                k.stt(y_fm[:, kt, c0:c0 + 128], xc[:, kt, c0:c0 + 128], dcol[:, kt:kt + 1], pt[:, 0:128], ALU.mult, ALU.add)

        def post(i, n, y_fm, zs, yn):
            for kt in range(16):
                k.tt(y_fm[:, kt, 0:n], y_fm[:, kt, 0:n], zs[:, kt, 0:n], ALU.mult)
            for gq in range(4):
                pss = self.psum()
                for a in range(4):
                    k.act(self.tmp_sq[:, a, 0:n], y_fm[:, 4 * gq + a, 0:n], AF.Square)
                for a in range(4):
                    k.mm(pss[:, 0:n], self.ones.a, self.tmp_sq[:, a, 0:n], start=(a == 0), stop=(a == 3))
                k.act(self.tmp_r[:, 0:n], pss[:, 0:n], AF.Sqrt, bias=self.eps_col[:, 0:1], scale=1.0 / 512)
                k.recip(self.tmp_r[:, 0:n], self.tmp_r[:, 0:n])
                for a in range(4):
                    kt = 4 * gq + a
                    k.stt(yn[:, kt, 0:n], y_fm[:, kt, 0:n], nw[:, kt:kt + 1], self.tmp_r[:, 0:n], ALU.mult, ALU.mult)
            for j in range(KC):
                w_ = wo[wcnt[1] % 2]; wcnt[1] += 1
                self.load_tile(w_, wout_bf, j, 16)
                pj = self.psum()
                for kt in range(16):
                    k.mm(pj[:, 0:n], w_[:, kt, :], yn[:, kt, 0:n], start=(kt == 0), stop=(kt == 15))
                k.tt(self.h[i][:, j, :], self.h[i][:, j, :], pj[:, 0:n], ALU.add)

        mk_ = k.mark()
        xn = sb("m_xn", [128, 8, BLK], BF16); zs = sb("m_zs", [128, 16, BLK], BF16)
        xc = sb("m_xc", [128, 24, BLK]); dtf = sb("m_dtf", [32, BLK]); xr_pl = [sb("m_xrp%d" % i, [128, 3 + BLK]) for i in range(2)]
        y_fm = sb("m_yfm", [128, 16, BLK]); yn = sb("m_yn", [128, 16, BLK], BF16)
        Xdt = sb("m_Xdt", [128, 2048]); y_tm = sb("m_ytm", [128, 2048])
        dt_tm = sb("m_dttm", [128, 32]); A_tm = sb("m_Atm", [128, 32]); decT = sb("m_decT", [128, 32]); E_tm = sb("m_Etm", [128, 32])
        B_tm = sb("m_Btm", [128, 4, 128]); CBt = sb("m_CBt", [128, 4, 128])
        ST = sb("m_ST", [128, 32, 64]); k.memset(ST.a, 0.0)
        ND = 4
        rhs4 = [sb("m_r4%d" % i, [128, 512]) for i in range(3)]; tD = [sb("m_tD%d" % i, [128, 128]) for i in range(ND)]
        negAc = sb("m_negAc", [128, 32]); m01 = sb("m_m01", [128, 128]); k.dma(m01.a, I["c_m01"].a)
        Lt = [sb("m_Lt%d" % i, [128, 128]) for i in range(ND)]; Gt = [sb("m_Gt%d" % i, [128, 128]) for i in range(ND)]
        yo = [sb("m_yo%d" % i, [128, 64]) for i in range(ND)]; Xde = [sb("m_Xde%d" % i, [128, 64]) for i in range(ND)]
        k.begin_defer()
        for i, (s0, n) in enumerate(TBS):
            if s0 >= TP:
                continue
            front(i, n, xn, zs, xc, dtf, False)
            for c in range(n // 128):
                c0 = c * 128
                first = (s0 + c0 == 0)
                to_tm(xc, dtf, c0, Xdt, dt_tm, A_tm)
                for g in range(4):
                    pt = self.psum()
                    k.tr(pt[:, 0:128], xc[:, 16 + g, c0:c0 + 128], self.ident.a)
                    k.copy(B_tm[:, g, :], pt[:, 0:128], en='act')
                    pc = self.psum()
                    k.mm(pc[:, 0:128], xc[:, 16 + g, c0:c0 + 128], xc[:, 20 + g, c0:c0 + 128])
                    k.tt(CBt[:, g, :], pc[:, 0:128], m01.a, ALU.mult)
                pa = self.psum()
                k.mm(pa[:, 0:32], self.ones.a, A_tm.a)
                k.act(decT.a, pa[:, 0:32], AF.Exp)
                pa = self.psum()
                k.mm(pa[:, 0:32], U.a, A_tm.a)
                k.act(E_tm.a, pa[:, 0:32], AF.Exp)
                k.act(negAc.a, pa[:, 0:32], AF.Copy, scale=-1.0)
                pD4 = {}

                def stageA1(hq):
                    r4 = rhs4[hq % 3]
                    k.tt(r4.a.re("p (a b) -> p a b", a=4), V(U, AP(U.h, 0, [[128, 128], [0, 4], [1, 128]])),
                         V(A_tm, AP(A_tm.h, 4 * hq, [[32, 128], [1, 4], [0, 128]])), ALU.mult)
                    pD4[hq] = self.psum(0, 3)
                    k.mm(pD4[hq][:, 0:512], self.ones.a, r4.a)

                def stageA2a(h):
                    td = tD[h % ND]; lt = Lt[h % ND]
                    hh = h % 4
                    k.ts(td.a, pD4[h // 4][:, hh * 128:(hh + 1) * 128], negAc[:, h:h + 1], ALU.add, 0.0, ALU.min)
                    k.act(lt.a, td.a, AF.Exp)

                def stageA2b(h):
                    g = h // 8
                    lt = Lt[h % ND]; gt = Gt[h % ND]; xde = Xde[h % ND]
                    k.tt(gt.a, lt.a, CBt[:, g, :], ALU.mult)
                    k.ts(xde.a, Xdt[:, h * 64:(h + 1) * 64], lt[:, 127:128], ALU.mult)

                pBd = {}

                def stageBpe(h):
                    g = h // 8
                    gt = Gt[h % ND]; xde = Xde[h % ND]
                    pB = self.psum(3, 8)
                    pBd[h] = pB
                    k.mm(pB[:, 0:64], gt.a, Xdt[:, h * 64:(h + 1) * 64])
                    if not first:
                        k.mm(pB[:, 64:128], xc[:, 20 + g, c0:c0 + 128], ST[:, h, :])
                    k.mm(pB[:, 128:192], B_tm[:, g, :], xde.a)

                def stageBev(h):
                    yo_ = yo[h % ND]
                    pB = pBd.pop(h)
                    if not first:
                        k.ts(yo_.a, pB[:, 64:128], E_tm[:, h:h + 1], ALU.mult)
                        k.tt(y_tm[:, h * 64:(h + 1) * 64], yo_.a, pB[:, 0:64], ALU.add)
                    else:
                        k.copy(y_tm[:, h * 64:(h + 1) * 64], pB[:, 0:64], en='dve')
                    k.stt(ST[:, h, :], ST[:, h, :], decT[:, h:h + 1], pB[:, 128:192], ALU.mult, ALU.add)

                stageA1(0)
                stageA1(1)
                stageA2a(0)
                for h in range(32):
                    if h % 4 == 0 and h // 4 + 2 < 8:
                        stageA1(h // 4 + 2)
                    if h + 1 < 32:
                        stageA2a(h + 1)
                    if h >= 1:
                        stageBpe(h - 1)
                    stageA2b(h)
                    if h >= 1:
                        stageBev(h - 1)
                stageBpe(31)
                stageBev(31)
                back_fm(y_tm, xc, y_fm, c0)
            post(i, n, y_fm, zs, yn)
        k.dma(O["m_pconv"].a, carry.a)
        k.dma(O["m_pssmT"].a, ST.a)
        k.end_defer()
        k.free_to(mk_)
        i = len(TBS) - 1
        n = TS
        xn = sb("ms_xn", [128, 8, TS], BF16); zs = sb("ms_zs", [128, 16, TS], BF16)
        xc = sb("ms_xc", [128, 24, TS]); dtf = sb("ms_dtf", [32, TS]); xr_s = sb("ms_xrs", [128, 16, 11])
        Xdt = sb("ms_Xdt", [128, 2048]); y_tm = sb("ms_ytm", [128, 2048])
        dt_tm = sb("ms_dttm", [128, 32]); A_tm = sb("ms_Atm", [128, 32])
        BC_tm = sb("ms_BCtm", [128, 8, 128])
        convst = sb("ms_convst", [128, 24, 16, 3]); k.dma(convst.a, I["m_convst"].a)
        sconv = sb("ms_sconv", [128, 24, 16, 3])
        alq = sb("ms_alq", [128, 1]); k.dma(alq.a, I["m_alogq"].a)
        k.act(alq.a, alq.a, AF.Exp); k.ts(alq.a, alq.a, -1.0, ALU.mult)
        k.begin_defer()
        front(i, n, xn, zs, xc, dtf, True, convst, sconv)
        k.dma(O["m_sconv"].a, sconv.a)
        to_tm(xc, dtf, 0, Xdt, dt_tm, A_tm)
        for g8 in range(8):
            pt = self.psum()
            k.tr(pt[:, 0:128], xc[:, 16 + g8, 0:128], self.ident.a)
            k.copy(BC_tm[:, g8, :], pt[:, 0:128], en='act')
        sx = k.dram_tmp("ms_sx", [128, 2048]); sbc = k.dram_tmp("ms_sbc", [128, 1024]); sdt = k.dram_tmp("ms_sdt", [128, 32])
        sy = k.dram_tmp("ms_sy", [128, 2048])
        k.dma(sx.a, Xdt.a); k.dma(sbc.a, BC_tm.a.re("p a b -> p (a b)")); k.dma(sdt.a, dt_tm.a)
        mk_r = k.mark()
        Xq = sb("ms_Xq", [128, 8, 64]); Bq = sb("ms_Bq", [128, 8, 128]); Cq = sb("ms_Cq", [128, 8, 128])
        dtq = sb("ms_dtq", [128, 8, 1]); dAq = sb("ms_dAq", [128, 8]); yq = sb("ms_yq", [128, 8, 64])
        S = sb("ms_S", [128, 32, 128]); tmp = sb("ms_tmp", [128, 32, 128])
        ssm_in = I["m_ssm"]; ssm_out = O["m_sssm"]
        for r in range(4):
            for b4 in range(4):
                tok0 = (4 * r + b4) * 8
                k.dma(Xq[b4 * 32:(b4 + 1) * 32, :, :], sx.v(tok0 * 2048, [[64, 32], [2048, 8], [1, 64]]))
                k.dma(dtq[b4 * 32:(b4 + 1) * 32, :, :], sdt.v(tok0 * 32, [[1, 32], [32, 8], [1, 1]]), allow_slow_non_contiguous=True)
                for g in range(4):
                    p0 = b4 * 32 + g * 8
                    k.dma(Bq[p0:p0 + 8, :, :], sbc.v(tok0 * 1024 + g * 128, [[0, 8], [1024, 8], [1, 128]]))
                    k.dma(Cq[p0:p0 + 8, :, :], sbc.v(tok0 * 1024 + 512 + g * 128, [[0, 8], [1024, 8], [1, 128]]))
            k.act(dAq.a, dtq.a.re("p a b -> p (a b)"), AF.Exp, scale=alq[:, 0:1])
            for ph in range(2):
                k.dma(S.a.re("p a b -> p (a b)"), ssm_in.v(4 * r * 32 * 8192 + ph * 4096, [[8192, 128], [1, 4096]]))
                for t in range(8):
                    xin = V(Xq, AP(Xq.h, t * 64 + ph * 32, [[512, 128], [1, 32], [0, 128]]))
                    bin_ = V(Bq, AP(Bq.h, t * 128, [[1024, 128], [0, 32], [1, 128]]))
                    cin = V(Cq, AP(Cq.h, t * 128, [[1024, 128], [0, 32], [1, 128]]))
                    k.tt(tmp.a, xin, bin_, ALU.mult)
                    k.stt(S.a, S.a, dAq[:, t:t + 1], tmp.a, ALU.mult, ALU.add)
                    k.tt(tmp.a, S.a, cin, ALU.mult)
                    k.reduce(yq[:, t, ph * 32:(ph + 1) * 32], tmp.a)
                k.dma(ssm_out.v(4 * r * 32 * 8192 + ph * 4096, [[8192, 128], [1, 4096]]), S.a.re("p a b -> p (a b)"))
            for b4 in range(4):
                tok0 = (4 * r + b4) * 8
                k.dma(sy.v(tok0 * 2048, [[64, 32], [2048, 8], [1, 64]]), yq[b4 * 32:(b4 + 1) * 32, :, :])
        k.end_defer()
        k.free_to(mk_r)
        y_fm = sb("ms_yfm", [128, 16, TS]); yn = sb("ms_yn", [128, 16, TS], BF16)
        k.begin_defer()
        k.dma(y_tm.a, sy.a)
        back_fm(y_tm, xc, y_fm, 0)
        post(i, n, y_fm, zs, yn)
        k.end_defer()
        k.end_phase()


SB = 128
CH = 64
NP = 4


class RwkvMixin:
    def rwkv_layer(self, l, I, O, W, pre, vfirst):
        k = self.k
        sb = k.sbp
        has_v = (l == 3)
        n = SB
        vec = sb("rw_vec", [128, 8, 8]); k.dma(vec.a, I[pre + "vec"].a)
        mu = sb("rw_mu", [128, 6, 8]); k.dma(mu.a, I[pre + "mu"].a)
        w1 = sb("rw_w1", [128, 8, 64], BF16); k.dma(w1.a, I[pre + "w1"].v(0, [[64, 128], [128 * 64, 8], [1, 64]]), q='pool')
        a1 = sb("rw_a1", [128, 8, 64], BF16); k.dma(a1.a, I[pre + "a1"].v(0, [[64, 128], [128 * 64, 8], [1, 64]]), q='pool')
        g1 = sb("rw_g1", [128, 8, 160], BF16); k.dma(g1.a, I[pre + "g1"].v(0, [[160, 128], [128 * 160, 8], [1, 160]]), q='pool')
        w2 = sb("rw_w2", [64, 1024], BF16); k.dma(w2.a, I[pre + "w2"].a, q='pool')
        a2 = sb("rw_a2", [64, 1024], BF16); k.dma(a2.a, I[pre + "a2"].a, q='pool')
        g2a = sb("rw_g2a", [128, 1024], BF16); k.dma(g2a.a, I[pre + "g2"].v(0, [[1024, 128], [1, 1024]]), q='pool')
        g2b = sb("rw_g2b", [32, 1024], BF16); k.dma(g2b.a, I[pre + "g2"].v(128 * 1024, [[1024, 32], [1, 1024]]), q='pool')
        if has_v:
            v1 = sb("rw_v1", [128, 8, 32], BF16); k.dma(v1.a, I[pre + "v1"].v(0, [[32, 128], [128 * 32, 8], [1, 32]]), q='pool')
            v2 = sb("rw_v2", [32, 1024], BF16); k.dma(v2.a, I[pre + "v2"].a, q='pool')
        bones = sb("rw_bones", [128, 128]); k.dma(bones.a, I["c_bones"].a)
        gneps = sb("rw_gneps", [128, 1]); k.memset(gneps.a, 64e-5)
        VW0, VA0, VKK, VKA, VRK, VLW, VLB, VV0 = range(8)
        vech = sb("rw_vech", [128, 8, 8]); k.ts(vech.a, vec.a, 0.5, ALU.mult)

        def colh(t, j):
            return vech[:, t, j:j + 1]

        def col(t, j):
            return vec[:, t, j:j + 1]

        xx = sb("rw_xx", [128, 8, SB])
        xm = [sb("rw_xm%d" % c, [128, 8, SB], BF16) for c in range(6)]
        tw = sb("rw_tw", [64, SB], BF16); ta = sb("rw_ta", [64, SB], BF16)
        tgf = sb("rw_tgf", [128, SB]); tga = sb("rw_tga", [128, SB], BF16); tgb = sb("rw_tgb", [32, SB], BF16)
        tv = sb("rw_tv", [32, SB], BF16)
        att = [sb("rw_att%d" % j, [128, SB], BF16) for j in range(8)]
        wtl = [[sb("rw_wt%d_%d" % (i, c), [128, 8, 128], BF16) for c in range(3)] for i in range(2)]

        wc = [0, 0]
        names = ["r", "lw", "k", "v", "an", "bn", "g", "bonus", "out", "t0", "t1", "t2", "cum", "P", "Pi", "Pe"]
        PT = [{nm: sb("rw_%s_%d" % (nm, q), [128, SB]) for nm in names} for q in range(2)]
        mk_p = k.mark()
        PT += [{nm: sb("rw_%s_%d" % (nm, q), [128, SB]) for nm in names} for q in range(2, NP)]
        MSU = sb("rw_MSU", [128, 128]); k.dma(MSU.a, I["c_MSU"].a)
        MIU = sb("rw_MIU", [128, 128]); k.dma(MIU.a, I["c_MIU"].a)
        MSL = sb("rw_MSL", [128, 128]); k.dma(MSL.a, I["c_MSL"].a)
        XNp = sb("rw_XNp", [128, 8, SB + 1]); k.memset(XNp.a, 0.0)
        ssp = sb("rw_ssp", [128, 8])
        woR = sb("rw_woR", [128, 8, 8, 128], BF16)
        for jo in range(8):
            self.load_tile(woR[:, jo, :, :], W["o"], jo, 8)
        bdn = ["b_bd", "k_bd", "v_bd", "Mrb", "Mak", "Mrk", "VT", "bT", "kT", "S0T", "ts"]
        BD = [{nm: sb("rw_%s_%d" % (nm, q), [128, 128], (F32 if nm == "ts" else BF16)) for nm in bdn} for q in range(NP)]
        AR = [sb("rw_AR_%d" % q, [128, 256], BF16) for q in range(NP)]
        NE = [[sb("rw_NE%d_%d" % (q, i), [128, 384], BF16) for i in range(2)] for q in range(NP)]
        identb = sb("rw_identb", [128, 128], BF16); k.copy(identb.a, self.ident.a)
        for q in range(NP):
            for nm in ["b_bd", "k_bd", "v_bd", "S0T"]:
                k.memset(BD[q][nm].a, 0.0)
            k.memset(AR[q].a, 0.0)
        SALL = [sb("rw_SALL%d" % j, [128, 128]) for j in range(8)]
        for j in range(8):
            k.memset(SALL[j].a, 0.0)

        def lo(q):
            return slice(0, 64) if q == 0 else slice(64, 128)

        def front(sbi):
            sample = (sbi == 16)
            if not sample:
                blk, half = sbi // 2, sbi % 2
                hv = lambda kk: self.h[blk][:, kk, half * SB:(half + 1) * SB]
                self.rmsnorm(hv, lambda kk: XNp[:, kk, 1:SB + 1], self.gain(0, l), n, self.tmp_sq, self.tmp_r)
                X = lambda kk: XNp[:, kk, 1:SB + 1]
                for kk in range(8):
                    k.tt(xx[:, kk, :], XNp[:, kk, 0:SB], XNp[:, kk, 1:SB + 1], ALU.subtract)
            else:
                hv = lambda kk: self.h[8][:, kk, :]
                self.rmsnorm(hv, lambda kk: xnc[:, kk, :], self.gain(0, l), n, self.tmp_sq, self.tmp_r)
                k.dma(sst.a, I[pre + "shift"].a)
                for kk in range(8):
                    k.copy(XNs[:, kk, :, 0], sst[:, kk, :], en='act')
                    k.copy(XNs[:, kk, :, 1:9], xnc[:, kk, :].re("p (b t) -> p b t", t=8), en='act')
                    k.tt(xx[:, kk, :].re("p (b t) -> p b t", t=8), XNs[:, kk, :, 0:8], XNs[:, kk, :, 1:9], ALU.subtract)
                X = lambda kk: xnc[:, kk, :]
            for c in range(6):
                for kk in range(8):
                    k.stt(xm[c][:, kk, :], xx[:, kk, :], mu[:, c, kk:kk + 1], X(kk), ALU.mult, ALU.add)
            p = self.psum()
            for kk in range(8):
                k.mm(p[0:64, 0:n], w1[:, kk, :], xm[1][:, kk, :], start=(kk == 0), stop=(kk == 7))
            k.act(tw.a, p[0:64, 0:n], AF.Tanh)
            p = self.psum()
            for kk in range(8):
                k.mm(p[0:64, 0:n], a1[:, kk, :], xm[4][:, kk, :], start=(kk == 0), stop=(kk == 7))
            k.copy(ta.a, p[0:64, 0:n], en='act')
            p = self.psum()
            for kk in range(8):
                k.mm(p[:, 0:n], g1[:, kk, 0:128], xm[5][:, kk, :], start=(kk == 0), stop=(kk == 7))
            k.act(tgf.a, p[:, 0:n], AF.Tanh, scale=0.5)
            k.ts(tga.a, tgf.a, 0.5, ALU.mult, 0.5, ALU.add)
            p = self.psum()
            for kk in range(8):
                k.mm(p[0:32, 0:n], g1[:, kk, 128:160], xm[5][:, kk, :], start=(kk == 0), stop=(kk == 7))
            k.act(tgf[0:32, :], p[0:32, 0:n], AF.Tanh, scale=0.5)
            k.ts(tgb.a, tgf[0:32, :], 0.5, ALU.mult, 0.5, ALU.add)
            if has_v:
                p = self.psum()
                for kk in range(8):
                    k.mm(p[0:32, 0:n], v1[:, kk, :], xm[3][:, kk, :], start=(kk == 0), stop=(kk == 7))
                k.copy(tv.a, p[0:32, 0:n], en='act')
            if not sample:
                if sbi == 15:
                    k.copy(ssp.a, XNp[:, :, SB], en='act')
                    k.dma(O[pre + "shift_p"].a, ssp.a)
                else:
                    for kk in range(8):
                        k.copy(XNp[:, kk, 0:1], XNp[:, kk, SB:SB + 1], en='act')
            else:
                for kk in range(8):
                    k.copy(sso[:, kk, :], xnc[:, kk, :].re("p (b t) -> p b t", t=8)[:, :, 7], en='act')
                k.dma(O[pre + "shift_s"].a, sso.a)

        def partA(sbi, j, q, pr=(0, 8)):
            T = PT[q]
            tok0 = sbi * SB
            jc = slice(j * 128, (j + 1) * 128)
            ws = wtl[(q // 2) % len(wtl)]
            for c, nm in enumerate(["r", "k", "v"]):
                self.load_tile(ws[c], W[nm], j, 8)
            for c, (nm, mi) in enumerate([("r", 0), ("t0", 2), ("v", 3)]):
                p = self.psum(*pr)
                for kk in range(8):
                    k.mm(p[:, 0:n], ws[c][:, kk, :], xm[mi][:, kk, :], start=(kk == 0), stop=(kk == 7))
                k.copy(T[nm].a, p[:, 0:n], en='act')
            k0 = T["t0"]
            p = self.psum(*pr)
            k.mm(p[:, 0:n], w2[0:64, jc], tw.a)
            k.act(T["lw"].a, p[:, 0:n], AF.Tanh, bias=colh(VW0, j), scale=0.5)
            k.ts(T["lw"].a, T["lw"].a, -0.3032653298563167, ALU.mult, -0.3032653298563167, ALU.add)
            p = self.psum(*pr)
            k.mm(p[:, 0:n], a2[0:64, jc], ta.a)
            a_ = T["t1"]
            k.act(a_.a, p[:, 0:n], AF.Tanh, bias=colh(VA0, j), scale=0.5)
            k.ts(a_.a, a_.a, 0.5, ALU.mult, 0.5, ALU.add)
            p = self.psum(*pr)
            k.mm(p[:, 0:n], g2a[:, jc], tga.a, start=True, stop=False, inc=False)
            k.mm(p[:, 0:n], g2b[0:32, jc], tgb.a, start=False, stop=True)
            k.copy(T["g"].a, p[:, 0:n], en='act')
            if has_v:
                p = self.psum(*pr)
                k.mm(p[:, 0:n], v2[0:32, jc], tv.a)
                sv = T["t2"]
                k.act(sv.a, p[:, 0:n], AF.Tanh, bias=colh(VV0, j), scale=0.5)
                k.ts(sv.a, sv.a, 0.5, ALU.mult, 0.5, ALU.add)
                vf = T["cum"]
                k.dma(vf.a, vfirst.v(j * 128 * TT + tok0, [[TT, 128], [1, n]]))
                k.tt(vf.a, vf.a, T["v"].a, ALU.subtract)
                k.tt(vf.a, vf.a, sv.a, ALU.mult)
                k.tt(T["v"].a, T["v"].a, vf.a, ALU.add)
            else:
                k.dma(vfirst.v(j * 128 * TT + tok0, [[TT, 128], [1, n]]), T["v"].a)
            kkn = T["t2"]
            k.ts(kkn.a, k0.a, col(VKK, j), ALU.mult)
            sq = T["cum"]
            k.tt(sq.a, kkn.a, kkn.a, ALU.mult)
            p = self.psum(*pr)
            k.mm(p[:, 0:n], bones.a, sq.a)
            k.ts(sq.a, p[:, 0:n], 1e-24, ALU.max)
            k.act(sq.a, sq.a, AF.Sqrt)
            k.recip(sq.a, sq.a)
            k.tt(kkn.a, kkn.a, sq.a, ALU.mult)
            tt_ = T["P"]
            k.ts(tt_.a, a_.a, -1.0, ALU.add, col(VKA, j), ALU.mult)
            k.ts(tt_.a, tt_.a, 1.0, ALU.add)
            k.tt(T["k"].a, k0.a, tt_.a, ALU.mult)
            k.ts(T["an"].a, kkn.a, -1.0, ALU.mult)
            k.tt(T["bn"].a, kkn.a, a_.a, ALU.mult)
            k.stt(tt_.a, T["r"].a, col(VRK, j), T["k"].a, ALU.mult, ALU.mult)
            p = self.psum(*pr)
            k.mm(p[:, 0:n], bones.a, tt_.a)
            k.tt(T["bonus"].a, T["v"].a, p[:, 0:n], ALU.mult)

        def chunk(js, c, slots=None, pr=(0, 8)):
            slots = list(range(len(js))) if slots is None else slots
            cs = slice(c * CH, (c + 1) * CH)
            onesb = V(self.ones, AP(self.ones.h, 0, [[128, 128], [0, CH]]))
            for u, q in enumerate(slots):
                T = PT[q]
                k.scan(T["cum"][:, cs], onesb, T["lw"][:, cs], 0.0)
                k.act(T["P"][:, cs], T["cum"][:, cs], AF.Exp)
                k.act(T["Pi"][:, cs], T["cum"][:, cs], AF.Exp, scale=-1.0)
                k.tt(T["Pe"][:, cs], T["cum"][:, cs], T["lw"][:, cs], ALU.subtract)
                k.act(T["Pe"][:, cs], T["Pe"][:, cs], AF.Exp)
            yield
            for u, q in enumerate(slots):
                T = PT[q]; B = BD[q]
                for hh in range(2):
                    ps_ = lo(hh); fs = slice(hh * 64, hh * 64 + 64)
                    k.tt(AR[q][ps_, fs], T["an"][ps_, cs], T["Pe"][ps_, cs], ALU.mult)
                    k.tt(AR[q][ps_, 128 + hh * 64:128 + hh * 64 + 64], T["r"][ps_, cs], T["P"][ps_, cs], ALU.mult)
                    k.tt(B["b_bd"][ps_, fs], T["bn"][ps_, cs], T["Pi"][ps_, cs], ALU.mult)
                    k.tt(B["k_bd"][ps_, fs], T["k"][ps_, cs], T["Pi"][ps_, cs], ALU.mult)
                    k.copy(B["v_bd"][ps_, fs], T["v"][ps_, cs], en='act')
                k.copy(B["S0T"].a, SALL[js[u]].a, en='act')
                yield
            b1 = {}; b2 = {}
            for q in slots:
                B = BD[q]
                b1[q] = self.psum(*pr)
                k.mm(b1[q][:, 0:256], B["b_bd"].a, AR[q].a)
                k.mm(b1[q][:, 256:512], B["k_bd"].a, AR[q].a)
                b2[q] = self.psum(*pr)
                k.mm(b2[q][:, 128:256], B["v_bd"].a, identb.a)
                k.mm(b2[q][:, 256:384], B["b_bd"].a, identb.a)
                k.mm(b2[q][:, 384:512], B["k_bd"].a, identb.a)
            yield
            for q in slots:
                B = BD[q]
                k.tt(NE[q][0][:, 256:384], b1[q][:, 0:128], MSU.a, ALU.mult)
                k.tt(B["Mrb"].a, b1[q][:, 128:256], MIU.a, ALU.mult)
                k.tt(B["Mak"].a, b1[q][:, 256:384], MSU.a, ALU.mult)
                k.tt(B["Mrk"].a, b1[q][:, 384:512], MIU.a, ALU.mult)
                k.copy(B["VT"].a, b2[q][:, 128:256], en='act')
                k.copy(B["bT"].a, b2[q][:, 256:384], en='act')
                k.copy(B["kT"].a, b2[q][:, 384:512], en='act')
                yield
            pW = {}
            for q in slots:
                B = BD[q]
                pW[q] = self.psum(*pr)
                k.mm(pW[q][:, 0:128], AR[q][:, 0:128], B["S0T"].a, start=True, stop=False, inc=False)
                k.mm(pW[q][:, 0:128], B["Mak"].a, B["VT"].a, start=False, stop=True)
                k.mm(pW[q][:, 128:256], NE[q][0][:, 256:384], identb.a)
            for q in slots:
                k.copy(NE[q][0][:, 0:256], pW[q][:, 0:256], en='act')
            yield
            for lev in range(6):
                pN = {}
                for q in slots:
                    src = NE[q][lev % 2]
                    X_ = src[:, 0:128]; At_ = src[:, 128:256]; A_ = src[:, 256:384]
                    pN[q] = self.psum(*pr)
                    k.mm(pN[q][:, 0:128], A_, X_, start=True, stop=False, inc=False)
                    k.mm(pN[q][:, 0:128], identb.a, X_, start=False, stop=True)
                    if lev < 5:
                        k.mm(pN[q][:, 128:256], A_, At_)
                        k.mm(pN[q][:, 256:384], At_, A_)
                for q in slots:
                    dst = NE[q][(lev + 1) % 2]
                    if lev < 5:
                        k.copy(dst[:, 0:384], pN[q][:, 0:384], en='act')
                    else:
                        k.copy(dst[:, 0:128], pN[q][:, 0:128], en='act')
                yield
            pO = {}
            for q in slots:
                B = BD[q]
                pO[q] = self.psum(*pr)
                k.mm(pO[q][:, 0:128], B["S0T"].a, AR[q][:, 128:256], start=True, stop=False, inc=False)
                k.mm(pO[q][:, 0:128], NE[q][0][:, 0:128], B["Mrb"].a, start=False, stop=False, inc=False)
                k.mm(pO[q][:, 0:128], B["VT"].a, B["Mrk"].a, start=False, stop=True)
                k.mm(pO[q][:, 128:256], B["bT"].a, NE[q][0][:, 0:128], start=True, stop=False, inc=False)
                k.mm(pO[q][:, 128:256], B["kT"].a, B["VT"].a, start=False, stop=True)
            yield
            for u, q in enumerate(slots):
                T = PT[q]; B = BD[q]
                for hh in range(2):
                    ps_ = lo(hh)
                    k.copy(T["out"][ps_, cs], pO[q][ps_, hh * 64:hh * 64 + 64], en='act')
                ptot = T["P"][:, c * CH + CH - 1:c * CH + CH]
                k.act(B["ts"].a, pO[q][:, 128:256], AF.Copy, scale=ptot)
                k.stt(SALL[js[u]].a, SALL[js[u]].a, ptot, B["ts"].a, ALU.mult, ALU.add)
            yield

        def partB(j, outv, bonv, gv, q=0, pr=(0, 8)):
            T = PT[q]
            p = self.psum(*pr)
            k.mm(p[:, 0:n], bones.a, outv)
            cen = T["t0"]
            k.stt(cen.a, p[:, 0:n], -1.0 / 64, outv, ALU.mult, ALU.add)
            sq = T["t1"]
            k.tt(sq.a, cen.a, cen.a, ALU.mult)
            p = self.psum(*pr)
            k.mm(p[:, 0:n], bones.a, sq.a)
            k.act(sq.a, p[:, 0:n], AF.Sqrt, bias=gneps[:, 0:1], scale=1.0 / 64)
            k.recip(sq.a, sq.a)
            k.tt(cen.a, cen.a, sq.a, ALU.mult)
            k.ts(cen.a, cen.a, col(VLW, j), ALU.mult, col(VLB, j), ALU.add)
            k.tt(cen.a, cen.a, bonv, ALU.add)
            k.tt(att[j].a, cen.a, gv, ALU.mult)

        def wo_apply(sbi):
            for jo in range(8):
                p = self.psum()
                if sbi < 16:
                    wv_ = lambda kk: woR[:, jo, kk, :]
                else:
                    w_ = wol[jo % 2]
                    self.load_tile(w_, W["o"], jo, 8)
                    wv_ = lambda kk: w_[:, kk, :]
                for kk in range(8):
                    k.mm(p[:, 0:n], wv_(kk), att[kk].a, start=(kk == 0), stop=(kk == 7))
                if sbi < 16:
                    blk, half = sbi // 2, sbi % 2
                    hv = self.h[blk][:, jo, half * SB:(half + 1) * SB]
                else:
                    hv = self.h[8][:, jo, :]
                k.tt(hv, hv, p[:, 0:n], ALU.add)

        def group_gen(sbi, js, slots, pr):
            for q_, j in zip(slots, js):
                partA(sbi, j, q_, pr)
                yield
            for c in range(SB // CH):
                yield from chunk(js, c, slots, pr)
            for q_, j in zip(slots, js):
                partB(j, PT[q_]["out"].a, PT[q_]["bonus"].a, PT[q_]["g"].a, q_, pr)
                yield

        LAG = 6
        k.begin_defer()
        for sbi in range(16):
            front(sbi)
            for g0 in (0, 4):
                gA = group_gen(sbi, [g0, g0 + 1], [0, 1], (0, 4))
                gB = group_gen(sbi, [g0 + 2, g0 + 3], [2, 3], (4, 8))
                aliveA = aliveB = True
                steps = 0
                while aliveA or aliveB:
                    if aliveA:
                        try:
                            next(gA)
                        except StopIteration:
                            aliveA = False
                    steps += 1
                    if aliveB and (steps > LAG or not aliveA):
                        try:
                            next(gB)
                        except StopIteration:
                            aliveB = False
            wo_apply(sbi)
            if sbi % 4 == 3:
                k.end_defer()
                if sbi < 15:
                    k.begin_defer()
        for j in range(8):
            p = self.psum()
            k.tr(p[:, 0:128], SALL[j].a, self.ident.a)
            sot = PT[(j // 2) % NP]["t0"]
            for hh in range(2):
                k.copy(sot[lo(hh), (j % 2) * 64:(j % 2) * 64 + 64], p[lo(hh), hh * 64:hh * 64 + 64], en='act')
            if j % 2 == 1:
                k.dma(O[pre + "wkv_p"][:, j - 1:j + 1, :], sot.a.re("p (a b) -> p a b", a=2))
        k.free_to(mk_p)
        gS = sb("rw_gS", [128, 8, SB]); bonS = sb("rw_bonS", [128, 8, SB])
        XNs = sb("rw_XNs", [128, 8, 16, 9]); xnc = sb("rw_xnc", [128, 8, SB])
        sst = sb("rw_sst", [128, 8, 16]); sso = sb("rw_sso", [128, 8, 16])
        wol = [sb("rw_wo%d" % i, [128, 8, 128], BF16) for i in range(2)]
        sbi = 16
        k.begin_defer()
        front(sbi)
        scr = {nm: k.dram_tmp("rw%d_s_%s" % (l, nm), [8, 128, 128]) for nm in ["an", "bn", "w", "k", "v", "r", "o"]}
        tmt = [sb("rw_tmt%d" % i, [128, 128]) for i in range(2)]
        tc_ = 0
        for j in range(8):
            partA(sbi, j, 0)
            T = PT[0]
            k.copy(gS[:, j, :], T["g"].a, en='act')
            k.copy(bonS[:, j, :], T["bonus"].a, en='act')
            k.act(T["P"].a, T["lw"].a, AF.Exp)
            for nm, src in [("an", "an"), ("bn", "bn"), ("w", "P"), ("k", "k"), ("v", "v"), ("r", "r")]:
                p = self.psum()
                k.tr(p[:, 0:128], T[src].a, self.ident.a)
                t_ = tmt[tc_ % 2]; tc_ += 1
                k.copy(t_.a, p[:, 0:128], en='act')
                k.dma(scr[nm].v(j * 128 * 128, [[128, 128], [1, 128]]), t_.a)
        Tq = {nm: sb("rw_q_%s" % nm, [128, 8, 128]) for nm in ["an", "bn", "w", "k", "v", "r", "o"]}
        for nm in ["an", "bn", "w", "k", "v", "r"]:
            for b in range(16):
                k.dma(Tq[nm][b * 8:(b + 1) * 8, :, :], scr[nm].v(b * 8 * 128, [[128 * 128, 8], [128, 8], [1, 128]]))
        NI = 16
        S = sb("rw_S", [128, NI, 64]); tmp = sb("rw_tmpS", [128, NI, 64]); sa = sb("rw_sa", [128, NI])
        wkv_in = I[pre + "wkv"]; wkv_out = O[pre + "wkv_s"]
        for h2 in range(2):
          for ih in range(64 // NI):
            soff = h2 * 4096 + ih * NI * 64
            k.dma(S.a.re("p a b -> p (a b)"), wkv_in.v(soff, [[8192, 128], [1, NI * 64]]))
            for t in range(8):
                def bi(nm):
                    return V(Tq[nm], AP(Tq[nm].h, t * 128 + h2 * 64, [[1024, 128], [0, NI], [1, 64]]))
                def bd_(nm):
                    return V(Tq[nm], AP(Tq[nm].h, t * 128 + h2 * 64 + ih * NI, [[1024, 128], [1, NI], [0, 64]]))
                k.tt(tmp.a, S.a, bi("an"), ALU.mult)
                k.reduce(sa.a, tmp.a)
                k.tt(S.a, S.a, bi("w"), ALU.mult)
                k.tt(tmp.a, V(sa, AP(sa.h, 0, [[NI, 128], [1, NI], [0, 64]])), bi("bn"), ALU.mult)
                k.tt(S.a, S.a, tmp.a, ALU.add)
                k.tt(tmp.a, bd_("v"), bi("k"), ALU.mult)
                k.tt(S.a, S.a, tmp.a, ALU.add)
                k.tt(tmp.a, S.a, bi("r"), ALU.mult)
                k.reduce(Tq["o"][:, t, h2 * 64 + ih * NI:h2 * 64 + ih * NI + NI], tmp.a)
            k.dma(wkv_out.v(soff, [[8192, 128], [1, NI * 64]]), S.a.re("p a b -> p (a b)"))
        for b in range(16):
            k.dma(scr["o"].v(b * 8 * 128, [[128 * 128, 8], [128, 8], [1, 128]]), Tq["o"][b * 8:(b + 1) * 8, :, :])
        for j in range(8):
            t_ = tmt[j % 2]
            k.dma(t_.a, scr["o"].v(j * 128 * 128, [[128, 128], [1, 128]]))
            p = self.psum()
            k.tr(p[:, 0:128], t_.a, self.ident.a)
            k.copy(PT[1]["out"].a, p[:, 0:128], en='act')
            partB(j, PT[1]["out"].a, bonS[:, j, :], gS[:, j, :])
        wo_apply(sbi)
        k.end_defer()
        k.end_phase()


class MK(MKBase, S5Mixin, MambaMixin, RwkvMixin):
    pass


def _fm(a):
    return np.ascontiguousarray(a.T.reshape(a.shape[1] // 128, 128, a.shape[0]))


def host_consts():
    c = {}
    c["c_ident"] = np.eye(128, dtype=np.float32)
    c["c_iota"] = np.ascontiguousarray(np.tile(np.arange(1, 129, dtype=np.float32), (128, 1)))
    f = np.arange(128)
    c["c_maskB"] = (f[:, None] // 16 == np.arange(8)[None, :]).astype(np.float32)
    return c


def host_shared(inp):
    d = {}
    gains = np.zeros((128, 13 * 8), np.float32)
    for typ, nm in enumerate(["norm_mix", "norm_ffn", "norm_ple"]):
        for l in range(4):
            gains[:, (typ * 4 + l) * 8:(typ * 4 + l + 1) * 8] = inp[nm][l].reshape(8, 128).T
    gains[:, 96:104] = inp["final_norm"].reshape(8, 128).T
    d["c_gains"] = gains
    for nm in ["ffn_w1", "ffn_w3", "ffn_w2", "ple_proj", "ple_gate", "l1_glu_v", "l1_glu_g"]:
        d[nm] = inp[nm]
    are, aim, ld = inp["l1_a_re"], inp["l1_a_im"], inp["l1_log_dt"]
    ldb = np.broadcast_to(ld[:, None], (64, 64))
    chan = np.stack([x.reshape(32, 2, 64).transpose(1, 2, 0).reshape(128, 32) for x in (are, aim, ldb)], 1)
    d["s5_chan"] = np.ascontiguousarray(chan)
    feat = np.stack([np.broadcast_to(x.reshape(8, 8, 1, 64), (8, 8, 16, 64)).transpose(1, 2, 0, 3).reshape(128, 512) for x in (are, aim, ldb)], 1)
    d["s5_feat"] = np.ascontiguousarray(feat)
    d["s5_bT"] = np.ascontiguousarray(np.stack([x.reshape(8, 8, 64, 16).transpose(1, 3, 0, 2).reshape(128, 512) for x in (inp["l1_b_re"], inp["l1_b_im"])], 1))
    d["s5_cF"] = np.ascontiguousarray(np.stack([x.reshape(8, 8, 16, 64).transpose(1, 2, 0, 3).reshape(128, 512) for x in (inp["l1_c_re"], inp["l1_c_im"])], 1))
    d["s5_d"] = np.ascontiguousarray(inp["l1_d"].reshape(8, 128).T)
    return d


def host_core(inp, c):
    d = {}
    xs = inp["x_sample"][16 * c:16 * c + 16].reshape(128, 1024)
    d["xT"] = _fm(np.concatenate([inp["x_prompt"][c], xs], 0))
    d["pT"] = np.stack([_fm(np.concatenate([inp["p_prompt"][l, c], inp["p_sample"][l, 16 * c:16 * c + 16].reshape(128, 256)], 0)) for l in range(4)])
    st = [inp["state_l1_s5_re"][16 * c:16 * c + 16], inp["state_l1_s5_im"][16 * c:16 * c + 16]]
    d["s5_state"] = np.ascontiguousarray(np.stack([x.reshape(16, 32, 2, 64).transpose(2, 3, 1, 0).reshape(128, 32, 16) for x in st], 1))
    return d


def unpack_s5(pst, sst):
    p = [pst[:, ri, :].reshape(2, 64, 32).transpose(2, 0, 1).reshape(1, 64, 64) for ri in range(2)]
    s = [sst[:, ri].reshape(2, 64, 32, 16).transpose(3, 2, 0, 1).reshape(16, 64, 64) for ri in range(2)]
    return p, s


def host_consts2(c):
    kk = np.arange(128)
    c["c_U"] = (kk[:, None] <= kk[None, :]).astype(np.float32)
    c["c_negP"] = np.where(kk[None, :] >= kk[:, None], 0.0, -1.0e5).astype(np.float32)
    c["c_m01"] = (kk[None, :] >= kk[:, None]).astype(np.float32)
    return c


def host_shared_mamba(inp, d):
    d["l2_in_proj"] = inp["l2_in_proj"]; d["l2_out_proj"] = inp["l2_out_proj"]
    d["m_convw"] = np.ascontiguousarray(inp["l2_conv_w"].T.reshape(24, 128, 4).transpose(1, 0, 2))
    d["m_convb"] = np.ascontiguousarray(inp["l2_conv_b"].reshape(24, 128).T)
    d["m_dtb"] = np.ascontiguousarray(inp["l2_dt_bias"].reshape(32, 1))
    d["m_alog"] = np.ascontiguousarray(np.broadcast_to(inp["l2_a_log"][None, :], (128, 32)))
    d["m_alogq"] = np.ascontiguousarray(np.tile(inp["l2_a_log"], 4).reshape(128, 1))
    d["m_dcol"] = np.ascontiguousarray(np.repeat(inp["l2_d"], 64).reshape(16, 128).T)
    d["m_normw"] = np.ascontiguousarray(inp["l2_norm_w"].reshape(16, 128).T)
    return d


def host_core_mamba(inp, c, d):
    st = inp["state_l2_conv"][16 * c:16 * c + 16]
    d["m_convst"] = np.ascontiguousarray(st.reshape(16, 3, 24, 128).transpose(3, 2, 0, 1))
    d["m_ssm"] = np.ascontiguousarray(inp["state_l2_ssm"][16 * c:16 * c + 16])
    return d


def unpack_mamba(r):
    pconv = np.asarray(r["m_pconv"]).transpose(2, 1, 0).reshape(1, 3, 3072)
    sconv = np.asarray(r["m_sconv"]).transpose(2, 3, 1, 0).reshape(16, 3, 3072)
    pssm = np.asarray(r["m_pssmT"]).transpose(1, 2, 0).reshape(1, 32, 64, 128)
    sssm = np.asarray(r["m_sssm"]).reshape(16, 32, 64, 128)
    return pconv, sconv, pssm, sssm


def host_consts3(c):
    kk = np.arange(128)
    c["c_bones"] = (kk[:, None] // 64 == kk[None, :] // 64).astype(np.float32)
    r = kk % 64
    c["c_MSU"] = (r[:, None] < r[None, :]).astype(np.float32)
    c["c_MIU"] = (r[:, None] <= r[None, :]).astype(np.float32)
    c["c_MSL"] = (r[:, None] > r[None, :]).astype(np.float32)
    return c


def host_shared_rwkv(inp, d, pre):
    names = ["w0", "a0", "k_k", "k_a", "r_k", "lnx_w", "lnx_b"]
    vs = [inp[pre + nm].reshape(1024) for nm in names]
    vs.append(inp[pre + "v0"].reshape(1024) if (pre + "v0") in inp else inp[pre + "w0"].reshape(1024))
    d[pre + "vec"] = np.ascontiguousarray(np.stack([v.reshape(8, 128).T for v in vs], 1))
    d[pre + "mu"] = np.ascontiguousarray(inp[pre + "mu"].reshape(6, 8, 128).transpose(2, 0, 1))
    for nm in ["w1", "a1", "g1", "w2", "a2", "g2", "w_rkv", "w_o"] + (["v1", "v2"] if (pre + "v1") in inp else []):
        d[pre + nm] = inp[pre + nm]
    return d


def host_core_rwkv(inp, c, d, l):
    pre = "l%d_" % l
    if l == 0:
        sh_all, wkv_all = inp["state_l0_shift"], inp["state_l0_wkv"]
    else:
        sh_all, wkv_all = inp["state_l3_shift"], inp["state_l3_wkv"]
    sh = sh_all[16 * c:16 * c + 16]
    d[pre + "shift"] = np.ascontiguousarray(sh.reshape(16, 8, 128).transpose(2, 1, 0))
    d[pre + "wkv"] = np.ascontiguousarray(wkv_all[16 * c:16 * c + 16])
    return d


def unpack_rwkv(r, pre):
    shp = np.asarray(r[pre + "shift_p"]).T.reshape(1, 1024)
    shs = np.asarray(r[pre + "shift_s"]).transpose(2, 1, 0).reshape(16, 1024)
    wp = np.asarray(r[pre + "wkv_p"]).reshape(2, 64, 8, 64).transpose(2, 0, 1, 3).reshape(1, 16, 64, 64)
    ws = np.asarray(r[pre + "wkv_s"]).reshape(16, 16, 64, 64)
    return shp, shs, wp, ws


def build_program():
    m = MK()
    k = m.k
    m.setup_consts()
    m.setup_state()
    I = {}

    def din(nm, shape):
        I[nm] = m.inp(nm, list(shape))

    for nm, shp in INPUT_SHAPES.items():
        if nm not in ("c_ident", "c_gains"):
            din(nm, shp)
    O = {nm: m.out(nm, list(shp)) for nm, shp in OUTPUT_SHAPES.items()}
    m.load_x(I["xT"])
    W0 = {nm: m.precast("w0%s_bf" % nm, I["l0_w_rkv"], 1024, c * 1024 * 1024, 1024, 8) for c, nm in enumerate(["r", "k", "v"])}
    W0["o"] = m.precast("w0o_bf", I["l0_w_o"], 1024, 0, 1024, 8)
    win_bf = m.precast("win_bf", I["l2_in_proj"], 5152, 0, 5152, 8)
    wout_bf = m.precast("wout_bf", I["l2_out_proj"], 1024, 0, 1024, 16)
    W3 = {nm: m.precast("w3%s_bf" % nm, I["l3_w_rkv"], 1024, c * 1024 * 1024, 1024, 8) for c, nm in enumerate(["r", "k", "v"])}
    W3["o"] = m.precast("w3o_bf", I["l3_w_o"], 1024, 0, 1024, 8)
    vfirst = k.dram_tmp("vfirst", [8, 128, TT])

    def ffn_ple(l):
        xn_all = [k.sbp("xn%d" % i, [128, 8, n], BF16) for i, (s, n) in enumerate(TBS)]
        wbuf = [(k.sbp("w1g%d" % i, [128, 8, 512], BF16), k.sbp("w3g%d" % i, [128, 8, 512], BF16), k.sbp("w2g%d" % i, [128, 4, 1024], BF16)) for i in range(2)]
        m.ffn(l, I["ffn_w1"], I["ffn_w3"], I["ffn_w2"], xn_all, wbuf)
        k.end_phase()
        wg = k.sbp("wg", [128, 8, 1024], BF16); wp = k.sbp("wp", [128, 2, 1024], BF16)
        xn_blk = [k.sbp("xnb%d" % i, [128, 8, BLK], BF16) for i in range(2)]
        p_blk = [k.sbp("pb%d" % i, [128, 2, BLK], BF16) for i in range(2)]
        m.ple(l, I["pT"], I["ple_proj"], I["ple_gate"], wg, wp, xn_blk, p_blk)
        k.end_phase()

    m.rwkv_layer(0, I, O, W0, "l0_", vfirst)
    ffn_ple(0)
    m.s5_layer(1, I, O)
    ffn_ple(1)
    m.mamba_layer(2, I, O, win_bf, wout_bf)
    ffn_ple(2)
    m.rwkv_layer(3, I, O, W3, "l3_", vfirst)
    ffn_ple(3)
    ybuf = [k.sbp("ybuf%d" % i, [128, 8, BLK]) for i in range(2)]
    m.final(O["yT"], ybuf)
    k.end_phase()
    k.finish()
    return m


OUTPUT_SHAPES = {
    "yT": (8, 128, TT),
    "l0_shift_p": (128, 8), "l0_shift_s": (128, 8, 16), "l0_wkv_p": (128, 8, 64), "l0_wkv_s": (16, 16, 64, 64),
    "s5_pstate": (128, 2, 32), "s5_sstate": (128, 2, 32, 16),
    "m_pconv": (128, 24, 3), "m_sconv": (128, 24, 16, 3), "m_pssmT": (128, 32, 64), "m_sssm": (16, 32, 64, 128),
    "l3_shift_p": (128, 8), "l3_shift_s": (128, 8, 16), "l3_wkv_p": (128, 8, 64), "l3_wkv_s": (16, 16, 64, 64),
}
INPUT_SHAPES = {}


def kernel(**inputs):
    inp = {k_: np.asarray(v) for k_, v in inputs.items()}
    consts = host_consts3(host_consts2(host_consts()))
    shared = host_shared(inp)
    host_shared_mamba(inp, shared)
    host_shared_rwkv(inp, shared, "l0_")
    host_shared_rwkv(inp, shared, "l3_")
    in_maps = []
    for c in range(8):
        d = dict(consts)
        d.update(shared)
        pc = host_core(inp, c)
        host_core_mamba(inp, c, pc)
        host_core_rwkv(inp, c, pc, 0)
        host_core_rwkv(inp, c, pc, 3)
        d.update(pc)
        in_maps.append({k_: np.ascontiguousarray(v, dtype=np.float32) for k_, v in d.items()})
    INPUT_SHAPES.clear()
    for k_, v in in_maps[0].items():
        INPUT_SHAPES[k_] = v.shape
    m = build_program()
    res = run_bass_kernel_spmd(m.k.nc, in_maps, core_ids=list(range(8)))
    R = res.results
    yp = np.zeros((8, 2048, 1024), np.float32); ys = np.zeros((128, 8, 1024), np.float32)
    outs = {nm: [] for nm in ["shp0", "shs0", "wp0", "ws0", "s5p_re", "s5p_im", "s5s_re", "s5s_im", "pconv", "sconv", "pssm", "sssm", "shp3", "shs3", "wp3", "ws3"]}
    for c in range(8):
        r = R[c]
        y = np.asarray(r["yT"]).reshape(1024, TT).T
        yp[c] = y[:2048]
        ys[16 * c:16 * c + 16] = y[2048:].reshape(16, 8, 1024)
        a, b, cc, dd = unpack_rwkv(r, "l0_")
        outs["shp0"].append(a); outs["shs0"].append(b); outs["wp0"].append(cc); outs["ws0"].append(dd)
        p, s = unpack_s5(np.asarray(r["s5_pstate"]), np.asarray(r["s5_sstate"]))
        outs["s5p_re"].append(p[0]); outs["s5p_im"].append(p[1]); outs["s5s_re"].append(s[0]); outs["s5s_im"].append(s[1])
        a, b, cc, dd = unpack_mamba(r)
        outs["pconv"].append(a); outs["sconv"].append(b); outs["pssm"].append(cc); outs["sssm"].append(dd)
        a, b, cc, dd = unpack_rwkv(r, "l3_")
        outs["shp3"].append(a); outs["shs3"].append(b); outs["wp3"].append(cc); outs["ws3"].append(dd)
    cat = lambda nm: np.ascontiguousarray(np.concatenate(outs[nm], 0), dtype=np.float32)
    return (yp, ys,
            cat("shp0"), cat("wp0"), cat("s5p_re"), cat("s5p_im"), cat("pconv"), cat("pssm"), cat("shp3"), cat("wp3"),
            cat("shs0"), cat("ws0"), cat("s5s_re"), cat("s5s_im"), cat("sconv"), cat("sssm"), cat("shs3"), cat("ws3"))
```

```python
import numpy as np
import concourse.bass as bass
import concourse.mybir as mybir
from concourse.ap import AP
from concourse.bass_utils import run_bass_kernel_spmd

F32 = mybir.dt.float32
BF16 = mybir.dt.bfloat16
I32 = mybir.dt.int32
ALU = mybir.AluOpType
AF = mybir.ActivationFunctionType
AX = mybir.AxisListType

SEM_ROT = 30000


class Eng:
    def __init__(self, kb, name, eng):
        self.kb = kb
        self.name = name
        self.eng = eng
        self.sem = kb.nc.alloc_semaphore("es_%s_0" % name)
        self.nsem = 1
        self.cnt = 0
        self.seen = {}

    def rotate(self):
        if self.cnt >= SEM_ROT:
            self.sem = self.kb.nc.alloc_semaphore("es_%s_%d" % (self.name, self.nsem))
            self.nsem += 1
            self.cnt = 0


class DSem:
    def __init__(self, kb, name):
        self.kb = kb
        self.name = name
        self.sem = kb.nc.alloc_semaphore("ds_" + name)
        self.n = 0
        self.gen = 0


class Tn:
    def __init__(self, kb, h, name, space):
        self.kb = kb
        self.h = h
        self.name = name
        self.space = space
        self.lw = None
        self.rd = []
        self.ds = None
        self.shape = list(h.shape)

    def __getitem__(self, idx):
        return V(self, self.h[idx])

    def v(self, offset, ap):
        return V(self, AP(self.h, offset, ap))

    @property
    def a(self):
        return V(self, self.h[:])


class SubTn(Tn):
    def __init__(self, parent, col0, ncols, name):
        self.kb = parent.kb
        self.h = parent.h
        self.name = name
        self.space = parent.space
        self.lw = None
        self.rd = []
        self.ds = None
        self.col0 = col0
        self.ncols = ncols
        self.shape = [parent.shape[0], ncols]

    def __getitem__(self, idx):
        r, c = idx
        a = 0 if c.start is None else c.start
        b = self.ncols if c.stop is None else c.stop
        return V(self, self.h[r, self.col0 + a:self.col0 + b])

    @property
    def a(self):
        return self[:, 0:self.ncols]


class V:
    def __init__(self, t, ap):
        self.t = t
        self.ap = ap

    def __getitem__(self, idx):
        return V(self.t, self.ap[idx])

    def re(self, s, **kw):
        return V(self.t, self.ap.rearrange(s, **kw))

    def bc(self, shape):
        return V(self.t, self.ap.broadcast_to(shape))

    def bitcast(self, dt):
        return V(self.t, self.ap.bitcast(dt))

    @property
    def shape(self):
        return self.ap.shape


def _ap(x):
    return x.ap if isinstance(x, V) else x


class KB:
    def __init__(self):
        self.nc = bass.Bass("TRN2", target_bir_lowering=False)
        nc = self.nc
        self.E = {
            'pe': Eng(self, 'pe', nc.tensor),
            'dve': Eng(self, 'dve', nc.vector),
            'act': Eng(self, 'act', nc.scalar),
            'pool': Eng(self, 'pool', nc.gpsimd),
            'sp': Eng(self, 'sp', nc.sync),
        }
        self.dsems = []
        self._frozen = {}
        self._phase = []
        self._dspool = []
        self._dspool_sw = []
        self.ntens = 0
        self.ninst = 0

    def sb(self, name, shape, dtype=F32):
        h = self.nc.alloc_sbuf_tensor(name, list(shape), dtype)
        return Tn(self, h, name, 'sb')

    def sbp(self, name, shape, dtype=F32):
        self._sbn = getattr(self, "_sbn", 0) + 1
        cm = self.nc.sbuf_tensor("%s_t%d" % (name, self._sbn), list(shape), dtype)
        h = cm.__enter__()
        t = Tn(self, h, name, 'sb')
        self.min_free = min(getattr(self, "min_free", 1 << 30), self.nc.sbuf_bytes_remaining)
        self._phase.append((cm, t))
        return t

    def barrier(self):
        ents = []
        for en, EE in self.E.items():
            if EE.cnt > 0:
                ents.append(('e', EE.sem, EE.cnt, None))
        for ds in self.dsems:
            if ds.n > 0:
                ents.append(('d', ds, ds.gen))
        for en, EE in self.E.items():
            self._wait(EE, ents)

    def mark(self):
        return len(self._phase)

    def free_to(self, mark):
        self.barrier()
        while len(self._phase) > mark:
            cm, t = self._phase.pop()
            if t.ds is not None:
                self._dspool.append(t.ds)
                t.ds = None
            if getattr(t, "ds_sw", None) is not None:
                self._dspool_sw.append(t.ds_sw)
                t.ds_sw = None
            cm.__exit__(None, None, None)

    def end_phase(self):
        self.barrier()
        while self._phase:
            cm, t = self._phase.pop()
            if t.ds is not None:
                self._dspool.append(t.ds)
                t.ds = None
            if getattr(t, "ds_sw", None) is not None:
                self._dspool_sw.append(t.ds_sw)
                t.ds_sw = None
            cm.__exit__(None, None, None)

    def ps(self, name, shape, dtype=F32):
        h = self.nc.alloc_psum_tensor(name, list(shape), dtype)
        return Tn(self, h, name, 'ps')

    def dram_in(self, name, shape, dtype=F32):
        h = self.nc.dram_tensor(name, list(shape), dtype, kind="ExternalInput")
        return Tn(self, h, name, 'din')

    def dram_out(self, name, shape, dtype=F32):
        h = self.nc.dram_tensor(name, list(shape), dtype, kind="ExternalOutput")
        return Tn(self, h, name, 'dout')

    def dram_tmp(self, name, shape, dtype=F32):
        h = self.nc.dram_tensor(name, list(shape), dtype, kind="Internal")
        return Tn(self, h, name, 'dtmp')

    def _entry_val(self, ent):
        kind = ent[0]
        if kind == 'e':
            return ent[1], ent[2]
        ds = ent[1]
        if ent[2] == ds.gen:
            return ds.sem, ds.n * 16
        return self._frozen[(id(ds), ent[2])]

    def _wait(self, E, ents):
        need = {}
        for ent in ents:
            if ent is None:
                continue
            if ent[0] == 'e' and ent[3] is E and E.name == 'pe':
                continue
            sem, val = self._entry_val(ent)
            key = id(sem)
            if E.seen.get(key, 0) >= val:
                continue
            if key not in need or need[key][1] < val:
                need[key] = (sem, val)
        for key, (sem, val) in need.items():
            E.eng.wait_ge(sem, val)
            E.seen[key] = val

    def _collect(self, outs, ins):
        ents = []
        for v in ins:
            if isinstance(v, V) and v.t is not None and v.t.space != 'din':
                ents.append(v.t.lw)
        for v in outs:
            if isinstance(v, V) and v.t is not None:
                ents.append(v.t.lw)
                ents.extend(v.t.rd)
        return ents

    def _record(self, ent, outs, ins):
        for v in ins:
            if isinstance(v, V) and v.t is not None and v.t.space != 'din':
                t = v.t
                t.rd = [r for r in t.rd if not (r[0] == ent[0] and r[1] is ent[1])]
                t.rd.append(ent)
        for v in outs:
            if isinstance(v, V) and v.t is not None:
                v.t.lw = ent
                v.t.rd = []

    def begin_defer(self):
        self._defer = []
        self._dtrk = {}

    def _drecord(self, kind, en, payload, outs, ins, est):
        idx = len(self._defer)
        deps = set()
        for v in ins:
            if isinstance(v, V) and v.t is not None and v.t.space != 'din':
                tr = self._dtrk.setdefault(id(v.t), [None, []])
                if tr[0] is not None:
                    deps.add(tr[0])
        for v in outs:
            if isinstance(v, V) and v.t is not None:
                tr = self._dtrk.setdefault(id(v.t), [None, []])
                if tr[0] is not None:
                    deps.add(tr[0])
                deps.update(tr[1])
        for v in ins:
            if isinstance(v, V) and v.t is not None and v.t.space != 'din':
                self._dtrk[id(v.t)][1].append(idx)
        for v in outs:
            if isinstance(v, V) and v.t is not None:
                self._dtrk[id(v.t)] = [idx, []]
        deps.discard(idx)
        self._defer.append((kind, en, payload, outs, ins, est, deps))

    def end_defer(self, sync_lat=0.25):
        import heapq
        ops = self._defer
        self._defer = None
        n = len(ops)
        succ = [[] for _ in range(n)]
        ndep = [0] * n
        for i, o in enumerate(ops):
            ndep[i] = len(o[6])
            for d in o[6]:
                succ[d].append(i)
        ready_t = [0.0] * n
        fin = [0.0] * n
        start = [0.0] * n
        efree = {}
        heap = [(0.0, i) for i in range(n) if ndep[i] == 0]
        heapq.heapify(heap)
        while heap:
            rt, i = heapq.heappop(heap)
            kind, en, payload, outs, ins, est, deps = ops[i]
            st = max(rt, efree.get(en, 0.0))
            start[i] = st
            if kind == 'dma':
                efree[en] = st + 0.05
                fin[i] = st + est
            else:
                efree[en] = st + est
                fin[i] = st + est
            for j in succ[i]:
                ready_t[j] = max(ready_t[j], fin[i] + sync_lat)
                ndep[j] -= 1
                if ndep[j] == 0:
                    heapq.heappush(heap, (ready_t[j], j))
        order = sorted(range(n), key=lambda i: (start[i], i))
        for i in order:
            kind, en, payload, outs, ins, est, deps = ops[i]
            if kind == 'op':
                fn, inc = payload
                self.op(en, fn, outs, ins, inc=True)
            else:
                out, in_, q, owner, kw = payload
                self.dma(out, in_, q=q, owner=owner, **kw)

    def _est(self, en, outs, ins):
        try:
            sh = outs[0].ap.shape
            free = 1
            for d in sh[1:]:
                free *= d
        except Exception:
            free = 128
        if en == 'pe':
            f32 = False
            try:
                f32 = any(isinstance(v, V) and v.t.space == 'sb' and str(v.ap.dtype).endswith('float32') for v in ins[:1])
            except Exception:
                pass
            return (0.07 + free / 2400.0) * (4.0 if f32 else 1.0)
        if en == 'dve':
            return 0.07 + free / 960.0
        if en == 'act':
            return 0.2 + free / 1200.0
        if en == 'pool':
            return 0.5 + free / 400.0
        return 0.1

    def op(self, en, fn, outs, ins, inc=True):
        if getattr(self, "_defer", None) is not None:
            self._drecord('op', en, (fn, inc), outs, ins, self._est(en, outs, ins))
            return None
        E = self.E[en]
        self._wait(E, self._collect(outs, ins))
        inst = fn(E.eng)
        self.ninst += 1
        if inc:
            E.cnt += 1
            inst.then_inc(E.sem, 1)
            ent = ('e', E.sem, E.cnt, E)
            self._record(ent, outs, ins)
            E.rotate()
        else:
            ent = ('e', E.sem, E.cnt + 1, E)
            self._record(ent, outs, ins)
        return inst

    def dma(self, out, in_, q='sp', owner=None, **kw):
        if getattr(self, "_defer", None) is not None:
            self._drecord('dma', q, (out, in_, q, owner, kw), [out], [in_], 2.2)
            return None
        E = self.E[q]
        self._wait(E, self._collect([out], [in_]))
        if owner is None:
            cands = [v.t for v in (out, in_) if v.t.space in ('sb',)]
            if not cands:
                cands = [v.t for v in (out, in_) if v.t.space in ('dtmp',)]
            if not cands:
                cands = [out.t]
            owner = cands[0]
        sw = (q == 'pool')
        attr = 'ds_sw' if sw else 'ds'
        pool = self._dspool_sw if sw else self._dspool
        if getattr(owner, attr, None) is None:
            if pool:
                setattr(owner, attr, pool.pop())
            else:
                nd = DSem(self, "%s%s_%d" % ("sw_" if sw else "", owner.name, len(self.dsems)))
                setattr(owner, attr, nd)
                self.dsems.append(nd)
        ds = getattr(owner, attr)
        inst = E.eng.dma_start(out=out.ap, in_=in_.ap, **kw)
        ds.n += 1
        inst.then_inc(ds.sem, 16)
        self.ninst += 1
        ent = ('d', ds, ds.gen)
        self._record(ent, [out], [in_])
        if ds.n >= 1800:
            old = (ds.sem, ds.n * 16)
            ds.gen += 1
            self._frozen[(id(ds), ds.gen - 1)] = old
            ds.sem = self.nc.alloc_semaphore("ds_%s_%d" % (owner.name, ds.gen))
            ds.n = 0
        return inst

    def finish(self):
        E = self.E['sp']
        for ds in self.dsems:
            if ds.n > 0:
                E.eng.wait_ge(ds.sem, ds.n * 16)
        for (k, g), (sem, val) in self._frozen.items():
            E.eng.wait_ge(sem, val)
        for en, EE in self.E.items():
            if en != 'sp' and EE.cnt > 0:
                E.eng.wait_ge(EE.sem, EE.cnt)

    def tt(self, out, a, b, op, en='dve'):
        return self.op(en, lambda e: e.tensor_tensor(out=out.ap, in0=a.ap, in1=b.ap, op=op), [out], [a, b])

    def ts(self, out, a, s1, op0, s2=None, op1=None, en='dve'):
        def f(e):
            if op1 is None:
                return e.tensor_scalar(out=out.ap, in0=a.ap, scalar1=_ap(s1), scalar2=None, op0=op0)
            return e.tensor_scalar(out=out.ap, in0=a.ap, scalar1=_ap(s1), scalar2=_ap(s2), op0=op0, op1=op1)
        return self.op(en, f, [out], [a, s1, s2])

    def stt(self, out, a, s, b, op0, op1, en='dve'):
        return self.op(en, lambda e: e.scalar_tensor_tensor(out=out.ap, in0=a.ap, scalar=_ap(s), in1=b.ap, op0=op0, op1=op1), [out], [a, s, b])

    def copy(self, out, a, en='dve'):
        if en == 'act':
            return self.op(en, lambda e: e.copy(out=out.ap, in_=a.ap), [out], [a])
        return self.op(en, lambda e: e.tensor_copy(out=out.ap, in_=a.ap), [out], [a])

    def memset(self, out, val, en='dve'):
        return self.op(en, lambda e: e.memset(out.ap, val), [out], [])

    def act(self, out, a, func, bias=None, scale=None, accum=None):
        def f(e):
            kw = {}
            if bias is not None:
                kw['bias'] = _ap(bias)
            if scale is not None:
                kw['scale'] = _ap(scale)
            if accum is not None:
                kw['accum_out'] = accum.ap
            return e.activation(out=out.ap, in_=a.ap, func=func, **kw)
        outs = [out] + ([accum] if accum is not None else [])
        return self.op('act', f, outs, [a, bias, scale])

    def mm(self, out, lhsT, rhs, start=True, stop=True, inc=None):
        if inc is None:
            inc = stop
        return self.op('pe', lambda e: e.matmul(out.ap, lhsT.ap, rhs.ap, start=start, stop=stop), [out], [lhsT, rhs], inc=inc)

    def tr(self, out, a, ident, inc=True):
        return self.op('pe', lambda e: e.transpose(out.ap, a.ap, ident.ap), [out], [a, ident], inc=inc)

    def scan(self, out, d0, d1, init, op0=ALU.mult, op1=ALU.add):
        return self.op('dve', lambda e: e.tensor_tensor_scan(out=out.ap, data0=d0.ap, data1=d1.ap, initial=_ap(init), op0=op0, op1=op1), [out], [d0, d1, init])

    def reduce(self, out, a, op=ALU.add, axis=AX.X):
        return self.op('dve', lambda e: e.tensor_reduce(out=out.ap, in_=a.ap, axis=axis, op=op), [out], [a])

    def recip(self, out, a):
        return self.op('dve', lambda e: e.reciprocal(out=out.ap, in_=a.ap), [out], [a])
D = 1024
KC = 8
TP = 2048
TS = 128
TT = TP + TS
DFF = 2816
NFC = DFF // 128
BLK = 256
TBS = [(i * BLK, BLK) for i in range(TP // BLK)] + [(TP, TS)]
EPS = 1e-6


class MKBase:
    def __init__(self, dbg=None):
        self.k = KB()
        self.dbg = dbg or {}
        self.outs = {}
        self.ins = {}
        k = self.k
        self.ps_banks = [k.ps("psb%d" % i, [128, 512]) for i in range(8)]
        self.ps_i = 0
        self._uid = 0

    def inp(self, name, shape, dtype=F32):
        t = self.k.dram_in(name, shape, dtype)
        self.ins[name] = t
        return t

    def out(self, name, shape):
        t = self.k.dram_out(name, shape)
        self.outs[name] = t
        return t

    def psum(self, lo=0, hi=8):
        n = hi - lo
        if not hasattr(self, "_psc"):
            self._psc = {}
        c = self._psc.get((lo, hi), 0)
        self._psc[(lo, hi)] = c + 1
        return self.ps_banks[lo + (c % n)]

    def uid(self, s):
        self._uid += 1
        return "%s_%d" % (s, self._uid)

    def dump(self, name, view, shape):
        o = self.out("dbg_" + name, shape)
        self.k.dma(o.a, view)

    def setup_consts(self):
        k = self.k
        ident_d = self.inp("c_ident", [128, 128])
        self.ident = k.sb("ident", [128, 128])
        k.dma(self.ident.a, ident_d.a)
        self.ones = k.sb("ones", [128, 128])
        k.memset(self.ones.a, 1.0)
        gd = self.inp("c_gains", [128, 13 * 8])
        self.gains = k.sb("gains", [128, 13 * 8])
        k.dma(self.gains.a, gd.a)

    def gain(self, typ, l):
        idx = (typ * 4 + l) if typ < 3 else 12
        return self.gains[:, idx * 8:(idx + 1) * 8]

    def rmsnorm(self, src, dst, gain, n, tmp_sq, tmp_r):
        k = self.k
        pss = self.psum(4, 8)
        for kk in range(KC):
            k.act(tmp_sq[:, kk, 0:n], src(kk), AF.Square)
        for kk in range(KC):
            k.mm(pss[:, 0:n], self.ones.a, tmp_sq[:, kk, 0:n], start=(kk == 0), stop=(kk == KC - 1))
        k.act(tmp_r[:, 0:n], pss[:, 0:n], AF.Sqrt, bias=self.eps_col[:, 0:1], scale=1.0 / D)
        k.recip(tmp_r[:, 0:n], tmp_r[:, 0:n])
        for kk in range(KC):
            k.stt(dst(kk), src(kk), gain[:, kk:kk + 1], tmp_r[:, 0:n], ALU.mult, ALU.mult)

    def setup_state(self):
        k = self.k
        self.h = [k.sb("h%d" % i, [128, KC, n]) for i, (s, n) in enumerate(TBS)]
        self.eps_col = k.sb("eps_col", [128, 1])
        k.memset(self.eps_col.a, EPS)
        self.tmp_sq = k.sb("tmp_sq", [128, KC, BLK])
        self.tmp_r = k.sb("tmp_r", [128, BLK])

    def load_x(self, xT):
        k = self.k
        for i, (s, n) in enumerate(TBS):
            k.dma(self.h[i].a, xT.v(s, [[TT, 128], [128 * TT, KC], [1, n]]))

    def ffn(self, l, w1, w3, w2, xn_all, wbuf):
        k = self.k
        def norm_blk(i):
            s, n = TBS[i]
            self.rmsnorm(lambda kk, i=i: self.h[i][:, kk, :], lambda kk, i=i, n=n: xn_all[i][:, kk, 0:n],
                         self.gain(1, l), n, self.tmp_sq, self.tmp_r)
        norm_blk(0)
        G = 4
        groups = [(c, min(G, NFC - c)) for c in range(0, NFC, G)]
        a_t = [[k.sbp(self.uid("ffn_a"), [128, BLK], BF16) for _ in range(G)] for _ in range(2)]
        s_t = [k.sbp(self.uid("ffn_s"), [128, BLK]) for _ in range(2)]

        stg = getattr(self, "_ffn_stage", None)

        def load(gi):
            c0, g = groups[gi]
            w1g, w3g, w2g = wbuf[gi % 2]
            base = l * D * DFF
            if stg is None:
                k.dma(w1g[:, :, 0:g * 128], w1.v(base + c0 * 128, [[DFF, 128], [128 * DFF, KC], [1, g * 128]]), q='pool')
                k.dma(w3g[:, :, 0:g * 128], w3.v(base + c0 * 128, [[DFF, 128], [128 * DFF, KC], [1, g * 128]]), q='pool')
                k.dma(w2g[:, 0:g, :], w2.v(l * DFF * D + c0 * 128 * D, [[D, 128], [128 * D, g], [1, D]]), q='pool')
            else:
                s1, s3, s2 = stg
                k.dma(s1[:, :, 0:g * 128], w1.v(base + c0 * 128, [[DFF, 128], [128 * DFF, KC], [1, g * 128]]))
                k.dma(s3[:, :, 0:g * 128], w3.v(base + c0 * 128, [[DFF, 128], [128 * DFF, KC], [1, g * 128]]))
                k.dma(s2[:, 0:g, :], w2.v(l * DFF * D + c0 * 128 * D, [[D, 128], [128 * D, g], [1, D]]))
                k.copy(w1g[:, :, 0:g * 128], s1[:, :, 0:g * 128], en='act')
                k.copy(w3g[:, :, 0:g * 128], s3[:, :, 0:g * 128], en='pool')
                k.copy(w2g[:, 0:g, :], s2[:, 0:g, :], en='act')

        load(0)
        cnt = 0
        bcnt = 0
        for gi, (c0, g) in enumerate(groups):
            if gi + 1 < len(groups):
                load(gi + 1)
            w1g, w3g, w2g = wbuf[gi % 2]
            for i, (s, n) in enumerate(TBS):
                if gi == 0 and i + 1 < len(TBS):
                    norm_blk(i + 1)
                py = self.ps_banks[0:4]
                ats = a_t[bcnt % 2]
                bcnt += 1
                for c in range(g):
                    ph1 = self.psum(4, 8)
                    for kk in range(KC):
                        k.mm(ph1[:, 0:n], w1g[:, kk, c * 128:(c + 1) * 128], xn_all[i][:, kk, 0:n], start=(kk == 0), stop=(kk == KC - 1))
                    ph3 = self.psum(4, 8)
                    for kk in range(KC):
                        k.mm(ph3[:, 0:n], w3g[:, kk, c * 128:(c + 1) * 128], xn_all[i][:, kk, 0:n], start=(kk == 0), stop=(kk == KC - 1))
                    st = s_t[cnt % 2]
                    cnt += 1
                    k.act(st[:, 0:n], ph1[:, 0:n], AF.Silu)
                    k.tt(ats[c][:, 0:n], st[:, 0:n], ph3[:, 0:n], ALU.mult)
                for j in range(KC):
                    for c in range(g):
                        k.mm(py[j // 2][:, (j % 2) * BLK:(j % 2) * BLK + n], w2g[:, c, j * 128:(j + 1) * 128], ats[c][:, 0:n],
                             start=(c == 0), stop=(c == g - 1))
                for jj in range(4):
                    hv = self.h[i][:, 2 * jj:2 * jj + 2, :]
                    pv = py[jj].a.re("p (a b) -> p a b", a=2)[:, :, 0:n]
                    k.tt(hv, hv, pv, ALU.add)

    def ple(self, l, pT, ple_proj, ple_gate, wg, wp, xn_blk, p_blk):
        k = self.k
        k.begin_defer()
        k.dma(wg.a, ple_gate.v(l * D * D, [[D, 128], [128 * D, KC], [1, D]]), q='pool')
        k.dma(wp.a, ple_proj.v(l * 256 * D, [[D, 128], [128 * D, 2], [1, D]]), q='pool')
        sg = [k.sbp(self.uid("ple_sg"), [128, BLK]) for _ in range(2)]
        def prep(i):
            s, n = TBS[i]
            xb = xn_blk[i % 2]
            pb = p_blk[i % 2]
            k.dma(pb[:, :, 0:n], pT.v(l * 256 * TT + s, [[TT, 128], [128 * TT, 2], [1, n]]), q='pool')
            self.rmsnorm(lambda kk, i=i: self.h[i][:, kk, :], lambda kk, xb=xb, n=n: xb[:, kk, 0:n],
                         self.gain(2, l), n, self.tmp_sq, self.tmp_r)
        prep(0)
        for i, (s, n) in enumerate(TBS):
            xb = xn_blk[i % 2]
            pb = p_blk[i % 2]
            for j in range(KC):
                pg = self.psum()
                for kk in range(KC):
                    k.mm(pg[:, 0:n], wg[:, kk, j * 128:(j + 1) * 128], xb[:, kk, 0:n], start=(kk == 0), stop=(kk == KC - 1))
                pp = self.psum()
                for kk in range(2):
                    k.mm(pp[:, 0:n], wp[:, kk, j * 128:(j + 1) * 128], pb[:, kk, 0:n], start=(kk == 0), stop=(kk == 1))
                sgt = sg[j % 2]
                k.act(sgt[:, 0:n], pg[:, 0:n], AF.Sigmoid)
                k.tt(sgt[:, 0:n], sgt[:, 0:n], pp[:, 0:n], ALU.mult)
                k.tt(self.h[i][:, j, :], self.h[i][:, j, :], sgt[:, 0:n], ALU.add)
                if j == 0 and i + 1 < len(TBS):
                    prep(i + 1)
        k.end_defer()

    def final(self, yT, ybuf):
        k = self.k
        k.begin_defer()
        for i, (s, n) in enumerate(TBS):
            yb = ybuf[i % 2]
            self.rmsnorm(lambda kk, i=i: self.h[i][:, kk, :], lambda kk, yb=yb, n=n: yb[:, kk, 0:n],
                         self.gain(3, 0), n, self.tmp_sq, self.tmp_r)
            k.dma(yT.v(s, [[TT, 128], [128 * TT, KC], [1, n]]), yb[:, :, 0:n])
        k.end_defer()


PI = 3.14159265358979
PIC = 3.141592


class S5Mixin:
    def trig(self, ang, sin_out, cos_out, tf, ti, tr):
        k = self.k
        for out, shift in ((sin_out, 0.0), (cos_out, PI / 2)):
            if shift != 0.0:
                k.ts(tr, ang, shift, ALU.add)
                src = tr
            else:
                src = ang
            k.ts(ti, src, 1.0 / (2 * PI), ALU.mult)
            k.copy(tf, ti)
            k.stt(tr, tf, -2 * PI, src, ALU.mult, ALU.add)
            k.ts(tr, tr, PIC, ALU.min, -PIC, ALU.max)
            k.act(out, tr, AF.Sin)

    def s5_params(self, ach, afe, out_q=False):
        raise NotImplementedError

    def s5_layer(self, l, I, O):
        k = self.k
        sb = k.sbp
        ch = sb("s5_ch", [128, 3, 32]); k.dma(ch.a, I["s5_chan"].a)
        dcol = sb("s5_d", [128, 8]); k.dma(dcol.a, I["s5_d"].a)
        maskB = sb("s5_maskB", [128, 8]); k.dma(maskB.a, I["c_maskB"].a)
        hst = sb("s5_hst", [128, 2, 32, 16]); k.dma(hst.a, I["s5_state"].a)
        hpr = sb("s5_hpr", [128, 2, 32]); k.memset(hpr.a, 0.0)
        hso = sb("s5_hso", [128, 2, 32, 16])
        rho = sb("s5_rho", [128, 32]); th = sb("s5_th", [128, 32]); dtc = sb("s5_dtc", [128, 32])
        LC = 128
        cosT = sb("s5_cosT", [128, 32, LC]); sinT = sb("s5_sinT", [128, 32, LC])
        LB = [sb("s5_LBr", [128, 32, 128], BF16), sb("s5_LBi", [128, 32, 128], BF16)]
        LCm = [sb("s5_LCr", [128, 32, 128], BF16), sb("s5_LCi", [128, 32, 128], BF16)]
        mk_ = k.mark()
        fe = sb("s5_fe", [128, 3, 512]); k.dma(fe.a, I["s5_feat"].a)
        bT = sb("s5_bT", [128, 2, 512]); k.dma(bT.a, I["s5_bT"].a)
        cF = sb("s5_cF", [128, 2, 512]); k.dma(cF.a, I["s5_cF"].a)
        iota = sb("s5_iota", [128, 128]); k.dma(iota.a, I["c_iota"].a)
        k.act(dtc.a, ch[:, 2, :], AF.Exp)
        k.tt(th.a, ch[:, 1, :], dtc.a, ALU.mult)
        k.tt(rho.a, ch[:, 0, :], dtc.a, ALU.mult)
        k.act(rho.a, rho.a, AF.Exp)
        TW = 8 * LC
        ang = sb("s5_ang", [128, TW]); tf = sb("s5_tf", [128, TW]); ti = sb("s5_ti", [128, TW], I32)
        tr = sb("s5_tr", [128, TW])
        for q in range(4):
            for c8 in range(8):
                ct = 8 * q + c8
                k.ts(ang[:, c8 * LC:(c8 + 1) * LC], iota.a, th[:, ct:ct + 1], ALU.mult)
            self.trig(ang.a, sinT[:, 8 * q:8 * q + 8, :].re("p a b -> p (a b)"), cosT[:, 8 * q:8 * q + 8, :].re("p a b -> p (a b)"), tf.a, ti.a, tr.a)
        F = 512
        dtf = sb("s5_dtf", [128, F]); thf = sb("s5_thf", [128, F]); magf = sb("s5_magf", [128, F])
        k.act(dtf.a, fe[:, 2, :], AF.Exp)
        k.tt(thf.a, fe[:, 1, :], dtf.a, ALU.mult)
        k.tt(magf.a, fe[:, 0, :], dtf.a, ALU.mult)
        k.act(magf.a, magf.a, AF.Exp)
        sf = sb("s5_sf", [128, F]); cf = sb("s5_cf", [128, F])
        self.trig(thf.a, sf.a, cf.a, tf[:, 0:F], ti[:, 0:F], tr[:, 0:F])
        abr = sb("s5_abr", [128, F]); abi = sb("s5_abi", [128, F])
        k.tt(abr.a, magf.a, cf.a, ALU.mult)
        k.ts(abr.a, abr.a, -1.0, ALU.add)
        k.tt(abi.a, magf.a, sf.a, ALU.mult)
        lr = fe[:, 0, :]; li = fe[:, 1, :]
        den = dtf; t1 = thf; t2 = magf
        k.tt(den.a, lr, lr, ALU.mult); k.tt(t1.a, li, li, ALU.mult); k.tt(den.a, den.a, t1.a, ALU.add)
        k.recip(den.a, den.a)
        qr = sf; qi = cf
        k.tt(t1.a, abr.a, lr, ALU.mult); k.tt(t2.a, abi.a, li, ALU.mult); k.tt(qr.a, t1.a, t2.a, ALU.add); k.tt(qr.a, qr.a, den.a, ALU.mult)
        k.tt(t1.a, abi.a, lr, ALU.mult); k.tt(t2.a, abr.a, li, ALU.mult); k.tt(qi.a, t1.a, t2.a, ALU.subtract); k.tt(qi.a, qi.a, den.a, ALU.mult)
        bbr = abr; bbi = abi
        k.tt(t1.a, qr.a, bT[:, 0, :], ALU.mult); k.tt(t2.a, qi.a, bT[:, 1, :], ALU.mult); k.tt(bbr.a, t1.a, t2.a, ALU.subtract)
        k.tt(t1.a, qr.a, bT[:, 1, :], ALU.mult); k.tt(t2.a, qi.a, bT[:, 0, :], ALU.mult); k.tt(bbi.a, t1.a, t2.a, ALU.add)
        for ri, bb in enumerate((bbr, bbi)):
            for ft in range(8):
                o = LB[ri][:, 4 * ft:4 * ft + 4, :].re("p a (g q) -> p (a g) q", g=2)
                i0 = V(bb, AP(bb.h, ft * 64, [[F, 128], [0, 8], [1, 64]]))
                i1 = V(maskB, AP(maskB.h, 0, [[8, 128], [1, 8], [0, 64]]))
                k.tt(o, i0, i1, ALU.mult)
        xt = [sb("s5_xt%d" % i, [128, 128]) for i in range(2)]
        n_ = 0
        for ri in range(2):
            for ft in range(8):
                for cl in range(4):
                    x = xt[n_ % 2]; n_ += 1
                    i0 = V(cF, AP(cF.h, ri * 512 + ft * 64, [[1024, 128], [0, 2], [1, 64]]))
                    i1 = V(maskB, AP(maskB.h, 2 * cl, [[8, 128], [1, 2], [0, 64]]))
                    k.tt(x.a.re("p (g q) -> p g q", g=2), i0, i1, ALU.mult)
                    pt = self.psum()
                    k.tr(pt[:, 0:128], x.a, self.ident.a)
                    k.act(LCm[ri][:, 4 * ft + cl, :], pt[:, 0:128], AF.Copy, scale=(1.0 if ri == 0 else -1.0))
        k.free_to(mk_)
        x_ = sb("s5_xn", [128, 8, BLK]); xb_ = sb("s5_xnb", [128, 8, BLK], BF16)
        gbf = sb("s5_gbf", [128, 8, BLK], BF16)
        Hre4 = sb("s5_Hre4", [128, 4, BLK], BF16); Him4 = sb("s5_Him4", [128, 4, BLK], BF16)
        W4 = 4 * 128
        ur = sb("s5_ur", [128, W4]); ui = sb("s5_ui", [128, W4]); ta = sb("s5_ta", [128, W4]); tb = sb("s5_tb", [128, W4])
        tc = sb("s5_tc", [128, W4]); td = sb("s5_td", [128, W4])
        hr = sb("s5_hr", [128, W4]); hi = sb("s5_hi", [128, W4]); h2r = sb("s5_h2r", [128, W4]); h2i = sb("s5_h2i", [128, W4])
        sgt = [sb("s5_sg%d" % i, [128, BLK]) for i in range(2)]
        yt = sb("s5_yt", [128, BLK]); gt = sb("s5_gt", [128, BLK])
        wvj = [sb("s5_wv%d" % i, [128, 8, 128], BF16) for i in range(3)]
        wgj = [sb("s5_wg%d" % i, [128, 8, 128], BF16) for i in range(3)]
        wn = 0
        k.begin_defer()
        for i, (s0, n) in enumerate(TBS):
            sample = (s0 >= TP)
            self.rmsnorm(lambda kk, i=i: self.h[i][:, kk, :], lambda kk, n=n: x_[:, kk, 0:n], self.gain(0, l), n, self.tmp_sq, self.tmp_r)
            for kk in range(KC):
                k.copy(xb_[:, kk, 0:n], x_[:, kk, 0:n], en='act')
            for ft in range(8):
                nch = 1 if sample else n // LC
                for c in range(nch):
                    sl = slice(c * LC, (c + 1) * LC)
                    pbr = self.psum(); pbi = self.psum()
                    for cl in range(4):
                        k.mm(pbr[:, cl * 128:(cl + 1) * 128], LB[0][:, 4 * ft + cl, :], xb_[:, ft, sl])
                    for cl in range(4):
                        k.mm(pbi[:, cl * 128:(cl + 1) * 128], LB[1][:, 4 * ft + cl, :], xb_[:, ft, sl])
                    if sample:
                        cs = V(cosT, AP(cosT.h, 4 * ft * LC, [[32 * LC, 128], [LC, 4], [0, 16], [1, 8]]))
                        sn = V(sinT, AP(sinT.h, 4 * ft * LC, [[32 * LC, 128], [LC, 4], [0, 16], [1, 8]]))
                        w3 = lambda v: v.re("p (a b t) -> p a b t", a=4, t=8)
                    else:
                        cs = cosT[:, 4 * ft:4 * ft + 4, :].re("p a b -> p (a b)"); sn = sinT[:, 4 * ft:4 * ft + 4, :].re("p a b -> p (a b)")
                        w3 = lambda v: v
                    br = w3(pbr[:, 0:W4]); bi = w3(pbi[:, 0:W4])
                    k.tt(w3(ta.a), cs, br, ALU.mult); k.tt(w3(tb.a), sn, bi, ALU.mult); k.tt(ur.a, ta.a, tb.a, ALU.add)
                    k.tt(w3(ta.a), cs, bi, ALU.mult); k.tt(w3(tb.a), sn, br, ALU.mult); k.tt(ui.a, ta.a, tb.a, ALU.subtract)
                    for cl in range(4):
                        ct = 4 * ft + cl
                        o_ = cl * 128
                        rb = V(rho, AP(rho.h, ct, [[32, 128], [0, 8 if sample else LC]]))
                        if sample:
                            for b in range(16):
                                k.scan(hr[:, o_ + b * 8:o_ + (b + 1) * 8], rb, ur[:, o_ + b * 8:o_ + (b + 1) * 8], hst[:, 0, ct, b:b + 1])
                                k.scan(hi[:, o_ + b * 8:o_ + (b + 1) * 8], rb, ui[:, o_ + b * 8:o_ + (b + 1) * 8], hst[:, 1, ct, b:b + 1])
                        else:
                            k.scan(hr[:, o_:o_ + 128], rb, ur[:, o_:o_ + 128], hpr[:, 0, ct:ct + 1])
                            k.scan(hi[:, o_:o_ + 128], rb, ui[:, o_:o_ + 128], hpr[:, 1, ct:ct + 1])
                    k.tt(w3(ta.a), cs, w3(hr.a), ALU.mult); k.tt(w3(tb.a), sn, w3(hi.a), ALU.mult); k.tt(h2r.a, ta.a, tb.a, ALU.subtract)
                    k.tt(w3(ta.a), cs, w3(hi.a), ALU.mult); k.tt(w3(tb.a), sn, w3(hr.a), ALU.mult); k.tt(h2i.a, ta.a, tb.a, ALU.add)
                    k.copy(Hre4[:, :, sl], h2r.a.re("p (a b) -> p a b", a=4), en='act')
                    k.copy(Him4[:, :, sl], h2i.a.re("p (a b) -> p a b", a=4), en='act')
                    if sample:
                        k.copy(hso[:, 0, 4 * ft:4 * ft + 4, :], h2r.a.re("p (a b t) -> p a b t", a=4, t=8)[:, :, :, 7], en='act')
                        k.copy(hso[:, 1, 4 * ft:4 * ft + 4, :], h2i.a.re("p (a b t) -> p a b t", a=4, t=8)[:, :, :, 7], en='act')
                    else:
                        k.copy(hpr[:, 0, 4 * ft:4 * ft + 4], h2r.a.re("p (a b) -> p a b", a=4)[:, :, LC - 1], en='act')
                        k.copy(hpr[:, 1, 4 * ft:4 * ft + 4], h2i.a.re("p (a b) -> p a b", a=4)[:, :, LC - 1], en='act')
                py = self.psum()
                for cl in range(4):
                    k.mm(py[:, 0:n], LCm[0][:, 4 * ft + cl, :], Hre4[:, cl, 0:n], start=(cl == 0), stop=False, inc=False)
                    k.mm(py[:, 0:n], LCm[1][:, 4 * ft + cl, :], Him4[:, cl, 0:n], start=False, stop=(cl == 3), inc=True)
                k.stt(yt[:, 0:n], x_[:, ft, 0:n], dcol[:, ft:ft + 1], py[:, 0:n], ALU.mult, ALU.add)
                k.tt(gt[:, 0:n], yt[:, 0:n], yt[:, 0:n], ALU.mult)
                k.ts(gt[:, 0:n], gt[:, 0:n], 0.044715, ALU.mult, 1.0, ALU.add)
                k.tt(gt[:, 0:n], gt[:, 0:n], yt[:, 0:n], ALU.mult)
                k.act(gt[:, 0:n], gt[:, 0:n], AF.Sigmoid, scale=2.0 * 0.7978845608028654)
                k.tt(gbf[:, ft, 0:n], gt[:, 0:n], yt[:, 0:n], ALU.mult)
            for j in range(KC):
                wv_ = wvj[wn % 3]; wg_ = wgj[wn % 3]; wn += 1
                k.dma(wv_.a, I["l1_glu_v"].v(j * 128, [[D, 128], [128 * D, KC], [1, 128]]), q='pool')
                k.dma(wg_.a, I["l1_glu_g"].v(j * 128, [[D, 128], [128 * D, KC], [1, 128]]), q='pool')
                pv = self.psum(); pg = self.psum()
                for kk in range(KC):
                    k.mm(pv[:, 0:n], wv_[:, kk, :], gbf[:, kk, 0:n], start=(kk == 0), stop=(kk == KC - 1))
                for kk in range(KC):
                    k.mm(pg[:, 0:n], wg_[:, kk, :], gbf[:, kk, 0:n], start=(kk == 0), stop=(kk == KC - 1))
                sg_ = sgt[j % 2]
                k.act(sg_[:, 0:n], pg[:, 0:n], AF.Sigmoid)
                k.tt(sg_[:, 0:n], sg_[:, 0:n], pv[:, 0:n], ALU.mult)
                k.tt(self.h[i][:, j, :], self.h[i][:, j, :], sg_[:, 0:n], ALU.add)
        k.dma(O["s5_pstate"].a, hpr.a)
        k.dma(O["s5_sstate"].a, hso.a)
        k.end_defer()
        k.end_phase()


M_IN = 2048
M_CD = 3072
NEG = -1.0e5


class MambaMixin:
    def precast(self, name, src, row_len, col0, ncols, nk, tile_cols=128):
        k = self.k
        nt = (ncols + tile_cols - 1) // tile_cols
        dst = k.dram_tmp(name, [nt, 128, nk, tile_cols], BF16)
        for m in range(nt):
            w = min(tile_cols, ncols - m * tile_cols)
            k.dma(dst.v(m * 128 * nk * tile_cols, [[nk * tile_cols, 128], [tile_cols, nk], [1, w]]),
                  src.v(col0 + m * tile_cols, [[row_len, 128], [128 * row_len, nk], [1, w]]), q='pool')
        return dst

    def load_tile(self, dst_sb, scratch, m, nk, tile_cols=128, w=None, q='sp'):
        w = w or tile_cols
        self.k.dma(dst_sb[:, :, 0:w], scratch.v(m * 128 * nk * tile_cols, [[nk * tile_cols, 128], [tile_cols, nk], [1, w]]), q=q)

    def mamba_layer(self, l, I, O, win_bf, wout_bf):
        k = self.k
        sb = k.sbp
        cw = sb("m_cw", [128, 24, 4]); k.dma(cw.a, I["m_convw"].a)
        cb = sb("m_cb", [128, 24]); k.dma(cb.a, I["m_convb"].a)
        dtb = sb("m_dtb", [32, 1]); k.dma(dtb.a, I["m_dtb"].a)
        aneg = sb("m_aneg", [128, 32]); k.dma(aneg.a, I["m_alog"].a)
        k.act(aneg.a, aneg.a, AF.Exp); k.ts(aneg.a, aneg.a, -1.0, ALU.mult)
        dcol = sb("m_dcol", [128, 16]); k.dma(dcol.a, I["m_dcol"].a)
        nw = sb("m_nw", [128, 16]); k.dma(nw.a, I["m_normw"].a)
        U = sb("m_U", [128, 128]); k.dma(U.a, I["c_U"].a)
        negm = sb("m_negm", [128, 128]); k.dma(negm.a, I["c_negP"].a)
        carry = sb("m_carry", [128, 24, 3]); k.memset(carry.a, 0.0)
        eps512 = self.eps_col
        wt = [sb("m_wt%d" % i, [128, 8, 128], BF16) for i in range(4)]
        wo = [sb("m_wo%d" % i, [128, 16, 128], BF16) for i in range(2)]
        wcnt = [0, 0]

        def front(i, n, xn, zs, xc, dtf, sample, convst=None, sconv=None):
            self.rmsnorm(lambda kk, i=i: self.h[i][:, kk, :], lambda kk: xn[:, kk, 0:n], self.gain(0, l), n, self.tmp_sq, self.tmp_r)
            for m in range(41):
                w_ = wt[wcnt[0] % 4]; wcnt[0] += 1
                wd = 128 if m < 40 else 32
                self.load_tile(w_, win_bf, m, 8, w=wd)
                pp = self.psum()
                for kk in range(KC):
                    k.mm(pp[0:wd, 0:n], w_[:, kk, 0:wd], xn[:, kk, 0:n], start=(kk == 0), stop=(kk == KC - 1))
                if m < 16:
                    k.act(zs[:, m, 0:n], pp[:, 0:n], AF.Silu)
                elif m < 40:
                    mc = m - 16
                    if sample:
                        xr3 = xr_s.a
                        k.copy(xr3[:, :, 0:3], convst[:, mc, :, :], en='act')
                        k.copy(xr3[:, :, 3:11], pp[:, 0:n].re("p (b t) -> p b t", t=8), en='act')
                        acc = xc[:, mc, 0:n].re("p (b t) -> p b t", t=8)
                        k.ts(acc, xr3[:, :, 3:11], cw[:, mc, 3:4], ALU.mult, cb[:, mc:mc + 1], ALU.add)
                        for j in range(3):
                            k.stt(acc, xr3[:, :, j:j + 8], cw[:, mc, j:j + 1], acc, ALU.mult, ALU.add)
                        k.copy(sconv[:, mc, :, :], xr3[:, :, 8:11], en='act')
                    else:
                        xr_p = xr_pl[mc % 2]
                        k.copy(xr_p[:, 0:3], carry[:, mc, :], en='act')
                        k.copy(xr_p[:, 3:3 + n], pp[:, 0:n], en='act')
                        acc = xc[:, mc, 0:n]
                        k.ts(acc, xr_p[:, 3:3 + n], cw[:, mc, 3:4], ALU.mult, cb[:, mc:mc + 1], ALU.add)
                        for j in range(3):
                            k.stt(acc, xr_p[:, j:j + n], cw[:, mc, j:j + 1], acc, ALU.mult, ALU.add)
                        k.copy(carry[:, mc, :], xr_p[:, n:n + 3], en='act')
                    k.act(xc[:, mc, 0:n], xc[:, mc, 0:n], AF.Silu)
                else:
                    k.act(dtf[0:32, 0:n], pp[0:32, 0:n], AF.Exp, bias=dtb[:, 0:1])
                    k.ts(dtf[0:32, 0:n], dtf[0:32, 0:n], 1.0, ALU.add)
                    k.act(dtf[0:32, 0:n], dtf[0:32, 0:n], AF.Ln)

        def to_tm(xc, dtf, c0, Xdt, dt_tm, A_tm):
            pt = self.psum()
            k.tr(pt[:, 0:32], dtf[0:32, c0:c0 + 128], self.ident[0:32, 0:32])
            k.copy(dt_tm.a, pt[:, 0:32], en='act')
            k.tt(A_tm.a, dt_tm.a, aneg.a, ALU.mult)
            for kt in range(16):
                pt = self.psum()
                k.tr(pt[:, 0:128], xc[:, kt, c0:c0 + 128], self.ident.a)
                k.tt(Xdt[:, kt * 128:(kt + 1) * 128].re("p (a b) -> p a b", a=2), pt[:, 0:128].re("p (a b) -> p a b", a=2),
                     V(dt_tm, AP(dt_tm.h, 2 * kt, [[32, 128], [1, 2], [0, 64]])), ALU.mult)

        def back_fm(y_tm, xc, y_fm, c0):
            for kt in range(16):
                pt = self.psum()
                k.tr(pt[:, 0:128], y_tm[:, kt * 128:(kt + 1) * 128], self.ident.a)
                k.stt(y_fm[:, kt, c0:c0 + 128], xc[:, kt, c0:c0 + 128], dcol[:, kt:kt + 1], pt[:, 0:128], ALU.mult, ALU.add)

        def post(i, n, y_fm, zs, yn):
            for kt in range(16):
                k.tt(y_fm[:, kt, 0:n], y_fm[:, kt, 0:n], zs[:, kt, 0:n], ALU.mult)
            for gq in range(4):
                pss = self.psum()
                for a in range(4):
                    k.act(self.tmp_sq[:, a, 0:n], y_fm[:, 4 * gq + a, 0:n], AF.Square)
                for a in range(4):
                    k.mm(pss[:, 0:n], self.ones.a, self.tmp_sq[:, a, 0:n], start=(a == 0), stop=(a == 3))
                k.act(self.tmp_r[:, 0:n], pss[:, 0:n], AF.Sqrt, bias=self.eps_col[:, 0:1], scale=1.0 / 512)
                k.recip(self.tmp_r[:, 0:n], self.tmp_r[:, 0:n])
                for a in range(4):
                    kt = 4 * gq + a
                    k.stt(yn[:, kt, 0:n], y_fm[:, kt, 0:n], nw[:, kt:kt + 1], self.tmp_r[:, 0:n], ALU.mult, ALU.mult)
            for j in range(KC):
                w_ = wo[wcnt[1] % 2]; wcnt[1] += 1
                self.load_tile(w_, wout_bf, j, 16)
                pj = self.psum()
                for kt in range(16):
                    k.mm(pj[:, 0:n], w_[:, kt, :], yn[:, kt, 0:n], start=(kt == 0), stop=(kt == 15))
                k.tt(self.h[i][:, j, :], self.h[i][:, j, :], pj[:, 0:n], ALU.add)

        mk_ = k.mark()
        xn = sb("m_xn", [128, 8, BLK], BF16); zs = sb("m_zs", [128, 16, BLK], BF16)
        xc = sb("m_xc", [128, 24, BLK]); dtf = sb("m_dtf", [32, BLK]); xr_pl = [sb("m_xrp%d" % i, [128, 3 + BLK]) for i in range(2)]
        y_fm = sb("m_yfm", [128, 16, BLK]); yn = sb("m_yn", [128, 16, BLK], BF16)
        Xdt = sb("m_Xdt", [128, 2048], BF16); y_tm = sb("m_ytm", [128, 2048])
        dt_tm = sb("m_dttm", [128, 32]); A_tm = sb("m_Atm", [128, 32]); decT = sb("m_decT", [128, 32]); E_tm = sb("m_Etm", [128, 32])
        B_tm = sb("m_Btm", [128, 4, 128], BF16); CBt = sb("m_CBt", [128, 4, 128])
        Cbf = sb("m_Cbf", [128, 4, BLK], BF16); STb = sb("m_STb", [128, 32, 64], BF16); k.memset(STb.a, 0.0)
        ST = sb("m_ST", [128, 32, 64]); k.memset(ST.a, 0.0)
        ND = 4
        rhs4 = [sb("m_r4%d" % i, [128, 512]) for i in range(3)]; tD = [sb("m_tD%d" % i, [128, 128]) for i in range(ND)]
        posAc = sb("m_posAc", [128, 32]); m01 = sb("m_m01", [128, 128]); k.dma(m01.a, I["c_m01"].a)
        Lt = [sb("m_Lt%d" % i, [128, 128]) for i in range(ND)]; Gt = [sb("m_Gt%d" % i, [128, 128], BF16) for i in range(ND)]
        yo = [sb("m_yo%d" % i, [128, 64]) for i in range(ND)]; Xde = [sb("m_Xde%d" % i, [128, 64], BF16) for i in range(ND)]
        k.begin_defer()
        for i, (s0, n) in enumerate(TBS):
            if s0 >= TP:
                continue
            front(i, n, xn, zs, xc, dtf, False)
            for g in range(4):
                k.copy(Cbf[:, g, 0:n], xc[:, 20 + g, 0:n], en='act')
            for c in range(n // 128):
                c0 = c * 128
                first = (s0 + c0 == 0)
                to_tm(xc, dtf, c0, Xdt, dt_tm, A_tm)
                for g in range(4):
                    pt = self.psum()
                    k.tr(pt[:, 0:128], xc[:, 16 + g, c0:c0 + 128], self.ident.a)
                    k.copy(B_tm[:, g, :], pt[:, 0:128], en='act')
                    pc = self.psum()
                    k.mm(pc[:, 0:128], xc[:, 16 + g, c0:c0 + 128], xc[:, 20 + g, c0:c0 + 128])
                    k.tt(CBt[:, g, :], pc[:, 0:128], m01.a, ALU.mult)
                pa = self.psum()
                k.mm(pa[:, 0:32], self.ones.a, A_tm.a)
                k.act(decT.a, pa[:, 0:32], AF.Exp)
                pa = self.psum()
                k.mm(pa[:, 0:32], U.a, A_tm.a)
                k.act(E_tm.a, pa[:, 0:32], AF.Exp)
                k.copy(posAc.a, pa[:, 0:32], en='act')
                pD4 = {}

                def stageA1(hq):
                    r4 = rhs4[hq % 3]
                    k.tt(r4.a.re("p (a b) -> p a b", a=4), V(U, AP(U.h, 0, [[128, 128], [0, 4], [1, 128]])),
                         V(A_tm, AP(A_tm.h, 4 * hq, [[32, 128], [1, 4], [0, 128]])), ALU.mult)
                    pD4[hq] = self.psum(0, 3)
                    k.mm(pD4[hq][:, 0:512], self.ones.a, r4.a)

                def stageA2a(h):
                    td = tD[h % ND]; lt = Lt[h % ND]
                    hh = h % 4
                    k.act(td.a, pD4[h // 4][:, hh * 128:(hh + 1) * 128], AF.Relu, bias=posAc[:, h:h + 1], scale=-1.0)
                    k.act(lt.a, td.a, AF.Exp, scale=-1.0)

                def stageA2b(h):
                    g = h // 8
                    lt = Lt[h % ND]; gt = Gt[h % ND]; xde = Xde[h % ND]
                    k.tt(gt.a, lt.a, CBt[:, g, :], ALU.mult)
                    k.act(xde.a, Xdt[:, h * 64:(h + 1) * 64], AF.Copy, scale=lt[:, 127:128])

                pBd = {}

                def stageBpe(h):
                    g = h // 8
                    gt = Gt[h % ND]; xde = Xde[h % ND]
                    pB = self.psum(3, 8)
                    pBd[h] = pB
                    k.mm(pB[:, 0:64], gt.a, Xdt[:, h * 64:(h + 1) * 64])
                    if not first:
                        k.mm(pB[:, 64:128], Cbf[:, g, c0:c0 + 128], STb[:, h, :])
                    k.mm(pB[:, 128:192], B_tm[:, g, :], xde.a)

                def stageBev(h):
                    yo_ = yo[h % ND]
                    pB = pBd.pop(h)
                    if not first:
                        k.ts(yo_.a, pB[:, 64:128], E_tm[:, h:h + 1], ALU.mult)
                        k.tt(y_tm[:, h * 64:(h + 1) * 64], yo_.a, pB[:, 0:64], ALU.add)
                    else:
                        k.copy(y_tm[:, h * 64:(h + 1) * 64], pB[:, 0:64], en='dve')
                    k.stt(ST[:, h, :], ST[:, h, :], decT[:, h:h + 1], pB[:, 128:192], ALU.mult, ALU.add)
                    k.copy(STb[:, h, :], ST[:, h, :], en='act')

                stageA1(0)
                stageA1(1)
                stageA2a(0)
                for h in range(32):
                    if h % 4 == 0 and h // 4 + 2 < 8:
                        stageA1(h // 4 + 2)
                    if h + 1 < 32:
                        stageA2a(h + 1)
                    if h >= 1:
                        stageBpe(h - 1)
                    stageA2b(h)
                    if h >= 1:
                        stageBev(h - 1)
                stageBpe(31)
                stageBev(31)
                back_fm(y_tm, xc, y_fm, c0)
            post(i, n, y_fm, zs, yn)
        k.dma(O["m_pconv"].a, carry.a)
        k.dma(O["m_pssmT"].a, ST.a)
        k.end_defer()
        k.free_to(mk_)
        i = len(TBS) - 1
        n = TS
        xn = sb("ms_xn", [128, 8, TS], BF16); zs = sb("ms_zs", [128, 16, TS], BF16)
        xc = sb("ms_xc", [128, 24, TS]); dtf = sb("ms_dtf", [32, TS]); xr_s = sb("ms_xrs", [128, 16, 11])
        Xdt = sb("ms_Xdt", [128, 2048]); y_tm = sb("ms_ytm", [128, 2048])
        dt_tm = sb("ms_dttm", [128, 32]); A_tm = sb("ms_Atm", [128, 32])
        BC_tm = sb("ms_BCtm", [128, 8, 128])
        convst = sb("ms_convst", [128, 24, 16, 3]); k.dma(convst.a, I["m_convst"].a)
        sconv = sb("ms_sconv", [128, 24, 16, 3])
        alq = sb("ms_alq", [128, 1]); k.dma(alq.a, I["m_alogq"].a)
        k.act(alq.a, alq.a, AF.Exp); k.ts(alq.a, alq.a, -1.0, ALU.mult)
        k.begin_defer()
        front(i, n, xn, zs, xc, dtf, True, convst, sconv)
        k.dma(O["m_sconv"].a, sconv.a)
        to_tm(xc, dtf, 0, Xdt, dt_tm, A_tm)
        for g8 in range(8):
            pt = self.psum()
            k.tr(pt[:, 0:128], xc[:, 16 + g8, 0:128], self.ident.a)
            k.copy(BC_tm[:, g8, :], pt[:, 0:128], en='act')
        sx = k.dram_tmp("ms_sx", [128, 2048]); sbc = k.dram_tmp("ms_sbc", [128, 1024]); sdt = k.dram_tmp("ms_sdt", [128, 32])
        sy = k.dram_tmp("ms_sy", [128, 2048])
        k.dma(sx.a, Xdt.a); k.dma(sbc.a, BC_tm.a.re("p a b -> p (a b)")); k.dma(sdt.a, dt_tm.a)
        mk_r = k.mark()
        Xq = sb("ms_Xq", [128, 8, 64]); Bq = sb("ms_Bq", [128, 8, 128]); Cq = sb("ms_Cq", [128, 8, 128])
        dtq = sb("ms_dtq", [128, 8, 1]); dAq = sb("ms_dAq", [128, 8]); yq = sb("ms_yq", [128, 8, 64])
        S = sb("ms_S", [128, 32, 128]); tmp = sb("ms_tmp", [128, 32, 128])
        ssm_in = I["m_ssm"]; ssm_out = O["m_sssm"]
        for r in range(4):
            for b4 in range(4):
                tok0 = (4 * r + b4) * 8
                k.dma(Xq[b4 * 32:(b4 + 1) * 32, :, :], sx.v(tok0 * 2048, [[64, 32], [2048, 8], [1, 64]]))
                k.dma(dtq[b4 * 32:(b4 + 1) * 32, :, :], sdt.v(tok0 * 32, [[1, 32], [32, 8], [1, 1]]), allow_slow_non_contiguous=True)
                for g in range(4):
                    p0 = b4 * 32 + g * 8
                    k.dma(Bq[p0:p0 + 8, :, :], sbc.v(tok0 * 1024 + g * 128, [[0, 8], [1024, 8], [1, 128]]))
                    k.dma(Cq[p0:p0 + 8, :, :], sbc.v(tok0 * 1024 + 512 + g * 128, [[0, 8], [1024, 8], [1, 128]]))
            k.act(dAq.a, dtq.a.re("p a b -> p (a b)"), AF.Exp, scale=alq[:, 0:1])
            for ph in range(2):
                k.dma(S.a.re("p a b -> p (a b)"), ssm_in.v(4 * r * 32 * 8192 + ph * 4096, [[8192, 128], [1, 4096]]))
                for t in range(8):
                    xin = V(Xq, AP(Xq.h, t * 64 + ph * 32, [[512, 128], [1, 32], [0, 128]]))
                    bin_ = V(Bq, AP(Bq.h, t * 128, [[1024, 128], [0, 32], [1, 128]]))
                    cin = V(Cq, AP(Cq.h, t * 128, [[1024, 128], [0, 32], [1, 128]]))
                    k.tt(tmp.a, xin, bin_, ALU.mult)
                    k.stt(S.a, S.a, dAq[:, t:t + 1], tmp.a, ALU.mult, ALU.add)
                    k.tt(tmp.a, S.a, cin, ALU.mult)
                    k.reduce(yq[:, t, ph * 32:(ph + 1) * 32], tmp.a)
                k.dma(ssm_out.v(4 * r * 32 * 8192 + ph * 4096, [[8192, 128], [1, 4096]]), S.a.re("p a b -> p (a b)"))
            for b4 in range(4):
                tok0 = (4 * r + b4) * 8
                k.dma(sy.v(tok0 * 2048, [[64, 32], [2048, 8], [1, 64]]), yq[b4 * 32:(b4 + 1) * 32, :, :])
        k.end_defer()
        k.free_to(mk_r)
        y_fm = sb("ms_yfm", [128, 16, TS]); yn = sb("ms_yn", [128, 16, TS], BF16)
        k.begin_defer()
        k.dma(y_tm.a, sy.a)
        back_fm(y_tm, xc, y_fm, 0)
        post(i, n, y_fm, zs, yn)
        k.end_defer()
        k.end_phase()


SB = 128
CH = 64
NP = 4


class RwkvMixin:
    def rwkv_layer(self, l, I, O, W, pre, vfirst):
        k = self.k
        sb = k.sbp
        has_v = (l == 3)
        n = SB
        vec = sb("rw_vec", [128, 8, 8]); k.dma(vec.a, I[pre + "vec"].a)
        mu = sb("rw_mu", [128, 6, 8]); k.dma(mu.a, I[pre + "mu"].a)
        w1 = sb("rw_w1", [128, 8, 64], BF16); k.dma(w1.a, I[pre + "w1"].v(0, [[64, 128], [128 * 64, 8], [1, 64]]), q='pool')
        a1 = sb("rw_a1", [128, 8, 64], BF16); k.dma(a1.a, I[pre + "a1"].v(0, [[64, 128], [128 * 64, 8], [1, 64]]), q='pool')
        g1 = sb("rw_g1", [128, 8, 160], BF16); k.dma(g1.a, I[pre + "g1"].v(0, [[160, 128], [128 * 160, 8], [1, 160]]), q='pool')
        w2 = sb("rw_w2", [64, 1024], BF16); k.dma(w2.a, I[pre + "w2"].a, q='pool')
        a2 = sb("rw_a2", [64, 1024], BF16); k.dma(a2.a, I[pre + "a2"].a, q='pool')
        g2a = sb("rw_g2a", [128, 1024], BF16); k.dma(g2a.a, I[pre + "g2"].v(0, [[1024, 128], [1, 1024]]), q='pool')
        g2b = sb("rw_g2b", [32, 1024], BF16); k.dma(g2b.a, I[pre + "g2"].v(128 * 1024, [[1024, 32], [1, 1024]]), q='pool')
        if has_v:
            v1 = sb("rw_v1", [128, 8, 32], BF16); k.dma(v1.a, I[pre + "v1"].v(0, [[32, 128], [128 * 32, 8], [1, 32]]), q='pool')
            v2 = sb("rw_v2", [32, 1024], BF16); k.dma(v2.a, I[pre + "v2"].a, q='pool')
        bones = sb("rw_bones", [128, 128]); k.dma(bones.a, I["c_bones"].a)
        gneps = sb("rw_gneps", [128, 1]); k.memset(gneps.a, 64e-5)
        VW0, VA0, VKK, VKA, VRK, VLW, VLB, VV0 = range(8)
        vech = sb("rw_vech", [128, 8, 8]); k.ts(vech.a, vec.a, 0.5, ALU.mult)

        def colh(t, j):
            return vech[:, t, j:j + 1]

        def col(t, j):
            return vec[:, t, j:j + 1]

        xx = sb("rw_xx", [128, 8, SB])
        xm = [sb("rw_xm%d" % c, [128, 8, SB], BF16) for c in range(6)]
        tw = sb("rw_tw", [64, SB], BF16); ta = sb("rw_ta", [64, SB], BF16)
        tgf = sb("rw_tgf", [128, SB]); tga = sb("rw_tga", [128, SB], BF16); tgb = sb("rw_tgb", [32, SB], BF16)
        tv = sb("rw_tv", [32, SB], BF16)
        att = [sb("rw_att%d" % j, [128, SB], BF16) for j in range(8)]
        wtl = [[sb("rw_wt%d_%d" % (i, c), [128, 8, 128], BF16) for c in range(3)] for i in range(2)]

        wc = [0, 0]
        names = ["r", "lw", "k", "v", "an", "bn", "g", "bonus", "out", "t0", "t1", "t2", "cum", "P", "Pi", "Pe"]
        PT = [{nm: sb("rw_%s_%d" % (nm, q), [128, SB]) for nm in names} for q in range(2)]
        mk_p = k.mark()
        PT += [{nm: sb("rw_%s_%d" % (nm, q), [128, SB]) for nm in names} for q in range(2, NP)]
        MSU = sb("rw_MSU", [128, 128]); k.dma(MSU.a, I["c_MSU"].a)
        MIU = sb("rw_MIU", [128, 128]); k.dma(MIU.a, I["c_MIU"].a)
        MSL = sb("rw_MSL", [128, 128]); k.dma(MSL.a, I["c_MSL"].a)
        XNp = sb("rw_XNp", [128, 8, SB + 1]); k.memset(XNp.a, 0.0)
        ssp = sb("rw_ssp", [128, 8])
        woR = sb("rw_woR", [128, 8, 8, 128], BF16)
        for jo in range(8):
            self.load_tile(woR[:, jo, :, :], W["o"], jo, 8)
        bdn = ["b_bd", "k_bd", "v_bd", "Mrb", "Mak", "Mrk", "VT", "bT", "kT", "S0T", "ts"]
        BD = [{nm: sb("rw_%s_%d" % (nm, q), [128, 128], (F32 if nm == "ts" else BF16)) for nm in bdn} for q in range(NP)]
        AR = [sb("rw_AR_%d" % q, [128, 256], BF16) for q in range(NP)]
        NE = [[sb("rw_NE%d_%d" % (q, i), [128, 384], BF16) for i in range(2)] for q in range(NP)]
        identb = sb("rw_identb", [128, 128], BF16); k.copy(identb.a, self.ident.a)
        for q in range(NP):
            for nm in ["b_bd", "k_bd", "v_bd", "S0T"]:
                k.memset(BD[q][nm].a, 0.0)
            k.memset(AR[q].a, 0.0)
        SALL = [sb("rw_SALL%d" % j, [128, 128]) for j in range(8)]
        for j in range(8):
            k.memset(SALL[j].a, 0.0)

        def lo(q):
            return slice(0, 64) if q == 0 else slice(64, 128)

        def front(sbi):
            sample = (sbi == 16)
            if not sample:
                blk, half = sbi // 2, sbi % 2
                hv = lambda kk: self.h[blk][:, kk, half * SB:(half + 1) * SB]
                self.rmsnorm(hv, lambda kk: XNp[:, kk, 1:SB + 1], self.gain(0, l), n, self.tmp_sq, self.tmp_r)
                X = lambda kk: XNp[:, kk, 1:SB + 1]
                for kk in range(8):
                    k.tt(xx[:, kk, :], XNp[:, kk, 0:SB], XNp[:, kk, 1:SB + 1], ALU.subtract)
            else:
                hv = lambda kk: self.h[8][:, kk, :]
                self.rmsnorm(hv, lambda kk: xnc[:, kk, :], self.gain(0, l), n, self.tmp_sq, self.tmp_r)
                k.dma(sst.a, I[pre + "shift"].a)
                for kk in range(8):
                    k.copy(XNs[:, kk, :, 0], sst[:, kk, :], en='act')
                    k.copy(XNs[:, kk, :, 1:9], xnc[:, kk, :].re("p (b t) -> p b t", t=8), en='act')
                    k.tt(xx[:, kk, :].re("p (b t) -> p b t", t=8), XNs[:, kk, :, 0:8], XNs[:, kk, :, 1:9], ALU.subtract)
                X = lambda kk: xnc[:, kk, :]
            for c in range(6):
                for kk in range(8):
                    k.stt(xm[c][:, kk, :], xx[:, kk, :], mu[:, c, kk:kk + 1], X(kk), ALU.mult, ALU.add)
            p = self.psum()
            for kk in range(8):
                k.mm(p[0:64, 0:n], w1[:, kk, :], xm[1][:, kk, :], start=(kk == 0), stop=(kk == 7))
            k.act(tw.a, p[0:64, 0:n], AF.Tanh)
            p = self.psum()
            for kk in range(8):
                k.mm(p[0:64, 0:n], a1[:, kk, :], xm[4][:, kk, :], start=(kk == 0), stop=(kk == 7))
            k.copy(ta.a, p[0:64, 0:n], en='act')
            p = self.psum()
            for kk in range(8):
                k.mm(p[:, 0:n], g1[:, kk, 0:128], xm[5][:, kk, :], start=(kk == 0), stop=(kk == 7))
            k.act(tgf.a, p[:, 0:n], AF.Tanh, scale=0.5)
            k.ts(tga.a, tgf.a, 0.5, ALU.mult, 0.5, ALU.add)
            p = self.psum()
            for kk in range(8):
                k.mm(p[0:32, 0:n], g1[:, kk, 128:160], xm[5][:, kk, :], start=(kk == 0), stop=(kk == 7))
            k.act(tgf[0:32, :], p[0:32, 0:n], AF.Tanh, scale=0.5)
            k.ts(tgb.a, tgf[0:32, :], 0.5, ALU.mult, 0.5, ALU.add)
            if has_v:
                p = self.psum()
                for kk in range(8):
                    k.mm(p[0:32, 0:n], v1[:, kk, :], xm[3][:, kk, :], start=(kk == 0), stop=(kk == 7))
                k.copy(tv.a, p[0:32, 0:n], en='act')
            if not sample:
                if sbi == 15:
                    k.copy(ssp.a, XNp[:, :, SB], en='act')
                    k.dma(O[pre + "shift_p"].a, ssp.a)
                else:
                    for kk in range(8):
                        k.copy(XNp[:, kk, 0:1], XNp[:, kk, SB:SB + 1], en='act')
            else:
                for kk in range(8):
                    k.copy(sso[:, kk, :], xnc[:, kk, :].re("p (b t) -> p b t", t=8)[:, :, 7], en='act')
                k.dma(O[pre + "shift_s"].a, sso.a)

        def partA(sbi, j, q, pr=(0, 8)):
            T = PT[q]
            tok0 = sbi * SB
            jc = slice(j * 128, (j + 1) * 128)
            ws = wtl[(q // 2) % len(wtl)]
            for c, nm in enumerate(["r", "k", "v"]):
                self.load_tile(ws[c], W[nm], j, 8)
            for c, (nm, mi) in enumerate([("r", 0), ("t0", 2), ("v", 3)]):
                p = self.psum(*pr)
                for kk in range(8):
                    k.mm(p[:, 0:n], ws[c][:, kk, :], xm[mi][:, kk, :], start=(kk == 0), stop=(kk == 7))
                k.copy(T[nm].a, p[:, 0:n], en='act')
            k0 = T["t0"]
            p = self.psum(*pr)
            k.mm(p[:, 0:n], w2[0:64, jc], tw.a)
            k.act(T["lw"].a, p[:, 0:n], AF.Tanh, bias=colh(VW0, j), scale=0.5)
            k.ts(T["lw"].a, T["lw"].a, -0.3032653298563167, ALU.mult, -0.3032653298563167, ALU.add)
            p = self.psum(*pr)
            k.mm(p[:, 0:n], a2[0:64, jc], ta.a)
            a_ = T["t1"]
            k.act(a_.a, p[:, 0:n], AF.Tanh, bias=colh(VA0, j), scale=0.5)
            k.ts(a_.a, a_.a, 0.5, ALU.mult, 0.5, ALU.add)
            p = self.psum(*pr)
            k.mm(p[:, 0:n], g2a[:, jc], tga.a, start=True, stop=False, inc=False)
            k.mm(p[:, 0:n], g2b[0:32, jc], tgb.a, start=False, stop=True)
            k.copy(T["g"].a, p[:, 0:n], en='act')
            if has_v:
                p = self.psum(*pr)
                k.mm(p[:, 0:n], v2[0:32, jc], tv.a)
                sv = T["t2"]
                k.act(sv.a, p[:, 0:n], AF.Tanh, bias=colh(VV0, j), scale=0.5)
                k.ts(sv.a, sv.a, 0.5, ALU.mult, 0.5, ALU.add)
                vf = T["cum"]
                k.dma(vf.a, vfirst.v(j * 128 * TT + tok0, [[TT, 128], [1, n]]))
                k.tt(vf.a, vf.a, T["v"].a, ALU.subtract)
                k.tt(vf.a, vf.a, sv.a, ALU.mult)
                k.tt(T["v"].a, T["v"].a, vf.a, ALU.add)
            else:
                k.dma(vfirst.v(j * 128 * TT + tok0, [[TT, 128], [1, n]]), T["v"].a)
            kkn = T["t2"]
            k.ts(kkn.a, k0.a, col(VKK, j), ALU.mult)
            sq = T["cum"]
            k.tt(sq.a, kkn.a, kkn.a, ALU.mult)
            p = self.psum(*pr)
            k.mm(p[:, 0:n], bones.a, sq.a)
            k.ts(sq.a, p[:, 0:n], 1e-24, ALU.max)
            k.act(sq.a, sq.a, AF.Sqrt)
            k.recip(sq.a, sq.a)
            k.tt(kkn.a, kkn.a, sq.a, ALU.mult)
            tt_ = T["P"]
            k.ts(tt_.a, a_.a, -1.0, ALU.add, col(VKA, j), ALU.mult)
            k.ts(tt_.a, tt_.a, 1.0, ALU.add)
            k.tt(T["k"].a, k0.a, tt_.a, ALU.mult)
            k.ts(T["an"].a, kkn.a, -1.0, ALU.mult)
            k.tt(T["bn"].a, kkn.a, a_.a, ALU.mult)
            k.stt(tt_.a, T["r"].a, col(VRK, j), T["k"].a, ALU.mult, ALU.mult)
            p = self.psum(*pr)
            k.mm(p[:, 0:n], bones.a, tt_.a)
            k.tt(T["bonus"].a, T["v"].a, p[:, 0:n], ALU.mult)

        def chunk(js, c, slots=None, pr=(0, 8)):
            slots = list(range(len(js))) if slots is None else slots
            cs = slice(c * CH, (c + 1) * CH)
            onesb = V(self.ones, AP(self.ones.h, 0, [[128, 128], [0, CH]]))
            for u, q in enumerate(slots):
                T = PT[q]
                k.scan(T["cum"][:, cs], onesb, T["lw"][:, cs], 0.0)
                k.act(T["P"][:, cs], T["cum"][:, cs], AF.Exp)
                k.act(T["Pi"][:, cs], T["cum"][:, cs], AF.Exp, scale=-1.0)
                k.tt(T["Pe"][:, cs], T["cum"][:, cs], T["lw"][:, cs], ALU.subtract)
                k.act(T["Pe"][:, cs], T["Pe"][:, cs], AF.Exp)
            yield
            for u, q in enumerate(slots):
                T = PT[q]; B = BD[q]
                for hh in range(2):
                    ps_ = lo(hh); fs = slice(hh * 64, hh * 64 + 64)
                    k.tt(AR[q][ps_, fs], T["an"][ps_, cs], T["Pe"][ps_, cs], ALU.mult)
                    k.tt(AR[q][ps_, 128 + hh * 64:128 + hh * 64 + 64], T["r"][ps_, cs], T["P"][ps_, cs], ALU.mult)
                    k.tt(B["b_bd"][ps_, fs], T["bn"][ps_, cs], T["Pi"][ps_, cs], ALU.mult)
                    k.tt(B["k_bd"][ps_, fs], T["k"][ps_, cs], T["Pi"][ps_, cs], ALU.mult)
                    k.copy(B["v_bd"][ps_, fs], T["v"][ps_, cs], en='act')
                k.copy(B["S0T"].a, SALL[js[u]].a, en='act')
                yield
            b1 = {}; b2 = {}
            for q in slots:
                B = BD[q]
                b1[q] = self.psum(*pr)
                k.mm(b1[q][:, 0:256], B["b_bd"].a, AR[q].a)
                k.mm(b1[q][:, 256:512], B["k_bd"].a, AR[q].a)
                b2[q] = self.psum(*pr)
                k.mm(b2[q][:, 128:256], B["v_bd"].a, identb.a)
                k.mm(b2[q][:, 256:384], B["b_bd"].a, identb.a)
                k.mm(b2[q][:, 384:512], B["k_bd"].a, identb.a)
            yield
            for q in slots:
                B = BD[q]
                k.tt(NE[q][0][:, 256:384], b1[q][:, 0:128], MSU.a, ALU.mult)
                k.tt(B["Mrb"].a, b1[q][:, 128:256], MIU.a, ALU.mult)
                k.tt(B["Mak"].a, b1[q][:, 256:384], MSU.a, ALU.mult)
                k.tt(B["Mrk"].a, b1[q][:, 384:512], MIU.a, ALU.mult)
                k.copy(B["VT"].a, b2[q][:, 128:256], en='act')
                k.copy(B["bT"].a, b2[q][:, 256:384], en='act')
                k.copy(B["kT"].a, b2[q][:, 384:512], en='act')
                yield
            pW = {}
            for q in slots:
                B = BD[q]
                pW[q] = self.psum(*pr)
                k.mm(pW[q][:, 0:128], AR[q][:, 0:128], B["S0T"].a, start=True, stop=False, inc=False)
                k.mm(pW[q][:, 0:128], B["Mak"].a, B["VT"].a, start=False, stop=True)
                k.mm(pW[q][:, 128:256], NE[q][0][:, 256:384], identb.a)
            for q in slots:
                k.copy(NE[q][0][:, 0:256], pW[q][:, 0:256], en='act')
            yield
            for lev in range(6):
                pN = {}
                for q in slots:
                    src = NE[q][lev % 2]
                    X_ = src[:, 0:128]; At_ = src[:, 128:256]; A_ = src[:, 256:384]
                    pN[q] = self.psum(*pr)
                    k.mm(pN[q][:, 0:128], A_, X_, start=True, stop=False, inc=False)
                    k.mm(pN[q][:, 0:128], identb.a, X_, start=False, stop=True)
                    if lev < 5:
                        k.mm(pN[q][:, 128:256], A_, At_)
                        k.mm(pN[q][:, 256:384], At_, A_)
                for q in slots:
                    dst = NE[q][(lev + 1) % 2]
                    if lev < 5:
                        k.copy(dst[:, 0:384], pN[q][:, 0:384], en='act')
                    else:
                        k.copy(dst[:, 0:128], pN[q][:, 0:128], en='act')
                yield
            pO = {}
            for q in slots:
                B = BD[q]
                pO[q] = self.psum(*pr)
                k.mm(pO[q][:, 0:128], B["S0T"].a, AR[q][:, 128:256], start=True, stop=False, inc=False)
                k.mm(pO[q][:, 0:128], NE[q][0][:, 0:128], B["Mrb"].a, start=False, stop=False, inc=False)
                k.mm(pO[q][:, 0:128], B["VT"].a, B["Mrk"].a, start=False, stop=True)
                k.mm(pO[q][:, 128:256], B["bT"].a, NE[q][0][:, 0:128], start=True, stop=False, inc=False)
                k.mm(pO[q][:, 128:256], B["kT"].a, B["VT"].a, start=False, stop=True)
            yield
            for u, q in enumerate(slots):
                T = PT[q]; B = BD[q]
                for hh in range(2):
                    ps_ = lo(hh)
                    k.copy(T["out"][ps_, cs], pO[q][ps_, hh * 64:hh * 64 + 64], en='act')
                ptot = T["P"][:, c * CH + CH - 1:c * CH + CH]
                k.act(B["ts"].a, pO[q][:, 128:256], AF.Copy, scale=ptot)
                k.stt(SALL[js[u]].a, SALL[js[u]].a, ptot, B["ts"].a, ALU.mult, ALU.add)
            yield

        def partB(j, outv, bonv, gv, q=0, pr=(0, 8)):
            T = PT[q]
            p = self.psum(*pr)
            k.mm(p[:, 0:n], bones.a, outv)
            cen = T["t0"]
            k.stt(cen.a, p[:, 0:n], -1.0 / 64, outv, ALU.mult, ALU.add)
            sq = T["t1"]
            k.tt(sq.a, cen.a, cen.a, ALU.mult)
            p = self.psum(*pr)
            k.mm(p[:, 0:n], bones.a, sq.a)
            k.act(sq.a, p[:, 0:n], AF.Sqrt, bias=gneps[:, 0:1], scale=1.0 / 64)
            k.recip(sq.a, sq.a)
            k.tt(cen.a, cen.a, sq.a, ALU.mult)
            k.ts(cen.a, cen.a, col(VLW, j), ALU.mult, col(VLB, j), ALU.add)
            k.tt(cen.a, cen.a, bonv, ALU.add)
            k.tt(att[j].a, cen.a, gv, ALU.mult)

        def wo_apply(sbi):
            for jo in range(8):
                p = self.psum()
                if sbi < 16:
                    wv_ = lambda kk: woR[:, jo, kk, :]
                else:
                    w_ = wol[jo % 2]
                    self.load_tile(w_, W["o"], jo, 8)
                    wv_ = lambda kk: w_[:, kk, :]
                for kk in range(8):
                    k.mm(p[:, 0:n], wv_(kk), att[kk].a, start=(kk == 0), stop=(kk == 7))
                if sbi < 16:
                    blk, half = sbi // 2, sbi % 2
                    hv = self.h[blk][:, jo, half * SB:(half + 1) * SB]
                else:
                    hv = self.h[8][:, jo, :]
                k.tt(hv, hv, p[:, 0:n], ALU.add)

        def group_gen(sbi, js, slots, pr):
            for q_, j in zip(slots, js):
                partA(sbi, j, q_, pr)
                yield
            for c in range(SB // CH):
                yield from chunk(js, c, slots, pr)
            for q_, j in zip(slots, js):
                partB(j, PT[q_]["out"].a, PT[q_]["bonus"].a, PT[q_]["g"].a, q_, pr)
                yield

        LAG = 6
        k.begin_defer()
        for sbi in range(16):
            front(sbi)
            for g0 in (0, 4):
                gA = group_gen(sbi, [g0, g0 + 1], [0, 1], (0, 4))
                gB = group_gen(sbi, [g0 + 2, g0 + 3], [2, 3], (4, 8))
                aliveA = aliveB = True
                steps = 0
                while aliveA or aliveB:
                    if aliveA:
                        try:
                            next(gA)
                        except StopIteration:
                            aliveA = False
                    steps += 1
                    if aliveB and (steps > LAG or not aliveA):
                        try:
                            next(gB)
                        except StopIteration:
                            aliveB = False
            wo_apply(sbi)
            if sbi % 4 == 3:
                k.end_defer()
                if sbi < 15:
                    k.begin_defer()
        for j in range(8):
            p = self.psum()
            k.tr(p[:, 0:128], SALL[j].a, self.ident.a)
            sot = PT[(j // 2) % NP]["t0"]
            for hh in range(2):
                k.copy(sot[lo(hh), (j % 2) * 64:(j % 2) * 64 + 64], p[lo(hh), hh * 64:hh * 64 + 64], en='act')
            if j % 2 == 1:
                k.dma(O[pre + "wkv_p"][:, j - 1:j + 1, :], sot.a.re("p (a b) -> p a b", a=2))
        k.free_to(mk_p)
        gS = sb("rw_gS", [128, 8, SB]); bonS = sb("rw_bonS", [128, 8, SB])
        XNs = sb("rw_XNs", [128, 8, 16, 9]); xnc = sb("rw_xnc", [128, 8, SB])
        sst = sb("rw_sst", [128, 8, 16]); sso = sb("rw_sso", [128, 8, 16])
        wol = [sb("rw_wo%d" % i, [128, 8, 128], BF16) for i in range(2)]
        sbi = 16
        k.begin_defer()
        front(sbi)
        scr = {nm: k.dram_tmp("rw%d_s_%s" % (l, nm), [8, 128, 128]) for nm in ["an", "bn", "w", "k", "v", "r", "o"]}
        tmt = [sb("rw_tmt%d" % i, [128, 128]) for i in range(2)]
        tc_ = 0
        for j in range(8):
            partA(sbi, j, 0)
            T = PT[0]
            k.copy(gS[:, j, :], T["g"].a, en='act')
            k.copy(bonS[:, j, :], T["bonus"].a, en='act')
            k.act(T["P"].a, T["lw"].a, AF.Exp)
            for nm, src in [("an", "an"), ("bn", "bn"), ("w", "P"), ("k", "k"), ("v", "v"), ("r", "r")]:
                p = self.psum()
                k.tr(p[:, 0:128], T[src].a, self.ident.a)
                t_ = tmt[tc_ % 2]; tc_ += 1
                k.copy(t_.a, p[:, 0:128], en='act')
                k.dma(scr[nm].v(j * 128 * 128, [[128, 128], [1, 128]]), t_.a)
        Tq = {nm: sb("rw_q_%s" % nm, [128, 8, 128]) for nm in ["an", "bn", "w", "k", "v", "r", "o"]}
        for nm in ["an", "bn", "w", "k", "v", "r"]:
            for b in range(16):
                k.dma(Tq[nm][b * 8:(b + 1) * 8, :, :], scr[nm].v(b * 8 * 128, [[128 * 128, 8], [128, 8], [1, 128]]))
        NI = 16
        S = sb("rw_S", [128, NI, 64]); tmp = sb("rw_tmpS", [128, NI, 64]); sa = sb("rw_sa", [128, NI])
        wkv_in = I[pre + "wkv"]; wkv_out = O[pre + "wkv_s"]
        for h2 in range(2):
          for ih in range(64 // NI):
            soff = h2 * 4096 + ih * NI * 64
            k.dma(S.a.re("p a b -> p (a b)"), wkv_in.v(soff, [[8192, 128], [1, NI * 64]]))
            for t in range(8):
                def bi(nm):
                    return V(Tq[nm], AP(Tq[nm].h, t * 128 + h2 * 64, [[1024, 128], [0, NI], [1, 64]]))
                def bd_(nm):
                    return V(Tq[nm], AP(Tq[nm].h, t * 128 + h2 * 64 + ih * NI, [[1024, 128], [1, NI], [0, 64]]))
                k.tt(tmp.a, S.a, bi("an"), ALU.mult)
                k.reduce(sa.a, tmp.a)
                k.tt(S.a, S.a, bi("w"), ALU.mult)
                k.tt(tmp.a, V(sa, AP(sa.h, 0, [[NI, 128], [1, NI], [0, 64]])), bi("bn"), ALU.mult)
                k.tt(S.a, S.a, tmp.a, ALU.add)
                k.tt(tmp.a, bd_("v"), bi("k"), ALU.mult)
                k.tt(S.a, S.a, tmp.a, ALU.add)
                k.tt(tmp.a, S.a, bi("r"), ALU.mult)
                k.reduce(Tq["o"][:, t, h2 * 64 + ih * NI:h2 * 64 + ih * NI + NI], tmp.a)
            k.dma(wkv_out.v(soff, [[8192, 128], [1, NI * 64]]), S.a.re("p a b -> p (a b)"))
        for b in range(16):
            k.dma(scr["o"].v(b * 8 * 128, [[128 * 128, 8], [128, 8], [1, 128]]), Tq["o"][b * 8:(b + 1) * 8, :, :])
        for j in range(8):
            t_ = tmt[j % 2]
            k.dma(t_.a, scr["o"].v(j * 128 * 128, [[128, 128], [1, 128]]))
            p = self.psum()
            k.tr(p[:, 0:128], t_.a, self.ident.a)
            k.copy(PT[1]["out"].a, p[:, 0:128], en='act')
            partB(j, PT[1]["out"].a, bonS[:, j, :], gS[:, j, :])
        wo_apply(sbi)
        k.end_defer()
        k.end_phase()


class MK(MKBase, S5Mixin, MambaMixin, RwkvMixin):
    pass


def _fm(a):
    return np.ascontiguousarray(a.T.reshape(a.shape[1] // 128, 128, a.shape[0]))


def host_consts():
    c = {}
    c["c_ident"] = np.eye(128, dtype=np.float32)
    c["c_iota"] = np.ascontiguousarray(np.tile(np.arange(1, 129, dtype=np.float32), (128, 1)))
    f = np.arange(128)
    c["c_maskB"] = (f[:, None] // 16 == np.arange(8)[None, :]).astype(np.float32)
    return c


def host_shared(inp):
    d = {}
    gains = np.zeros((128, 13 * 8), np.float32)
    for typ, nm in enumerate(["norm_mix", "norm_ffn", "norm_ple"]):
        for l in range(4):
            gains[:, (typ * 4 + l) * 8:(typ * 4 + l + 1) * 8] = inp[nm][l].reshape(8, 128).T
    gains[:, 96:104] = inp["final_norm"].reshape(8, 128).T
    d["c_gains"] = gains
    for nm in ["ffn_w1", "ffn_w3", "ffn_w2", "ple_proj", "ple_gate", "l1_glu_v", "l1_glu_g"]:
        d[nm] = inp[nm]
    are, aim, ld = inp["l1_a_re"], inp["l1_a_im"], inp["l1_log_dt"]
    ldb = np.broadcast_to(ld[:, None], (64, 64))
    chan = np.stack([x.reshape(32, 2, 64).transpose(1, 2, 0).reshape(128, 32) for x in (are, aim, ldb)], 1)
    d["s5_chan"] = np.ascontiguousarray(chan)
    feat = np.stack([np.broadcast_to(x.reshape(8, 8, 1, 64), (8, 8, 16, 64)).transpose(1, 2, 0, 3).reshape(128, 512) for x in (are, aim, ldb)], 1)
    d["s5_feat"] = np.ascontiguousarray(feat)
    d["s5_bT"] = np.ascontiguousarray(np.stack([x.reshape(8, 8, 64, 16).transpose(1, 3, 0, 2).reshape(128, 512) for x in (inp["l1_b_re"], inp["l1_b_im"])], 1))
    d["s5_cF"] = np.ascontiguousarray(np.stack([x.reshape(8, 8, 16, 64).transpose(1, 2, 0, 3).reshape(128, 512) for x in (inp["l1_c_re"], inp["l1_c_im"])], 1))
    d["s5_d"] = np.ascontiguousarray(inp["l1_d"].reshape(8, 128).T)
    return d


def host_core(inp, c):
    d = {}
    xs = inp["x_sample"][16 * c:16 * c + 16].reshape(128, 1024)
    d["xT"] = _fm(np.concatenate([inp["x_prompt"][c], xs], 0))
    d["pT"] = np.stack([_fm(np.concatenate([inp["p_prompt"][l, c], inp["p_sample"][l, 16 * c:16 * c + 16].reshape(128, 256)], 0)) for l in range(4)])
    st = [inp["state_l1_s5_re"][16 * c:16 * c + 16], inp["state_l1_s5_im"][16 * c:16 * c + 16]]
    d["s5_state"] = np.ascontiguousarray(np.stack([x.reshape(16, 32, 2, 64).transpose(2, 3, 1, 0).reshape(128, 32, 16) for x in st], 1))
    return d


def unpack_s5(pst, sst):
    p = [pst[:, ri, :].reshape(2, 64, 32).transpose(2, 0, 1).reshape(1, 64, 64) for ri in range(2)]
    s = [sst[:, ri].reshape(2, 64, 32, 16).transpose(3, 2, 0, 1).reshape(16, 64, 64) for ri in range(2)]
    return p, s


def host_consts2(c):
    kk = np.arange(128)
    c["c_U"] = (kk[:, None] <= kk[None, :]).astype(np.float32)
    c["c_negP"] = np.where(kk[None, :] >= kk[:, None], 0.0, -1.0e5).astype(np.float32)
    c["c_m01"] = (kk[None, :] >= kk[:, None]).astype(np.float32)
    return c


def host_shared_mamba(inp, d):
    d["l2_in_proj"] = inp["l2_in_proj"]; d["l2_out_proj"] = inp["l2_out_proj"]
    d["m_convw"] = np.ascontiguousarray(inp["l2_conv_w"].T.reshape(24, 128, 4).transpose(1, 0, 2))
    d["m_convb"] = np.ascontiguousarray(inp["l2_conv_b"].reshape(24, 128).T)
    d["m_dtb"] = np.ascontiguousarray(inp["l2_dt_bias"].reshape(32, 1))
    d["m_alog"] = np.ascontiguousarray(np.broadcast_to(inp["l2_a_log"][None, :], (128, 32)))
    d["m_alogq"] = np.ascontiguousarray(np.tile(inp["l2_a_log"], 4).reshape(128, 1))
    d["m_dcol"] = np.ascontiguousarray(np.repeat(inp["l2_d"], 64).reshape(16, 128).T)
    d["m_normw"] = np.ascontiguousarray(inp["l2_norm_w"].reshape(16, 128).T)
    return d


def host_core_mamba(inp, c, d):
    st = inp["state_l2_conv"][16 * c:16 * c + 16]
    d["m_convst"] = np.ascontiguousarray(st.reshape(16, 3, 24, 128).transpose(3, 2, 0, 1))
    d["m_ssm"] = np.ascontiguousarray(inp["state_l2_ssm"][16 * c:16 * c + 16])
    return d


def unpack_mamba(r):
    pconv = np.asarray(r["m_pconv"]).transpose(2, 1, 0).reshape(1, 3, 3072)
    sconv = np.asarray(r["m_sconv"]).transpose(2, 3, 1, 0).reshape(16, 3, 3072)
    pssm = np.asarray(r["m_pssmT"]).transpose(1, 2, 0).reshape(1, 32, 64, 128)
    sssm = np.asarray(r["m_sssm"]).reshape(16, 32, 64, 128)
    return pconv, sconv, pssm, sssm


def host_consts3(c):
    kk = np.arange(128)
    c["c_bones"] = (kk[:, None] // 64 == kk[None, :] // 64).astype(np.float32)
    r = kk % 64
    c["c_MSU"] = (r[:, None] < r[None, :]).astype(np.float32)
    c["c_MIU"] = (r[:, None] <= r[None, :]).astype(np.float32)
    c["c_MSL"] = (r[:, None] > r[None, :]).astype(np.float32)
    return c


def host_shared_rwkv(inp, d, pre):
    names = ["w0", "a0", "k_k", "k_a", "r_k", "lnx_w", "lnx_b"]
    vs = [inp[pre + nm].reshape(1024) for nm in names]
    vs.append(inp[pre + "v0"].reshape(1024) if (pre + "v0") in inp else inp[pre + "w0"].reshape(1024))
    d[pre + "vec"] = np.ascontiguousarray(np.stack([v.reshape(8, 128).T for v in vs], 1))
    d[pre + "mu"] = np.ascontiguousarray(inp[pre + "mu"].reshape(6, 8, 128).transpose(2, 0, 1))
    for nm in ["w1", "a1", "g1", "w2", "a2", "g2", "w_rkv", "w_o"] + (["v1", "v2"] if (pre + "v1") in inp else []):
        d[pre + nm] = inp[pre + nm]
    return d


def host_core_rwkv(inp, c, d, l):
    pre = "l%d_" % l
    if l == 0:
        sh_all, wkv_all = inp["state_l0_shift"], inp["state_l0_wkv"]
    else:
        sh_all, wkv_all = inp["state_l3_shift"], inp["state_l3_wkv"]
    sh = sh_all[16 * c:16 * c + 16]
    d[pre + "shift"] = np.ascontiguousarray(sh.reshape(16, 8, 128).transpose(2, 1, 0))
    d[pre + "wkv"] = np.ascontiguousarray(wkv_all[16 * c:16 * c + 16])
    return d


def unpack_rwkv(r, pre):
    shp = np.asarray(r[pre + "shift_p"]).T.reshape(1, 1024)
    shs = np.asarray(r[pre + "shift_s"]).transpose(2, 1, 0).reshape(16, 1024)
    wp = np.asarray(r[pre + "wkv_p"]).reshape(2, 64, 8, 64).transpose(2, 0, 1, 3).reshape(1, 16, 64, 64)
    ws = np.asarray(r[pre + "wkv_s"]).reshape(16, 16, 64, 64)
    return shp, shs, wp, ws


def build_program():
    m = MK()
    k = m.k
    m.setup_consts()
    m.setup_state()
    I = {}

    def din(nm, shape):
        I[nm] = m.inp(nm, list(shape))

    for nm, shp in INPUT_SHAPES.items():
        if nm not in ("c_ident", "c_gains"):
            din(nm, shp)
    O = {nm: m.out(nm, list(shp)) for nm, shp in OUTPUT_SHAPES.items()}
    m.load_x(I["xT"])
    W0 = {nm: m.precast("w0%s_bf" % nm, I["l0_w_rkv"], 1024, c * 1024 * 1024, 1024, 8) for c, nm in enumerate(["r", "k", "v"])}
    W0["o"] = m.precast("w0o_bf", I["l0_w_o"], 1024, 0, 1024, 8)
    win_bf = m.precast("win_bf", I["l2_in_proj"], 5152, 0, 5152, 8)
    wout_bf = m.precast("wout_bf", I["l2_out_proj"], 1024, 0, 1024, 16)
    W3 = {nm: m.precast("w3%s_bf" % nm, I["l3_w_rkv"], 1024, c * 1024 * 1024, 1024, 8) for c, nm in enumerate(["r", "k", "v"])}
    W3["o"] = m.precast("w3o_bf", I["l3_w_o"], 1024, 0, 1024, 8)
    vfirst = k.dram_tmp("vfirst", [8, 128, TT])

    def ffn_ple(l):
        xn_all = [k.sbp("xn%d" % i, [128, 8, n], BF16) for i, (s, n) in enumerate(TBS)]
        wbuf = [(k.sbp("w1g%d" % i, [128, 8, 512], BF16), k.sbp("w3g%d" % i, [128, 8, 512], BF16), k.sbp("w2g%d" % i, [128, 4, 1024], BF16)) for i in range(2)]
        m.ffn(l, I["ffn_w1"], I["ffn_w3"], I["ffn_w2"], xn_all, wbuf)
        k.end_phase()
        wg = k.sbp("wg", [128, 8, 1024], BF16); wp = k.sbp("wp", [128, 2, 1024], BF16)
        xn_blk = [k.sbp("xnb%d" % i, [128, 8, BLK], BF16) for i in range(2)]
        p_blk = [k.sbp("pb%d" % i, [128, 2, BLK], BF16) for i in range(2)]
        m.ple(l, I["pT"], I["ple_proj"], I["ple_gate"], wg, wp, xn_blk, p_blk)
        k.end_phase()

    m.rwkv_layer(0, I, O, W0, "l0_", vfirst)
    ffn_ple(0)
    m.s5_layer(1, I, O)
    ffn_ple(1)
    m.mamba_layer(2, I, O, win_bf, wout_bf)
    ffn_ple(2)
    m.rwkv_layer(3, I, O, W3, "l3_", vfirst)
    ffn_ple(3)
    ybuf = [k.sbp("ybuf%d" % i, [128, 8, BLK]) for i in range(2)]
    m.final(O["yT"], ybuf)
    k.end_phase()
    k.finish()
    return m


OUTPUT_SHAPES = {
    "yT": (8, 128, TT),
    "l0_shift_p": (128, 8), "l0_shift_s": (128, 8, 16), "l0_wkv_p": (128, 8, 64), "l0_wkv_s": (16, 16, 64, 64),
    "s5_pstate": (128, 2, 32), "s5_sstate": (128, 2, 32, 16),
    "m_pconv": (128, 24, 3), "m_sconv": (128, 24, 16, 3), "m_pssmT": (128, 32, 64), "m_sssm": (16, 32, 64, 128),
    "l3_shift_p": (128, 8), "l3_shift_s": (128, 8, 16), "l3_wkv_p": (128, 8, 64), "l3_wkv_s": (16, 16, 64, 64),
}
INPUT_SHAPES = {}


def kernel(**inputs):
    inp = {k_: np.asarray(v) for k_, v in inputs.items()}
    consts = host_consts3(host_consts2(host_consts()))
    shared = host_shared(inp)
    host_shared_mamba(inp, shared)
    host_shared_rwkv(inp, shared, "l0_")
    host_shared_rwkv(inp, shared, "l3_")
    in_maps = []
    for c in range(8):
        d = dict(consts)
        d.update(shared)
        pc = host_core(inp, c)
        host_core_mamba(inp, c, pc)
        host_core_rwkv(inp, c, pc, 0)
        host_core_rwkv(inp, c, pc, 3)
        d.update(pc)
        in_maps.append({k_: np.ascontiguousarray(v, dtype=np.float32) for k_, v in d.items()})
    INPUT_SHAPES.clear()
    for k_, v in in_maps[0].items():
        INPUT_SHAPES[k_] = v.shape
    m = build_program()
    res = run_bass_kernel_spmd(m.k.nc, in_maps, core_ids=list(range(8)))
    R = res.results
    yp = np.zeros((8, 2048, 1024), np.float32); ys = np.zeros((128, 8, 1024), np.float32)
    outs = {nm: [] for nm in ["shp0", "shs0", "wp0", "ws0", "s5p_re", "s5p_im", "s5s_re", "s5s_im", "pconv", "sconv", "pssm", "sssm", "shp3", "shs3", "wp3", "ws3"]}
    for c in range(8):
        r = R[c]
        y = np.asarray(r["yT"]).reshape(1024, TT).T
        yp[c] = y[:2048]
        ys[16 * c:16 * c + 16] = y[2048:].reshape(16, 8, 1024)
        a, b, cc, dd = unpack_rwkv(r, "l0_")
        outs["shp0"].append(a); outs["shs0"].append(b); outs["wp0"].append(cc); outs["ws0"].append(dd)
        p, s = unpack_s5(np.asarray(r["s5_pstate"]), np.asarray(r["s5_sstate"]))
        outs["s5p_re"].append(p[0]); outs["s5p_im"].append(p[1]); outs["s5s_re"].append(s[0]); outs["s5s_im"].append(s[1])
        a, b, cc, dd = unpack_mamba(r)
        outs["pconv"].append(a); outs["sconv"].append(b); outs["pssm"].append(cc); outs["sssm"].append(dd)
        a, b, cc, dd = unpack_rwkv(r, "l3_")
        outs["shp3"].append(a); outs["shs3"].append(b); outs["wp3"].append(cc); outs["ws3"].append(dd)
    cat = lambda nm: np.ascontiguousarray(np.concatenate(outs[nm], 0), dtype=np.float32)
    return (yp, ys,
            cat("shp0"), cat("wp0"), cat("s5p_re"), cat("s5p_im"), cat("pconv"), cat("pssm"), cat("shp3"), cat("wp3"),
            cat("shs0"), cat("ws0"), cat("s5s_re"), cat("s5s_im"), cat("sconv"), cat("sssm"), cat("shs3"), cat("ws3"))
```

```python
import numpy as np
import concourse.bass as bass
import concourse.mybir as mybir
from concourse.ap import AP
from concourse.bass_utils import run_bass_kernel_spmd

F32 = mybir.dt.float32
BF16 = mybir.dt.bfloat16
I32 = mybir.dt.int32
ALU = mybir.AluOpType
AF = mybir.ActivationFunctionType
AX = mybir.AxisListType

SEM_ROT = 30000


class Eng:
    def __init__(self, kb, name, eng):
        self.kb = kb
        self.name = name
        self.eng = eng
        self.sem = kb.nc.alloc_semaphore("es_%s_0" % name)
        self.nsem = 1
        self.cnt = 0
        self.seen = {}

    def rotate(self):
        if self.cnt >= SEM_ROT:
            self.sem = self.kb.nc.alloc_semaphore("es_%s_%d" % (self.name, self.nsem))
            self.nsem += 1
            self.cnt = 0


class DSem:
    def __init__(self, kb, name):
        self.kb = kb
        self.name = name
        self.sem = kb.nc.alloc_semaphore("ds_" + name)
        self.n = 0
        self.gen = 0


class Tn:
    def __init__(self, kb, h, name, space):
        self.kb = kb
        self.h = h
        self.name = name
        self.space = space
        self.lw = None
        self.rd = []
        self.ds = None
        self.shape = list(h.shape)

    def __getitem__(self, idx):
        return V(self, self.h[idx])

    def v(self, offset, ap):
        return V(self, AP(self.h, offset, ap))

    @property
    def a(self):
        return V(self, self.h[:])


class SubTn(Tn):
    def __init__(self, parent, col0, ncols, name):
        self.kb = parent.kb
        self.h = parent.h
        self.name = name
        self.space = parent.space
        self.lw = None
        self.rd = []
        self.ds = None
        self.col0 = col0
        self.ncols = ncols
        self.shape = [parent.shape[0], ncols]

    def __getitem__(self, idx):
        r, c = idx
        a = 0 if c.start is None else c.start
        b = self.ncols if c.stop is None else c.stop
        return V(self, self.h[r, self.col0 + a:self.col0 + b])

    @property
    def a(self):
        return self[:, 0:self.ncols]


class V:
    def __init__(self, t, ap):
        self.t = t
        self.ap = ap

    def __getitem__(self, idx):
        return V(self.t, self.ap[idx])

    def re(self, s, **kw):
        return V(self.t, self.ap.rearrange(s, **kw))

    def bc(self, shape):
        return V(self.t, self.ap.broadcast_to(shape))

    def bitcast(self, dt):
        return V(self.t, self.ap.bitcast(dt))

    @property
    def shape(self):
        return self.ap.shape


def _ap(x):
    return x.ap if isinstance(x, V) else x


class KB:
    def __init__(self):
        self.nc = bass.Bass("TRN2", target_bir_lowering=False)
        nc = self.nc
        self.E = {
            'pe': Eng(self, 'pe', nc.tensor),
            'dve': Eng(self, 'dve', nc.vector),
            'act': Eng(self, 'act', nc.scalar),
            'pool': Eng(self, 'pool', nc.gpsimd),
            'sp': Eng(self, 'sp', nc.sync),
        }
        self.dsems = []
        self._frozen = {}
        self._phase = []
        self._dspool = []
        self._dspool_sw = []
        self.ntens = 0
        self.ninst = 0

    def sb(self, name, shape, dtype=F32):
        h = self.nc.alloc_sbuf_tensor(name, list(shape), dtype)
        return Tn(self, h, name, 'sb')

    def sbp(self, name, shape, dtype=F32):
        self._sbn = getattr(self, "_sbn", 0) + 1
        cm = self.nc.sbuf_tensor("%s_t%d" % (name, self._sbn), list(shape), dtype)
        h = cm.__enter__()
        t = Tn(self, h, name, 'sb')
        self.min_free = min(getattr(self, "min_free", 1 << 30), self.nc.sbuf_bytes_remaining)
        self._phase.append((cm, t))
        return t

    def barrier(self):
        ents = []
        for en, EE in self.E.items():
            if EE.cnt > 0:
                ents.append(('e', EE.sem, EE.cnt, None))
        for ds in self.dsems:
            if ds.n > 0:
                ents.append(('d', ds, ds.gen))
        for en, EE in self.E.items():
            self._wait(EE, ents)

    def mark(self):
        return len(self._phase)

    def free_to(self, mark):
        self.barrier()
        while len(self._phase) > mark:
            cm, t = self._phase.pop()
            if t.ds is not None:
                self._dspool.append(t.ds)
                t.ds = None
            if getattr(t, "ds_sw", None) is not None:
                self._dspool_sw.append(t.ds_sw)
                t.ds_sw = None
            cm.__exit__(None, None, None)

    def end_phase(self):
        self.barrier()
        while self._phase:
            cm, t = self._phase.pop()
            if t.ds is not None:
                self._dspool.append(t.ds)
                t.ds = None
            if getattr(t, "ds_sw", None) is not None:
                self._dspool_sw.append(t.ds_sw)
                t.ds_sw = None
            cm.__exit__(None, None, None)

    def ps(self, name, shape, dtype=F32):
        h = self.nc.alloc_psum_tensor(name, list(shape), dtype)
        return Tn(self, h, name, 'ps')

    def dram_in(self, name, shape, dtype=F32):
        h = self.nc.dram_tensor(name, list(shape), dtype, kind="ExternalInput")
        return Tn(self, h, name, 'din')

    def dram_out(self, name, shape, dtype=F32):
        h = self.nc.dram_tensor(name, list(shape), dtype, kind="ExternalOutput")
        return Tn(self, h, name, 'dout')

    def dram_tmp(self, name, shape, dtype=F32):
        h = self.nc.dram_tensor(name, list(shape), dtype, kind="Internal")
        return Tn(self, h, name, 'dtmp')

    def _entry_val(self, ent):
        kind = ent[0]
        if kind == 'e':
            return ent[1], ent[2]
        ds = ent[1]
        if ent[2] == ds.gen:
            return ds.sem, ds.n * 16
        return self._frozen[(id(ds), ent[2])]

    def _wait(self, E, ents):
        need = {}
        for ent in ents:
            if ent is None:
                continue
            if ent[0] == 'e' and ent[3] is E and E.name == 'pe':
                continue
            sem, val = self._entry_val(ent)
            key = id(sem)
            if E.seen.get(key, 0) >= val:
                continue
            if key not in need or need[key][1] < val:
                need[key] = (sem, val)
        for key, (sem, val) in need.items():
            E.eng.wait_ge(sem, val)
            E.seen[key] = val

    def _collect(self, outs, ins):
        ents = []
        for v in ins:
            if isinstance(v, V) and v.t is not None and v.t.space != 'din':
                ents.append(v.t.lw)
        for v in outs:
            if isinstance(v, V) and v.t is not None:
                ents.append(v.t.lw)
                ents.extend(v.t.rd)
        return ents

    def _record(self, ent, outs, ins):
        for v in ins:
            if isinstance(v, V) and v.t is not None and v.t.space != 'din':
                t = v.t
                t.rd = [r for r in t.rd if not (r[0] == ent[0] and r[1] is ent[1])]
                t.rd.append(ent)
        for v in outs:
            if isinstance(v, V) and v.t is not None:
                v.t.lw = ent
                v.t.rd = []

    def begin_defer(self):
        self._defer = []
        self._dtrk = {}

    def _drecord(self, kind, en, payload, outs, ins, est):
        idx = len(self._defer)
        deps = set()
        for v in ins:
            if isinstance(v, V) and v.t is not None and v.t.space != 'din':
                tr = self._dtrk.setdefault(id(v.t), [None, []])
                if tr[0] is not None:
                    deps.add(tr[0])
        for v in outs:
            if isinstance(v, V) and v.t is not None:
                tr = self._dtrk.setdefault(id(v.t), [None, []])
                if tr[0] is not None:
                    deps.add(tr[0])
                deps.update(tr[1])
        for v in ins:
            if isinstance(v, V) and v.t is not None and v.t.space != 'din':
                self._dtrk[id(v.t)][1].append(idx)
        for v in outs:
            if isinstance(v, V) and v.t is not None:
                self._dtrk[id(v.t)] = [idx, []]
        deps.discard(idx)
        self._defer.append((kind, en, payload, outs, ins, est, deps))

    def end_defer(self, sync_lat=0.25):
        import heapq
        ops = self._defer
        self._defer = None
        n = len(ops)
        succ = [[] for _ in range(n)]
        ndep = [0] * n
        for i, o in enumerate(ops):
            ndep[i] = len(o[6])
            for d in o[6]:
                succ[d].append(i)
        ready_t = [0.0] * n
        fin = [0.0] * n
        start = [0.0] * n
        efree = {}
        heap = [(0.0, i) for i in range(n) if ndep[i] == 0]
        heapq.heapify(heap)
        while heap:
            rt, i = heapq.heappop(heap)
            kind, en, payload, outs, ins, est, deps = ops[i]
            st = max(rt, efree.get(en, 0.0))
            start[i] = st
            if kind == 'dma':
                efree[en] = st + 0.05
                fin[i] = st + est
            else:
                efree[en] = st + est
                fin[i] = st + est
            for j in succ[i]:
                ready_t[j] = max(ready_t[j], fin[i] + sync_lat)
                ndep[j] -= 1
                if ndep[j] == 0:
                    heapq.heappush(heap, (ready_t[j], j))
        order = sorted(range(n), key=lambda i: (start[i], i))
        for i in order:
            kind, en, payload, outs, ins, est, deps = ops[i]
            if kind == 'op':
                fn, inc = payload
                self.op(en, fn, outs, ins, inc=True)
            else:
                out, in_, q, owner, kw = payload
                self.dma(out, in_, q=q, owner=owner, **kw)

    def _est(self, en, outs, ins):
        try:
            sh = outs[0].ap.shape
            free = 1
            for d in sh[1:]:
                free *= d
        except Exception:
            free = 128
        if en == 'pe':
            f32 = False
            try:
                f32 = any(isinstance(v, V) and v.t.space == 'sb' and str(v.ap.dtype).endswith('float32') for v in ins[:1])
            except Exception:
                pass
            return (0.07 + free / 2400.0) * (4.0 if f32 else 1.0)
        if en == 'dve':
            return 0.07 + free / 960.0
        if en == 'act':
            return 0.2 + free / 1200.0
        if en == 'pool':
            return 0.5 + free / 400.0
        return 0.1

    def op(self, en, fn, outs, ins, inc=True):
        if getattr(self, "_defer", None) is not None:
            self._drecord('op', en, (fn, inc), outs, ins, self._est(en, outs, ins))
            return None
        E = self.E[en]
        self._wait(E, self._collect(outs, ins))
        inst = fn(E.eng)
        self.ninst += 1
        if inc:
            E.cnt += 1
            inst.then_inc(E.sem, 1)
            ent = ('e', E.sem, E.cnt, E)
            self._record(ent, outs, ins)
            E.rotate()
        else:
            ent = ('e', E.sem, E.cnt + 1, E)
            self._record(ent, outs, ins)
        return inst

    def dma(self, out, in_, q='sp', owner=None, **kw):
        if getattr(self, "_defer", None) is not None:
            self._drecord('dma', q, (out, in_, q, owner, kw), [out], [in_], 2.2)
            return None
        E = self.E[q]
        self._wait(E, self._collect([out], [in_]))
        if owner is None:
            cands = [v.t for v in (out, in_) if v.t.space in ('sb',)]
            if not cands:
                cands = [v.t for v in (out, in_) if v.t.space in ('dtmp',)]
            if not cands:
                cands = [out.t]
            owner = cands[0]
        sw = (q == 'pool')
        attr = 'ds_sw' if sw else 'ds'
        pool = self._dspool_sw if sw else self._dspool
        if getattr(owner, attr, None) is None:
            if pool:
                setattr(owner, attr, pool.pop())
            else:
                nd = DSem(self, "%s%s_%d" % ("sw_" if sw else "", owner.name, len(self.dsems)))
                setattr(owner, attr, nd)
                self.dsems.append(nd)
        ds = getattr(owner, attr)
        inst = E.eng.dma_start(out=out.ap, in_=in_.ap, **kw)
        ds.n += 1
        inst.then_inc(ds.sem, 16)
        self.ninst += 1
        ent = ('d', ds, ds.gen)
        self._record(ent, [out], [in_])
        if ds.n >= 1800:
            old = (ds.sem, ds.n * 16)
            ds.gen += 1
            self._frozen[(id(ds), ds.gen - 1)] = old
            ds.sem = self.nc.alloc_semaphore("ds_%s_%d" % (owner.name, ds.gen))
            ds.n = 0
        return inst

    def finish(self):
        E = self.E['sp']
        for ds in self.dsems:
            if ds.n > 0:
                E.eng.wait_ge(ds.sem, ds.n * 16)
        for (k, g), (sem, val) in self._frozen.items():
            E.eng.wait_ge(sem, val)
        for en, EE in self.E.items():
            if en != 'sp' and EE.cnt > 0:
                E.eng.wait_ge(EE.sem, EE.cnt)

    def tt(self, out, a, b, op, en='dve'):
        return self.op(en, lambda e: e.tensor_tensor(out=out.ap, in0=a.ap, in1=b.ap, op=op), [out], [a, b])

    def ts(self, out, a, s1, op0, s2=None, op1=None, en='dve'):
        def f(e):
            if op1 is None:
                return e.tensor_scalar(out=out.ap, in0=a.ap, scalar1=_ap(s1), scalar2=None, op0=op0)
            return e.tensor_scalar(out=out.ap, in0=a.ap, scalar1=_ap(s1), scalar2=_ap(s2), op0=op0, op1=op1)
        return self.op(en, f, [out], [a, s1, s2])

    def stt(self, out, a, s, b, op0, op1, en='dve'):
        return self.op(en, lambda e: e.scalar_tensor_tensor(out=out.ap, in0=a.ap, scalar=_ap(s), in1=b.ap, op0=op0, op1=op1), [out], [a, s, b])

    def copy(self, out, a, en='dve'):
        if en == 'act':
            return self.op(en, lambda e: e.copy(out=out.ap, in_=a.ap), [out], [a])
        return self.op(en, lambda e: e.tensor_copy(out=out.ap, in_=a.ap), [out], [a])

    def memset(self, out, val, en='dve'):
        return self.op(en, lambda e: e.memset(out.ap, val), [out], [])

    def act(self, out, a, func, bias=None, scale=None, accum=None):
        def f(e):
            kw = {}
            if bias is not None:
                kw['bias'] = _ap(bias)
            if scale is not None:
                kw['scale'] = _ap(scale)
            if accum is not None:
                kw['accum_out'] = accum.ap
            return e.activation(out=out.ap, in_=a.ap, func=func, **kw)
        outs = [out] + ([accum] if accum is not None else [])
        return self.op('act', f, outs, [a, bias, scale])

    def mm(self, out, lhsT, rhs, start=True, stop=True, inc=None):
        if inc is None:
            inc = stop
        return self.op('pe', lambda e: e.matmul(out.ap, lhsT.ap, rhs.ap, start=start, stop=stop), [out], [lhsT, rhs], inc=inc)

    def tr(self, out, a, ident, inc=True):
        return self.op('pe', lambda e: e.transpose(out.ap, a.ap, ident.ap), [out], [a, ident], inc=inc)

    def scan(self, out, d0, d1, init, op0=ALU.mult, op1=ALU.add):
        return self.op('dve', lambda e: e.tensor_tensor_scan(out=out.ap, data0=d0.ap, data1=d1.ap, initial=_ap(init), op0=op0, op1=op1), [out], [d0, d1, init])

    def reduce(self, out, a, op=ALU.add, axis=AX.X):
        return self.op('dve', lambda e: e.tensor_reduce(out=out.ap, in_=a.ap, axis=axis, op=op), [out], [a])

    def recip(self, out, a):
        return self.op('dve', lambda e: e.reciprocal(out=out.ap, in_=a.ap), [out], [a])
D = 1024
KC = 8
TP = 2048
TS = 128
TT = TP + TS
DFF = 2816
NFC = DFF // 128
BLK = 256
TBS = [(i * BLK, BLK) for i in range(TP // BLK)] + [(TP, TS)]
EPS = 1e-6


class MKBase:
    def __init__(self, dbg=None):
        self.k = KB()
        self.dbg = dbg or {}
        self.outs = {}
        self.ins = {}
        k = self.k
        self.ps_banks = [k.ps("psb%d" % i, [128, 512]) for i in range(8)]
        self.ps_i = 0
        self._uid = 0

    def inp(self, name, shape, dtype=F32):
        t = self.k.dram_in(name, shape, dtype)
        self.ins[name] = t
        return t

    def out(self, name, shape):
        t = self.k.dram_out(name, shape)
        self.outs[name] = t
        return t

    def psum(self, lo=0, hi=8):
        n = hi - lo
        if not hasattr(self, "_psc"):
            self._psc = {}
        c = self._psc.get((lo, hi), 0)
        self._psc[(lo, hi)] = c + 1
        return self.ps_banks[lo + (c % n)]

    def uid(self, s):
        self._uid += 1
        return "%s_%d" % (s, self._uid)

    def dump(self, name, view, shape):
        o = self.out("dbg_" + name, shape)
        self.k.dma(o.a, view)

    def setup_consts(self):
        k = self.k
        ident_d = self.inp("c_ident", [128, 128])
        self.ident = k.sb("ident", [128, 128])
        k.dma(self.ident.a, ident_d.a)
        self.ones = k.sb("ones", [128, 128])
        k.memset(self.ones.a, 1.0)
        gd = self.inp("c_gains", [128, 13 * 8])
        self.gains = k.sb("gains", [128, 13 * 8])
        k.dma(self.gains.a, gd.a)

    def gain(self, typ, l):
        idx = (typ * 4 + l) if typ < 3 else 12
        return self.gains[:, idx * 8:(idx + 1) * 8]

    def rmsnorm(self, src, dst, gain, n, tmp_sq, tmp_r):
        k = self.k
        pss = self.psum(4, 8)
        for kk in range(KC):
            k.act(tmp_sq[:, kk, 0:n], src(kk), AF.Square)
        for kk in range(KC):
            k.mm(pss[:, 0:n], self.ones.a, tmp_sq[:, kk, 0:n], start=(kk == 0), stop=(kk == KC - 1))
        k.act(tmp_r[:, 0:n], pss[:, 0:n], AF.Sqrt, bias=self.eps_col[:, 0:1], scale=1.0 / D)
        k.recip(tmp_r[:, 0:n], tmp_r[:, 0:n])
        for kk in range(KC):
            k.stt(dst(kk), src(kk), gain[:, kk:kk + 1], tmp_r[:, 0:n], ALU.mult, ALU.mult)

    def setup_state(self):
        k = self.k
        self.h = [k.sb("h%d" % i, [128, KC, n]) for i, (s, n) in enumerate(TBS)]
        self.eps_col = k.sb("eps_col", [128, 1])
        k.memset(self.eps_col.a, EPS)
        self.tmp_sq = k.sb("tmp_sq", [128, KC, BLK])
        self.tmp_r = k.sb("tmp_r", [128, BLK])

    def load_x(self, xT):
        k = self.k
        for i, (s, n) in enumerate(TBS):
            k.dma(self.h[i].a, xT.v(s, [[TT, 128], [128 * TT, KC], [1, n]]))

    def ffn(self, l, w1, w3, w2, xn_all, wbuf):
        k = self.k
        def norm_blk(i):
            s, n = TBS[i]
            self.rmsnorm(lambda kk, i=i: self.h[i][:, kk, :], lambda kk, i=i, n=n: xn_all[i][:, kk, 0:n],
                         self.gain(1, l), n, self.tmp_sq, self.tmp_r)
        norm_blk(0)
        G = 4
        groups = [(c, min(G, NFC - c)) for c in range(0, NFC, G)]
        a_t = [[k.sbp(self.uid("ffn_a"), [128, BLK], BF16) for _ in range(G)] for _ in range(2)]
        s_t = [k.sbp(self.uid("ffn_s"), [128, BLK]) for _ in range(2)]

        stg = getattr(self, "_ffn_stage", None)

        def load(gi):
            c0, g = groups[gi]
            w1g, w3g, w2g = wbuf[gi % 2]
            base = l * D * DFF
            if stg is None:
                k.dma(w1g[:, :, 0:g * 128], w1.v(base + c0 * 128, [[DFF, 128], [128 * DFF, KC], [1, g * 128]]), q='pool')
                k.dma(w3g[:, :, 0:g * 128], w3.v(base + c0 * 128, [[DFF, 128], [128 * DFF, KC], [1, g * 128]]), q='pool')
                k.dma(w2g[:, 0:g, :], w2.v(l * DFF * D + c0 * 128 * D, [[D, 128], [128 * D, g], [1, D]]), q='pool')
            else:
                s1, s3, s2 = stg
                k.dma(s1[:, :, 0:g * 128], w1.v(base + c0 * 128, [[DFF, 128], [128 * DFF, KC], [1, g * 128]]))
                k.dma(s3[:, :, 0:g * 128], w3.v(base + c0 * 128, [[DFF, 128], [128 * DFF, KC], [1, g * 128]]))
                k.dma(s2[:, 0:g, :], w2.v(l * DFF * D + c0 * 128 * D, [[D, 128], [128 * D, g], [1, D]]))
                k.copy(w1g[:, :, 0:g * 128], s1[:, :, 0:g * 128], en='act')
                k.copy(w3g[:, :, 0:g * 128], s3[:, :, 0:g * 128], en='pool')
                k.copy(w2g[:, 0:g, :], s2[:, 0:g, :], en='act')

        load(0)
        cnt = 0
        bcnt = 0
        for gi, (c0, g) in enumerate(groups):
            if gi + 1 < len(groups):
                load(gi + 1)
            w1g, w3g, w2g = wbuf[gi % 2]
            for i, (s, n) in enumerate(TBS):
                if gi == 0 and i + 1 < len(TBS):
                    norm_blk(i + 1)
                py = self.ps_banks[0:4]
                ats = a_t[bcnt % 2]
                bcnt += 1
                for c in range(g):
                    ph1 = self.psum(4, 8)
                    for kk in range(KC):
                        k.mm(ph1[:, 0:n], w1g[:, kk, c * 128:(c + 1) * 128], xn_all[i][:, kk, 0:n], start=(kk == 0), stop=(kk == KC - 1))
                    ph3 = self.psum(4, 8)
                    for kk in range(KC):
                        k.mm(ph3[:, 0:n], w3g[:, kk, c * 128:(c + 1) * 128], xn_all[i][:, kk, 0:n], start=(kk == 0), stop=(kk == KC - 1))
                    st = s_t[cnt % 2]
                    cnt += 1
                    k.act(st[:, 0:n], ph1[:, 0:n], AF.Silu)
                    k.tt(ats[c][:, 0:n], st[:, 0:n], ph3[:, 0:n], ALU.mult)
                for j in range(KC):
                    for c in range(g):
                        k.mm(py[j // 2][:, (j % 2) * BLK:(j % 2) * BLK + n], w2g[:, c, j * 128:(j + 1) * 128], ats[c][:, 0:n],
                             start=(c == 0), stop=(c == g - 1))
                for jj in range(4):
                    hv = self.h[i][:, 2 * jj:2 * jj + 2, :]
                    pv = py[jj].a.re("p (a b) -> p a b", a=2)[:, :, 0:n]
                    k.tt(hv, hv, pv, ALU.add)

    def ple(self, l, pT, ple_proj, ple_gate, wg, wp, xn_blk, p_blk):
        k = self.k
        k.begin_defer()
        k.dma(wg.a, ple_gate.v(l * D * D, [[D, 128], [128 * D, KC], [1, D]]), q='pool')
        k.dma(wp.a, ple_proj.v(l * 256 * D, [[D, 128], [128 * D, 2], [1, D]]), q='pool')
        sg = [k.sbp(self.uid("ple_sg"), [128, BLK]) for _ in range(2)]
        def prep(i):
            s, n = TBS[i]
            xb = xn_blk[i % 2]
            pb = p_blk[i % 2]
            k.dma(pb[:, :, 0:n], pT.v(l * 256 * TT + s, [[TT, 128], [128 * TT, 2], [1, n]]), q='pool')
            self.rmsnorm(lambda kk, i=i: self.h[i][:, kk, :], lambda kk, xb=xb, n=n: xb[:, kk, 0:n],
                         self.gain(2, l), n, self.tmp_sq, self.tmp_r)
        prep(0)
        for i, (s, n) in enumerate(TBS):
            xb = xn_blk[i % 2]
            pb = p_blk[i % 2]
            for j in range(KC):
                pg = self.psum()
                for kk in range(KC):
                    k.mm(pg[:, 0:n], wg[:, kk, j * 128:(j + 1) * 128], xb[:, kk, 0:n], start=(kk == 0), stop=(kk == KC - 1))
                pp = self.psum()
                for kk in range(2):
                    k.mm(pp[:, 0:n], wp[:, kk, j * 128:(j + 1) * 128], pb[:, kk, 0:n], start=(kk == 0), stop=(kk == 1))
                sgt = sg[j % 2]
                k.act(sgt[:, 0:n], pg[:, 0:n], AF.Sigmoid)
                k.tt(sgt[:, 0:n], sgt[:, 0:n], pp[:, 0:n], ALU.mult)
                k.tt(self.h[i][:, j, :], self.h[i][:, j, :], sgt[:, 0:n], ALU.add)
                if j == 0 and i + 1 < len(TBS):
                    prep(i + 1)
        k.end_defer()

    def final(self, yT, ybuf):
        k = self.k
        k.begin_defer()
        for i, (s, n) in enumerate(TBS):
            yb = ybuf[i % 2]
            self.rmsnorm(lambda kk, i=i: self.h[i][:, kk, :], lambda kk, yb=yb, n=n: yb[:, kk, 0:n],
                         self.gain(3, 0), n, self.tmp_sq, self.tmp_r)
            k.dma(yT.v(s, [[TT, 128], [128 * TT, KC], [1, n]]), yb[:, :, 0:n])
        k.end_defer()


PI = 3.14159265358979
PIC = 3.141592


class S5Mixin:
    def trig(self, ang, sin_out, cos_out, tf, ti, tr):
        k = self.k
        for out, shift in ((sin_out, 0.0), (cos_out, PI / 2)):
            if shift != 0.0:
                k.ts(tr, ang, shift, ALU.add)
                src = tr
            else:
                src = ang
            k.ts(ti, src, 1.0 / (2 * PI), ALU.mult)
            k.copy(tf, ti)
            k.stt(tr, tf, -2 * PI, src, ALU.mult, ALU.add)
            k.ts(tr, tr, PIC, ALU.min, -PIC, ALU.max)
            k.act(out, tr, AF.Sin)

    def s5_params(self, ach, afe, out_q=False):
        raise NotImplementedError

    def s5_layer(self, l, I, O):
        k = self.k
        sb = k.sbp
        ch = sb("s5_ch", [128, 3, 32]); k.dma(ch.a, I["s5_chan"].a)
        dcol = sb("s5_d", [128, 8]); k.dma(dcol.a, I["s5_d"].a)
        maskB = sb("s5_maskB", [128, 8]); k.dma(maskB.a, I["c_maskB"].a)
        hst = sb("s5_hst", [128, 2, 32, 16]); k.dma(hst.a, I["s5_state"].a)
        hpr = sb("s5_hpr", [128, 2, 32]); k.memset(hpr.a, 0.0)
        hso = sb("s5_hso", [128, 2, 32, 16])
        rho = sb("s5_rho", [128, 32]); th = sb("s5_th", [128, 32]); dtc = sb("s5_dtc", [128, 32])
        LC = 128
        cosT = sb("s5_cosT", [128, 32, LC]); sinT = sb("s5_sinT", [128, 32, LC])
        LB = [sb("s5_LBr", [128, 32, 128], BF16), sb("s5_LBi", [128, 32, 128], BF16)]
        LCm = [sb("s5_LCr", [128, 32, 128], BF16), sb("s5_LCi", [128, 32, 128], BF16)]
        mk_ = k.mark()
        fe = sb("s5_fe", [128, 3, 512]); k.dma(fe.a, I["s5_feat"].a)
        bT = sb("s5_bT", [128, 2, 512]); k.dma(bT.a, I["s5_bT"].a)
        cF = sb("s5_cF", [128, 2, 512]); k.dma(cF.a, I["s5_cF"].a)
        iota = sb("s5_iota", [128, 128]); k.dma(iota.a, I["c_iota"].a)
        k.act(dtc.a, ch[:, 2, :], AF.Exp)
        k.tt(th.a, ch[:, 1, :], dtc.a, ALU.mult)
        k.tt(rho.a, ch[:, 0, :], dtc.a, ALU.mult)
        k.act(rho.a, rho.a, AF.Exp)
        TW = 8 * LC
        ang = sb("s5_ang", [128, TW]); tf = sb("s5_tf", [128, TW]); ti = sb("s5_ti", [128, TW], I32)
        tr = sb("s5_tr", [128, TW])
        for q in range(4):
            for c8 in range(8):
                ct = 8 * q + c8
                k.ts(ang[:, c8 * LC:(c8 + 1) * LC], iota.a, th[:, ct:ct + 1], ALU.mult)
            self.trig(ang.a, sinT[:, 8 * q:8 * q + 8, :].re("p a b -> p (a b)"), cosT[:, 8 * q:8 * q + 8, :].re("p a b -> p (a b)"), tf.a, ti.a, tr.a)
        F = 512
        dtf = sb("s5_dtf", [128, F]); thf = sb("s5_thf", [128, F]); magf = sb("s5_magf", [128, F])
        k.act(dtf.a, fe[:, 2, :], AF.Exp)
        k.tt(thf.a, fe[:, 1, :], dtf.a, ALU.mult)
        k.tt(magf.a, fe[:, 0, :], dtf.a, ALU.mult)
        k.act(magf.a, magf.a, AF.Exp)
        sf = sb("s5_sf", [128, F]); cf = sb("s5_cf", [128, F])
        self.trig(thf.a, sf.a, cf.a, tf[:, 0:F], ti[:, 0:F], tr[:, 0:F])
        abr = sb("s5_abr", [128, F]); abi = sb("s5_abi", [128, F])
        k.tt(abr.a, magf.a, cf.a, ALU.mult)
        k.ts(abr.a, abr.a, -1.0, ALU.add)
        k.tt(abi.a, magf.a, sf.a, ALU.mult)
        lr = fe[:, 0, :]; li = fe[:, 1, :]
        den = dtf; t1 = thf; t2 = magf
        k.tt(den.a, lr, lr, ALU.mult); k.tt(t1.a, li, li, ALU.mult); k.tt(den.a, den.a, t1.a, ALU.add)
        k.recip(den.a, den.a)
        qr = sf; qi = cf
        k.tt(t1.a, abr.a, lr, ALU.mult); k.tt(t2.a, abi.a, li, ALU.mult); k.tt(qr.a, t1.a, t2.a, ALU.add); k.tt(qr.a, qr.a, den.a, ALU.mult)
        k.tt(t1.a, abi.a, lr, ALU.mult); k.tt(t2.a, abr.a, li, ALU.mult); k.tt(qi.a, t1.a, t2.a, ALU.subtract); k.tt(qi.a, qi.a, den.a, ALU.mult)
        bbr = abr; bbi = abi
        k.tt(t1.a, qr.a, bT[:, 0, :], ALU.mult); k.tt(t2.a, qi.a, bT[:, 1, :], ALU.mult); k.tt(bbr.a, t1.a, t2.a, ALU.subtract)
        k.tt(t1.a, qr.a, bT[:, 1, :], ALU.mult); k.tt(t2.a, qi.a, bT[:, 0, :], ALU.mult); k.tt(bbi.a, t1.a, t2.a, ALU.add)
        for ri, bb in enumerate((bbr, bbi)):
            for ft in range(8):
                o = LB[ri][:, 4 * ft:4 * ft + 4, :].re("p a (g q) -> p (a g) q", g=2)
                i0 = V(bb, AP(bb.h, ft * 64, [[F, 128], [0, 8], [1, 64]]))
                i1 = V(maskB, AP(maskB.h, 0, [[8, 128], [1, 8], [0, 64]]))
                k.tt(o, i0, i1, ALU.mult)
        xt = [sb("s5_xt%d" % i, [128, 128]) for i in range(2)]
        n_ = 0
        for ri in range(2):
            for ft in range(8):
                for cl in range(4):
                    x = xt[n_ % 2]; n_ += 1
                    i0 = V(cF, AP(cF.h, ri * 512 + ft * 64, [[1024, 128], [0, 2], [1, 64]]))
                    i1 = V(maskB, AP(maskB.h, 2 * cl, [[8, 128], [1, 2], [0, 64]]))
                    k.tt(x.a.re("p (g q) -> p g q", g=2), i0, i1, ALU.mult)
                    pt = self.psum()
                    k.tr(pt[:, 0:128], x.a, self.ident.a)
                    k.act(LCm[ri][:, 4 * ft + cl, :], pt[:, 0:128], AF.Copy, scale=(1.0 if ri == 0 else -1.0))
        k.free_to(mk_)
        x_ = sb("s5_xn", [128, 8, BLK]); xb_ = sb("s5_xnb", [128, 8, BLK], BF16)
        gbf = sb("s5_gbf", [128, 8, BLK], BF16)
        Hre4 = sb("s5_Hre4", [128, 4, BLK], BF16); Him4 = sb("s5_Him4", [128, 4, BLK], BF16)
        W4 = 4 * 128
        ur = sb("s5_ur", [128, W4]); ui = sb("s5_ui", [128, W4]); ta = sb("s5_ta", [128, W4]); tb = sb("s5_tb", [128, W4])
        tc = sb("s5_tc", [128, W4]); td = sb("s5_td", [128, W4])
        hr = sb("s5_hr", [128, W4]); hi = sb("s5_hi", [128, W4]); h2r = sb("s5_h2r", [128, W4]); h2i = sb("s5_h2i", [128, W4])
        sgt = [sb("s5_sg%d" % i, [128, BLK]) for i in range(2)]
        yt = sb("s5_yt", [128, BLK]); gt = sb("s5_gt", [128, BLK])
        wvj = [sb("s5_wv%d" % i, [128, 8, 128], BF16) for i in range(3)]
        wgj = [sb("s5_wg%d" % i, [128, 8, 128], BF16) for i in range(3)]
        wn = 0
        k.begin_defer()
        for i, (s0, n) in enumerate(TBS):
            sample = (s0 >= TP)
            self.rmsnorm(lambda kk, i=i: self.h[i][:, kk, :], lambda kk, n=n: x_[:, kk, 0:n], self.gain(0, l), n, self.tmp_sq, self.tmp_r)
            for kk in range(KC):
                k.copy(xb_[:, kk, 0:n], x_[:, kk, 0:n], en='act')
            for ft in range(8):
                nch = 1 if sample else n // LC
                for c in range(nch):
                    sl = slice(c * LC, (c + 1) * LC)
                    pbr = self.psum(); pbi = self.psum()
                    for cl in range(4):
                        k.mm(pbr[:, cl * 128:(cl + 1) * 128], LB[0][:, 4 * ft + cl, :], xb_[:, ft, sl])
                    for cl in range(4):
                        k.mm(pbi[:, cl * 128:(cl + 1) * 128], LB[1][:, 4 * ft + cl, :], xb_[:, ft, sl])
                    if sample:
                        cs = V(cosT, AP(cosT.h, 4 * ft * LC, [[32 * LC, 128], [LC, 4], [0, 16], [1, 8]]))
                        sn = V(sinT, AP(sinT.h, 4 * ft * LC, [[32 * LC, 128], [LC, 4], [0, 16], [1, 8]]))
                        w3 = lambda v: v.re("p (a b t) -> p a b t", a=4, t=8)
                    else:
                        cs = cosT[:, 4 * ft:4 * ft + 4, :].re("p a b -> p (a b)"); sn = sinT[:, 4 * ft:4 * ft + 4, :].re("p a b -> p (a b)")
                        w3 = lambda v: v
                    br = w3(pbr[:, 0:W4]); bi = w3(pbi[:, 0:W4])
                    k.tt(w3(ta.a), cs, br, ALU.mult); k.tt(w3(tb.a), sn, bi, ALU.mult); k.tt(ur.a, ta.a, tb.a, ALU.add)
                    k.tt(w3(ta.a), cs, bi, ALU.mult); k.tt(w3(tb.a), sn, br, ALU.mult); k.tt(ui.a, ta.a, tb.a, ALU.subtract)
                    for cl in range(4):
                        ct = 4 * ft + cl
                        o_ = cl * 128
                        rb = V(rho, AP(rho.h, ct, [[32, 128], [0, 8 if sample else LC]]))
                        if sample:
                            for b in range(16):
                                k.scan(hr[:, o_ + b * 8:o_ + (b + 1) * 8], rb, ur[:, o_ + b * 8:o_ + (b + 1) * 8], hst[:, 0, ct, b:b + 1])
                                k.scan(hi[:, o_ + b * 8:o_ + (b + 1) * 8], rb, ui[:, o_ + b * 8:o_ + (b + 1) * 8], hst[:, 1, ct, b:b + 1])
                        else:
                            k.scan(hr[:, o_:o_ + 128], rb, ur[:, o_:o_ + 128], hpr[:, 0, ct:ct + 1])
                            k.scan(hi[:, o_:o_ + 128], rb, ui[:, o_:o_ + 128], hpr[:, 1, ct:ct + 1])
                    k.tt(w3(ta.a), cs, w3(hr.a), ALU.mult); k.tt(w3(tb.a), sn, w3(hi.a), ALU.mult); k.tt(h2r.a, ta.a, tb.a, ALU.subtract)
                    k.tt(w3(ta.a), cs, w3(hi.a), ALU.mult); k.tt(w3(tb.a), sn, w3(hr.a), ALU.mult); k.tt(h2i.a, ta.a, tb.a, ALU.add)
                    k.copy(Hre4[:, :, sl], h2r.a.re("p (a b) -> p a b", a=4), en='act')
                    k.copy(Him4[:, :, sl], h2i.a.re("p (a b) -> p a b", a=4), en='act')
                    if sample:
                        k.copy(hso[:, 0, 4 * ft:4 * ft + 4, :], h2r.a.re("p (a b t) -> p a b t", a=4, t=8)[:, :, :, 7], en='act')
                        k.copy(hso[:, 1, 4 * ft:4 * ft + 4, :], h2i.a.re("p (a b t) -> p a b t", a=4, t=8)[:, :, :, 7], en='act')
                    else:
                        k.copy(hpr[:, 0, 4 * ft:4 * ft + 4], h2r.a.re("p (a b) -> p a b", a=4)[:, :, LC - 1], en='act')
                        k.copy(hpr[:, 1, 4 * ft:4 * ft + 4], h2i.a.re("p (a b) -> p a b", a=4)[:, :, LC - 1], en='act')
                py = self.psum()
                for cl in range(4):
                    k.mm(py[:, 0:n], LCm[0][:, 4 * ft + cl, :], Hre4[:, cl, 0:n], start=(cl == 0), stop=False, inc=False)
                    k.mm(py[:, 0:n], LCm[1][:, 4 * ft + cl, :], Him4[:, cl, 0:n], start=False, stop=(cl == 3), inc=True)
                k.stt(yt[:, 0:n], x_[:, ft, 0:n], dcol[:, ft:ft + 1], py[:, 0:n], ALU.mult, ALU.add)
                k.tt(gt[:, 0:n], yt[:, 0:n], yt[:, 0:n], ALU.mult)
                k.ts(gt[:, 0:n], gt[:, 0:n], 0.044715, ALU.mult, 1.0, ALU.add)
                k.tt(gt[:, 0:n], gt[:, 0:n], yt[:, 0:n], ALU.mult)
                k.act(gt[:, 0:n], gt[:, 0:n], AF.Sigmoid, scale=2.0 * 0.7978845608028654)
                k.tt(gbf[:, ft, 0:n], gt[:, 0:n], yt[:, 0:n], ALU.mult)
            for j in range(KC):
                wv_ = wvj[wn % 3]; wg_ = wgj[wn % 3]; wn += 1
                k.dma(wv_.a, I["l1_glu_v"].v(j * 128, [[D, 128], [128 * D, KC], [1, 128]]), q='pool')
                k.dma(wg_.a, I["l1_glu_g"].v(j * 128, [[D, 128], [128 * D, KC], [1, 128]]), q='pool')
                pv = self.psum(); pg = self.psum()
                for kk in range(KC):
                    k.mm(pv[:, 0:n], wv_[:, kk, :], gbf[:, kk, 0:n], start=(kk == 0), stop=(kk == KC - 1))
                for kk in range(KC):
                    k.mm(pg[:, 0:n], wg_[:, kk, :], gbf[:, kk, 0:n], start=(kk == 0), stop=(kk == KC - 1))
                sg_ = sgt[j % 2]
                k.act(sg_[:, 0:n], pg[:, 0:n], AF.Sigmoid)
                k.tt(sg_[:, 0:n], sg_[:, 0:n], pv[:, 0:n], ALU.mult)
                k.tt(self.h[i][:, j, :], self.h[i][:, j, :], sg_[:, 0:n], ALU.add)
        k.dma(O["s5_pstate"].a, hpr.a)
        k.dma(O["s5_sstate"].a, hso.a)
        k.end_defer()
        k.end_phase()


M_IN = 2048
M_CD = 3072
NEG = -1.0e5


class MambaMixin:
    def precast(self, name, src, row_len, col0, ncols, nk, tile_cols=128):
        k = self.k
        nt = (ncols + tile_cols - 1) // tile_cols
        dst = k.dram_tmp(name, [nt, 128, nk, tile_cols], BF16)
        for m in range(nt):
            w = min(tile_cols, ncols - m * tile_cols)
            k.dma(dst.v(m * 128 * nk * tile_cols, [[nk * tile_cols, 128], [tile_cols, nk], [1, w]]),
                  src.v(col0 + m * tile_cols, [[row_len, 128], [128 * row_len, nk], [1, w]]), q='pool')
        return dst

    def load_tile(self, dst_sb, scratch, m, nk, tile_cols=128, w=None, q='sp'):
        w = w or tile_cols
        self.k.dma(dst_sb[:, :, 0:w], scratch.v(m * 128 * nk * tile_cols, [[nk * tile_cols, 128], [tile_cols, nk], [1, w]]), q=q)

    def mamba_layer(self, l, I, O, win_bf, wout_bf):
        k = self.k
        sb = k.sbp
        cw = sb("m_cw", [128, 24, 4]); k.dma(cw.a, I["m_convw"].a)
        cb = sb("m_cb", [128, 24]); k.dma(cb.a, I["m_convb"].a)
        dtb = sb("m_dtb", [32, 1]); k.dma(dtb.a, I["m_dtb"].a)
        aneg = sb("m_aneg", [128, 32]); k.dma(aneg.a, I["m_alog"].a)
        k.act(aneg.a, aneg.a, AF.Exp); k.ts(aneg.a, aneg.a, -1.0, ALU.mult)
        dcol = sb("m_dcol", [128, 16]); k.dma(dcol.a, I["m_dcol"].a)
        nw = sb("m_nw", [128, 16]); k.dma(nw.a, I["m_normw"].a)
        U = sb("m_U", [128, 128]); k.dma(U.a, I["c_U"].a)
        negm = sb("m_negm", [128, 128]); k.dma(negm.a, I["c_negP"].a)
        carry = sb("m_carry", [128, 24, 3]); k.memset(carry.a, 0.0)
        eps512 = self.eps_col
        wt = [sb("m_wt%d" % i, [128, 8, 128], BF16) for i in range(4)]
        wo = [sb("m_wo%d" % i, [128, 16, 128], BF16) for i in range(2)]
        wcnt = [0, 0]

        def front(i, n, xn, zs, xc, dtf, sample, convst=None, sconv=None):
            self.rmsnorm(lambda kk, i=i: self.h[i][:, kk, :], lambda kk: xn[:, kk, 0:n], self.gain(0, l), n, self.tmp_sq, self.tmp_r)
            for m in range(41):
                w_ = wt[wcnt[0] % 4]; wcnt[0] += 1
                wd = 128 if m < 40 else 32
                self.load_tile(w_, win_bf, m, 8, w=wd)
                pp = self.psum()
                for kk in range(KC):
                    k.mm(pp[0:wd, 0:n], w_[:, kk, 0:wd], xn[:, kk, 0:n], start=(kk == 0), stop=(kk == KC - 1))
                if m < 16:
                    k.act(zs[:, m, 0:n], pp[:, 0:n], AF.Silu)
                elif m < 40:
                    mc = m - 16
                    if sample:
                        xr3 = xr_s.a
                        k.copy(xr3[:, :, 0:3], convst[:, mc, :, :], en='act')
                        k.copy(xr3[:, :, 3:11], pp[:, 0:n].re("p (b t) -> p b t", t=8), en='act')
                        acc = xc[:, mc, 0:n].re("p (b t) -> p b t", t=8)
                        k.ts(acc, xr3[:, :, 3:11], cw[:, mc, 3:4], ALU.mult, cb[:, mc:mc + 1], ALU.add)
                        for j in range(3):
                            k.stt(acc, xr3[:, :, j:j + 8], cw[:, mc, j:j + 1], acc, ALU.mult, ALU.add)
                        k.copy(sconv[:, mc, :, :], xr3[:, :, 8:11], en='act')
                    else:
                        xr_p = xr_pl[mc % 2]
                        k.copy(xr_p[:, 0:3], carry[:, mc, :], en='act')
                        k.copy(xr_p[:, 3:3 + n], pp[:, 0:n], en='act')
                        acc = xc[:, mc, 0:n]
                        k.ts(acc, xr_p[:, 3:3 + n], cw[:, mc, 3:4], ALU.mult, cb[:, mc:mc + 1], ALU.add)
                        for j in range(3):
                            k.stt(acc, xr_p[:, j:j + n], cw[:, mc, j:j + 1], acc, ALU.mult, ALU.add)
                        k.copy(carry[:, mc, :], xr_p[:, n:n + 3], en='act')
                    k.act(xc[:, mc, 0:n], xc[:, mc, 0:n], AF.Silu)
                else:
                    k.act(dtf[0:32, 0:n], pp[0:32, 0:n], AF.Exp, bias=dtb[:, 0:1])
                    k.ts(dtf[0:32, 0:n], dtf[0:32, 0:n], 1.0, ALU.add)
                    k.act(dtf[0:32, 0:n], dtf[0:32, 0:n], AF.Ln)

        def to_tm(xc, dtf, c0, Xdt, dt_tm, A_tm):
            pt = self.psum()
            k.tr(pt[:, 0:32], dtf[0:32, c0:c0 + 128], self.ident[0:32, 0:32])
            k.copy(dt_tm.a, pt[:, 0:32], en='act')
            k.tt(A_tm.a, dt_tm.a, aneg.a, ALU.mult)
            for kt in range(16):
                pt = self.psum()
                k.tr(pt[:, 0:128], xc[:, kt, c0:c0 + 128], self.ident.a)
                k.tt(Xdt[:, kt * 128:(kt + 1) * 128].re("p (a b) -> p a b", a=2), pt[:, 0:128].re("p (a b) -> p a b", a=2),
                     V(dt_tm, AP(dt_tm.h, 2 * kt, [[32, 128], [1, 2], [0, 64]])), ALU.mult)

        def back_fm(y_tm, xc, y_fm, c0):
            for kt in range(16):
                pt = self.psum()
                k.tr(pt[:, 0:128], y_tm[:, kt * 128:(kt + 1) * 128], self.ident.a)
                k.stt(y_fm[:, kt, c0:c0 + 128], xc[:, kt, c0:c0 + 128], dcol[:, kt:kt + 1], pt[:, 0:128], ALU.mult, ALU.add)

        def post(i, n, y_fm, zs, yn):
            for kt in range(16):
                k.tt(y_fm[:, kt, 0:n], y_fm[:, kt, 0:n], zs[:, kt, 0:n], ALU.mult)
            for gq in range(4):
                pss = self.psum()
                for a in range(4):
                    k.act(self.tmp_sq[:, a, 0:n], y_fm[:, 4 * gq + a, 0:n], AF.Square)
                for a in range(4):
                    k.mm(pss[:, 0:n], self.ones.a, self.tmp_sq[:, a, 0:n], start=(a == 0), stop=(a == 3))
                k.act(self.tmp_r[:, 0:n], pss[:, 0:n], AF.Sqrt, bias=self.eps_col[:, 0:1], scale=1.0 / 512)
                k.recip(self.tmp_r[:, 0:n], self.tmp_r[:, 0:n])
                for a in range(4):
                    kt = 4 * gq + a
                    k.stt(yn[:, kt, 0:n], y_fm[:, kt, 0:n], nw[:, kt:kt + 1], self.tmp_r[:, 0:n], ALU.mult, ALU.mult)
            for j in range(KC):
                w_ = wo[wcnt[1] % 2]; wcnt[1] += 1
                self.load_tile(w_, wout_bf, j, 16)
                pj = self.psum()
                for kt in range(16):
                    k.mm(pj[:, 0:n], w_[:, kt, :], yn[:, kt, 0:n], start=(kt == 0), stop=(kt == 15))
                k.tt(self.h[i][:, j, :], self.h[i][:, j, :], pj[:, 0:n], ALU.add)

        mk_ = k.mark()
        xn = sb("m_xn", [128, 8, BLK], BF16); zs = sb("m_zs", [128, 16, BLK], BF16)
        xc = sb("m_xc", [128, 24, BLK]); dtf = sb("m_dtf", [32, BLK]); xr_pl = [sb("m_xrp%d" % i, [128, 3 + BLK]) for i in range(2)]
        y_fm = sb("m_yfm", [128, 16, BLK]); yn = sb("m_yn", [128, 16, BLK], BF16)
        Xdt = sb("m_Xdt", [128, 2048], BF16); y_tm = sb("m_ytm", [128, 2048])
        dt_tm = sb("m_dttm", [128, 32]); A_tm = sb("m_Atm", [128, 32]); decT = sb("m_decT", [128, 32]); E_tm = sb("m_Etm", [128, 32])
        B_tm = sb("m_Btm", [128, 4, 128], BF16); CBt = sb("m_CBt", [128, 4, 128])
        Cbf = sb("m_Cbf", [128, 4, BLK], BF16); STb = sb("m_STb", [128, 32, 64], BF16); k.memset(STb.a, 0.0)
        ST = sb("m_ST", [128, 32, 64]); k.memset(ST.a, 0.0)
        ND = 4
        rhs4 = [sb("m_r4%d" % i, [128, 512]) for i in range(3)]; tD = [sb("m_tD%d" % i, [128, 128]) for i in range(ND)]
        posAc = sb("m_posAc", [128, 32]); m01 = sb("m_m01", [128, 128]); k.dma(m01.a, I["c_m01"].a)
        Lt = [sb("m_Lt%d" % i, [128, 128]) for i in range(ND)]; Gt = [sb("m_Gt%d" % i, [128, 128], BF16) for i in range(ND)]
        yo = [sb("m_yo%d" % i, [128, 64]) for i in range(ND)]; Xde = [sb("m_Xde%d" % i, [128, 64], BF16) for i in range(ND)]
        k.begin_defer()
        for i, (s0, n) in enumerate(TBS):
            if s0 >= TP:
                continue
            front(i, n, xn, zs, xc, dtf, False)
            for g in range(4):
                k.copy(Cbf[:, g, 0:n], xc[:, 20 + g, 0:n], en='act')
            for c in range(n // 128):
                c0 = c * 128
                first = (s0 + c0 == 0)
                to_tm(xc, dtf, c0, Xdt, dt_tm, A_tm)
                for g in range(4):
                    pt = self.psum()
                    k.tr(pt[:, 0:128], xc[:, 16 + g, c0:c0 + 128], self.ident.a)
                    k.copy(B_tm[:, g, :], pt[:, 0:128], en='act')
                    pc = self.psum()
                    k.mm(pc[:, 0:128], xc[:, 16 + g, c0:c0 + 128], xc[:, 20 + g, c0:c0 + 128])
                    k.tt(CBt[:, g, :], pc[:, 0:128], m01.a, ALU.mult)
                pa = self.psum()
                k.mm(pa[:, 0:32], self.ones.a, A_tm.a)
                k.act(decT.a, pa[:, 0:32], AF.Exp)
                pa = self.psum()
                k.mm(pa[:, 0:32], U.a, A_tm.a)
                k.act(E_tm.a, pa[:, 0:32], AF.Exp)
                k.copy(posAc.a, pa[:, 0:32], en='act')
                pD4 = {}

                def stageA1(hq):
                    r4 = rhs4[hq % 3]
                    k.tt(r4.a.re("p (a b) -> p a b", a=4), V(U, AP(U.h, 0, [[128, 128], [0, 4], [1, 128]])),
                         V(A_tm, AP(A_tm.h, 4 * hq, [[32, 128], [1, 4], [0, 128]])), ALU.mult)
                    pD4[hq] = self.psum(0, 3)
                    k.mm(pD4[hq][:, 0:512], self.ones.a, r4.a)

                def stageA2a(h):
                    td = tD[h % ND]; lt = Lt[h % ND]
                    hh = h % 4
                    k.act(td.a, pD4[h // 4][:, hh * 128:(hh + 1) * 128], AF.Relu, bias=posAc[:, h:h + 1], scale=-1.0)
                    k.act(lt.a, td.a, AF.Exp, scale=-1.0)

                def stageA2b(h):
                    g = h // 8
                    lt = Lt[h % ND]; gt = Gt[h % ND]; xde = Xde[h % ND]
                    k.tt(gt.a, lt.a, CBt[:, g, :], ALU.mult)
                    k.act(xde.a, Xdt[:, h * 64:(h + 1) * 64], AF.Copy, scale=lt[:, 127:128])

                pBd = {}

                def stageBpe(h):
                    g = h // 8
                    gt = Gt[h % ND]; xde = Xde[h % ND]
                    pB = self.psum(3, 8)
                    pBd[h] = pB
                    k.mm(pB[:, 0:64], gt.a, Xdt[:, h * 64:(h + 1) * 64])
                    if not first:
                        k.mm(pB[:, 64:128], Cbf[:, g, c0:c0 + 128], STb[:, h, :])
                    k.mm(pB[:, 128:192], B_tm[:, g, :], xde.a)

                def stageBev(h):
                    yo_ = yo[h % ND]
                    pB = pBd.pop(h)
                    if not first:
                        k.ts(yo_.a, pB[:, 64:128], E_tm[:, h:h + 1], ALU.mult)
                        k.tt(y_tm[:, h * 64:(h + 1) * 64], yo_.a, pB[:, 0:64], ALU.add)
                    else:
                        k.copy(y_tm[:, h * 64:(h + 1) * 64], pB[:, 0:64], en='dve')
                    k.stt(ST[:, h, :], ST[:, h, :], decT[:, h:h + 1], pB[:, 128:192], ALU.mult, ALU.add)
                    k.copy(STb[:, h, :], ST[:, h, :], en='act')

                stageA1(0)
                stageA1(1)
                stageA2a(0)
                for h in range(32):
                    if h % 4 == 0 and h // 4 + 2 < 8:
                        stageA1(h // 4 + 2)
                    if h + 1 < 32:
                        stageA2a(h + 1)
                    if h >= 1:
                        stageBpe(h - 1)
                    stageA2b(h)
                    if h >= 1:
                        stageBev(h - 1)
                stageBpe(31)
                stageBev(31)
                back_fm(y_tm, xc, y_fm, c0)
            post(i, n, y_fm, zs, yn)
        k.dma(O["m_pconv"].a, carry.a)
        k.dma(O["m_pssmT"].a, ST.a)
        k.end_defer()
        k.free_to(mk_)
        i = len(TBS) - 1
        n = TS
        xn = sb("ms_xn", [128, 8, TS], BF16); zs = sb("ms_zs", [128, 16, TS], BF16)
        xc = sb("ms_xc", [128, 24, TS]); dtf = sb("ms_dtf", [32, TS]); xr_s = sb("ms_xrs", [128, 16, 11])
        Xdt = sb("ms_Xdt", [128, 2048]); y_tm = sb("ms_ytm", [128, 2048])
        dt_tm = sb("ms_dttm", [128, 32]); A_tm = sb("ms_Atm", [128, 32])
        BC_tm = sb("ms_BCtm", [128, 8, 128])
        convst = sb("ms_convst", [128, 24, 16, 3]); k.dma(convst.a, I["m_convst"].a)
        sconv = sb("ms_sconv", [128, 24, 16, 3])
        alq = sb("ms_alq", [128, 1]); k.dma(alq.a, I["m_alogq"].a)
        k.act(alq.a, alq.a, AF.Exp); k.ts(alq.a, alq.a, -1.0, ALU.mult)
        k.begin_defer()
        front(i, n, xn, zs, xc, dtf, True, convst, sconv)
        k.dma(O["m_sconv"].a, sconv.a)
        to_tm(xc, dtf, 0, Xdt, dt_tm, A_tm)
        for g8 in range(8):
            pt = self.psum()
            k.tr(pt[:, 0:128], xc[:, 16 + g8, 0:128], self.ident.a)
            k.copy(BC_tm[:, g8, :], pt[:, 0:128], en='act')
        sx = k.dram_tmp("ms_sx", [128, 2048]); sbc = k.dram_tmp("ms_sbc", [128, 1024]); sdt = k.dram_tmp("ms_sdt", [128, 32])
        sy = k.dram_tmp("ms_sy", [128, 2048])
        k.dma(sx.a, Xdt.a); k.dma(sbc.a, BC_tm.a.re("p a b -> p (a b)")); k.dma(sdt.a, dt_tm.a)
        mk_r = k.mark()
        Xq = sb("ms_Xq", [128, 8, 64]); Bq = sb("ms_Bq", [128, 8, 128]); Cq = sb("ms_Cq", [128, 8, 128])
        dtq = sb("ms_dtq", [128, 8, 1]); dAq = sb("ms_dAq", [128, 8]); yq = sb("ms_yq", [128, 8, 64])
        S = sb("ms_S", [128, 32, 128]); tmp = sb("ms_tmp", [128, 32, 128])
        ssm_in = I["m_ssm"]; ssm_out = O["m_sssm"]
        for r in range(4):
            for b4 in range(4):
                tok0 = (4 * r + b4) * 8
                k.dma(Xq[b4 * 32:(b4 + 1) * 32, :, :], sx.v(tok0 * 2048, [[64, 32], [2048, 8], [1, 64]]))
                k.dma(dtq[b4 * 32:(b4 + 1) * 32, :, :], sdt.v(tok0 * 32, [[1, 32], [32, 8], [1, 1]]), allow_slow_non_contiguous=True)
                for g in range(4):
                    p0 = b4 * 32 + g * 8
                    k.dma(Bq[p0:p0 + 8, :, :], sbc.v(tok0 * 1024 + g * 128, [[0, 8], [1024, 8], [1, 128]]))
                    k.dma(Cq[p0:p0 + 8, :, :], sbc.v(tok0 * 1024 + 512 + g * 128, [[0, 8], [1024, 8], [1, 128]]))
            k.act(dAq.a, dtq.a.re("p a b -> p (a b)"), AF.Exp, scale=alq[:, 0:1])
            for ph in range(2):
                k.dma(S.a.re("p a b -> p (a b)"), ssm_in.v(4 * r * 32 * 8192 + ph * 4096, [[8192, 128], [1, 4096]]))
                for t in range(8):
                    xin = V(Xq, AP(Xq.h, t * 64 + ph * 32, [[512, 128], [1, 32], [0, 128]]))
                    bin_ = V(Bq, AP(Bq.h, t * 128, [[1024, 128], [0, 32], [1, 128]]))
                    cin = V(Cq, AP(Cq.h, t * 128, [[1024, 128], [0, 32], [1, 128]]))
                    k.tt(tmp.a, xin, bin_, ALU.mult)
                    k.stt(S.a, S.a, dAq[:, t:t + 1], tmp.a, ALU.mult, ALU.add)
                    k.tt(tmp.a, S.a, cin, ALU.mult)
                    k.reduce(yq[:, t, ph * 32:(ph + 1) * 32], tmp.a)
                k.dma(ssm_out.v(4 * r * 32 * 8192 + ph * 4096, [[8192, 128], [1, 4096]]), S.a.re("p a b -> p (a b)"))
            for b4 in range(4):
                tok0 = (4 * r + b4) * 8
                k.dma(sy.v(tok0 * 2048, [[64, 32], [2048, 8], [1, 64]]), yq[b4 * 32:(b4 + 1) * 32, :, :])
        k.end_defer()
        k.free_to(mk_r)
        y_fm = sb("ms_yfm", [128, 16, TS]); yn = sb("ms_yn", [128, 16, TS], BF16)
        k.begin_defer()
        k.dma(y_tm.a, sy.a)
        back_fm(y_tm, xc, y_fm, 0)
        post(i, n, y_fm, zs, yn)
        k.end_defer()
        k.end_phase()


SB = 128
CH = 64
NP = 4


class RwkvMixin:
    def rwkv_layer(self, l, I, O, W, pre, vfirst):
        k = self.k
        sb = k.sbp
        has_v = (l == 3)
        n = SB
        vec = sb("rw_vec", [128, 8, 8]); k.dma(vec.a, I[pre + "vec"].a)
        mu = sb("rw_mu", [128, 6, 8]); k.dma(mu.a, I[pre + "mu"].a)
        w1 = sb("rw_w1", [128, 8, 64], BF16); k.dma(w1.a, I[pre + "w1"].v(0, [[64, 128], [128 * 64, 8], [1, 64]]), q='pool')
        a1 = sb("rw_a1", [128, 8, 64], BF16); k.dma(a1.a, I[pre + "a1"].v(0, [[64, 128], [128 * 64, 8], [1, 64]]), q='pool')
        g1 = sb("rw_g1", [128, 8, 160], BF16); k.dma(g1.a, I[pre + "g1"].v(0, [[160, 128], [128 * 160, 8], [1, 160]]), q='pool')
        w2 = sb("rw_w2", [64, 1024], BF16); k.dma(w2.a, I[pre + "w2"].a, q='pool')
        a2 = sb("rw_a2", [64, 1024], BF16); k.dma(a2.a, I[pre + "a2"].a, q='pool')
        g2a = sb("rw_g2a", [128, 1024], BF16); k.dma(g2a.a, I[pre + "g2"].v(0, [[1024, 128], [1, 1024]]), q='pool')
        g2b = sb("rw_g2b", [32, 1024], BF16); k.dma(g2b.a, I[pre + "g2"].v(128 * 1024, [[1024, 32], [1, 1024]]), q='pool')
        if has_v:
            v1 = sb("rw_v1", [128, 8, 32], BF16); k.dma(v1.a, I[pre + "v1"].v(0, [[32, 128], [128 * 32, 8], [1, 32]]), q='pool')
            v2 = sb("rw_v2", [32, 1024], BF16); k.dma(v2.a, I[pre + "v2"].a, q='pool')
        bones = sb("rw_bones", [128, 128]); k.dma(bones.a, I["c_bones"].a)
        gneps = sb("rw_gneps", [128, 1]); k.memset(gneps.a, 64e-5)
        VW0, VA0, VKK, VKA, VRK, VLW, VLB, VV0 = range(8)
        vech = sb("rw_vech", [128, 8, 8]); k.ts(vech.a, vec.a, 0.5, ALU.mult)

        def colh(t, j):
            return vech[:, t, j:j + 1]

        def col(t, j):
            return vec[:, t, j:j + 1]

        xx = sb("rw_xx", [128, 8, SB])
        xm = [sb("rw_xm%d" % c, [128, 8, SB], BF16) for c in range(6)]
        tw = sb("rw_tw", [64, SB], BF16); ta = sb("rw_ta", [64, SB], BF16)
        tgf = sb("rw_tgf", [128, SB]); tga = sb("rw_tga", [128, SB], BF16); tgb = sb("rw_tgb", [32, SB], BF16)
        tv = sb("rw_tv", [32, SB], BF16)
        att = [sb("rw_att%d" % j, [128, SB], BF16) for j in range(8)]
        wtl = [[sb("rw_wt%d_%d" % (i, c), [128, 8, 128], BF16) for c in range(3)] for i in range(2)]

        wc = [0, 0]
        names = ["r", "lw", "k", "v", "an", "bn", "g", "bonus", "out", "t0", "t1", "t2", "cum", "P", "Pi", "Pe"]
        PT = [{nm: sb("rw_%s_%d" % (nm, q), [128, SB]) for nm in names} for q in range(2)]
        mk_p = k.mark()
        PT += [{nm: sb("rw_%s_%d" % (nm, q), [128, SB]) for nm in names} for q in range(2, NP)]
        MSU = sb("rw_MSU", [128, 128]); k.dma(MSU.a, I["c_MSU"].a)
        MIU = sb("rw_MIU", [128, 128]); k.dma(MIU.a, I["c_MIU"].a)
        MSL = sb("rw_MSL", [128, 128]); k.dma(MSL.a, I["c_MSL"].a)
        XNp = sb("rw_XNp", [128, 8, SB + 1]); k.memset(XNp.a, 0.0)
        ssp = sb("rw_ssp", [128, 8])
        woR = sb("rw_woR", [128, 8, 8, 128], BF16)
        for jo in range(8):
            self.load_tile(woR[:, jo, :, :], W["o"], jo, 8)
        bdn = ["b_bd", "k_bd", "v_bd", "S0T", "ts"]
        BD = [{nm: sb("rw_%s_%d" % (nm, q), [128, 128], (F32 if nm == "ts" else BF16)) for nm in bdn} for q in range(NP)]
        AR = [sb("rw_AR_%d" % q, [128, 256], BF16) for q in range(NP)]
        NE = [[sb("rw_NE%d_%d" % (q, i), [128, 384], BF16) for i in range(2)] for q in range(NP)]
        M3 = [sb("rw_M3_%d" % q, [128, 384], BF16) for q in range(NP)]
        T3 = [sb("rw_T3_%d" % q, [128, 384], BF16) for q in range(NP)]
        MK3 = sb("rw_MK3", [128, 384])
        k.copy(MK3[:, 0:128], MIU.a); k.copy(MK3[:, 128:256], MSU.a); k.copy(MK3[:, 256:384], MIU.a)
        identb = sb("rw_identb", [128, 128], BF16); k.copy(identb.a, self.ident.a)
        for q in range(NP):
            for nm in ["b_bd", "k_bd", "v_bd", "S0T"]:
                k.memset(BD[q][nm].a, 0.0)
            k.memset(AR[q].a, 0.0)
        SALL = [sb("rw_SALL%d" % j, [128, 128]) for j in range(8)]
        for j in range(8):
            k.memset(SALL[j].a, 0.0)

        def lo(q):
            return slice(0, 64) if q == 0 else slice(64, 128)

        def front(sbi):
            sample = (sbi == 16)
            if not sample:
                blk, half = sbi // 2, sbi % 2
                hv = lambda kk: self.h[blk][:, kk, half * SB:(half + 1) * SB]
                self.rmsnorm(hv, lambda kk: XNp[:, kk, 1:SB + 1], self.gain(0, l), n, self.tmp_sq, self.tmp_r)
                X = lambda kk: XNp[:, kk, 1:SB + 1]
                for kk in range(8):
                    k.tt(xx[:, kk, :], XNp[:, kk, 0:SB], XNp[:, kk, 1:SB + 1], ALU.subtract)
            else:
                hv = lambda kk: self.h[8][:, kk, :]
                self.rmsnorm(hv, lambda kk: xnc[:, kk, :], self.gain(0, l), n, self.tmp_sq, self.tmp_r)
                k.dma(sst.a, I[pre + "shift"].a)
                for kk in range(8):
                    k.copy(XNs[:, kk, :, 0], sst[:, kk, :], en='act')
                    k.copy(XNs[:, kk, :, 1:9], xnc[:, kk, :].re("p (b t) -> p b t", t=8), en='act')
                    k.tt(xx[:, kk, :].re("p (b t) -> p b t", t=8), XNs[:, kk, :, 0:8], XNs[:, kk, :, 1:9], ALU.subtract)
                X = lambda kk: xnc[:, kk, :]
            for c in range(6):
                for kk in range(8):
                    k.stt(xm[c][:, kk, :], xx[:, kk, :], mu[:, c, kk:kk + 1], X(kk), ALU.mult, ALU.add)
            p = self.psum()
            for kk in range(8):
                k.mm(p[0:64, 0:n], w1[:, kk, :], xm[1][:, kk, :], start=(kk == 0), stop=(kk == 7))
            k.act(tw.a, p[0:64, 0:n], AF.Tanh)
            p = self.psum()
            for kk in range(8):
                k.mm(p[0:64, 0:n], a1[:, kk, :], xm[4][:, kk, :], start=(kk == 0), stop=(kk == 7))
            k.copy(ta.a, p[0:64, 0:n], en='act')
            p = self.psum()
            for kk in range(8):
                k.mm(p[:, 0:n], g1[:, kk, 0:128], xm[5][:, kk, :], start=(kk == 0), stop=(kk == 7))
            k.act(tgf.a, p[:, 0:n], AF.Tanh, scale=0.5)
            k.ts(tga.a, tgf.a, 0.5, ALU.mult, 0.5, ALU.add)
            p = self.psum()
            for kk in range(8):
                k.mm(p[0:32, 0:n], g1[:, kk, 128:160], xm[5][:, kk, :], start=(kk == 0), stop=(kk == 7))
            k.act(tgf[0:32, :], p[0:32, 0:n], AF.Tanh, scale=0.5)
            k.ts(tgb.a, tgf[0:32, :], 0.5, ALU.mult, 0.5, ALU.add)
            if has_v:
                p = self.psum()
                for kk in range(8):
                    k.mm(p[0:32, 0:n], v1[:, kk, :], xm[3][:, kk, :], start=(kk == 0), stop=(kk == 7))
                k.copy(tv.a, p[0:32, 0:n], en='act')
            if not sample:
                if sbi == 15:
                    k.copy(ssp.a, XNp[:, :, SB], en='act')
                    k.dma(O[pre + "shift_p"].a, ssp.a)
                else:
                    for kk in range(8):
                        k.copy(XNp[:, kk, 0:1], XNp[:, kk, SB:SB + 1], en='act')
            else:
                for kk in range(8):
                    k.copy(sso[:, kk, :], xnc[:, kk, :].re("p (b t) -> p b t", t=8)[:, :, 7], en='act')
                k.dma(O[pre + "shift_s"].a, sso.a)

        def partA(sbi, j, q, pr=(0, 8)):
            T = PT[q]
            tok0 = sbi * SB
            jc = slice(j * 128, (j + 1) * 128)
            ws = wtl[(q // 2) % len(wtl)]
            for c, nm in enumerate(["r", "k", "v"]):
                self.load_tile(ws[c], W[nm], j, 8)
            for c, (nm, mi) in enumerate([("r", 0), ("t0", 2), ("v", 3)]):
                p = self.psum(*pr)
                for kk in range(8):
                    k.mm(p[:, 0:n], ws[c][:, kk, :], xm[mi][:, kk, :], start=(kk == 0), stop=(kk == 7))
                k.copy(T[nm].a, p[:, 0:n], en='act')
            k0 = T["t0"]
            p = self.psum(*pr)
            k.mm(p[:, 0:n], w2[0:64, jc], tw.a)
            k.act(T["lw"].a, p[:, 0:n], AF.Tanh, bias=colh(VW0, j), scale=0.5)
            k.ts(T["lw"].a, T["lw"].a, -0.3032653298563167, ALU.mult, -0.3032653298563167, ALU.add)
            p = self.psum(*pr)
            k.mm(p[:, 0:n], a2[0:64, jc], ta.a)
            a_ = T["t1"]
            k.act(a_.a, p[:, 0:n], AF.Tanh, bias=colh(VA0, j), scale=0.5)
            k.ts(a_.a, a_.a, 0.5, ALU.mult, 0.5, ALU.add)
            p = self.psum(*pr)
            k.mm(p[:, 0:n], g2a[:, jc], tga.a, start=True, stop=False, inc=False)
            k.mm(p[:, 0:n], g2b[0:32, jc], tgb.a, start=False, stop=True)
            k.copy(T["g"].a, p[:, 0:n], en='act')
            if has_v:
                p = self.psum(*pr)
                k.mm(p[:, 0:n], v2[0:32, jc], tv.a)
                sv = T["t2"]
                k.act(sv.a, p[:, 0:n], AF.Tanh, bias=colh(VV0, j), scale=0.5)
                k.ts(sv.a, sv.a, 0.5, ALU.mult, 0.5, ALU.add)
                vf = T["cum"]
                k.dma(vf.a, vfirst.v(j * 128 * TT + tok0, [[TT, 128], [1, n]]))
                k.tt(vf.a, vf.a, T["v"].a, ALU.subtract)
                k.tt(vf.a, vf.a, sv.a, ALU.mult)
                k.tt(T["v"].a, T["v"].a, vf.a, ALU.add)
            else:
                k.dma(vfirst.v(j * 128 * TT + tok0, [[TT, 128], [1, n]]), T["v"].a)
            kkn = T["t2"]
            k.ts(kkn.a, k0.a, col(VKK, j), ALU.mult)
            sq = T["cum"]
            k.tt(sq.a, kkn.a, kkn.a, ALU.mult)
            p = self.psum(*pr)
            k.mm(p[:, 0:n], bones.a, sq.a)
            k.ts(sq.a, p[:, 0:n], 1e-24, ALU.max)
            k.act(sq.a, sq.a, AF.Sqrt)
            k.recip(sq.a, sq.a)
            k.tt(kkn.a, kkn.a, sq.a, ALU.mult)
            tt_ = T["P"]
            k.ts(tt_.a, a_.a, -1.0, ALU.add, col(VKA, j), ALU.mult)
            k.ts(tt_.a, tt_.a, 1.0, ALU.add)
            k.tt(T["k"].a, k0.a, tt_.a, ALU.mult)
            k.ts(T["an"].a, kkn.a, -1.0, ALU.mult)
            k.tt(T["bn"].a, kkn.a, a_.a, ALU.mult)
            k.stt(tt_.a, T["r"].a, col(VRK, j), T["k"].a, ALU.mult, ALU.mult)
            p = self.psum(*pr)
            k.mm(p[:, 0:n], bones.a, tt_.a)
            k.tt(T["bonus"].a, T["v"].a, p[:, 0:n], ALU.mult)

        def chunk(js, c, slots=None, pr=(0, 8)):
            slots = list(range(len(js))) if slots is None else slots
            cs = slice(c * CH, (c + 1) * CH)
            onesb = V(self.ones, AP(self.ones.h, 0, [[128, 128], [0, CH]]))
            for u, q in enumerate(slots):
                T = PT[q]
                k.scan(T["cum"][:, cs], onesb, T["lw"][:, cs], 0.0)
                k.act(T["P"][:, cs], T["cum"][:, cs], AF.Exp)
                k.act(T["Pi"][:, cs], T["cum"][:, cs], AF.Exp, scale=-1.0)
                k.tt(T["Pe"][:, cs], T["cum"][:, cs], T["lw"][:, cs], ALU.subtract)
                k.act(T["Pe"][:, cs], T["Pe"][:, cs], AF.Exp)
            yield
            for u, q in enumerate(slots):
                T = PT[q]; B = BD[q]
                for hh in range(2):
                    ps_ = lo(hh); fs = slice(hh * 64, hh * 64 + 64)
                    k.tt(AR[q][ps_, fs], T["an"][ps_, cs], T["Pe"][ps_, cs], ALU.mult)
                    k.tt(AR[q][ps_, 128 + hh * 64:128 + hh * 64 + 64], T["r"][ps_, cs], T["P"][ps_, cs], ALU.mult)
                    k.tt(B["b_bd"][ps_, fs], T["bn"][ps_, cs], T["Pi"][ps_, cs], ALU.mult)
                    k.tt(B["k_bd"][ps_, fs], T["k"][ps_, cs], T["Pi"][ps_, cs], ALU.mult)
                    k.copy(B["v_bd"][ps_, fs], T["v"][ps_, cs], en='act')
                k.copy(B["S0T"].a, SALL[js[u]].a, en='act')
                yield
            b1 = {}; b2 = {}
            for q in slots:
                B = BD[q]
                b1[q] = self.psum(*pr)
                k.mm(b1[q][:, 0:256], B["b_bd"].a, AR[q].a)
                k.mm(b1[q][:, 256:512], B["k_bd"].a, AR[q].a)
                b2[q] = self.psum(*pr)
                k.mm(b2[q][:, 128:256], B["v_bd"].a, identb.a)
                k.mm(b2[q][:, 256:384], B["b_bd"].a, identb.a)
                k.mm(b2[q][:, 384:512], B["k_bd"].a, identb.a)
            yield
            for q in slots:
                B = BD[q]
                k.tt(NE[q][0][:, 256:384], b1[q][:, 0:128], MSU.a, ALU.mult)
                k.tt(M3[q].a, b1[q][:, 128:512], MK3.a, ALU.mult)
                k.copy(T3[q].a, b2[q][:, 128:512], en='act')
                yield
            pW = {}
            for q in slots:
                B = BD[q]
                pW[q] = self.psum(*pr)
                k.mm(pW[q][:, 0:128], AR[q][:, 0:128], B["S0T"].a, start=True, stop=False, inc=False)
                k.mm(pW[q][:, 0:128], M3[q][:, 128:256], T3[q][:, 0:128], start=False, stop=True)
                k.mm(pW[q][:, 128:256], NE[q][0][:, 256:384], identb.a)
            for q in slots:
                k.copy(NE[q][0][:, 0:256], pW[q][:, 0:256], en='act')
            yield
            for lev in range(6):
                pN = {}
                for q in slots:
                    src = NE[q][lev % 2]
                    X_ = src[:, 0:128]; At_ = src[:, 128:256]; A_ = src[:, 256:384]
                    pN[q] = self.psum(*pr)
                    k.mm(pN[q][:, 0:128], A_, X_, start=True, stop=False, inc=False)
                    k.mm(pN[q][:, 0:128], identb.a, X_, start=False, stop=True)
                    if lev < 5:
                        k.mm(pN[q][:, 128:256], A_, At_)
                        k.mm(pN[q][:, 256:384], At_, A_)
                for q in slots:
                    dst = NE[q][(lev + 1) % 2]
                    if lev < 5:
                        k.copy(dst[:, 0:384], pN[q][:, 0:384], en='act')
                    else:
                        k.copy(dst[:, 0:128], pN[q][:, 0:128], en='act')
                yield
            pO = {}
            for q in slots:
                B = BD[q]
                pO[q] = self.psum(*pr)
                k.mm(pO[q][:, 0:128], B["S0T"].a, AR[q][:, 128:256], start=True, stop=False, inc=False)
                k.mm(pO[q][:, 0:128], NE[q][0][:, 0:128], M3[q][:, 0:128], start=False, stop=False, inc=False)
                k.mm(pO[q][:, 0:128], T3[q][:, 0:128], M3[q][:, 256:384], start=False, stop=True)
                k.mm(pO[q][:, 128:256], T3[q][:, 128:256], NE[q][0][:, 0:128], start=True, stop=False, inc=False)
                k.mm(pO[q][:, 128:256], T3[q][:, 256:384], T3[q][:, 0:128], start=False, stop=True)
            yield
            for u, q in enumerate(slots):
                T = PT[q]; B = BD[q]
                for hh in range(2):
                    ps_ = lo(hh)
                    k.copy(T["out"][ps_, cs], pO[q][ps_, hh * 64:hh * 64 + 64], en='act')
                ptot = T["P"][:, c * CH + CH - 1:c * CH + CH]
                k.act(B["ts"].a, pO[q][:, 128:256], AF.Copy, scale=ptot)
                k.stt(SALL[js[u]].a, SALL[js[u]].a, ptot, B["ts"].a, ALU.mult, ALU.add)
            yield

        def partB(j, outv, bonv, gv, q=0, pr=(0, 8)):
            T = PT[q]
            p = self.psum(*pr)
            k.mm(p[:, 0:n], bones.a, outv)
            cen = T["t0"]
            k.stt(cen.a, p[:, 0:n], -1.0 / 64, outv, ALU.mult, ALU.add)
            sq = T["t1"]
            k.tt(sq.a, cen.a, cen.a, ALU.mult)
            p = self.psum(*pr)
            k.mm(p[:, 0:n], bones.a, sq.a)
            k.act(sq.a, p[:, 0:n], AF.Sqrt, bias=gneps[:, 0:1], scale=1.0 / 64)
            k.recip(sq.a, sq.a)
            k.tt(cen.a, cen.a, sq.a, ALU.mult)
            k.ts(cen.a, cen.a, col(VLW, j), ALU.mult, col(VLB, j), ALU.add)
            k.tt(cen.a, cen.a, bonv, ALU.add)
            k.tt(att[j].a, cen.a, gv, ALU.mult)

        def wo_apply(sbi):
            for jo in range(8):
                p = self.psum()
                if sbi < 16:
                    wv_ = lambda kk: woR[:, jo, kk, :]
                else:
                    w_ = wol[jo % 2]
                    self.load_tile(w_, W["o"], jo, 8)
                    wv_ = lambda kk: w_[:, kk, :]
                for kk in range(8):
                    k.mm(p[:, 0:n], wv_(kk), att[kk].a, start=(kk == 0), stop=(kk == 7))
                if sbi < 16:
                    blk, half = sbi // 2, sbi % 2
                    hv = self.h[blk][:, jo, half * SB:(half + 1) * SB]
                else:
                    hv = self.h[8][:, jo, :]
                k.tt(hv, hv, p[:, 0:n], ALU.add)

        def group_gen(sbi, js, slots, pr):
            for q_, j in zip(slots, js):
                partA(sbi, j, q_, pr)
                yield
            for c in range(SB // CH):
                yield from chunk(js, c, slots, pr)
            for q_, j in zip(slots, js):
                partB(j, PT[q_]["out"].a, PT[q_]["bonus"].a, PT[q_]["g"].a, q_, pr)
                yield

        LAG = 6
        k.begin_defer()
        for sbi in range(16):
            front(sbi)
            for g0 in (0, 4):
                gA = group_gen(sbi, [g0, g0 + 1], [0, 1], (0, 4))
                gB = group_gen(sbi, [g0 + 2, g0 + 3], [2, 3], (4, 8))
                aliveA = aliveB = True
                steps = 0
                while aliveA or aliveB:
                    if aliveA:
                        try:
                            next(gA)
                        except StopIteration:
                            aliveA = False
                    steps += 1
                    if aliveB and (steps > LAG or not aliveA):
                        try:
                            next(gB)
                        except StopIteration:
                            aliveB = False
            wo_apply(sbi)
            if sbi % 4 == 3:
                k.end_defer()
                if sbi < 15:
                    k.begin_defer()
        for j in range(8):
            p = self.psum()
            k.tr(p[:, 0:128], SALL[j].a, self.ident.a)
            sot = PT[(j // 2) % NP]["t0"]
            for hh in range(2):
                k.copy(sot[lo(hh), (j % 2) * 64:(j % 2) * 64 + 64], p[lo(hh), hh * 64:hh * 64 + 64], en='act')
            if j % 2 == 1:
                k.dma(O[pre + "wkv_p"][:, j - 1:j + 1, :], sot.a.re("p (a b) -> p a b", a=2))
        k.free_to(mk_p)
        gS = sb("rw_gS", [128, 8, SB]); bonS = sb("rw_bonS", [128, 8, SB])
        XNs = sb("rw_XNs", [128, 8, 16, 9]); xnc = sb("rw_xnc", [128, 8, SB])
        sst = sb("rw_sst", [128, 8, 16]); sso = sb("rw_sso", [128, 8, 16])
        wol = [sb("rw_wo%d" % i, [128, 8, 128], BF16) for i in range(2)]
        sbi = 16
        k.begin_defer()
        front(sbi)
        scr = {nm: k.dram_tmp("rw%d_s_%s" % (l, nm), [8, 128, 128]) for nm in ["an", "bn", "w", "k", "v", "r", "o"]}
        tmt = [sb("rw_tmt%d" % i, [128, 128]) for i in range(2)]
        tc_ = 0
        for j in range(8):
            partA(sbi, j, 0)
            T = PT[0]
            k.copy(gS[:, j, :], T["g"].a, en='act')
            k.copy(bonS[:, j, :], T["bonus"].a, en='act')
            k.act(T["P"].a, T["lw"].a, AF.Exp)
            for nm, src in [("an", "an"), ("bn", "bn"), ("w", "P"), ("k", "k"), ("v", "v"), ("r", "r")]:
                p = self.psum()
                k.tr(p[:, 0:128], T[src].a, self.ident.a)
                t_ = tmt[tc_ % 2]; tc_ += 1
                k.copy(t_.a, p[:, 0:128], en='act')
                k.dma(scr[nm].v(j * 128 * 128, [[128, 128], [1, 128]]), t_.a)
        Tq = {nm: sb("rw_q_%s" % nm, [128, 8, 128]) for nm in ["an", "bn", "w", "k", "v", "r", "o"]}
        for nm in ["an", "bn", "w", "k", "v", "r"]:
            for b in range(16):
                k.dma(Tq[nm][b * 8:(b + 1) * 8, :, :], scr[nm].v(b * 8 * 128, [[128 * 128, 8], [128, 8], [1, 128]]))
        NI = 16
        S = sb("rw_S", [128, NI, 64]); tmp = sb("rw_tmpS", [128, NI, 64]); sa = sb("rw_sa", [128, NI])
        wkv_in = I[pre + "wkv"]; wkv_out = O[pre + "wkv_s"]
        for h2 in range(2):
          for ih in range(64 // NI):
            soff = h2 * 4096 + ih * NI * 64
            k.dma(S.a.re("p a b -> p (a b)"), wkv_in.v(soff, [[8192, 128], [1, NI * 64]]))
            for t in range(8):
                def bi(nm):
                    return V(Tq[nm], AP(Tq[nm].h, t * 128 + h2 * 64, [[1024, 128], [0, NI], [1, 64]]))
                def bd_(nm):
                    return V(Tq[nm], AP(Tq[nm].h, t * 128 + h2 * 64 + ih * NI, [[1024, 128], [1, NI], [0, 64]]))
                k.tt(tmp.a, S.a, bi("an"), ALU.mult)
                k.reduce(sa.a, tmp.a)
                k.tt(S.a, S.a, bi("w"), ALU.mult)
                k.tt(tmp.a, V(sa, AP(sa.h, 0, [[NI, 128], [1, NI], [0, 64]])), bi("bn"), ALU.mult)
                k.tt(S.a, S.a, tmp.a, ALU.add)
                k.tt(tmp.a, bd_("v"), bi("k"), ALU.mult)
                k.tt(S.a, S.a, tmp.a, ALU.add)
                k.tt(tmp.a, S.a, bi("r"), ALU.mult)
                k.reduce(Tq["o"][:, t, h2 * 64 + ih * NI:h2 * 64 + ih * NI + NI], tmp.a)
            k.dma(wkv_out.v(soff, [[8192, 128], [1, NI * 64]]), S.a.re("p a b -> p (a b)"))
        for b in range(16):
            k.dma(scr["o"].v(b * 8 * 128, [[128 * 128, 8], [128, 8], [1, 128]]), Tq["o"][b * 8:(b + 1) * 8, :, :])
        for j in range(8):
            t_ = tmt[j % 2]
            k.dma(t_.a, scr["o"].v(j * 128 * 128, [[128, 128], [1, 128]]))
            p = self.psum()
            k.tr(p[:, 0:128], t_.a, self.ident.a)
            k.copy(PT[1]["out"].a, p[:, 0:128], en='act')
            partB(j, PT[1]["out"].a, bonS[:, j, :], gS[:, j, :])
        wo_apply(sbi)
        k.end_defer()
        k.end_phase()


class MK(MKBase, S5Mixin, MambaMixin, RwkvMixin):
    pass


def _fm(a):
    return np.ascontiguousarray(a.T.reshape(a.shape[1] // 128, 128, a.shape[0]))


def host_consts():
    c = {}
    c["c_ident"] = np.eye(128, dtype=np.float32)
    c["c_iota"] = np.ascontiguousarray(np.tile(np.arange(1, 129, dtype=np.float32), (128, 1)))
    f = np.arange(128)
    c["c_maskB"] = (f[:, None] // 16 == np.arange(8)[None, :]).astype(np.float32)
    return c


def host_shared(inp):
    d = {}
    gains = np.zeros((128, 13 * 8), np.float32)
    for typ, nm in enumerate(["norm_mix", "norm_ffn", "norm_ple"]):
        for l in range(4):
            gains[:, (typ * 4 + l) * 8:(typ * 4 + l + 1) * 8] = inp[nm][l].reshape(8, 128).T
    gains[:, 96:104] = inp["final_norm"].reshape(8, 128).T
    d["c_gains"] = gains
    for nm in ["ffn_w1", "ffn_w3", "ffn_w2", "ple_proj", "ple_gate", "l1_glu_v", "l1_glu_g"]:
        d[nm] = inp[nm]
    are, aim, ld = inp["l1_a_re"], inp["l1_a_im"], inp["l1_log_dt"]
    ldb = np.broadcast_to(ld[:, None], (64, 64))
    chan = np.stack([x.reshape(32, 2, 64).transpose(1, 2, 0).reshape(128, 32) for x in (are, aim, ldb)], 1)
    d["s5_chan"] = np.ascontiguousarray(chan)
    feat = np.stack([np.broadcast_to(x.reshape(8, 8, 1, 64), (8, 8, 16, 64)).transpose(1, 2, 0, 3).reshape(128, 512) for x in (are, aim, ldb)], 1)
    d["s5_feat"] = np.ascontiguousarray(feat)
    d["s5_bT"] = np.ascontiguousarray(np.stack([x.reshape(8, 8, 64, 16).transpose(1, 3, 0, 2).reshape(128, 512) for x in (inp["l1_b_re"], inp["l1_b_im"])], 1))
    d["s5_cF"] = np.ascontiguousarray(np.stack([x.reshape(8, 8, 16, 64).transpose(1, 2, 0, 3).reshape(128, 512) for x in (inp["l1_c_re"], inp["l1_c_im"])], 1))
    d["s5_d"] = np.ascontiguousarray(inp["l1_d"].reshape(8, 128).T)
    return d


def host_core(inp, c):
    d = {}
    xs = inp["x_sample"][16 * c:16 * c + 16].reshape(128, 1024)
    d["xT"] = _fm(np.concatenate([inp["x_prompt"][c], xs], 0))
    d["pT"] = np.stack([_fm(np.concatenate([inp["p_prompt"][l, c], inp["p_sample"][l, 16 * c:16 * c + 16].reshape(128, 256)], 0)) for l in range(4)])
    st = [inp["state_l1_s5_re"][16 * c:16 * c + 16], inp["state_l1_s5_im"][16 * c:16 * c + 16]]
    d["s5_state"] = np.ascontiguousarray(np.stack([x.reshape(16, 32, 2, 64).transpose(2, 3, 1, 0).reshape(128, 32, 16) for x in st], 1))
    return d


def unpack_s5(pst, sst):
    p = [pst[:, ri, :].reshape(2, 64, 32).transpose(2, 0, 1).reshape(1, 64, 64) for ri in range(2)]
    s = [sst[:, ri].reshape(2, 64, 32, 16).transpose(3, 2, 0, 1).reshape(16, 64, 64) for ri in range(2)]
    return p, s


def host_consts2(c):
    kk = np.arange(128)
    c["c_U"] = (kk[:, None] <= kk[None, :]).astype(np.float32)
    c["c_negP"] = np.where(kk[None, :] >= kk[:, None], 0.0, -1.0e5).astype(np.float32)
    c["c_m01"] = (kk[None, :] >= kk[:, None]).astype(np.float32)
    return c


def host_shared_mamba(inp, d):
    d["l2_in_proj"] = inp["l2_in_proj"]; d["l2_out_proj"] = inp["l2_out_proj"]
    d["m_convw"] = np.ascontiguousarray(inp["l2_conv_w"].T.reshape(24, 128, 4).transpose(1, 0, 2))
    d["m_convb"] = np.ascontiguousarray(inp["l2_conv_b"].reshape(24, 128).T)
    d["m_dtb"] = np.ascontiguousarray(inp["l2_dt_bias"].reshape(32, 1))
    d["m_alog"] = np.ascontiguousarray(np.broadcast_to(inp["l2_a_log"][None, :], (128, 32)))
    d["m_alogq"] = np.ascontiguousarray(np.tile(inp["l2_a_log"], 4).reshape(128, 1))
    d["m_dcol"] = np.ascontiguousarray(np.repeat(inp["l2_d"], 64).reshape(16, 128).T)
    d["m_normw"] = np.ascontiguousarray(inp["l2_norm_w"].reshape(16, 128).T)
    return d


def host_core_mamba(inp, c, d):
    st = inp["state_l2_conv"][16 * c:16 * c + 16]
    d["m_convst"] = np.ascontiguousarray(st.reshape(16, 3, 24, 128).transpose(3, 2, 0, 1))
    d["m_ssm"] = np.ascontiguousarray(inp["state_l2_ssm"][16 * c:16 * c + 16])
    return d


def unpack_mamba(r):
    pconv = np.asarray(r["m_pconv"]).transpose(2, 1, 0).reshape(1, 3, 3072)
    sconv = np.asarray(r["m_sconv"]).transpose(2, 3, 1, 0).reshape(16, 3, 3072)
    pssm = np.asarray(r["m_pssmT"]).transpose(1, 2, 0).reshape(1, 32, 64, 128)
    sssm = np.asarray(r["m_sssm"]).reshape(16, 32, 64, 128)
    return pconv, sconv, pssm, sssm


def host_consts3(c):
    kk = np.arange(128)
    c["c_bones"] = (kk[:, None] // 64 == kk[None, :] // 64).astype(np.float32)
    r = kk % 64
    c["c_MSU"] = (r[:, None] < r[None, :]).astype(np.float32)
    c["c_MIU"] = (r[:, None] <= r[None, :]).astype(np.float32)
    c["c_MSL"] = (r[:, None] > r[None, :]).astype(np.float32)
    return c


def host_shared_rwkv(inp, d, pre):
    names = ["w0", "a0", "k_k", "k_a", "r_k", "lnx_w", "lnx_b"]
    vs = [inp[pre + nm].reshape(1024) for nm in names]
    vs.append(inp[pre + "v0"].reshape(1024) if (pre + "v0") in inp else inp[pre + "w0"].reshape(1024))
    d[pre + "vec"] = np.ascontiguousarray(np.stack([v.reshape(8, 128).T for v in vs], 1))
    d[pre + "mu"] = np.ascontiguousarray(inp[pre + "mu"].reshape(6, 8, 128).transpose(2, 0, 1))
    for nm in ["w1", "a1", "g1", "w2", "a2", "g2", "w_rkv", "w_o"] + (["v1", "v2"] if (pre + "v1") in inp else []):
        d[pre + nm] = inp[pre + nm]
    return d


def host_core_rwkv(inp, c, d, l):
    pre = "l%d_" % l
    if l == 0:
        sh_all, wkv_all = inp["state_l0_shift"], inp["state_l0_wkv"]
    else:
        sh_all, wkv_all = inp["state_l3_shift"], inp["state_l3_wkv"]
    sh = sh_all[16 * c:16 * c + 16]
    d[pre + "shift"] = np.ascontiguousarray(sh.reshape(16, 8, 128).transpose(2, 1, 0))
    d[pre + "wkv"] = np.ascontiguousarray(wkv_all[16 * c:16 * c + 16])
    return d


def unpack_rwkv(r, pre):
    shp = np.asarray(r[pre + "shift_p"]).T.reshape(1, 1024)
    shs = np.asarray(r[pre + "shift_s"]).transpose(2, 1, 0).reshape(16, 1024)
    wp = np.asarray(r[pre + "wkv_p"]).reshape(2, 64, 8, 64).transpose(2, 0, 1, 3).reshape(1, 16, 64, 64)
    ws = np.asarray(r[pre + "wkv_s"]).reshape(16, 16, 64, 64)
    return shp, shs, wp, ws


def build_program():
    m = MK()
    k = m.k
    m.setup_consts()
    m.setup_state()
    I = {}

    def din(nm, shape):
        I[nm] = m.inp(nm, list(shape))

    for nm, shp in INPUT_SHAPES.items():
        if nm not in ("c_ident", "c_gains"):
            din(nm, shp)
    O = {nm: m.out(nm, list(shp)) for nm, shp in OUTPUT_SHAPES.items()}
    m.load_x(I["xT"])
    W0 = {nm: m.precast("w0%s_bf" % nm, I["l0_w_rkv"], 1024, c * 1024 * 1024, 1024, 8) for c, nm in enumerate(["r", "k", "v"])}
    W0["o"] = m.precast("w0o_bf", I["l0_w_o"], 1024, 0, 1024, 8)
    win_bf = m.precast("win_bf", I["l2_in_proj"], 5152, 0, 5152, 8)
    wout_bf = m.precast("wout_bf", I["l2_out_proj"], 1024, 0, 1024, 16)
    W3 = {nm: m.precast("w3%s_bf" % nm, I["l3_w_rkv"], 1024, c * 1024 * 1024, 1024, 8) for c, nm in enumerate(["r", "k", "v"])}
    W3["o"] = m.precast("w3o_bf", I["l3_w_o"], 1024, 0, 1024, 8)
    vfirst = k.dram_tmp("vfirst", [8, 128, TT])

    def ffn_ple(l):
        xn_all = [k.sbp("xn%d" % i, [128, 8, n], BF16) for i, (s, n) in enumerate(TBS)]
        wbuf = [(k.sbp("w1g%d" % i, [128, 8, 512], BF16), k.sbp("w3g%d" % i, [128, 8, 512], BF16), k.sbp("w2g%d" % i, [128, 4, 1024], BF16)) for i in range(2)]
        m.ffn(l, I["ffn_w1"], I["ffn_w3"], I["ffn_w2"], xn_all, wbuf)
        k.end_phase()
        wg = k.sbp("wg", [128, 8, 1024], BF16); wp = k.sbp("wp", [128, 2, 1024], BF16)
        xn_blk = [k.sbp("xnb%d" % i, [128, 8, BLK], BF16) for i in range(2)]
        p_blk = [k.sbp("pb%d" % i, [128, 2, BLK], BF16) for i in range(2)]
        m.ple(l, I["pT"], I["ple_proj"], I["ple_gate"], wg, wp, xn_blk, p_blk)
        k.end_phase()

    m.rwkv_layer(0, I, O, W0, "l0_", vfirst)
    ffn_ple(0)
    m.s5_layer(1, I, O)
    ffn_ple(1)
    m.mamba_layer(2, I, O, win_bf, wout_bf)
    ffn_ple(2)
    m.rwkv_layer(3, I, O, W3, "l3_", vfirst)
    ffn_ple(3)
    ybuf = [k.sbp("ybuf%d" % i, [128, 8, BLK]) for i in range(2)]
    m.final(O["yT"], ybuf)
    k.end_phase()
    k.finish()
    return m


OUTPUT_SHAPES = {
    "yT": (8, 128, TT),
    "l0_shift_p": (128, 8), "l0_shift_s": (128, 8, 16), "l0_wkv_p": (128, 8, 64), "l0_wkv_s": (16, 16, 64, 64),
    "s5_pstate": (128, 2, 32), "s5_sstate": (128, 2, 32, 16),
    "m_pconv": (128, 24, 3), "m_sconv": (128, 24, 16, 3), "m_pssmT": (128, 32, 64), "m_sssm": (16, 32, 64, 128),
    "l3_shift_p": (128, 8), "l3_shift_s": (128, 8, 16), "l3_wkv_p": (128, 8, 64), "l3_wkv_s": (16, 16, 64, 64),
}
INPUT_SHAPES = {}


def kernel(**inputs):
    inp = {k_: np.asarray(v) for k_, v in inputs.items()}
    consts = host_consts3(host_consts2(host_consts()))
    shared = host_shared(inp)
    host_shared_mamba(inp, shared)
    host_shared_rwkv(inp, shared, "l0_")
    host_shared_rwkv(inp, shared, "l3_")
    in_maps = []
    for c in range(8):
        d = dict(consts)
        d.update(shared)
        pc = host_core(inp, c)
        host_core_mamba(inp, c, pc)
        host_core_rwkv(inp, c, pc, 0)
        host_core_rwkv(inp, c, pc, 3)
        d.update(pc)
        in_maps.append({k_: np.ascontiguousarray(v, dtype=np.float32) for k_, v in d.items()})
    INPUT_SHAPES.clear()
    for k_, v in in_maps[0].items():
        INPUT_SHAPES[k_] = v.shape
    m = build_program()
    res = run_bass_kernel_spmd(m.k.nc, in_maps, core_ids=list(range(8)))
    R = res.results
    yp = np.zeros((8, 2048, 1024), np.float32); ys = np.zeros((128, 8, 1024), np.float32)
    outs = {nm: [] for nm in ["shp0", "shs0", "wp0", "ws0", "s5p_re", "s5p_im", "s5s_re", "s5s_im", "pconv", "sconv", "pssm", "sssm", "shp3", "shs3", "wp3", "ws3"]}
    for c in range(8):
        r = R[c]
        y = np.asarray(r["yT"]).reshape(1024, TT).T
        yp[c] = y[:2048]
        ys[16 * c:16 * c + 16] = y[2048:].reshape(16, 8, 1024)
        a, b, cc, dd = unpack_rwkv(r, "l0_")
        outs["shp0"].append(a); outs["shs0"].append(b); outs["wp0"].append(cc); outs["ws0"].append(dd)
        p, s = unpack_s5(np.asarray(r["s5_pstate"]), np.asarray(r["s5_sstate"]))
        outs["s5p_re"].append(p[0]); outs["s5p_im"].append(p[1]); outs["s5s_re"].append(s[0]); outs["s5s_im"].append(s[1])
        a, b, cc, dd = unpack_mamba(r)
        outs["pconv"].append(a); outs["sconv"].append(b); outs["pssm"].append(cc); outs["sssm"].append(dd)
        a, b, cc, dd = unpack_rwkv(r, "l3_")
        outs["shp3"].append(a); outs["shs3"].append(b); outs["wp3"].append(cc); outs["ws3"].append(dd)
    cat = lambda nm: np.ascontiguousarray(np.concatenate(outs[nm], 0), dtype=np.float32)
    return (yp, ys,
            cat("shp0"), cat("wp0"), cat("s5p_re"), cat("s5p_im"), cat("pconv"), cat("pssm"), cat("shp3"), cat("wp3"),
            cat("shs0"), cat("ws0"), cat("s5s_re"), cat("s5s_im"), cat("sconv"), cat("sssm"), cat("shs3"), cat("ws3"))
```
